# Optimizing a Trainium2 kernel written in Bass

```python
import math
import jax
import jax.numpy as jnp
from jax import lax
import numpy as np

D_MODEL = 1024
BATCH = 16
SEQ = 256
DEPTH = 4
DEC_BATCH = 2
DEC_SEQ = 4096
PAST_LEN = 512

GRID_W = 64
N_MIXERS = 3
CHUNK = 64
SHORT_CONV = 3
NORM_EPS = 1e-6
D_FF = 2816

SSM_DI = 2 * D_MODEL
SSM_HEADDIM = 64
SSM_HEADS = SSM_DI // SSM_HEADDIM
SSM_GROUPS = 8
SSM_HPG = SSM_HEADS // SSM_GROUPS
SSM_STATE = 128
SSM_GN = SSM_GROUPS * SSM_STATE
SSM_CONV_DIM = SSM_DI + 2 * SSM_GN
SSM_PROJ = SSM_DI + SSM_CONV_DIM + 2 * SSM_HEADS

GLA_HEADS = 4
GLA_K = D_MODEL // 2
GLA_V = D_MODEL
GLA_DK = GLA_K // GLA_HEADS
GLA_DV = GLA_V // GLA_HEADS
GLA_RANK = 16
GLA_NORMALIZER = 16.0
GLA_PROJ = 2 * GLA_K + 2 * GLA_V

GDN_HEADS = 8
GDN_DK = 128
GDN_DV = 256
GDN_K = GDN_HEADS * GDN_DK
GDN_V = GDN_HEADS * GDN_DV
GDN_CONV_DIM = 2 * GDN_K + GDN_V
GDN_PROJ = GDN_CONV_DIM + GDN_V + 4 * GDN_HEADS

N_SSM_LAYERS = (DEPTH + N_MIXERS - 1) // N_MIXERS
N_GLA_LAYERS = (DEPTH + N_MIXERS - 2) // N_MIXERS
N_GDN_LAYERS = DEPTH // N_MIXERS

kernel_name = 'bidir_hybrid_ssd_gla_gdn_prefix_step'


def rmsnorm(x, w):
    xf = x.astype(jnp.float32)
    y = xf * lax.rsqrt(jnp.mean(xf * xf, axis=-1, keepdims=True) + NORM_EPS)
    return (y * w.astype(jnp.float32)).astype(x.dtype)


def l2norm(x):
    xf = x.astype(jnp.float32)
    return xf * lax.rsqrt(jnp.sum(xf * xf, axis=-1, keepdims=True) + 1e-6)


def modulate(x, norm_w, shift, scale):
    return rmsnorm(x, norm_w) * (1.0 + scale[:, None]) + shift[:, None]


def dwconv1d(x, w, b=None):
    k, L = w.shape[0], x.shape[1]
    pad = k // 2
    xp = jnp.pad(x, ((0, 0), (pad, k - 1 - pad), (0, 0)))
    y = xp[:, 0:L] * w[0]
    for i in range(1, k):
        y = y + xp[:, i:i + L] * w[i]
    return y if b is None else y + b


def dwconv2d_grid(x, w, b):
    bsz, L, C = x.shape
    rows = L // GRID_W
    img = x.reshape(bsz, rows, GRID_W, C)
    y = lax.conv_general_dilated(img, w[:, :, None, :].astype(x.dtype), (1, 1), 'SAME',
                                 dimension_numbers=('NHWC', 'HWIO', 'NHWC'), feature_group_count=C)
    return y.reshape(bsz, L, C) + b


def _chunk(t, nc):
    return t.reshape(t.shape[0], nc, CHUNK, *t.shape[2:])


def _bidir(scan_fn, fwd_inputs, bwd_inputs, h0):
    y_f, s_f = scan_fn(*fwd_inputs, h0[:, 0])
    y_b, s_b = scan_fn(*(jnp.flip(t, axis=1) for t in bwd_inputs), h0[:, 1])
    return y_f + jnp.flip(y_b, axis=1), jnp.stack([s_f, s_b], axis=1)


def ssd_chunked(xv, loga, bm, cm, h0):
    f32 = jnp.float32
    bsz, L = xv.shape[:2]
    nc = L // CHUNK
    xv, loga, bm, cm = (_chunk(t.astype(f32), nc) for t in (xv, loga, bm, cm))
    causal = jnp.tril(jnp.ones((CHUNK, CHUNK), dtype=bool))
    cs = jnp.cumsum(loga, axis=2)
    seg = cs[:, :, :, None] - cs[:, :, None, :]
    decay = jnp.exp(jnp.where(causal[:, :, None, None], seg, -jnp.inf))
    scores = jnp.einsum('bcign,bcjgn->bcijg', cm, bm)
    y_intra = jnp.einsum('bcijgr,bcjgrp->bcigrp', scores[..., None] * decay, xv)
    xw = xv * jnp.exp(cs[:, :, -1:] - cs)[..., None]

    def step(h, inp):
        cm_c, bm_c, xw_c, cs_c = inp
        y_c = jnp.einsum('bign,bgrnp->bigrp', cm_c, h) * jnp.exp(cs_c)[..., None]
        h = h * jnp.exp(cs_c[:, -1])[..., None, None] + jnp.einsum('bjgn,bjgrp->bgrnp', bm_c, xw_c)
        return h, y_c

    h_last, y_inter = lax.scan(step, h0.astype(f32), tuple(jnp.moveaxis(t, 1, 0) for t in (cm, bm, xw, cs)))
    y = y_intra + jnp.moveaxis(y_inter, 0, 1)
    return y.reshape(bsz, L, *y.shape[3:]), h_last


def gla_chunked(q, k, v, logg, h0):
    f32 = jnp.float32
    bsz, L = q.shape[:2]
    nc = L // CHUNK
    q, k, v, logg = (_chunk(t.astype(f32), nc) for t in (q, k, v, logg))
    causal = jnp.tril(jnp.ones((CHUNK, CHUNK), dtype=bool))
    cs = jnp.cumsum(logg, axis=2)
    qg = q * jnp.exp(cs)
    kg = k * jnp.exp(-cs)
    scores = jnp.where(causal[:, :, None], jnp.einsum('bcihd,bcjhd->bcijh', qg, kg), 0.0)
    y_intra = jnp.einsum('bcijh,bcjhe->bcihe', scores, v)
    kw = k * jnp.exp(cs[:, :, -1:] - cs)

    def step(s, inp):
        qg_c, kw_c, v_c, cs_c = inp
        y_c = jnp.einsum('bihd,bhde->bihe', qg_c, s)
        s = s * jnp.exp(cs_c[:, -1])[..., None] + jnp.einsum('bjhd,bjhe->bhde', kw_c, v_c)
        return s, y_c

    s_last, y_inter = lax.scan(step, h0.astype(f32), tuple(jnp.moveaxis(t, 1, 0) for t in (qg, kw, v, cs)))
    y = y_intra + jnp.moveaxis(y_inter, 0, 1)
    return y.reshape(bsz, L, *y.shape[3:]), s_last


def gated_delta_chunked(q, k, v, beta, logg, h0):
    f32 = jnp.float32
    bsz, L = q.shape[:2]
    nc = L // CHUNK
    q, k, v, beta, logg = (_chunk(t.astype(f32), nc) for t in (q, k, v, beta, logg))
    incl = jnp.tril(jnp.ones((CHUNK, CHUNK), dtype=bool))
    strict = jnp.tril(jnp.ones((CHUNK, CHUNK), dtype=bool), -1)
    eye = jnp.eye(CHUNK, dtype=f32)
    cs = jnp.cumsum(logg, axis=2)
    seg = cs[:, :, :, None] - cs[:, :, None, :]
    lmask = jnp.moveaxis(jnp.exp(jnp.where(incl[:, :, None], seg, -jnp.inf)), -1, 2)
    kb = k * beta[..., None]
    m = jnp.where(strict, jnp.einsum('bcihd,bcjhd->bchij', kb, k) * lmask, 0.0)
    t_inv = lax.linalg.triangular_solve(m + eye, jnp.broadcast_to(eye, m.shape),
                                        left_side=True, lower=True, unit_diagonal=True)
    u = jnp.einsum('bchij,bcjhe->bcihe', t_inv, v * beta[..., None])
    wk = jnp.einsum('bchij,bcjhd->bcihd', t_inv, kb * jnp.exp(cs)[..., None])
    a_intra = jnp.einsum('bcihd,bcjhd->bchij', q, k) * lmask
    qd = q * jnp.exp(cs)[..., None]
    kd = k * jnp.exp(cs[:, :, -1:] - cs)[..., None]

    def step(s, inp):
        qd_c, kd_c, u_c, w_c, a_c, cs_c = inp
        v_new = u_c - jnp.einsum('bihd,bhde->bihe', w_c, s)
        o_c = jnp.einsum('bihd,bhde->bihe', qd_c, s) + jnp.einsum('bhij,bjhe->bihe', a_c, v_new)
        s = s * jnp.exp(cs_c[:, -1])[..., None, None] + jnp.einsum('bjhd,bjhe->bhde', kd_c, v_new)
        return s, o_c

    s_last, o = lax.scan(step, h0.astype(f32),
                         tuple(jnp.moveaxis(t, 1, 0) for t in (qd, kd, u, wk, a_intra, cs)))
    o = jnp.moveaxis(o, 0, 1)
    return o.reshape(bsz, L, *o.shape[3:]), s_last


def mamba2_mixer(h, in_w, conv_w, conv_b, dt_bias, a_log, d_skip, norm_w, out_w, h0):
    bsz, L, _ = h.shape
    G, R, N, P = SSM_GROUPS, SSM_HPG, SSM_STATE, SSM_HEADDIM
    z, xbc, dt_raw = jnp.split(h @ in_w, [SSM_DI, SSM_DI + SSM_CONV_DIM], axis=-1)
    xbc = jax.nn.silu(dwconv1d(xbc, conv_w, conv_b))
    xs, bm, cm = jnp.split(xbc, [SSM_DI, SSM_DI + SSM_GN], axis=-1)
    xs = xs.reshape(bsz, L, G, R, P)
    bm = bm.reshape(bsz, L, G, N)
    cm = cm.reshape(bsz, L, G, N)
    dt = jax.nn.softplus((dt_raw.reshape(bsz, L, 2, SSM_HEADS) + dt_bias).astype(jnp.float32))
    dt = dt.reshape(bsz, L, 2, G, R)
    loga = dt * (-jnp.exp(a_log.astype(jnp.float32))).reshape(2, G, R)
    if h0 is None:
        h0 = jnp.zeros((bsz, 2, SSM_HEADS, N, P), jnp.float32)
    h0 = h0.reshape(bsz, 2, G, R, N, P)
    y, state = _bidir(ssd_chunked,
                      (xs * dt[:, :, 0][..., None], loga[:, :, 0], bm, cm),
                      (xs * dt[:, :, 1][..., None], loga[:, :, 1], bm, cm), h0)
    y = (y + xs * d_skip.reshape(G, R, 1)).reshape(bsz, L, SSM_DI).astype(h.dtype) * jax.nn.silu(z)
    y = rmsnorm(y.reshape(bsz, L, G, SSM_DI // G), norm_w.reshape(G, SSM_DI // G)).reshape(bsz, L, SSM_DI)
    return y @ out_w, state.reshape(bsz, 2, SSM_HEADS, N, P).astype(h.dtype)


def gla_mixer(h, in_w, gate_w1, gate_w2, gate_b, norm_w, out_w, h0):
    bsz, L, _ = h.shape
    q, k, v, r = jnp.split(h @ in_w, [GLA_K, 2 * GLA_K, 2 * GLA_K + GLA_V], axis=-1)
    q = q.reshape(bsz, L, GLA_HEADS, GLA_DK) * GLA_DK ** -0.5
    k = k.reshape(bsz, L, GLA_HEADS, GLA_DK)
    v = v.reshape(bsz, L, GLA_HEADS, GLA_DV)
    gate_lr = jnp.einsum('bld,edr->bler', h, gate_w1)
    gate_logit = jnp.einsum('bler,erk->blek', gate_lr, gate_w2) + gate_b
    logg = jax.nn.log_sigmoid(gate_logit.astype(jnp.float32)) / GLA_NORMALIZER
    logg = logg.reshape(bsz, L, 2, GLA_HEADS, GLA_DK)
    if h0 is None:
        h0 = jnp.zeros((bsz, 2, GLA_HEADS, GLA_DK, GLA_DV), jnp.float32)
    o, state = _bidir(gla_chunked, (q, k, v, logg[:, :, 0]), (q, k, v, logg[:, :, 1]), h0)
    o = rmsnorm(o.astype(h.dtype), norm_w) * jax.nn.silu(r.reshape(bsz, L, GLA_HEADS, GLA_DV))
    return o.reshape(bsz, L, GLA_V) @ out_w, state.astype(h.dtype)


def gdn_mixer(h, in_w, conv_w, dt_bias, a_log, norm_w, out_w, h0):
    bsz, L, _ = h.shape
    qkv, z, a_raw, b_raw = jnp.split(h @ in_w, [GDN_CONV_DIM, GDN_CONV_DIM + GDN_V,
                                                GDN_CONV_DIM + GDN_V + 2 * GDN_HEADS], axis=-1)
    qkv = jax.nn.silu(dwconv1d(qkv, conv_w))
    q, k, v = jnp.split(qkv, [GDN_K, 2 * GDN_K], axis=-1)
    q = l2norm(q.reshape(bsz, L, GDN_HEADS, GDN_DK)) * GDN_DK ** -0.5
    k = l2norm(k.reshape(bsz, L, GDN_HEADS, GDN_DK))
    v = v.reshape(bsz, L, GDN_HEADS, GDN_DV)
    beta = jax.nn.sigmoid(b_raw.reshape(bsz, L, 2, GDN_HEADS).astype(jnp.float32))
    logg = -jnp.exp(a_log.astype(jnp.float32)) * jax.nn.softplus(
        (a_raw.reshape(bsz, L, 2, GDN_HEADS) + dt_bias).astype(jnp.float32))
    if h0 is None:
        h0 = jnp.zeros((bsz, 2, GDN_HEADS, GDN_DK, GDN_DV), jnp.float32)
    o, state = _bidir(gated_delta_chunked, (q, k, v, beta[:, :, 0], logg[:, :, 0]),
                      (q, k, v, beta[:, :, 1], logg[:, :, 1]), h0)
    o = rmsnorm(o.astype(h.dtype), norm_w) * jax.nn.silu(z.reshape(bsz, L, GDN_HEADS, GDN_DV))
    return o.reshape(bsz, L, GDN_V) @ out_w, state.astype(h.dtype)


def conv_glu(h, in_w, conv_w, conv_b, out_w, on_grid):
    a, v = jnp.split(h @ in_w, 2, axis=-1)
    a = dwconv2d_grid(a, conv_w, conv_b) if on_grid else dwconv1d(a, conv_w[1], conv_b)
    return (jax.nn.silu(a) * v) @ out_w


def run_trunk(x, cond, ssm_state, gla_state, gdn_state, on_grid, w):
    new_ssm, new_gla, new_gdn = [], [], []
    cond_act = jax.nn.silu(cond)
    for i in range(DEPTH):
        mod = cond_act @ w['ada_w'][i] + w['ada_b'][i]
        sh1, sc1, g1, sh2, sc2, g2 = jnp.split(mod, 6, axis=-1)
        h = modulate(x, w['norm_mix_w'][i], sh1, sc1)
        kind, j = i % N_MIXERS, i // N_MIXERS
        if kind == 0:
            y, st = mamba2_mixer(h, w['ssm_in_w'][j], w['ssm_conv_w'][j], w['ssm_conv_b'][j],
                                 w['ssm_dt_bias'][j], w['ssm_a_log'][j], w['ssm_d'][j],
                                 w['ssm_norm_w'][j], w['ssm_out_w'][j],
                                 None if ssm_state is None else ssm_state[:, j])
            new_ssm.append(st)
        elif kind == 1:
            y, st = gla_mixer(h, w['gla_in_w'][j], w['gla_gate_w1'][j], w['gla_gate_w2'][j],
                              w['gla_gate_b'][j], w['gla_norm_w'][j], w['gla_out_w'][j],
                              None if gla_state is None else gla_state[:, j])
            new_gla.append(st)
        else:
            y, st = gdn_mixer(h, w['gdn_in_w'][j], w['gdn_conv_w'][j], w['gdn_dt_bias'][j],
                              w['gdn_a_log'][j], w['gdn_norm_w'][j], w['gdn_out_w'][j],
                              None if gdn_state is None else gdn_state[:, j])
            new_gdn.append(st)
        x = x + g1[:, None] * y
        h = modulate(x, w['norm_ffn_w'][i], sh2, sc2)
        x = x + g2[:, None] * conv_glu(h, w['ffn_in_w'][i], w['ffn_conv_w'][i], w['ffn_conv_b'][i],
                                       w['ffn_out_w'][i], on_grid)
    return (rmsnorm(x, w['final_norm_w']), jnp.stack(new_ssm, axis=1),
            jnp.stack(new_gla, axis=1), jnp.stack(new_gdn, axis=1))


def setup_inputs(seed: int = 0) -> dict:
    key = jax.random.key(seed)
    keys = iter(jax.random.split(key, 48))
    f32 = jnp.float32
    D = D_MODEL
    nA, nB, nC = N_SSM_LAYERS, N_GLA_LAYERS, N_GDN_LAYERS

    def normal(shape, scale):
        return jax.random.normal(next(keys), shape, f32) * scale

    def gain(shape):
        return 1.0 + normal(shape, 0.02)

    def dt_bias(shape):
        dt = jnp.exp(jax.random.uniform(next(keys), shape, f32, math.log(1e-3), math.log(1e-1)))
        return dt + jnp.log(-jnp.expm1(-dt))

    def a_log(shape):
        return jnp.log(jax.random.uniform(next(keys), shape, f32, 1.0, 16.0))

    return {
        'x_prompt': normal((BATCH, SEQ, D), 1.0),
        'x_sample': normal((DEC_BATCH, DEC_SEQ, D), 1.0),
        'state_ssm': normal((DEC_BATCH, nA, 2, SSM_HEADS, SSM_STATE, SSM_HEADDIM), 0.1),
        'state_gla': normal((DEC_BATCH, nB, 2, GLA_HEADS, GLA_DK, GLA_DV), 0.1),
        'state_gdn': normal((DEC_BATCH, nC, 2, GDN_HEADS, GDN_DK, GDN_DV), 0.1),
        'c': normal((DEC_BATCH, D), 1.0),
        'c_ctx': normal((D,), 1.0),
        'norm_mix_w': gain((DEPTH, D)),
        'norm_ffn_w': gain((DEPTH, D)),
        'ada_w': normal((DEPTH, D, 6 * D), 0.5 * D ** -0.5),
        'ada_b': normal((DEPTH, 6 * D), 0.02),
        'ffn_in_w': normal((DEPTH, D, 2 * D_FF), D ** -0.5),
        'ffn_conv_w': normal((DEPTH, 3, 3, D_FF), 1.0 / 3.0),
        'ffn_conv_b': normal((DEPTH, D_FF), 0.02),
        'ffn_out_w': normal((DEPTH, D_FF, D), D_FF ** -0.5),
        'ssm_in_w': normal((nA, D, SSM_PROJ), D ** -0.5),
        'ssm_conv_w': normal((nA, SHORT_CONV, SSM_CONV_DIM), SHORT_CONV ** -0.5),
        'ssm_conv_b': normal((nA, SSM_CONV_DIM), 0.02),
        'ssm_dt_bias': dt_bias((nA, 2, SSM_HEADS)),
        'ssm_a_log': a_log((nA, 2, SSM_HEADS)),
        'ssm_d': gain((nA, SSM_HEADS)),
        'ssm_norm_w': gain((nA, SSM_DI)),
        'ssm_out_w': normal((nA, SSM_DI, D), SSM_DI ** -0.5),
        'gla_in_w': normal((nB, D, GLA_PROJ), D ** -0.5),
        'gla_gate_w1': normal((nB, 2, D, GLA_RANK), D ** -0.5),
        'gla_gate_w2': normal((nB, 2, GLA_RANK, GLA_K), GLA_RANK ** -0.5),
        'gla_gate_b': normal((nB, 2, GLA_K), 0.1),
        'gla_norm_w': gain((nB, GLA_DV)),
        'gla_out_w': normal((nB, GLA_V, D), GLA_V ** -0.5),
        'gdn_in_w': normal((nC, D, GDN_PROJ), D ** -0.5),
        'gdn_conv_w': normal((nC, SHORT_CONV, GDN_CONV_DIM), SHORT_CONV ** -0.5),
        'gdn_dt_bias': dt_bias((nC, 2, GDN_HEADS)),
        'gdn_a_log': a_log((nC, 2, GDN_HEADS)),
        'gdn_norm_w': gain((nC, GDN_DV)),
        'gdn_out_w': normal((nC, GDN_V, D), GDN_V ** -0.5),
        'final_norm_w': gain((D,)),
    }


def reference(x_prompt, x_sample, state_ssm, state_gla, state_gdn, c, c_ctx,
              norm_mix_w, norm_ffn_w, ada_w, ada_b, ffn_in_w, ffn_conv_w, ffn_conv_b, ffn_out_w,
              ssm_in_w, ssm_conv_w, ssm_conv_b, ssm_dt_bias, ssm_a_log, ssm_d, ssm_norm_w, ssm_out_w,
              gla_in_w, gla_gate_w1, gla_gate_w2, gla_gate_b, gla_norm_w, gla_out_w,
              gdn_in_w, gdn_conv_w, gdn_dt_bias, gdn_a_log, gdn_norm_w, gdn_out_w,
              final_norm_w):
    w = dict(norm_mix_w=norm_mix_w, norm_ffn_w=norm_ffn_w, ada_w=ada_w, ada_b=ada_b,
             ffn_in_w=ffn_in_w, ffn_conv_w=ffn_conv_w, ffn_conv_b=ffn_conv_b, ffn_out_w=ffn_out_w,
             ssm_in_w=ssm_in_w, ssm_conv_w=ssm_conv_w, ssm_conv_b=ssm_conv_b, ssm_dt_bias=ssm_dt_bias,
             ssm_a_log=ssm_a_log, ssm_d=ssm_d, ssm_norm_w=ssm_norm_w, ssm_out_w=ssm_out_w,
             gla_in_w=gla_in_w, gla_gate_w1=gla_gate_w1, gla_gate_w2=gla_gate_w2, gla_gate_b=gla_gate_b,
             gla_norm_w=gla_norm_w, gla_out_w=gla_out_w,
             gdn_in_w=gdn_in_w, gdn_conv_w=gdn_conv_w, gdn_dt_bias=gdn_dt_bias, gdn_a_log=gdn_a_log,
             gdn_norm_w=gdn_norm_w, gdn_out_w=gdn_out_w, final_norm_w=final_norm_w)
    cond_ctx = jnp.broadcast_to(c_ctx, (x_prompt.shape[0], c_ctx.shape[0]))
    y_prompt, new_state_ssm, new_state_gla, new_state_gdn = run_trunk(
        x_prompt, cond_ctx, None, None, None, False, w)
    y_sample, _, _, _ = run_trunk(x_sample, c, state_ssm, state_gla, state_gdn, True, w)
    return (y_prompt, y_sample, new_state_ssm, new_state_gla, new_state_gdn)
```

```python
import contextlib
import numpy as np
import concourse.bass as bass
import concourse.mybir as mybir
from concourse.bass_utils import run_bass_kernel_spmd

F32 = mybir.dt.float32
BF16 = mybir.dt.bfloat16
AF = mybir.ActivationFunctionType
ALU = mybir.AluOpType
AX = mybir.AxisListType
ENGS = ['pe', 'act', 'dve', 'pool', 'sp']
EPS = 1e-6
DBG_STOP = None


class Buf:
    __slots__ = ('name', 'lw', 'rd', 'acc')

    def __init__(self, name, acc=False):
        self.name = name
        self.lw = {}
        self.rd = {}
        self.acc = acc


class _Rec:
    def __init__(self):
        self.call = None

    def __getattr__(self, name):
        def f(*a, **kw):
            self.call = (name, a, kw)
            return self
        return f


class Prog:
    def __init__(self, nc):
        self.nc = nc
        self.q = {e: [] for e in ENGS}
        self.cnt = {e: 0 for e in ENGS}
        self.known = {e: {} for e in ENGS}
        self.dma_cnt = {}
        self.ninst = 0

    def _need(self, eng, key, val, src, waits):
        if src == eng and eng == 'pe':
            return
        if self.known[eng].get(key, 0) >= val:
            return
        if waits.get(key, 0) < val:
            waits[key] = val

    def emit(self, eng, fn, reads=(), writes=(), dma_key=None):
        rec = _Rec()
        fn(rec)
        _c = rec.call
        fn = lambda e, _c=_c: getattr(e, _c[0])(*_c[1], **_c[2])
        waits = {}
        for b in reads:
            for k, (v, s) in b.lw.items():
                self._need(eng, k, v, s, waits)
        for b in writes:
            for k, (v, s) in b.lw.items():
                self._need(eng, k, v, s, waits)
            for k, (v, s) in b.rd.items():
                self._need(eng, k, v, s, waits)
        for k, v in waits.items():
            self.known[eng][k] = v
        if dma_key is None:
            self.cnt[eng] += 1
            key, val, src, inc = eng, self.cnt[eng], eng, 1
        else:
            self.dma_cnt[dma_key] = self.dma_cnt.get(dma_key, 0) + 16
            key, val, src, inc = dma_key, self.dma_cnt[dma_key], None, 16
        for b in reads:
            if b.rd.get(key, (0, None))[0] < val:
                b.rd[key] = (val, src)
        for b in writes:
            if b.acc:
                b.lw[key] = (val, src)
            else:
                b.lw = {key: (val, src)}
            b.rd = {}
        self.q[eng].append((list(waits.items()), fn, (key, inc)))
        self.ninst += 1
        return (key, val, src)

    def wait_all(self, eng, bufs):
        waits = {}
        for b in bufs:
            for k, (v, s) in b.lw.items():
                self._need(eng, k, v, s, waits)
        self.q[eng].append((list(waits.items()), None, None))

    def run(self):
        nc = self.nc
        keys = list(ENGS) + sorted(self.dma_cnt.keys())
        with contextlib.ExitStack() as st:
            sems = {k: st.enter_context(nc.semaphore("s%d" % i)) for i, k in enumerate(keys)}
            block = st.enter_context(nc.Block())

            def mk(engname):
                def body(e):
                    for waits, fn, inc in self.q[engname]:
                        for k, v in waits:
                            e.wait_ge(sems[k], v)
                        if fn is not None:
                            fn(e).then_inc(sems[inc[0]], inc[1])
                return body
            block.tensor(mk('pe'))
            block.scalar(mk('act'))
            block.vector(mk('dve'))
            block.gpsimd(mk('pool'))
            block.sync(mk('sp'))


class Tl:
    def __init__(self, t, name, acc=False):
        self.t = t
        self.b = Buf(name, acc)
        self.name = name

    def __getitem__(self, k):
        return self.t[k]


class KB:
    def __init__(self, nc, st):
        self.nc = nc
        self.st = st
        self.P = Prog(nc)
        self.ps = [Tl(st.enter_context(nc.psum_tensor("ps%d" % i, [128, 512], F32)), "ps%d" % i) for i in range(8)]
        self.psi = 0
        self.nkey = 0
        self.keymap = {}
        self.groups = {}
        self.nview = 0
        self.cast_tok = Buf('castdma', acc=True)
        self.tr_tok = Buf('trdma', acc=True)

    def sb(self, name, shape, dt):
        return Tl(self.st.enter_context(self.nc.sbuf_tensor(name, shape, dt)), name)

    def dram(self, name, shape, dt, kind=None):
        if kind is None:
            t = self.nc.dram_tensor(name, shape, dt)
        else:
            t = self.nc.dram_tensor(name, shape, dt, kind=kind)
        return Tl(t.ap(), name, acc=True)

    def view(self, parent, ap, group=None, share=None):
        v = Tl(ap, parent.name)
        if group is None:
            v.b = parent.b
        else:
            self.nview += 1
            v.name = "%s#%d" % (parent.name, self.nview)
            if share is not None:
                v.b = share.b
                v.name = share.name
            self.groups.setdefault(group, []).append((parent, v))
        return v

    @staticmethod
    def _union(dicts):
        out = {}
        for d_ in dicts:
            for k_, (v_, s_) in d_.items():
                if out.get(k_, (0, None))[0] < v_:
                    out[k_] = (v_, s_)
        return out

    def enter(self, group):
        for parent, v in self.groups.get(group, []):
            v.b.lw = self._union([parent.b.lw, parent.b.rd])
            v.b.rd = {}

    def leave(self, group):
        parents = {}
        for parent, v in self.groups.get(group, []):
            parents.setdefault(id(parent), (parent, []))[1].append(v)
        for parent, vs in parents.values():
            parent.b.lw = self._union([parent.b.lw, parent.b.rd] + [v.b.lw for v in vs] + [v.b.rd for v in vs])
            parent.b.rd = {}

    def psum(self):
        p = self.ps[self.psi % 8]
        self.psi += 1
        return p

    def _b(self, l):
        return [x.b for x in l]

    def M(self, fn, r=(), w=()):
        return self.P.emit('pe', fn, self._b(r), self._b(w))

    def A(self, fn, r=(), w=()):
        return self.P.emit('act', fn, self._b(r), self._b(w))

    def V(self, fn, r=(), w=()):
        return self.P.emit('dve', fn, self._b(r), self._b(w))

    def G(self, fn, r=(), w=()):
        return self.P.emit('pool', fn, self._b(r), self._b(w))

    def _key(self, s):
        if s not in self.keymap:
            self.keymap[s] = "d%04d" % len(self.keymap)
        return self.keymap[s]

    def ld(self, dst, dst_ap, src, src_ap, q='sp', tr=False):
        key = self._key('ld:' + dst.name)
        if tr:
            fn = lambda e: e.dma_start_transpose(out=dst_ap, in_=src_ap)
        else:
            fn = lambda e: e.dma_start(out=dst_ap, in_=src_ap)
        if tr:
            return self.P.emit(q, fn, [src.b, self.cast_tok], [dst.b, self.tr_tok], dma_key=key)
        if q == 'pool':
            return self.P.emit(q, fn, [src.b, self.tr_tok], [dst.b, self.cast_tok], dma_key=key)
        return self.P.emit(q, fn, [src.b], [dst.b], dma_key=key)

    def stq(self, dst, dst_ap, src, src_ap, q='sp'):
        key = self._key('st:' + src.name)
        return self.P.emit(q, lambda e: e.dma_start(out=dst_ap, in_=src_ap), [src.b], [dst.b], dma_key=key)


def bc(ap, shape, axis):
    return ap.unsqueeze(axis).broadcast_to(shape)


def build_program(SEQS, LAYERS, dbg=()):
    T = sum(L for _, L, _ in SEQS)
    NT = T // 512
    assert T % 512 == 0
    NCH = T // 128
    NP = sum(1 for s in SEQS if s[2] == 'p')
    nc = bass.Bass("TRN2", target_bir_lowering=False)
    st = contextlib.ExitStack()
    k = KB(nc, st)
    IN = lambda name, shape, dt=F32: k.dram(name, shape, dt, kind="ExternalInput")
    OUT = lambda name, shape, dt=F32: k.dram(name, shape, dt, kind="ExternalOutput")
    dbgset = set(dbg)
    SCR = lambda name, shape, dt=BF16: k.dram(name, shape, dt, kind=("ExternalOutput" if name in dbgset else None))

    xT = IN("xT", [1024, T])
    condT = IN("condT", [128, 8, 2])
    cm = IN("cmask", [128, 10, 128])
    nmw = IN("nmw", [128, 4, 8]); nfw = IN("nfw", [128, 4, 8]); fnw = IN("fnw", [128, 8])
    ada_w = IN("ada_w", [4, 1024, 6144]); ada_b = IN("ada_b", [128, 4, 48])
    ffn_in_w = IN("ffn_in_w", [4, 1024, 5632]); ffn_cw = IN("ffn_cw", [128, 4, 22, 9]); ffn_cb = IN("ffn_cb", [128, 4, 22])
    ffn_out_w = IN("ffn_out_w", [4, 2816, 1024])
    ssm_in_w = IN("ssm_in_w", [2, 1024, 6208]); ssm_cw = IN("ssm_cw", [128, 2, 32, 3]); ssm_cb = IN("ssm_cb", [128, 2, 32])
    ssm_dtb = IN("ssm_dtb", [2, 1, 64]); ssm_alog = IN("ssm_alog", [2, 1, 64]); ssm_d = IN("ssm_d", [2, 1, 32])
    ssm_nw = IN("ssm_nw", [2, 1, 2048]); ssm_out_w = IN("ssm_out_w", [2, 2048, 1024])
    gla_in_w = IN("gla_in_w", [1, 1024, 3072]); gla_w1 = IN("gla_w1", [1, 1024, 128]); gla_w2 = IN("gla_w2", [1, 64, 1024]); gla_gb = IN("gla_gb", [1, 1, 1024])
    gla_nw = IN("gla_nw", [128, 1, 2]); gla_out_w = IN("gla_out_w", [1, 1024, 1024])
    gdn_in_w = IN("gdn_in_w", [1, 1024, 6176]); gdn_cw = IN("gdn_cw", [128, 1, 32, 3])
    gdn_dtb = IN("gdn_dtb", [1, 1, 16]); gdn_alog = IN("gdn_alog", [1, 1, 16]); gdn_nw = IN("gdn_nw", [128, 1, 2])
    gdn_out_w = IN("gdn_out_w", [1, 2048, 1024])
    lmk = IN("lmask", [128, 14, 128])
    st_ssm = IN("st_ssm", [2, 2, 128, 2048]); st_gla = IN("st_gla", [1, 2, 4, 128, 256]); st_gdn = IN("st_gdn", [1, 2, 8, 128, 256])
    yT = OUT("yT", [1024, T])
    ns_ssm = OUT("ns_ssm", [max(NP, 1), 2, 2, 128, 2048]); ns_gla = OUT("ns_gla", [max(NP, 1), 1, 2, 4, 128, 256])
    ns_gdn = OUT("ns_gdn", [max(NP, 1), 1, 2, 8, 128, 256])
    X = k.dram("X", [1024, T], F32, kind=("ExternalOutput" if "X" in dbgset else None))
    H = SCR("H", [1024, T])
    S1 = SCR("S1", [4096, T])
    S2 = SCR("S2", [4096, T])
    S3 = SCR("S3", [T, 2048])
    S4 = SCR("S4", [T, 2048])
    S5 = SCR("S5", [2816, T])
    S6 = SCR("S6", [2048, T])
    S7 = SCR("S7", [T, 1024])

    cmb = k.sb("cmb", [128, 10, 128], BF16)
    k.ld(cmb, cmb[:], cm, cm.t[:, :, :], q='pool')
    Uf, Ub, Vf, Vb, ID, ONES, UGf, UGb, VGf, VGb = range(10)
    UG_ = {0: UGf, 1: UGb}; VG_ = {0: VGf, 1: VGb}
    U_ = {0: Uf, 1: Ub}; V_ = {0: Vf, 1: Vb}
    nmw_s = k.sb("nmw_s", [128, 4, 8], F32); nfw_s = k.sb("nfw_s", [128, 4, 8], F32); fnw_s = k.sb("fnw_s", [128, 8], F32)
    k.ld(nmw_s, nmw_s[:], nmw, nmw.t[:, :, :]); k.ld(nfw_s, nfw_s[:], nfw, nfw.t[:, :, :]); k.ld(fnw_s, fnw_s[:], fnw, fnw.t[:, :])
    adab_s = k.sb("adab_s", [128, 4, 48], F32)
    k.ld(adab_s, adab_s[:], ada_b, ada_b.t[:, :, :])
    mod = k.sb("mod", [128, 4, 48, 2], F32)
    gm = k.sb("gm", [128, 4, 2, 8, 2], F32)

    xbuf = [k.sb("xb%d" % i, [128, 8, 512], F32) for i in range(2)]
    for t in range(NT):
        xb = xbuf[t % 2]
        k.ld(xb, xb[:], xT, xT.t[:, t * 512:(t + 1) * 512].rearrange("(c p) t -> p c t", p=128))
        k.stq(X, X.t[:, t * 512:(t + 1) * 512].rearrange("(c p) t -> p c t", p=128), xb, xb[:])
    cnd = k.sb("cnd", [128, 8, 2], F32); cndb = k.sb("cndb", [128, 8, 2], BF16)
    k.ld(cnd, cnd[:], condT, condT.t[:, :, :])
    k.A(lambda e: e.activation(out=cndb[:], in_=cnd[:], func=AF.Silu), r=[cnd], w=[cndb])
    wbuf = [k.sb("wb%d" % i, [128, 12288], BF16) for i in range(2)]
    wi = [0]

    def load_w(src, src_ap3, kc, ncols, after=()):
        wb = wbuf[wi[0] % 2]; wi[0] += 1
        view = wb[:, 0:kc * ncols].rearrange("p (c n) -> p c n", c=kc)
        key = k._key('ld:' + wb.name)
        for c in range(kc):
            k.P.emit('pool', lambda e: e.dma_start(out=view[:, c, :], in_=src_ap3[c * 128:(c + 1) * 128, :]), [src.b, k.tr_tok] + [a.b for a in after], [wb.b, k.cast_tok], dma_key=key)
        return wb, view

    for (li, kind, j) in LAYERS:
        for qd in range(6):
            wb, wv = load_w(ada_w, ada_w.t[li, :, qd * 1024:(qd + 1) * 1024], 8, 1024)
            ps = k.psum()
            for cb in range(8):
                for c in range(8):
                    k.M(lambda e, c=c, cb=cb, wv=wv, ps=ps: e.matmul(ps[:, cb * 2:cb * 2 + 2], wv[:, c, cb * 128:(cb + 1) * 128], cndb[:, c, :],
                                                                       start=(c == 0), stop=(c == 7)), r=[wb, cndb], w=[ps])
            k.V(lambda e, ps=ps, qd=qd, li=li: e.tensor_tensor(out=mod[:, li, qd * 8:(qd + 1) * 8, :], in0=ps[:, 0:16].rearrange("p (a b) -> p a b", b=2),
                                                                in1=bc(adab_s[:, li, qd * 8:(qd + 1) * 8], [128, 8, 2], 2), op=ALU.add), r=[ps, adab_s], w=[mod])
        for sub, nw in ((0, nmw_s), (1, nfw_s)):
            sc = 1 + 3 * sub
            k.V(lambda e, sub=sub, li=li, sc=sc: e.tensor_scalar(out=gm[:, li, sub], in0=mod[:, li, sc * 8:(sc + 1) * 8, :], scalar1=1.0, scalar2=None, op0=ALU.add), r=[mod], w=[gm])
            k.V(lambda e, sub=sub, li=li, nw=nw: e.tensor_tensor(out=gm[:, li, sub], in0=gm[:, li, sub], in1=bc(nw[:, li, :], [128, 8, 2], 2), op=ALU.mult), r=[gm, nw], w=[gm])

    def ci_of_tile(t):
        off = t * 512
        for (o, L, kd) in SEQS:
            if o <= off < o + L:
                return 0 if kd == 'p' else 1
        raise ValueError

    sqb = k.sb("sqb", [128, 8, 512], BF16)
    rsb = k.sb("rsb", [128, 512], F32)
    inb = [k.sb("inb%d" % i, [128, 22 * 512], BF16) for i in range(2)]
    hb = [k.view(inb[i], inb[i][:, 0:4096].rearrange("p (c n) -> p c n", c=8)) for i in range(2)]
    tmpf = k.sb("tmpf", [128, 512], F32)

    def norm_stage(li, sub, final=False):
        for t in range(NT):
            ci = ci_of_tile(t)
            xb = xbuf[t % 2]
            sl = slice(t * 512, (t + 1) * 512)
            k.ld(xb, xb[:], X, X.t[:, sl].rearrange("(c p) t -> p c t", p=128))
            k.A(lambda e, xb=xb: e.activation(out=sqb[:], in_=xb[:], func=AF.Square), r=[xb], w=[sqb])
            ps = k.psum()
            for c in range(8):
                k.M(lambda e, c=c, ps=ps: e.matmul(ps[:], cmb[:, ONES, :], sqb[:, c, :], start=(c == 0), stop=(c == 7)), r=[cmb, sqb], w=[ps])
            k.A(lambda e, ps=ps: e.activation(out=rsb[:], in_=ps[:], func=AF.Sqrt, bias=EPS, scale=1.0 / 1024), r=[ps], w=[rsb])
            k.V(lambda e: e.reciprocal(out=rsb[:], in_=rsb[:]), r=[rsb], w=[rsb])
            if final:
                for c in range(8):
                    k.V(lambda e, c=c, xb=xb: e.scalar_tensor_tensor(out=xb[:, c, :], in0=xb[:, c, :], scalar=fnw_s[:, c:c + 1], in1=rsb[:], op0=ALU.mult, op1=ALU.mult),
                        r=[xb, fnw_s, rsb], w=[xb])
                k.stq(yT, yT.t[:, sl].rearrange("(c p) t -> p c t", p=128), xb, xb[:])
            else:
                h = hb[t % 2]
                shq = 0 if sub == 0 else 3
                for c in range(8):
                    k.V(lambda e, c=c, xb=xb: e.scalar_tensor_tensor(out=tmpf[:], in0=xb[:, c, :], scalar=gm[:, li, sub, c, ci:ci + 1], in1=rsb[:], op0=ALU.mult, op1=ALU.mult),
                        r=[xb, gm, rsb], w=[tmpf])
                    k.A(lambda e, c=c, h=h: e.activation(out=h[:, c, :], in_=tmpf[:], func=AF.Identity, bias=mod[:, li, shq * 8 + c, ci:ci + 1], scale=1.0),
                        r=[tmpf, mod], w=[h])
                k.stq(H, H.t[:, sl].rearrange("(c p) t -> p c t", p=128), h, h[:])

    ini = [0]

    def gemm(src, src_tm, kc, W, Wap, ncols, out_mode, epi, gw=None, pre=None, post=None):
        if gw is None:
            gw = max(128, (12288 // kc) // 128 * 128)
            gw = min(gw, 2048)
        seq = [(g0, min(gw, ncols - g0), t) for g0 in range(0, ncols, gw) for t in range(NT)]
        loaded = {}

        def emit_load(i, wb_dep):
            g0, gn, t = seq[i]
            ib = inb[ini[0] % 2]; ini[0] += 1
            iv = ib[:, 0:kc * 512].rearrange("p (c n) -> p c n", c=kc)
            sl = slice(t * 512, (t + 1) * 512)
            if src_tm:
                key = k._key('ld:' + ib.name)
                for c in range(kc):
                    k.P.emit('sp', lambda e: e.dma_start_transpose(out=iv[:, c, :], in_=src.t[sl, c * 128:(c + 1) * 128]), [src.b, k.cast_tok] + ([wb_dep.b] if wb_dep is not None else []), [ib.b, k.tr_tok], dma_key=key)
            else:
                k.ld(ib, iv[:], src, src.t[0:kc * 128, sl].rearrange("(c p) t -> p c t", p=128))
            loaded[i] = (ib, iv)
        wb = wv = None
        emit_load(0, None)
        for i, (g0, gn, t) in enumerate(seq):
            if t == 0:
                wb, wv = load_w(W, Wap[:, g0:g0 + gn], kc, gn, after=(inb if src_tm else ()))
            if i + 1 < len(seq):
                emit_load(i + 1, wb)
            ib, iv = loaded.pop(i)
            ncb = gn // 128
            if pre is not None:
                pre(t, g0 // 128, g0 // 128 + ncb)
            if out_mode == 'fm':
                for cb in range(ncb):
                    ps = k.psum()
                    for c in range(kc):
                        k.M(lambda e: e.matmul(ps[:], wv[:, c, cb * 128:(cb + 1) * 128], iv[:, c, :], start=(c == 0), stop=(c == kc - 1)), r=[wb, ib], w=[ps])
                    epi(ps, (g0 // 128) + cb, t, None, cb == ncb - 1)
            else:
                for sbk in range(4):
                    for p0 in range(0, gn, 512):
                        pw = min(512, gn - p0)
                        ps = k.psum()
                        for c in range(kc):
                            k.M(lambda e: e.matmul(ps[:, 0:pw], iv[:, c, sbk * 128:(sbk + 1) * 128], wv[:, c, p0:p0 + pw], start=(c == 0), stop=(c == kc - 1)), r=[wb, ib], w=[ps])
                        epi(ps, g0 + p0, t, sbk, False)
            if post is not None:
                post(t, g0 // 128, g0 // 128 + ncb)

    stg = [k.sb("stg%d" % i, [128, 512], BF16) for i in range(3)]
    sti = [0]

    stgb = [k.sb("stgb%d" % i, [128, 4, 512], BF16) for i in range(2)]
    stgi = [0]

    def epi_store_fm(dst, row0, func=None, scale=1.0):
        state = {'n': 0, 'cb0': None, 'buf': None}

        def epi(ps, cb, t, sub, last):
            if state['n'] == 0:
                state['buf'] = stgb[stgi[0] % 2]; stgi[0] += 1
                state['cb0'] = cb
            s_ = state['buf']; n = state['n']
            if func is None:
                if cb % 2 == 0:
                    k.V(lambda e: e.tensor_copy(out=s_[:, n, :], in_=ps[:]), r=[ps], w=[s_])
                else:
                    k.A(lambda e: e.activation(out=s_[:, n, :], in_=ps[:], func=AF.Copy), r=[ps], w=[s_])
            else:
                k.A(lambda e: e.activation(out=s_[:, n, :], in_=ps[:], func=func, scale=scale), r=[ps], w=[s_])
            state['n'] += 1
            if state['n'] == 4 or last:
                n_ = state['n']; c0 = state['cb0']
                k.stq(dst, dst.t[row0 + c0 * 128:row0 + (c0 + n_) * 128, t * 512:(t + 1) * 512].rearrange("(c p) t -> p c t", p=128), s_, s_[:, 0:n_, :])
                state['n'] = 0
        return epi

    def epi_store_tm(dst, col0, func=None):
        def epi(ps, c0, t, sub, last):
            s = stg[sti[0] % 3]; sti[0] += 1
            if func is None:
                k.V(lambda e, s=s, ps=ps: e.tensor_copy(out=s[:], in_=ps[:]), r=[ps], w=[s])
            else:
                k.A(lambda e, s=s, ps=ps: e.activation(out=s[:], in_=ps[:], func=func), r=[ps], w=[s])
            r0 = t * 512 + sub * 128
            k.stq(dst, dst.t[r0:r0 + 128, col0 + c0:col0 + c0 + 512], s, s[:])
        return epi

    def resid_hooks(li, gq):
        def pre(t, c_lo, c_hi):
            xb = xbuf[t % 2]
            k.ld(xb, xb[:, c_lo:c_hi, :], X, X.t[c_lo * 128:c_hi * 128, t * 512:(t + 1) * 512].rearrange("(c p) t -> p c t", p=128))

        def epi(ps, cb, t, sub, last):
            ci = ci_of_tile(t)
            xb = xbuf[t % 2]
            k.V(lambda e: e.scalar_tensor_tensor(out=xb[:, cb, :], in0=ps[:], scalar=mod[:, li, gq * 8 + cb, ci:ci + 1], in1=xb[:, cb, :], op0=ALU.mult, op1=ALU.add),
                r=[ps, mod, xb], w=[xb])

        def post(t, c_lo, c_hi):
            xb = xbuf[t % 2]
            k.stq(X, X.t[c_lo * 128:c_hi * 128, t * 512:(t + 1) * 512].rearrange("(c p) t -> p c t", p=128), xb, xb[:, c_lo:c_hi, :])
        return dict(epi=epi, pre=pre, post=post)

    cvin = [k.sb("cvin%d" % i, [128, 10, 66], BF16) for i in range(2)]
    dgb = [k.sb("dgb%d" % i, [128, 9, 128], BF16) for i in range(2)]
    cvout = [k.sb("cvout%d" % i, [128, 512], BF16) for i in range(2)]
    cvi = [0]

    def seq_of(tok):
        for (o, L, kd) in SEQS:
            if o <= tok < o + L:
                return (o, L, kd)
        raise ValueError

    def conv_stage(src, dst, nblk, wfn, bfn, grid_for_sample, wdeps, l2blocks=(), qscale_blocks=()):
        for blk in range(nblk):
            rs_ = slice(blk * 128, (blk + 1) * 128)
            dg = dgb[blk % 2]
            for tap in (range(9) if grid_for_sample else range(3, 6)):
                k.G(lambda e: e.tensor_scalar(out=dg[:, tap, :], in0=cmb[:, ID, :], scalar1=wfn(blk, tap), scalar2=None, op0=ALU.mult), r=[cmb] + wdeps, w=[dg])
            for t in range(NT):
                t0 = t * 512
                (o, L, kd) = seq_of(t0)
                ci_ = cvin[cvi[0] % 2]; co = cvout[cvi[0] % 2]; cvi[0] += 1
                grid = grid_for_sample and kd == 's'
                k.G(lambda e, ci_=ci_: e.memset(ci_[:], 0.0), w=[ci_])
                if grid:
                    ROWS_ = L // 64
                    r0 = (t0 - o) // 64
                    lo = max(0, r0 - 1); hi = min(ROWS_, r0 + 9)
                    k.ld(ci_, ci_[:, lo - (r0 - 1):hi - (r0 - 1), 1:65], src, src.t[rs_, o + lo * 64:o + hi * 64].rearrange("p (r c) -> p r c", c=64))
                    taps = [(dy, dx) for dy in range(3) for dx in range(3)]
                    srcv = lambda dy, dx, ci_=ci_: ci_[:, dy:dy + 8, dx:dx + 64]
                else:
                    flat = ci_[:].rearrange("p r c -> p (r c)")
                    taps = [(1, dx) for dx in range(3)]
                    pieces = []
                    pos = t0
                    while pos < t0 + 512:
                        (o2, L2, _) = seq_of(pos)
                        pe_ = min(t0 + 512, o2 + L2)
                        pieces.append((pos, pe_, o2, L2)); pos = pe_
                ps = k.psum()
                if grid:
                    pv = ps[:].rearrange("p (r c) -> p r c", c=64)
                    for ti, (dy, dx) in enumerate(taps):
                        k.M(lambda e: e.matmul(pv, dg[:, dy * 3 + dx, :], srcv(dy, dx), start=(ti == 0), stop=(ti == 8)), r=[dg, ci_], w=[ps])
                else:
                    base = 0
                    for (pa, pb, o2, L2) in pieces:
                        n_ = pb - pa
                        lo = max(o2, pa - 1); hi = min(o2 + L2, pb + 1)
                        k.ld(ci_, flat[:, base + (lo - (pa - 1)):base + (hi - (pa - 1))], src, src.t[rs_, lo:hi])
                        for dx in range(3):
                            k.M(lambda e: e.matmul(ps[:, pa - t0:pb - t0], dg[:, 3 + dx, :], flat[:, base + dx:base + dx + n_], start=(dx == 0), stop=(dx == 2)), r=[dg, ci_], w=[ps])
                        base += n_ + 2
                bap = bfn(blk) if bfn is not None else None
                if bap is not None:
                    k.A(lambda e: e.activation(out=co[:], in_=ps[:], func=AF.Silu, bias=bap, scale=1.0), r=[ps] + wdeps, w=[co])
                else:
                    k.A(lambda e: e.activation(out=co[:], in_=ps[:], func=AF.Silu), r=[ps], w=[co])
                if blk in l2blocks:
                    k.V(lambda e, co=co: e.tensor_tensor(out=sqb[:, 0, :], in0=co[:], in1=co[:], op=ALU.mult), r=[co], w=[sqb])
                    ps = k.psum()
                    k.M(lambda e, ps=ps: e.matmul(ps[:], cmb[:, ONES, :], sqb[:, 0, :], start=True, stop=True), r=[cmb, sqb], w=[ps])
                    k.A(lambda e, ps=ps: e.activation(out=rsb[:], in_=ps[:], func=AF.Sqrt, bias=1e-6, scale=1.0), r=[ps], w=[rsb])
                    k.V(lambda e: e.reciprocal(out=rsb[:], in_=rsb[:]), r=[rsb], w=[rsb])
                    qs = (128.0 ** -0.5) if blk in qscale_blocks else 1.0
                    k.V(lambda e, co=co, qs=qs: e.scalar_tensor_tensor(out=co[:], in0=co[:], scalar=qs, in1=rsb[:], op0=ALU.mult, op1=ALU.mult), r=[co, rsb], w=[co])
                k.stq(dst, dst.t[rs_, t0:t0 + 512], co, co[:])

    fcw_s = k.sb("fcw_s", [128, 4, 22, 9], F32); fcb_s = k.sb("fcb_s", [128, 4, 22], F32)
    k.ld(fcw_s, fcw_s[:], ffn_cw, ffn_cw.t[:, :, :, :]); k.ld(fcb_s, fcb_s[:], ffn_cb, ffn_cb.t[:, :, :])
    aab = [k.sb("aab%d" % i, [128, 512], BF16) for i in range(2)]
    aai = [0]

    def ffn(li):
        norm_stage(li, 1)
        gemm(H, False, 8, ffn_in_w, ffn_in_w.t[li, :, 0:2816], 2816, 'fm', epi_store_fm(S1, 0))
        conv_stage(S1, S2, 22, lambda blk, tap: fcw_s[:, li, blk, tap:tap + 1], lambda blk: fcb_s[:, li, blk:blk + 1], True, [fcw_s, fcb_s])

        def epi_v(ps, cb, t, sub, last):
            a = aab[aai[0] % 2]; aai[0] += 1
            k.ld(a, a[:], S2, S2.t[cb * 128:(cb + 1) * 128, t * 512:(t + 1) * 512])
            s = stg[sti[0] % 3]; sti[0] += 1
            k.V(lambda e, s=s, ps=ps, a=a: e.tensor_tensor(out=s[:], in0=ps[:], in1=a[:], op=ALU.mult), r=[ps, a], w=[s])
            k.stq(S5, S5.t[cb * 128:(cb + 1) * 128, t * 512:(t + 1) * 512], s, s[:])
        gemm(H, False, 8, ffn_in_w, ffn_in_w.t[li, :, 2816:5632], 2816, 'fm', epi_v)
        gemm(S5, False, 22, ffn_out_w, ffn_out_w.t[li, :, :], 1024, 'fm', gw=512, **resid_hooks(li, 5))

    scw_s = k.sb("scw_s", [128, 2, 32, 3], F32); scb_s = k.sb("scb_s", [128, 2, 32], F32)
    k.ld(scw_s, scw_s[:], ssm_cw, ssm_cw.t[:, :, :, :]); k.ld(scb_s, scb_s[:], ssm_cb, ssm_cb.t[:, :, :])
    dt_sb = k.view(xbuf[0], xbuf[0][:].rearrange("p c n -> p (c n)")[:, 0:NCH * 64].rearrange("p (c n) -> p c n", n=64), group='ssdx')
    la_sb = k.view(xbuf[1], xbuf[1][:].rearrange("p c n -> p (c n)")[:, 0:NCH * 64].rearrange("p (c n) -> p c n", n=64), group='ssdx')
    lab_sb = k.sb("lab_sb", [128, 1, 64], BF16)
    bc64 = k.sb("bc64", [128, 64], F32); nega = k.sb("nega", [128, 64], F32); dsk = k.sb("dsk", [128, 32], F32)
    nwb = k.sb("nwb", [128, 2048], BF16)
    W0, W1 = wbuf[0], wbuf[1]
    Ebuf = k.view(W0, W0[:, 0:4096], group='ssd'); DTb = k.view(W0, W0[:, 4096:8192], group='ssd')
    xv = k.view(W0, W0[:, 8192:10240], group='ssd'); xw = k.view(W0, W0[:, 10240:12288], group='ssd')
    Sbf = k.view(W1, W1[:, 0:2048], group='ssd'); STb = k.view(W1, W1[:, 2048:3072], group='ssd')
    yo = [k.view(W1, W1[:, 3072 + i * 2048:3072 + (i + 1) * 2048], group='ssd') for i in range(2)]
    xs_tm = [k.view(inb[i], inb[i][:, 0:2048], group='ssd') for i in range(2)]
    B_tm = [k.view(inb[i], inb[i][:, 2048:3072], group='ssd') for i in range(2)]
    BC_fm = [k.view(inb[i], inb[i][:, 3072:5120].rearrange("p (g t) -> p g t", t=128), group='ssd') for i in range(2)]
    yfb = [k.view(inb[i], inb[i][:, 5120:7168], group='ssd') for i in range(2)]
    zsb = [k.view(inb[i], inb[i][:, 7168:9216], group='ssd') for i in range(2)]
    ecs = k.sb("ecs", [128, 64], F32)
    Sst = k.sb("Sst", [128, 2048], F32)
    yc = k.sb("yc", [128, 2048], F32)
    ytmp = tmpf
    ss8 = k.sb("ss8", [128, 16], F32)

    def chunk_order(L, d):
        n = L // 128
        return list(range(n)) if d == 0 else list(range(n - 1, -1, -1))

    def ssd(li, j):
        norm_stage(li, 0)
        W = ssm_in_w
        gemm(H, False, 8, W, W.t[j, :, 0:2048], 2048, 'tm', epi_store_tm(S3, 0, AF.Silu))
        gemm(H, False, 8, W, W.t[j, :, 2048:6144], 4096, 'fm', epi_store_fm(S1, 0))
        if DBG_STOP == 'gemm':
            return
        conv_stage(S1, S2, 32, lambda blk, tap: scw_s[:, j, blk, tap - 3:tap - 2], lambda blk: scb_s[:, j, blk:blk + 1], False, [scw_s, scb_s])
        if DBG_STOP == 'conv':
            return
        k.ld(bc64, bc64[:], ssm_dtb, ssm_dtb.t[j].partition_broadcast(128))
        k.ld(nega, nega[:], ssm_alog, ssm_alog.t[j].partition_broadcast(128))
        k.A(lambda e: e.activation(out=nega[:], in_=nega[:], func=AF.Exp), r=[nega], w=[nega])
        k.V(lambda e: e.tensor_scalar(out=nega[:], in0=nega[:], scalar1=-1.0, scalar2=None, op0=ALU.mult), r=[nega], w=[nega])
        k.ld(dsk, dsk[:], ssm_d, ssm_d.t[j].partition_broadcast(128))
        k.ld(nwb, nwb[:], ssm_nw, ssm_nw.t[j].partition_broadcast(128), q='pool')

        def epi_dt(ps, c0, t, sub, last):
            ch = t * 4 + sub
            k.V(lambda e, ps=ps, ch=ch: e.tensor_tensor(out=dt_sb[:, ch, :], in0=ps[:, 0:64], in1=bc64[:], op=ALU.add), r=[ps, bc64], w=[dt_sb])
            k.A(lambda e, ch=ch: e.activation(out=dt_sb[:, ch, :], in_=dt_sb[:, ch, :], func=AF.Exp), r=[dt_sb], w=[dt_sb])
            k.A(lambda e, ch=ch: e.activation(out=dt_sb[:, ch, :], in_=dt_sb[:, ch, :], func=AF.Ln, bias=1.0, scale=1.0), r=[dt_sb], w=[dt_sb])
            k.V(lambda e, ch=ch: e.tensor_tensor(out=la_sb[:, ch, :], in0=dt_sb[:, ch, :], in1=nega[:], op=ALU.mult), r=[dt_sb, nega], w=[la_sb])
        k.enter('ssdx')
        gemm(H, False, 8, W, W.t[j, :, 6144:6208], 64, 'tm', epi_dt)
        if DBG_STOP == 'dt':
            return
        k.enter('ssd')
        pidx = 0
        for (o, L, kd) in (SEQS[:1] if DBG_STOP in ('scan1', 'scan1b') else SEQS):
            for d in ((0,) if DBG_STOP == 'scan1' else (0, 1)):
                if kd == 'p':
                    k.V(lambda e: e.memset(Sst[:], 0.0), w=[Sst])
                    k.V(lambda e: e.memset(Sbf[:], 0.0), w=[Sbf])
                else:
                    k.ld(Sst, Sst[:], st_ssm, st_ssm.t[j, d])
                    k.A(lambda e: e.activation(out=Sbf[:], in_=Sst[:], func=AF.Copy), r=[Sst], w=[Sbf])
                ie = 127 if d == 0 else 0
                order = chunk_order(L, d)

                def ssd_loads(n_, c):
                    t0 = o + c * 128
                    tsl = slice(t0, t0 + 128)
                    xs = xs_tm[n_ % 2]; Bt = B_tm[n_ % 2]; BC = BC_fm[n_ % 2]
                    for q4 in range(4):
                        k.ld(xs, xs[:, q4 * 512:(q4 + 1) * 512], S2, S2.t[q4 * 512:(q4 + 1) * 512, tsl], tr=True)
                    for q4 in range(2):
                        k.ld(Bt, Bt[:, q4 * 512:(q4 + 1) * 512], S2, S2.t[2048 + q4 * 512:2048 + (q4 + 1) * 512, tsl], tr=True)
                    k.ld(BC, BC[:], S2, S2.t[2048:4096, tsl].rearrange("(g n) t -> n g t", n=128))
                    if d == 1:
                        k.ld(yfb[n_ % 2], yfb[n_ % 2][:], S4, S4.t[tsl, :])
                        k.ld(zsb[n_ % 2], zsb[n_ % 2][:], S3, S3.t[tsl, :])
                ssd_loads(0, order[0])
                for n_, c in enumerate(order):
                    if n_ + 1 < len(order):
                        ssd_loads(n_ + 1, order[n_ + 1])
                    t0 = o + c * 128; ch = t0 // 128
                    tsl = slice(t0, t0 + 128)
                    xs = xs_tm[n_ % 2]; Bt = B_tm[n_ % 2]; BC = BC_fm[n_ % 2]
                    la = la_sb[:, ch, d * 32:(d + 1) * 32]; lab = lab_sb[:, 0, 0:32]; dtc = dt_sb[:, ch, d * 32:(d + 1) * 32]
                    k.V(lambda e: e.tensor_copy(out=lab_sb[:, 0, 0:32], in_=la), r=[la_sb], w=[lab_sb])
                    E3 = Ebuf[:].rearrange("p (h i) -> p h i", i=128)
                    D3 = DTb[:].rearrange("p (h i) -> p h i", i=128)
                    k.G(lambda e, la=la: e.tensor_tensor(out=E3, in0=bc(la, [128, 32, 128], 2), in1=bc(cmb[:, U_[d], :], [128, 32, 128], 1), op=ALU.mult), r=[la_sb, cmb], w=[Ebuf])
                    for q8 in range(8):
                        ps = k.psum()
                        k.M(lambda e, ps=ps, q8=q8: e.matmul(ps[:], cmb[:, V_[d], :], Ebuf[:, q8 * 512:(q8 + 1) * 512], start=True, stop=True), r=[cmb, Ebuf], w=[ps])
                        k.A(lambda e, ps=ps, q8=q8: e.activation(out=DTb[:, q8 * 512:(q8 + 1) * 512], in_=ps[:], func=AF.Exp), r=[ps], w=[DTb])
                    for half in range(2):
                        ps = k.psum()
                        for g4 in range(4):
                            g = half * 4 + g4
                            k.M(lambda e, ps=ps, g=g, g4=g4, BC=BC: e.matmul(ps[:, g4 * 128:(g4 + 1) * 128], BC[:, g, :], BC[:, 8 + g, :], start=True, stop=True), r=[BC], w=[ps])
                        k.V(lambda e, ps=ps, half=half: e.tensor_tensor(out=STb[:, half * 512:(half + 1) * 512].rearrange("p (g i) -> p g i", i=128), in0=ps[:].rearrange("p (g i) -> p g i", i=128),
                                                                        in1=bc(cmb[:, U_[d], :], [128, 4, 128], 1), op=ALU.mult), r=[ps, cmb], w=[STb])
                    k.V(lambda e: e.tensor_copy(out=wdec[:], in_=DTb[:].rearrange("p (h i) -> p h i", i=128)[:, :, ie]), r=[DTb], w=[wdec])
                    k.V(lambda e: e.tensor_tensor(out=DTb[:].rearrange("p (g r i) -> p g r i", r=4, i=128), in0=DTb[:].rearrange("p (g r i) -> p g r i", r=4, i=128),
                                                   in1=bc(STb[:].rearrange("p (g i) -> p g i", i=128), [128, 8, 4, 128], 2), op=ALU.mult), r=[DTb, STb], w=[DTb])
                    k.G(lambda e, xs=xs, dtc=dtc: e.tensor_tensor(out=xv[:].rearrange("p (h q) -> p h q", q=64), in0=xs[:].rearrange("p (h q) -> p h q", q=64),
                                                                   in1=bc(dtc, [128, 32, 64], 2), op=ALU.mult), r=[xs, dt_sb], w=[xv])
                    k.G(lambda e: e.tensor_tensor(out=xw[:].rearrange("p (h q) -> p h q", q=64), in0=xv[:].rearrange("p (h q) -> p h q", q=64),
                                                   in1=bc(wdec[:], [128, 32, 64], 2), op=ALU.mult), r=[xv, wdec], w=[xw])
                    ps = k.psum()
                    k.M(lambda e, ps=ps, lab=lab: e.matmul(ps[:, 0:32], cmb[:, U_[d], :], lab, start=True, stop=True), r=[cmb, lab_sb], w=[ps])
                    k.M(lambda e, ps=ps, lab=lab: e.matmul(ps[:, 32:64], cmb[:, ONES, :], lab, start=True, stop=True), r=[cmb, lab_sb], w=[ps])
                    k.A(lambda e, ps=ps: e.activation(out=ecs[:], in_=ps[:, 0:64], func=AF.Exp), r=[ps], w=[ecs])
                    for pr in range(4):
                        psA = k.psum(); psB = k.psum()
                        for gg in range(2):
                            g = pr * 2 + gg
                            for r_ in range(4):
                                h = g * 4 + r_
                                k.M(lambda e, psA=psA, h=h, gg=gg, r_=r_: e.matmul(psA[:, gg * 256 + r_ * 64:gg * 256 + (r_ + 1) * 64], DTb[:, h * 128:(h + 1) * 128], xv[:, h * 64:(h + 1) * 64],
                                                                                       start=True, stop=True), r=[DTb, xv], w=[psA])
                            k.M(lambda e, psB=psB, g=g, gg=gg, BC=BC: e.matmul(psB[:, gg * 256:(gg + 1) * 256], BC[:, 8 + g, :], Sbf[:, g * 256:(g + 1) * 256], start=True, stop=True), r=[BC, Sbf], w=[psB])
                        k.V(lambda e, psB=psB, pr=pr: e.tensor_tensor(out=ytmp[:].rearrange("p (h q) -> p h q", q=64), in0=psB[:].rearrange("p (h q) -> p h q", q=64),
                                                                       in1=bc(ecs[:, pr * 8:(pr + 1) * 8], [128, 8, 64], 2), op=ALU.mult), r=[psB, ecs], w=[ytmp])
                        k.V(lambda e, psA=psA, pr=pr: e.tensor_tensor(out=yc[:, pr * 512:(pr + 1) * 512], in0=psA[:], in1=ytmp[:], op=ALU.add), r=[psA, ytmp], w=[yc])
                    k.G(lambda e: e.tensor_tensor(out=Sst[:].rearrange("p (h q) -> p h q", q=64), in0=Sst[:].rearrange("p (h q) -> p h q", q=64),
                                                   in1=bc(ecs[:, 32:64], [128, 32, 64], 2), op=ALU.mult), r=[Sst, ecs], w=[Sst])
                    for pr in range(4):
                        ps = k.psum()
                        for gg in range(2):
                            g = pr * 2 + gg
                            k.M(lambda e, ps=ps, g=g, gg=gg, Bt=Bt: e.matmul(ps[:, gg * 256:(gg + 1) * 256], Bt[:, g * 128:(g + 1) * 128], xw[:, g * 256:(g + 1) * 256], start=True, stop=True), r=[Bt, xw], w=[ps])
                        k.V(lambda e, ps=ps, pr=pr: e.tensor_tensor(out=Sst[:, pr * 512:(pr + 1) * 512], in0=ps[:], in1=Sst[:, pr * 512:(pr + 1) * 512], op=ALU.add), r=[ps, Sst], w=[Sst])
                    k.A(lambda e: e.activation(out=Sbf[:], in_=Sst[:], func=AF.Copy), r=[Sst], w=[Sbf])
                    if d == 0:
                        y_ = yo[n_ % 2]
                        k.A(lambda e, y_=y_: e.activation(out=y_[:], in_=yc[:], func=AF.Copy), r=[yc], w=[y_])
                        k.stq(S4, S4.t[tsl, :], y_, y_[:])
                    else:
                        yf_ = yfb[n_ % 2]; zs_ = zsb[n_ % 2]; y_ = yo[n_ % 2]
                        k.V(lambda e, yf_=yf_: e.tensor_tensor(out=yc[:], in0=yc[:], in1=yf_[:], op=ALU.add), r=[yc, yf_], w=[yc])
                        k.G(lambda e, xs=xs: e.tensor_tensor(out=xv[:].rearrange("p (h q) -> p h q", q=64), in0=xs[:].rearrange("p (h q) -> p h q", q=64),
                                                              in1=bc(dsk[:], [128, 32, 64], 2), op=ALU.mult), r=[xs, dsk], w=[xv])
                        k.V(lambda e: e.tensor_tensor(out=yc[:], in0=yc[:], in1=xv[:], op=ALU.add), r=[yc, xv], w=[yc])
                        k.V(lambda e, zs_=zs_: e.tensor_tensor(out=yc[:], in0=yc[:], in1=zs_[:], op=ALU.mult), r=[yc, zs_], w=[yc])
                        k.G(lambda e: e.tensor_tensor(out=Ebuf[:, 0:2048], in0=yc[:], in1=yc[:], op=ALU.mult), r=[yc], w=[Ebuf])
                        k.V(lambda e: e.tensor_reduce(out=ss8[:, 0:8], in_=Ebuf[:, 0:2048].rearrange("p (g q) -> p g q", q=256), axis=AX.X, op=ALU.add), r=[Ebuf], w=[ss8])
                        k.A(lambda e: e.activation(out=ss8[:, 0:8], in_=ss8[:, 0:8], func=AF.Sqrt, bias=EPS, scale=1.0 / 256), r=[ss8], w=[ss8])
                        k.V(lambda e: e.reciprocal(out=ss8[:, 0:8], in_=ss8[:, 0:8]), r=[ss8], w=[ss8])
                        k.V(lambda e: e.tensor_tensor(out=yc[:].rearrange("p (g q) -> p g q", q=256), in0=yc[:].rearrange("p (g q) -> p g q", q=256),
                                                       in1=bc(ss8[:, 0:8], [128, 8, 256], 2), op=ALU.mult), r=[yc, ss8], w=[yc])
                        k.V(lambda e, y_=y_: e.tensor_tensor(out=y_[:], in0=yc[:], in1=nwb[:], op=ALU.mult), r=[yc, nwb], w=[y_])
                        k.stq(S4, S4.t[tsl, :], y_, y_[:])
                if kd == 'p':
                    k.stq(ns_ssm, ns_ssm.t[pidx, j, d], Sst, Sst[:])
            if kd == 'p':
                pidx += 1
        k.leave('ssd'); k.leave('ssdx')
        if DBG_STOP in ('scan1', 'scan1b', 'scan'):
            return
        gemm(S4, True, 16, ssm_out_w, ssm_out_w.t[j, :, :], 1024, 'fm', **resid_hooks(li, 2))

    HAS_GLA = any(kd_ == 'gla' for _, kd_, _ in LAYERS)
    lrT = k.sb("lrT", [64, 512], BF16)
    gnw_s = k.sb("gnw_s", [128, 1, 2], F32)
    k.ld(gnw_s, gnw_s[:], gla_nw, gla_nw.t[:, :, :])
    w2s = k.sb("w2s", [64, 1024 if HAS_GLA else 8], BF16)
    gbb = k.sb("gbb", [128, 1024 if HAS_GLA else 8], BF16)
    gq_fm = [k.view(inb[i], inb[i][:, 0:512].rearrange("p (h t) -> p h t", t=128), group='gla') for i in range(2)]
    gk_fm = [k.view(inb[i], inb[i][:, 512:1024].rearrange("p (h t) -> p h t", t=128), group='gla') for i in range(2)]
    gk_tm = [k.view(inb[i], inb[i][:, 1024:1536], group='gla') for i in range(2)]
    gv_tm = [k.view(inb[i], inb[i][:, 1536:2560], group='gla') for i in range(2)]
    gsp_tm = [k.view(inb[i], inb[i][:, 2560:3072], group='gla') for i in range(2)]
    gof = [k.view(inb[i], inb[i][:, 3072:4096].rearrange("p (b t) -> p b t", t=128), group='gla') for i in range(2)]
    grs = [k.view(inb[i], inb[i][:, 4096:5120].rearrange("p (b t) -> p b t", t=128), group='gla') for i in range(2)]
    gqg = k.view(W0, W0[:, 0:512].rearrange("p (h t) -> p h t", t=128), group='gla')
    gkg = k.view(W0, W0[:, 512:1024].rearrange("p (h t) -> p h t", t=128), group='gla')
    gAT = k.view(W0, W0[:, 1024:1536].rearrange("p (h t) -> p h t", t=128), group='gla')
    gkw = k.view(W0, W0[:, 1536:2048], group='gla')
    gsq = k.view(W0, W0[:, 2048:3072].rearrange("p (b t) -> p b t", t=128), group='gla')
    gon = [k.view(W1, W1[:, 3072 + i * 1024:3072 + (i + 1) * 1024].rearrange("p (b t) -> p b t", t=128), group='gla') for i in range(2)]
    gsps = [k.sb("gsps%d" % i, [128, 1024 if HAS_GLA else 8], BF16) for i in range(2)]
    Sg = k.view(Sst, Sst[:, 0:1024].rearrange("p (h e) -> p h e", e=256), group='gla')
    Sgb = k.view(W1, W1[:, 0:1024].rearrange("p (h e) -> p h e", e=256), group='gla')
    gob = k.view(yc, yc[:, 0:1024].rearrange("p (b t) -> p b t", t=128), group='gla')

    def gla(li, j):
        norm_stage(li, 0)
        W = gla_in_w
        gemm(H, False, 8, W, W.t[j, :, 0:512], 512, 'fm', epi_store_fm(S1, 0, AF.Copy, 128.0 ** -0.5))
        gemm(H, False, 8, W, W.t[j, :, 512:1024], 512, 'fm', epi_store_fm(S1, 512))
        gemm(H, False, 8, W, W.t[j, :, 1024:2048], 1024, 'tm', epi_store_tm(S3, 0))
        gemm(H, False, 8, W, W.t[j, :, 2048:3072], 1024, 'fm', epi_store_fm(S6, 0, AF.Silu))

        k.ld(w2s, w2s[:], gla_w2, gla_w2.t[j], q='pool')
        k.ld(gbb, gbb[:], gla_gb, gla_gb.t[j].partition_broadcast(128), q='pool')

        def epi_lr(ps, cb, t, sub, last):
            k.V(lambda e: e.tensor_copy(out=lrT[:], in_=ps[0:64, :]), r=[ps], w=[lrT])
            for sbk in range(4):
                ch = t * 4 + sbk
                tsl = slice(ch * 128, (ch + 1) * 128)
                sp_ = gsps[ch % 2]
                for hf in range(2):
                    ps2 = k.psum()
                    k.M(lambda e: e.matmul(ps2[:], lrT[:, sbk * 128:(sbk + 1) * 128], w2s[:, hf * 512:(hf + 1) * 512], start=True, stop=True), r=[lrT, w2s], w=[ps2])
                    k.V(lambda e: e.tensor_tensor(out=rsb[:], in0=ps2[:], in1=gbb[:, hf * 512:(hf + 1) * 512], op=ALU.add), r=[ps2, gbb], w=[rsb])
                    k.A(lambda e: e.activation(out=rsb[:], in_=rsb[:], func=AF.Exp, scale=-1.0), r=[rsb], w=[rsb])
                    k.A(lambda e: e.activation(out=sp_[:, hf * 512:(hf + 1) * 512], in_=rsb[:], func=AF.Ln, bias=1.0, scale=1.0), r=[rsb], w=[sp_])
                k.stq(S7, S7.t[tsl, :], sp_, sp_[:])
        gemm(H, False, 8, gla_w1, gla_w1.t[j, :, :], 128, 'fm', epi_lr)
        k.enter('gla')
        pidx = 0
        for (o, L, kd) in SEQS:
            for d in (0, 1):
                if kd == 'p':
                    k.V(lambda e: e.memset(Sg[:], 0.0), w=[Sg])
                    k.V(lambda e: e.memset(Sgb[:], 0.0), w=[Sgb])
                else:
                    k.ld(Sg, Sg[:], st_gla, st_gla.t[j, d].rearrange("h d e -> d h e"))
                    k.A(lambda e: e.activation(out=Sgb[:], in_=Sg[:], func=AF.Copy), r=[Sg], w=[Sgb])
                ie = 127 if d == 0 else 0
                order = chunk_order(L, d)

                def gla_loads(n_, c):
                    t0 = o + c * 128
                    tsl = slice(t0, t0 + 128)
                    q_ = gq_fm[n_ % 2]; k_ = gk_fm[n_ % 2]; kt = gk_tm[n_ % 2]; v_ = gv_tm[n_ % 2]; sp_ = gsp_tm[n_ % 2]
                    k.ld(q_, q_[:], S1, S1.t[0:512, tsl].rearrange("(h d) t -> d h t", d=128))
                    k.ld(k_, k_[:], S1, S1.t[512:1024, tsl].rearrange("(h d) t -> d h t", d=128))
                    k.ld(kt, kt[:], S1, S1.t[512:1024, tsl], tr=True)
                    k.ld(v_, v_[:], S3, S3.t[tsl, 0:1024])
                    k.ld(sp_, sp_[:], S7, S7.t[tsl, d * 512:(d + 1) * 512])
                    if d == 1:
                        k.ld(gof[n_ % 2], gof[n_ % 2][:], S5, S5.t[0:1024, tsl].rearrange("(b p) t -> p b t", p=128))
                        k.ld(grs[n_ % 2], grs[n_ % 2][:], S6, S6.t[0:1024, tsl].rearrange("(b p) t -> p b t", p=128))
                gla_loads(0, order[0])
                for n_, c in enumerate(order):
                    if n_ + 1 < len(order):
                        gla_loads(n_ + 1, order[n_ + 1])
                    t0 = o + c * 128
                    tsl = slice(t0, t0 + 128)
                    q_ = gq_fm[n_ % 2]; k_ = gk_fm[n_ % 2]; kt = gk_tm[n_ % 2]; v_ = gv_tm[n_ % 2]; sp_ = gsp_tm[n_ % 2]
                    ps = k.psum()
                    k.M(lambda e: e.matmul(ps[:], cmb[:, VG_[d], :], sp_[:], start=True, stop=True), r=[cmb, sp_], w=[ps])
                    k.A(lambda e: e.activation(out=rsb[:], in_=ps[:], func=AF.Exp), r=[ps], w=[rsb])
                    k.V(lambda e: e.tensor_tensor(out=gkw[:], in0=kt[:], in1=rsb[:], op=ALU.mult), r=[kt, rsb], w=[gkw])
                    psc = k.psum()
                    for h in range(4):
                        k.M(lambda e: e.matmul(psc[:, h * 128:(h + 1) * 128], sp_[:, h * 128:(h + 1) * 128], cmb[:, UG_[d], :], start=True, stop=True), r=[sp_, cmb], w=[psc])
                    k.A(lambda e: e.activation(out=tmpf[:], in_=psc[:], func=AF.Exp), r=[psc], w=[tmpf])
                    k.V(lambda e: e.tensor_tensor(out=gqg[:], in0=q_[:], in1=tmpf[:].rearrange("p (h t) -> p h t", t=128), op=ALU.mult), r=[q_, tmpf], w=[gqg])
                    k.A(lambda e: e.activation(out=tmpf[:], in_=psc[:], func=AF.Exp, scale=-1.0), r=[psc], w=[tmpf])
                    k.V(lambda e: e.tensor_tensor(out=gkg[:], in0=k_[:], in1=tmpf[:].rearrange("p (h t) -> p h t", t=128), op=ALU.mult), r=[k_, tmpf], w=[gkg])
                    k.A(lambda e: e.activation(out=ecs[:, 0:4], in_=psc[:].rearrange("p (h t) -> p h t", t=128)[:, :, ie], func=AF.Exp), r=[psc], w=[ecs])
                    pss = k.psum()
                    for h in range(4):
                        k.M(lambda e: e.matmul(pss[:, h * 128:(h + 1) * 128], gkg[:, h, :], gqg[:, h, :], start=True, stop=True), r=[gkg, gqg], w=[pss])
                    k.V(lambda e: e.tensor_tensor(out=gAT[:], in0=pss[:].rearrange("p (h t) -> p h t", t=128), in1=bc(cmb[:, U_[d], :], [128, 4, 128], 1), op=ALU.mult), r=[pss, cmb], w=[gAT])
                    pso = [k.psum(), k.psum()]
                    for h in range(4):
                        for eb in range(2):
                            blk = h * 2 + eb
                            po = pso[blk // 4]
                            k.M(lambda e: e.matmul(po[:, (blk % 4) * 128:(blk % 4 + 1) * 128], v_[:, h * 256 + eb * 128:h * 256 + (eb + 1) * 128], gAT[:, h, :], start=True, stop=False), r=[v_, gAT], w=[po])
                            k.M(lambda e: e.matmul(po[:, (blk % 4) * 128:(blk % 4 + 1) * 128], Sgb[:, h, eb * 128:(eb + 1) * 128], gqg[:, h, :], start=False, stop=True), r=[Sgb, gqg], w=[po])
                    for h in range(4):
                        pst = k.psum()
                        k.M(lambda e: e.matmul(pst[:, 0:256], gkw[:, h * 128:(h + 1) * 128], v_[:, h * 256:(h + 1) * 256], start=True, stop=True), r=[gkw, v_], w=[pst])
                        k.V(lambda e: e.scalar_tensor_tensor(out=Sg[:, h, :], in0=Sg[:, h, :], scalar=ecs[:, h:h + 1], in1=pst[:, 0:256], op0=ALU.mult, op1=ALU.add), r=[Sg, ecs, pst], w=[Sg])
                    on_ = gon[n_ % 2]
                    if d == 0:
                        for hf in range(2):
                            k.A(lambda e: e.activation(out=on_[:, hf * 4:(hf + 1) * 4, :], in_=pso[hf][:].rearrange("p (b t) -> p b t", t=128), func=AF.Copy), r=[pso[hf]], w=[on_])
                        k.stq(S5, S5.t[0:1024, tsl].rearrange("(b p) t -> p b t", p=128), on_, on_[:])
                    else:
                        of_ = gof[n_ % 2]; rs_ = grs[n_ % 2]
                        for hf in range(2):
                            k.V(lambda e: e.tensor_tensor(out=gob[:, hf * 4:(hf + 1) * 4, :], in0=pso[hf][:].rearrange("p (b t) -> p b t", t=128), in1=of_[:, hf * 4:(hf + 1) * 4, :], op=ALU.add), r=[pso[hf], of_], w=[gob])
                        k.G(lambda e: e.tensor_tensor(out=gsq[:], in0=gob[:], in1=gob[:], op=ALU.mult), r=[gob], w=[gsq])
                        psn = k.psum()
                        for h in range(4):
                            for eb in range(2):
                                k.M(lambda e: e.matmul(psn[:, h * 128:(h + 1) * 128], cmb[:, ONES, :], gsq[:, h * 2 + eb, :], start=(eb == 0), stop=(eb == 1)), r=[cmb, gsq], w=[psn])
                        k.A(lambda e: e.activation(out=rsb[:], in_=psn[:], func=AF.Sqrt, bias=EPS, scale=1.0 / 256), r=[psn], w=[rsb])
                        k.V(lambda e: e.reciprocal(out=rsb[:], in_=rsb[:]), r=[rsb], w=[rsb])
                        for eb in range(2):
                            gv_ = gob[:].rearrange("p (h b) t -> p h b t", b=2)[:, :, eb, :]
                            k.V(lambda e: e.scalar_tensor_tensor(out=gv_, in0=gv_, scalar=gnw_s[:, j, eb:eb + 1], in1=rsb[:].rearrange("p (h t) -> p h t", t=128), op0=ALU.mult, op1=ALU.mult), r=[gob, gnw_s, rsb], w=[gob])
                        k.V(lambda e: e.tensor_tensor(out=on_[:], in0=gob[:], in1=rs_[:], op=ALU.mult), r=[gob, rs_], w=[on_])
                        k.stq(S5, S5.t[0:1024, tsl].rearrange("(b p) t -> p b t", p=128), on_, on_[:])
                    k.A(lambda e: e.activation(out=Sgb[:], in_=Sg[:], func=AF.Copy), r=[Sg], w=[Sgb])
                if kd == 'p':
                    k.stq(ns_gla, ns_gla.t[pidx, j, d].rearrange("h d e -> d h e"), Sg, Sg[:])
            if kd == 'p':
                pidx += 1
        k.leave('gla')
        gemm(S5, False, 8, gla_out_w, gla_out_w.t[j, :, :], 1024, 'fm', **resid_hooks(li, 2))

    gcw_s = k.sb("gcw_s", [128, 1, 32, 3], F32)
    k.ld(gcw_s, gcw_s[:], gdn_cw, gdn_cw.t[:, :, :, :])
    dnw_s = k.sb("dnw_s", [128, 1, 2], F32)
    k.ld(dnw_s, dnw_s[:], gdn_nw, gdn_nw.t[:, :, :])
    xb0f = xbuf[0][:].rearrange("p c n -> p (c n)")
    d_beta = k.view(xbuf[0], xb0f[:, 0:NCH * 16].rearrange("p (c n) -> p c n", n=16), group='gdnx')
    d_nbeta = k.view(xbuf[0], xb0f[:, NCH * 16:2 * NCH * 16].rearrange("p (c n) -> p c n", n=16), group='gdnx')
    d_lg = k.view(xbuf[0], xb0f[:, 2 * NCH * 16:3 * NCH * 16].rearrange("p (c n) -> p c n", n=16), group='gdnx')
    d_lgb = k.sb("d_lgb", [128, NCH, 16], BF16)
    dbc16 = k.sb("dbc16", [128, 16], F32); dnega = k.sb("dnega", [128, 16], F32)
    h3 = lambda ap: ap.rearrange("p (h t) -> p h t", t=128)
    dq_fm = [k.view(inb[i], h3(inb[i][:, 0:1024]), group='gdn') for i in range(2)]
    dk_fm = [k.view(inb[i], h3(inb[i][:, 1024:2048]), group='gdn') for i in range(2)]
    dk_tm = [k.view(inb[i], h3(inb[i][:, 2048:3072]), group='gdn') for i in range(2)]
    dv_tm = [k.view(inb[i], inb[i][:, 3072:5120].rearrange("p (h e) -> p h e", e=256), group='gdn') for i in range(2)]
    dof = [k.view(inb[i], h3(inb[i][:, 5120:7168]), group='gdn') for i in range(2)]
    dzs = [k.view(inb[i], h3(inb[i][:, 7168:9216]), group='gdn') for i in range(2)]
    dET = k.view(W0, h3(W0[:, 0:1024]), group='gdn'); dEN = k.view(W0, h3(W0[:, 1024:2048]), group='gdn')
    dLT = k.view(W0, h3(W0[:, 2048:3072]), group='gdn'); dLTm = k.view(W0, h3(W0[:, 3072:4096]), group='gdn'); dLms = k.view(W0, h3(W0[:, 4096:5120]), group='gdn')
    dB = [k.view(W0, h3(W0[:, 5120 + i * 1024:6144 + i * 1024]), group='gdn') for i in range(2)]
    dA = [k.view(W0, h3(W0[:, 7168 + i * 1024:8192 + i * 1024]), group='gdn') for i in range(2)]
    dRTb = k.view(W0, h3(W0[:, 9216:10240]), group='gdn'); daT = k.view(W0, h3(W0[:, 10240:11264]), group='gdn'); dqd = k.view(W0, h3(W0[:, 11264:12288]), group='gdn')
    dSb = k.view(W1, W1[:, 0:2048].rearrange("p (h e) -> p h e", e=256), group='gdn')
    dvb = k.view(W1, W1[:, 2048:4096].rearrange("p (h e) -> p h e", e=256), group='gdn')
    dsq = k.view(W1, h3(W1[:, 2048:4096]), group='gdn', share=dvb)
    dkbe = k.view(W1, h3(W1[:, 4096:5120]), group='gdn'); dnwk = k.view(W1, h3(W1[:, 5120:6144]), group='gdn')
    dvn = k.view(W1, W1[:, 6144:8192].rearrange("p (h e) -> p h e", e=256), group='gdn')
    dkd = k.view(W1, h3(W1[:, 8192:9216]), group='gdn')
    don = k.view(W1, h3(W1[:, 9216:11264]), group='gdn')
    dY = k.view(W1, h3(W1[:, 11264:12288]), group='gdn'); dY2 = k.view(W1, h3(W1[:, 5120:6144]), group='gdn', share=dnwk)
    lmask = k.sb("lmask_s", [128, 14, 128], BF16)
    k.ld(lmask, lmask[:], lmk, lmk.t[:, :, :], q='pool')
    dS = k.view(Sst, Sst[:].rearrange("p (h e) -> p h e", e=256), group='gdn')
    dRTf = k.view(yc, h3(yc[:, 0:1024]), group='gdn')
    dob = k.view(yc, h3(yc[:]), group='gdn', share=dRTf)

    def gdn(li, j):
        norm_stage(li, 0)
        W = gdn_in_w
        gemm(H, False, 8, W, W.t[j, :, 0:4096], 4096, 'fm', epi_store_fm(S1, 0))
        conv_stage(S1, S2, 32, lambda blk, tap: gcw_s[:, j, blk, tap - 3:tap - 2], None, False, [gcw_s], l2blocks=tuple(range(16)), qscale_blocks=tuple(range(8)))
        gemm(H, False, 8, W, W.t[j, :, 4096:6144], 2048, 'fm', epi_store_fm(S6, 0, AF.Silu))
        k.ld(dbc16, dbc16[:], gdn_dtb, gdn_dtb.t[j].partition_broadcast(128))
        k.ld(dnega, dnega[:], gdn_alog, gdn_alog.t[j].partition_broadcast(128))
        k.A(lambda e: e.activation(out=dnega[:], in_=dnega[:], func=AF.Exp), r=[dnega], w=[dnega])
        k.V(lambda e: e.tensor_scalar(out=dnega[:], in0=dnega[:], scalar1=-1.0, scalar2=None, op0=ALU.mult), r=[dnega], w=[dnega])

        def epi_ab(ps, c0, t, sub, last):
            ch = t * 4 + sub
            k.A(lambda e: e.activation(out=d_beta[:, ch, :], in_=ps[:, 16:32], func=AF.Sigmoid), r=[ps], w=[d_beta])
            k.V(lambda e: e.tensor_scalar(out=d_nbeta[:, ch, :], in0=d_beta[:, ch, :], scalar1=-1.0, scalar2=None, op0=ALU.mult), r=[d_beta], w=[d_nbeta])
            k.V(lambda e: e.tensor_tensor(out=d_lg[:, ch, :], in0=ps[:, 0:16], in1=dbc16[:], op=ALU.add), r=[ps, dbc16], w=[d_lg])
            k.A(lambda e: e.activation(out=d_lg[:, ch, :], in_=d_lg[:, ch, :], func=AF.Exp), r=[d_lg], w=[d_lg])
            k.A(lambda e: e.activation(out=d_lg[:, ch, :], in_=d_lg[:, ch, :], func=AF.Ln, bias=1.0, scale=1.0), r=[d_lg], w=[d_lg])
            k.V(lambda e: e.tensor_tensor(out=d_lg[:, ch, :], in0=d_lg[:, ch, :], in1=dnega[:], op=ALU.mult), r=[d_lg, dnega], w=[d_lg])
            k.V(lambda e: e.tensor_copy(out=d_lgb[:, ch, :], in_=d_lg[:, ch, :]), r=[d_lg], w=[d_lgb])
        k.enter('gdnx')
        gemm(H, False, 8, W, W.t[j, :, 6144:6176], 32, 'tm', epi_ab)
        k.enter('gdn')
        pidx = 0
        for (o, L, kd) in SEQS:
            for d in (0, 1):
                if kd == 'p':
                    k.V(lambda e: e.memset(dS[:], 0.0), w=[dS])
                    k.V(lambda e: e.memset(dSb[:], 0.0), w=[dSb])
                else:
                    k.ld(dS, dS[:], st_gdn, st_gdn.t[j, d].rearrange("h d e -> d h e"))
                    k.A(lambda e: e.activation(out=dSb[:], in_=dS[:], func=AF.Copy), r=[dS], w=[dSb])
                ie = 127 if d == 0 else 0
                hs = slice(d * 8, (d + 1) * 8)
                order = chunk_order(L, d)

                def gdn_loads(n_, c):
                    t0 = o + c * 128
                    tsl = slice(t0, t0 + 128)
                    q_ = dq_fm[n_ % 2]; k_ = dk_fm[n_ % 2]; kt = dk_tm[n_ % 2]; v_ = dv_tm[n_ % 2]
                    k.ld(q_, q_[:], S2, S2.t[0:1024, tsl].rearrange("(h d) t -> d h t", d=128))
                    k.ld(k_, k_[:], S2, S2.t[1024:2048, tsl].rearrange("(h d) t -> d h t", d=128))
                    for q2 in range(2):
                        k.ld(kt, kt[:, q2 * 4:(q2 + 1) * 4, :].rearrange("p h t -> p (h t)"), S2, S2.t[1024 + q2 * 512:1024 + (q2 + 1) * 512, tsl], tr=True)
                    for q4 in range(4):
                        k.ld(v_, v_[:, q4 * 2:(q4 + 1) * 2, :].rearrange("p h e -> p (h e)"), S2, S2.t[2048 + q4 * 512:2048 + (q4 + 1) * 512, tsl], tr=True)
                    if d == 1:
                        k.ld(dof[n_ % 2], dof[n_ % 2][:], S5, S5.t[0:2048, tsl].rearrange("(b p) t -> p b t", p=128))
                        k.ld(dzs[n_ % 2], dzs[n_ % 2][:], S6, S6.t[0:2048, tsl].rearrange("(b p) t -> p b t", p=128))
                gdn_loads(0, order[0])
                for n_, c in enumerate(order):
                    if n_ + 1 < len(order):
                        gdn_loads(n_ + 1, order[n_ + 1])
                    t0 = o + c * 128; ch = t0 // 128
                    tsl = slice(t0, t0 + 128)
                    q_ = dq_fm[n_ % 2]; k_ = dk_fm[n_ % 2]; kt = dk_tm[n_ % 2]; v_ = dv_tm[n_ % 2]
                    lg = d_lg[:, ch, hs]; lgb = d_lgb[:, ch, hs]; beta = d_beta[:, ch, hs]; nbeta = d_nbeta[:, ch, hs]
                    k.G(lambda e: e.tensor_tensor(out=dET[:], in0=bc(lg, [128, 8, 128], 2), in1=bc(cmb[:, U_[d], :], [128, 8, 128], 1), op=ALU.mult), r=[d_lg, cmb], w=[dET])
                    k.G(lambda e: e.tensor_tensor(out=dEN[:], in0=bc(lg, [128, 8, 128], 2), in1=bc(cmb[:, V_[d], :], [128, 8, 128], 1), op=ALU.mult), r=[d_lg, cmb], w=[dEN])
                    for hf in range(2):
                        ps = k.psum()
                        k.M(lambda e: e.matmul(ps[:], cmb[:, V_[d], :], dET[:, hf * 4:(hf + 1) * 4, :].rearrange("p h t -> p (h t)"), start=True, stop=True), r=[cmb, dET], w=[ps])
                        k.A(lambda e: e.activation(out=dLT[:, hf * 4:(hf + 1) * 4, :], in_=h3(ps[:]), func=AF.Exp), r=[ps], w=[dLT])
                        ps2 = k.psum()
                        k.M(lambda e: e.matmul(ps2[:], cmb[:, U_[d], :], dEN[:, hf * 4:(hf + 1) * 4, :].rearrange("p h t -> p (h t)"), start=True, stop=True), r=[cmb, dEN], w=[ps2])
                        k.A(lambda e: e.activation(out=dLms[:, hf * 4:(hf + 1) * 4, :], in_=h3(ps2[:]), func=AF.Exp), r=[ps2], w=[dLms])
                        ps3 = k.psum()
                        k.M(lambda e: e.matmul(ps3[:], cmb[:, ONES, :], dET[:, hf * 4:(hf + 1) * 4, :].rearrange("p h t -> p (h t)"), start=True, stop=True), r=[cmb, dET], w=[ps3])
                        k.A(lambda e: e.activation(out=tmpf[:], in_=ps3[:], func=AF.Exp), r=[ps3], w=[tmpf])
                        k.V(lambda e: e.tensor_tensor(out=dqd[:, hf * 4:(hf + 1) * 4, :], in0=q_[:, hf * 4:(hf + 1) * 4, :], in1=h3(tmpf[:]), op=ALU.mult), r=[q_, tmpf], w=[dqd])
                    k.V(lambda e: e.tensor_copy(out=ecs[:, 24:32], in_=dLT[:, :, ie]), r=[dLT], w=[ecs])
                    k.V(lambda e: e.tensor_tensor(out=dLTm[:], in0=dLT[:], in1=bc(cmb[:, U_[d], :], [128, 8, 128], 1), op=ALU.mult), r=[dLT, cmb], w=[dLTm])
                    k.V(lambda e: e.tensor_tensor(out=dLms[:], in0=dLms[:], in1=bc(cmb[:, V_[d], :], [128, 8, 128], 1), op=ALU.mult), r=[dLms, cmb], w=[dLms])
                    ps = k.psum()
                    k.M(lambda e: e.matmul(ps[:, 0:8], cmb[:, U_[d], :], lgb, start=True, stop=True), r=[cmb, d_lgb], w=[ps])
                    k.M(lambda e: e.matmul(ps[:, 8:16], cmb[:, ONES, :], lgb, start=True, stop=True), r=[cmb, d_lgb], w=[ps])
                    k.A(lambda e: e.activation(out=ecs[:, 0:16], in_=ps[:, 0:16], func=AF.Exp), r=[ps], w=[ecs])
                    k.V(lambda e: e.tensor_tensor(out=ecs[:, 16:24], in0=ecs[:, 0:8], in1=beta, op=ALU.mult), r=[ecs, d_beta], w=[ecs])
                    mN, mT, OffN, OffT = dB[0], dA[0], dB[1], dA[1]
                    dD = dLms
                    for hf in range(2):
                        ps = k.psum()
                        for h4 in range(4):
                            h = hf * 4 + h4
                            k.M(lambda e: e.matmul(ps[:, h4 * 128:(h4 + 1) * 128], k_[:, h, :], k_[:, h, :], start=True, stop=True), r=[k_], w=[ps])
                        for h4 in range(4):
                            h = hf * 4 + h4
                            k.V(lambda e: e.scalar_tensor_tensor(out=mN[:, h, :], in0=ps[:, h4 * 128:(h4 + 1) * 128], scalar=d_beta[:, ch, d * 8 + h:d * 8 + h + 1], in1=dLms[:, h, :], op0=ALU.mult, op1=ALU.mult),
                                r=[ps, d_beta, dLms], w=[mN])
                    for hf in range(2):
                        ps = k.psum()
                        for h4 in range(4):
                            h = hf * 4 + h4
                            k.M(lambda e: e.matmul(ps[:, h4 * 128:(h4 + 1) * 128], mN[:, h, :], cmb[:, ID, :], start=True, stop=True), r=[mN, cmb], w=[ps])
                        k.A(lambda e: e.activation(out=mT[:, hf * 4:(hf + 1) * 4, :], in_=h3(ps[:]), func=AF.Copy), r=[ps], w=[mT])
                    MN = lambda lv: lmask[:, (lv if d == 0 else 7 + lv), :]
                    MT_ = lambda lv: lmask[:, (7 + lv if d == 0 else lv), :]
                    k.V(lambda e: e.tensor_tensor(out=OffN[:], in0=mN[:], in1=bc(MN(0), [128, 8, 128], 1), op=ALU.mult), r=[mN, lmask], w=[OffN])
                    k.V(lambda e: e.tensor_tensor(out=dD[:], in0=bc(cmb[:, ID, :], [128, 8, 128], 1), in1=OffN[:], op=ALU.subtract), r=[cmb, OffN], w=[dD])
                    k.V(lambda e: e.tensor_tensor(out=OffT[:], in0=mT[:], in1=bc(MT_(0), [128, 8, 128], 1), op=ALU.mult), r=[mT, lmask], w=[OffT])
                    k.V(lambda e: e.tensor_tensor(out=dRTb[:], in0=bc(cmb[:, ID, :], [128, 8, 128], 1), in1=OffT[:], op=ALU.subtract), r=[cmb, OffT], w=[dRTb])
                    for lv in range(1, 7):
                        k.V(lambda e: e.tensor_tensor(out=OffN[:], in0=mN[:], in1=bc(MN(lv), [128, 8, 128], 1), op=ALU.mult), r=[mN, lmask], w=[OffN])
                        k.G(lambda e: e.tensor_tensor(out=OffT[:], in0=mT[:], in1=bc(MT_(lv), [128, 8, 128], 1), op=ALU.mult), r=[mT, lmask], w=[OffT])
                        for hf in range(2):
                            ps = k.psum()
                            for h4 in range(4):
                                h = hf * 4 + h4
                                k.M(lambda e: e.matmul(ps[:, h4 * 128:(h4 + 1) * 128], OffT[:, h, :], dD[:, h, :], start=True, stop=True), r=[OffT, dD], w=[ps])
                            k.A(lambda e: e.activation(out=dY[:, hf * 4:(hf + 1) * 4, :], in_=h3(ps[:]), func=AF.Copy), r=[ps], w=[dY])
                            ps2 = k.psum()
                            for h4 in range(4):
                                h = hf * 4 + h4
                                k.M(lambda e: e.matmul(ps2[:, h4 * 128:(h4 + 1) * 128], OffN[:, h, :], dRTb[:, h, :], start=True, stop=True), r=[OffN, dRTb], w=[ps2])
                            k.V(lambda e: e.tensor_copy(out=dY2[:, hf * 4:(hf + 1) * 4, :], in_=h3(ps2[:])), r=[ps2], w=[dY2])
                        pX = [k.psum(), k.psum()]; pX2 = [k.psum(), k.psum()]
                        for hf in range(2):
                            for h4 in range(4):
                                h = hf * 4 + h4
                                k.M(lambda e: e.matmul(pX[hf][:, h4 * 128:(h4 + 1) * 128], dRTb[:, h, :], dY[:, h, :], start=True, stop=True), r=[dRTb, dY], w=[pX[hf]])
                            for h4 in range(4):
                                h = hf * 4 + h4
                                k.M(lambda e: e.matmul(pX2[hf][:, h4 * 128:(h4 + 1) * 128], dD[:, h, :], dY2[:, h, :], start=True, stop=True), r=[dD, dY2], w=[pX2[hf]])
                        for hf in range(2):
                            k.V(lambda e: e.tensor_tensor(out=dD[:, hf * 4:(hf + 1) * 4, :], in0=dD[:, hf * 4:(hf + 1) * 4, :], in1=h3(pX[hf][:]), op=ALU.subtract), r=[dD, pX[hf]], w=[dD])
                            k.V(lambda e: e.tensor_tensor(out=dRTb[:, hf * 4:(hf + 1) * 4, :], in0=dRTb[:, hf * 4:(hf + 1) * 4, :], in1=h3(pX2[hf][:]), op=ALU.subtract), r=[dRTb, pX2[hf]], w=[dRTb])
                    k.G(lambda e: e.tensor_tensor(out=dvb[:], in0=v_[:], in1=bc(beta, [128, 8, 256], 2), op=ALU.mult), r=[v_, d_beta], w=[dvb])
                    k.G(lambda e: e.tensor_tensor(out=dkbe[:], in0=kt[:], in1=bc(ecs[:, 16:24], [128, 8, 128], 2), op=ALU.mult), r=[kt, ecs], w=[dkbe])
                    k.G(lambda e: e.tensor_tensor(out=dkd[:], in0=kt[:], in1=bc(ecs[:, 24:32], [128, 8, 128], 2), op=ALU.mult), r=[kt, ecs], w=[dkd])
                    for hf in range(2):
                        ps = k.psum()
                        for h4 in range(4):
                            h = hf * 4 + h4
                            k.M(lambda e: e.matmul(ps[:, h4 * 128:(h4 + 1) * 128], dkbe[:, h, :], dRTb[:, h, :], start=True, stop=True), r=[dkbe, dRTb], w=[ps])
                        k.A(lambda e: e.activation(out=dnwk[:, hf * 4:(hf + 1) * 4, :], in_=h3(ps[:]), func=AF.Copy, scale=-1.0), r=[ps], w=[dnwk])
                    for hp in range(4):
                        ps = k.psum()
                        for h2 in range(2):
                            h = hp * 2 + h2
                            k.M(lambda e: e.matmul(ps[:, h2 * 256:(h2 + 1) * 256], dRTb[:, h, :], dvb[:, h, :], start=True, stop=False), r=[dRTb, dvb], w=[ps])
                            k.M(lambda e: e.matmul(ps[:, h2 * 256:(h2 + 1) * 256], dnwk[:, h, :], dSb[:, h, :], start=False, stop=True), r=[dnwk, dSb], w=[ps])
                        k.A(lambda e: e.activation(out=dvn[:, hp * 2:(hp + 1) * 2, :], in_=ps[:].rearrange("p (h e) -> p h e", e=256), func=AF.Copy), r=[ps], w=[dvn])
                    for hf in range(2):
                        ps = k.psum()
                        for h4 in range(4):
                            h = hf * 4 + h4
                            k.M(lambda e: e.matmul(ps[:, h4 * 128:(h4 + 1) * 128], k_[:, h, :], q_[:, h, :], start=True, stop=True), r=[k_, q_], w=[ps])
                        k.V(lambda e: e.tensor_tensor(out=daT[:, hf * 4:(hf + 1) * 4, :], in0=h3(ps[:]), in1=dLTm[:, hf * 4:(hf + 1) * 4, :], op=ALU.mult), r=[ps, dLTm], w=[daT])
                    pso = [k.psum() for _ in range(4)]
                    for h in range(8):
                        for eb in range(2):
                            blk = h * 2 + eb
                            po = pso[blk // 4]; cs_ = slice((blk % 4) * 128, (blk % 4 + 1) * 128)
                            k.M(lambda e: e.matmul(po[:, cs_], dSb[:, h, eb * 128:(eb + 1) * 128], dqd[:, h, :], start=True, stop=False), r=[dSb, dqd], w=[po])
                            k.M(lambda e: e.matmul(po[:, cs_], dvn[:, h, eb * 128:(eb + 1) * 128], daT[:, h, :], start=False, stop=True), r=[dvn, daT], w=[po])
                    if d == 0:
                        for q4 in range(4):
                            k.A(lambda e: e.activation(out=don[:, q4 * 4:(q4 + 1) * 4, :], in_=h3(pso[q4][:]), func=AF.Copy), r=[pso[q4]], w=[don])
                        k.stq(S5, S5.t[0:2048, tsl].rearrange("(b p) t -> p b t", p=128), don, don[:])
                    else:
                        of_ = dof[n_ % 2]; zs_ = dzs[n_ % 2]
                        for q4 in range(4):
                            k.V(lambda e: e.tensor_tensor(out=dob[:, q4 * 4:(q4 + 1) * 4, :], in0=h3(pso[q4][:]), in1=of_[:, q4 * 4:(q4 + 1) * 4, :], op=ALU.add), r=[pso[q4], of_], w=[dob])
                    for h in range(8):
                        pst = k.psum()
                        k.M(lambda e: e.matmul(pst[:, 0:256], dkd[:, h, :], dvn[:, h, :], start=True, stop=True), r=[dkd, dvn], w=[pst])
                        k.V(lambda e: e.scalar_tensor_tensor(out=dS[:, h, :], in0=dS[:, h, :], scalar=ecs[:, 8 + h:9 + h], in1=pst[:, 0:256], op0=ALU.mult, op1=ALU.add), r=[dS, ecs, pst], w=[dS])
                    k.A(lambda e: e.activation(out=dSb[:], in_=dS[:], func=AF.Copy), r=[dS], w=[dSb])
                    if d == 1:
                        k.G(lambda e: e.tensor_tensor(out=dsq[:], in0=dob[:], in1=dob[:], op=ALU.mult), r=[dob], w=[dsq])
                        for hf in range(2):
                            psn = k.psum()
                            for h4 in range(4):
                                h = hf * 4 + h4
                                for eb in range(2):
                                    k.M(lambda e: e.matmul(psn[:, h4 * 128:(h4 + 1) * 128], cmb[:, ONES, :], dsq[:, h * 2 + eb, :], start=(eb == 0), stop=(eb == 1)), r=[cmb, dsq], w=[psn])
                            k.A(lambda e: e.activation(out=rsb[:], in_=psn[:], func=AF.Sqrt, bias=EPS, scale=1.0 / 256), r=[psn], w=[rsb])
                            k.V(lambda e: e.reciprocal(out=rsb[:], in_=rsb[:]), r=[rsb], w=[rsb])
                            for eb in range(2):
                                gv_ = dob[:, hf * 8:(hf + 1) * 8, :].rearrange("p (h b) t -> p h b t", b=2)[:, :, eb, :]
                                k.V(lambda e: e.scalar_tensor_tensor(out=gv_, in0=gv_, scalar=dnw_s[:, j, eb:eb + 1], in1=h3(rsb[:]), op0=ALU.mult, op1=ALU.mult), r=[dob, dnw_s, rsb], w=[dob])
                        k.V(lambda e: e.tensor_tensor(out=don[:], in0=dob[:], in1=zs_[:], op=ALU.mult), r=[dob, zs_], w=[don])
                        k.stq(S5, S5.t[0:2048, tsl].rearrange("(b p) t -> p b t", p=128), don, don[:])
                if kd == 'p':
                    k.stq(ns_gdn, ns_gdn.t[pidx, j, d].rearrange("h d e -> d h e"), dS, dS[:])
            if kd == 'p':
                pidx += 1
        k.leave('gdn'); k.leave('gdnx')
        gemm(S5, False, 16, gdn_out_w, gdn_out_w.t[j, :, :], 1024, 'fm', **resid_hooks(li, 2))

    wdec = k.sb("wdec", [128, 32], F32)

    for (li, kind, j) in LAYERS:
        if kind == 'ssd':
            ssd(li, j)
        elif kind == 'gla':
            gla(li, j)
        elif kind == 'gdn':
            gdn(li, j)
        elif kind == 'none':
            pass
        else:
            raise NotImplementedError(kind)
        if kind != 'none' or True:
            ffn(li)
    norm_stage(0, 0, final=True)
    outs = [yT, X, ns_ssm, ns_gla, ns_gdn, S1, S2, S3, S4, S5, H]
    k.P.wait_all('sp', [o.b for o in outs])
    k.P.run()
    st.close()
    return nc, k.P.ninst


def _consts():
    i = np.arange(128)
    kk, ii = np.meshgrid(i, i, indexing='ij')
    cm = np.zeros((128, 10, 128), np.float32)
    cm[:, 0] = (kk <= ii); cm[:, 1] = (kk >= ii); cm[:, 2] = (kk > ii); cm[:, 3] = (kk < ii)
    cm[:, 4] = (kk == ii); cm[:, 5] = 1.0
    cm[:, 6] = (kk <= ii) / -16.0; cm[:, 7] = (kk >= ii) / -16.0; cm[:, 8] = (kk > ii) / -16.0; cm[:, 9] = (kk < ii) / -16.0
    return cm


def _fm(v, nblk):
    v = np.asarray(v, np.float32)
    lead = v.shape[:-1]
    v = v.reshape(*lead, nblk, 128)
    return np.ascontiguousarray(np.moveaxis(v, -1, 0))


def prep_common(w):
    d = {}
    d['cmask'] = _consts()
    ii, jj = np.meshgrid(np.arange(128), np.arange(128), indexing='ij')
    lm = np.zeros((128, 14, 128), np.float32)
    for lv in range(7):
        b = 1 << lv
        m = ((ii // (2 * b)) == (jj // (2 * b))) & ((ii % (2 * b)) >= b) & ((jj % (2 * b)) < b)
        lm[:, lv] = m; lm[:, 7 + lv] = m.T
    d['lmask'] = lm
    d['nmw'] = _fm(w['norm_mix_w'], 8); d['nfw'] = _fm(w['norm_ffn_w'], 8); d['fnw'] = _fm(w['final_norm_w'], 8)
    d['ada_w'] = w['ada_w']; d['ada_b'] = _fm(w['ada_b'], 48)
    d['ffn_in_w'] = w['ffn_in_w']; d['ffn_out_w'] = w['ffn_out_w']
    d['ffn_cw'] = _fm(np.moveaxis(np.asarray(w['ffn_conv_w']).reshape(4, 9, 2816), 1, 2).reshape(4, 2816 * 9).reshape(4, 22, 128, 9).transpose(0, 1, 3, 2).reshape(4, 22, 9, 128), 1)[..., 0] if False else None
    cw = np.asarray(w['ffn_conv_w'], np.float32).reshape(4, 9, 22, 128)
    d['ffn_cw'] = np.ascontiguousarray(cw.transpose(3, 0, 2, 1))
    d['ffn_cb'] = _fm(w['ffn_conv_b'], 22)
    d['ssm_in_w'] = w['ssm_in_w']
    cw = np.asarray(w['ssm_conv_w'], np.float32).reshape(2, 3, 32, 128)
    d['ssm_cw'] = np.ascontiguousarray(cw.transpose(3, 0, 2, 1))
    d['ssm_cb'] = _fm(w['ssm_conv_b'], 32)
    d['ssm_dtb'] = np.asarray(w['ssm_dt_bias'], np.float32).reshape(2, 1, 64)
    d['ssm_alog'] = np.asarray(w['ssm_a_log'], np.float32).reshape(2, 1, 64)
    d['ssm_d'] = np.asarray(w['ssm_d'], np.float32).reshape(2, 1, 32)
    d['ssm_nw'] = np.asarray(w['ssm_norm_w'], np.float32).reshape(2, 1, 2048)
    d['ssm_out_w'] = w['ssm_out_w']
    d['gla_in_w'] = w['gla_in_w']
    w1 = np.zeros((1, 1024, 128), np.float32)
    g1 = np.asarray(w['gla_gate_w1'], np.float32)
    w1[:, :, 0:16] = g1[:, 0]; w1[:, :, 32:48] = g1[:, 1]
    d['gla_w1'] = w1
    w2 = np.zeros((1, 64, 2, 512), np.float32)
    g2 = np.asarray(w['gla_gate_w2'], np.float32)
    w2[:, 0:16, 0] = g2[:, 0]; w2[:, 32:48, 1] = g2[:, 1]
    d['gla_w2'] = w2.reshape(1, 64, 1024)
    d['gla_gb'] = np.asarray(w['gla_gate_b'], np.float32).reshape(1, 1, 1024)
    d['gla_nw'] = _fm(w['gla_norm_w'], 2); d['gla_out_w'] = w['gla_out_w']
    d['gdn_in_w'] = w['gdn_in_w']
    cw = np.asarray(w['gdn_conv_w'], np.float32).reshape(1, 3, 32, 128)
    d['gdn_cw'] = np.ascontiguousarray(cw.transpose(3, 0, 2, 1))
    d['gdn_dtb'] = np.asarray(w['gdn_dt_bias'], np.float32).reshape(1, 1, 16)
    d['gdn_alog'] = np.asarray(w['gdn_a_log'], np.float32).reshape(1, 1, 16)
    d['gdn_nw'] = _fm(w['gdn_norm_w'], 2); d['gdn_out_w'] = w['gdn_out_w']
    return {k_: np.ascontiguousarray(np.asarray(v, np.float32)) for k_, v in d.items()}


FULL_SEQS = [(0, 256, 'p'), (256, 256, 'p'), (512, 4096, 's')]
FULL_LAYERS = [(0, 'ssd', 0), (1, 'gla', 0), (2, 'gdn', 0), (3, 'ssd', 1)]


def kernel(**inp):
    w = {k_: np.asarray(v) for k_, v in inp.items()}
    common = prep_common(w)
    nc, ninst = build_program(FULL_SEQS, FULL_LAYERS)
    xp = np.asarray(w['x_prompt'], np.float32); xs = np.asarray(w['x_sample'], np.float32)
    in_maps = []
    for core in range(8):
        b = core % 2
        xcat = np.concatenate([xp[2 * core], xp[2 * core + 1], xs[b]], axis=0)
        m = dict(common)
        m['xT'] = np.ascontiguousarray(xcat.T)
        cond = np.stack([np.asarray(w['c_ctx'], np.float32), np.asarray(w['c'], np.float32)[b]], axis=1)
        m['condT'] = np.ascontiguousarray(cond.reshape(8, 128, 2).transpose(1, 0, 2))
        ss = np.asarray(w['state_ssm'], np.float32)[b]
        m['st_ssm'] = np.ascontiguousarray(ss.transpose(0, 1, 3, 2, 4).reshape(2, 2, 128, 2048))
        m['st_gla'] = np.ascontiguousarray(np.asarray(w['state_gla'], np.float32)[b])
        m['st_gdn'] = np.ascontiguousarray(np.asarray(w['state_gdn'], np.float32)[b])
        in_maps.append(m)
    res = run_bass_kernel_spmd(nc, in_maps, core_ids=list(range(8)))
    y_prompt = np.zeros((16, 256, 1024), np.float32); y_sample = np.zeros((2, 4096, 1024), np.float32)
    n_ssm = np.zeros((16, 2, 2, 32, 128, 64), np.float32); n_gla = np.zeros((16, 1, 2, 4, 128, 256), np.float32)
    n_gdn = np.zeros((16, 1, 2, 8, 128, 256), np.float32)
    for core in range(8):
        r = res.results[core]
        y = np.asarray(r['yT']).T
        y_prompt[2 * core] = y[0:256]; y_prompt[2 * core + 1] = y[256:512]
        if core < 2:
            y_sample[core] = y[512:]
        s_ = np.asarray(r['ns_ssm']).reshape(2, 2, 2, 128, 32, 64).transpose(0, 1, 2, 4, 3, 5)
        n_ssm[2 * core:2 * core + 2] = s_
        n_gla[2 * core:2 * core + 2] = np.asarray(r['ns_gla'])
        n_gdn[2 * core:2 * core + 2] = np.asarray(r['ns_gdn'])
    return (y_prompt, y_sample, n_ssm, n_gla, n_gdn)
```

```python
import contextlib
import numpy as np
import concourse.bass as bass
import concourse.mybir as mybir
from concourse.bass_utils import run_bass_kernel_spmd

F32 = mybir.dt.float32
BF16 = mybir.dt.bfloat16
AF = mybir.ActivationFunctionType
ALU = mybir.AluOpType
AX = mybir.AxisListType
ENGS = ['pe', 'act', 'dve', 'pool', 'sp']
EPS = 1e-6
DBG_STOP = None


class Buf:
    __slots__ = ('name', 'lw', 'rd', 'acc')

    def __init__(self, name, acc=False):
        self.name = name
        self.lw = {}
        self.rd = {}
        self.acc = acc


class _Rec:
    def __init__(self):
        self.call = None

    def __getattr__(self, name):
        def f(*a, **kw):
            self.call = (name, a, kw)
            return self
        return f


class Prog:
    def __init__(self, nc):
        self.nc = nc
        self.q = {e: [] for e in ENGS}
        self.cnt = {e: 0 for e in ENGS}
        self.known = {e: {} for e in ENGS}
        self.dma_cnt = {}
        self.ninst = 0

    def _need(self, eng, key, val, src, waits):
        if src == eng and eng == 'pe':
            return
        if self.known[eng].get(key, 0) >= val:
            return
        if waits.get(key, 0) < val:
            waits[key] = val

    def emit(self, eng, fn, reads=(), writes=(), dma_key=None):
        rec = _Rec()
        fn(rec)
        _c = rec.call
        fn = lambda e, _c=_c: getattr(e, _c[0])(*_c[1], **_c[2])
        waits = {}
        for b in reads:
            for k, (v, s) in b.lw.items():
                self._need(eng, k, v, s, waits)
        for b in writes:
            if not b.acc:
                for k, (v, s) in b.lw.items():
                    self._need(eng, k, v, s, waits)
            for k, (v, s) in b.rd.items():
                self._need(eng, k, v, s, waits)
        for k, v in waits.items():
            self.known[eng][k] = v
        if dma_key is None:
            self.cnt[eng] += 1
            key, val, src, inc = eng, self.cnt[eng], eng, 1
        else:
            self.dma_cnt[dma_key] = self.dma_cnt.get(dma_key, 0) + 16
            key, val, src, inc = dma_key, self.dma_cnt[dma_key], None, 16
        for b in reads:
            if b.rd.get(key, (0, None))[0] < val:
                b.rd[key] = (val, src)
        for b in writes:
            if b.acc:
                b.lw[key] = (val, src)
            else:
                b.lw = {key: (val, src)}
            b.rd = {}
        self.q[eng].append((list(waits.items()), fn, (key, inc)))
        self.ninst += 1
        return (key, val, src)

    def wait_all(self, eng, bufs):
        waits = {}
        for b in bufs:
            for k, (v, s) in b.lw.items():
                self._need(eng, k, v, s, waits)
        self.q[eng].append((list(waits.items()), None, None))

    def run(self):
        nc = self.nc
        keys = list(ENGS) + sorted(self.dma_cnt.keys())
        with contextlib.ExitStack() as st:
            sems = {k: st.enter_context(nc.semaphore("s%d" % i)) for i, k in enumerate(keys)}
            block = st.enter_context(nc.Block())

            def mk(engname):
                def body(e):
                    for waits, fn, inc in self.q[engname]:
                        for k, v in waits:
                            e.wait_ge(sems[k], v)
                        if fn is not None:
                            fn(e).then_inc(sems[inc[0]], inc[1])
                return body
            block.tensor(mk('pe'))
            block.scalar(mk('act'))
            block.vector(mk('dve'))
            block.gpsimd(mk('pool'))
            block.sync(mk('sp'))


class Tl:
    def __init__(self, t, name, acc=False):
        self.t = t
        self.b = Buf(name, acc)
        self.name = name

    def __getitem__(self, k):
        return self.t[k]


class KB:
    def __init__(self, nc, st):
        self.nc = nc
        self.st = st
        self.P = Prog(nc)
        self.ps = [Tl(st.enter_context(nc.psum_tensor("ps%d" % i, [128, 512], F32)), "ps%d" % i) for i in range(8)]
        self.psi = 0
        self.nkey = 0
        self.keymap = {}
        self.groups = {}
        self.nview = 0
        self.cast_tok = Buf('castdma', acc=True)
        self.tr_tok = Buf('trdma', acc=True)

    def sb(self, name, shape, dt):
        return Tl(self.st.enter_context(self.nc.sbuf_tensor(name, shape, dt)), name)

    def dram(self, name, shape, dt, kind=None):
        if kind is None:
            t = self.nc.dram_tensor(name, shape, dt)
        else:
            t = self.nc.dram_tensor(name, shape, dt, kind=kind)
        return Tl(t.ap(), name, acc=True)

    def view(self, parent, ap, group=None, share=None):
        v = Tl(ap, parent.name)
        if group is None:
            v.b = parent.b
        else:
            self.nview += 1
            v.name = "%s#%d" % (parent.name, self.nview)
            if share is not None:
                v.b = share.b
                v.name = share.name
            self.groups.setdefault(group, []).append((parent, v))
        return v

    @staticmethod
    def _union(dicts):
        out = {}
        for d_ in dicts:
            for k_, (v_, s_) in d_.items():
                if out.get(k_, (0, None))[0] < v_:
                    out[k_] = (v_, s_)
        return out

    def enter(self, group):
        for parent, v in self.groups.get(group, []):
            v.b.lw = self._union([parent.b.lw, parent.b.rd])
            v.b.rd = {}

    def leave(self, group):
        parents = {}
        for parent, v in self.groups.get(group, []):
            parents.setdefault(id(parent), (parent, []))[1].append(v)
        for parent, vs in parents.values():
            parent.b.lw = self._union([parent.b.lw, parent.b.rd] + [v.b.lw for v in vs] + [v.b.rd for v in vs])
            parent.b.rd = {}

    def psum(self):
        p = self.ps[self.psi % 8]
        self.psi += 1
        return p

    def _b(self, l):
        return [x.b for x in l]

    def M(self, fn, r=(), w=()):
        return self.P.emit('pe', fn, self._b(r), self._b(w))

    def A(self, fn, r=(), w=()):
        return self.P.emit('act', fn, self._b(r), self._b(w))

    def V(self, fn, r=(), w=()):
        return self.P.emit('dve', fn, self._b(r), self._b(w))

    def G(self, fn, r=(), w=()):
        return self.P.emit('pool', fn, self._b(r), self._b(w))

    def _key(self, s):
        if s not in self.keymap:
            self.keymap[s] = "d%04d" % len(self.keymap)
        return self.keymap[s]

    def ld(self, dst, dst_ap, src, src_ap, q='sp', tr=False):
        key = self._key('ld:' + dst.name)
        if tr:
            fn = lambda e: e.dma_start_transpose(out=dst_ap, in_=src_ap)
        else:
            fn = lambda e: e.dma_start(out=dst_ap, in_=src_ap)
        if tr:
            return self.P.emit(q, fn, [src.b, self.cast_tok], [dst.b, self.tr_tok], dma_key=key)
        if q == 'pool':
            return self.P.emit(q, fn, [src.b, self.tr_tok], [dst.b, self.cast_tok], dma_key=key)
        return self.P.emit(q, fn, [src.b], [dst.b], dma_key=key)

    def stq(self, dst, dst_ap, src, src_ap, q='sp'):
        key = self._key('st:' + src.name)
        return self.P.emit(q, lambda e: e.dma_start(out=dst_ap, in_=src_ap), [src.b], [dst.b], dma_key=key)


def bc(ap, shape, axis):
    return ap.unsqueeze(axis).broadcast_to(shape)


def build_program(SEQS, LAYERS, dbg=()):
    T = sum(L for _, L, _ in SEQS)
    NT = T // 512
    assert T % 512 == 0
    NCH = T // 128
    NP = sum(1 for s in SEQS if s[2] == 'p')
    nc = bass.Bass("TRN2", target_bir_lowering=False)
    st = contextlib.ExitStack()
    k = KB(nc, st)
    IN = lambda name, shape, dt=F32: k.dram(name, shape, dt, kind="ExternalInput")
    OUT = lambda name, shape, dt=F32: k.dram(name, shape, dt, kind="ExternalOutput")
    dbgset = set(dbg)
    SCR = lambda name, shape, dt=BF16: k.dram(name, shape, dt, kind=("ExternalOutput" if name in dbgset else None))

    xT = IN("xT", [1024, T])
    condT = IN("condT", [128, 8, 2])
    cm = IN("cmask", [128, 10, 128])
    nmw = IN("nmw", [128, 4, 8]); nfw = IN("nfw", [128, 4, 8]); fnw = IN("fnw", [128, 8])
    ada_w = IN("ada_w", [4, 1024, 6144]); ada_b = IN("ada_b", [128, 4, 48])
    ffn_in_w = IN("ffn_in_w", [4, 1024, 5632]); ffn_cw = IN("ffn_cw", [128, 4, 22, 9]); ffn_cb = IN("ffn_cb", [128, 4, 22])
    ffn_out_w = IN("ffn_out_w", [4, 2816, 1024])
    ssm_in_w = IN("ssm_in_w", [2, 1024, 6208]); ssm_cw = IN("ssm_cw", [128, 2, 32, 3]); ssm_cb = IN("ssm_cb", [128, 2, 32])
    ssm_dtb = IN("ssm_dtb", [2, 1, 64]); ssm_alog = IN("ssm_alog", [2, 1, 64]); ssm_d = IN("ssm_d", [2, 1, 32])
    ssm_nw = IN("ssm_nw", [2, 1, 2048]); ssm_out_w = IN("ssm_out_w", [2, 2048, 1024])
    gla_in_w = IN("gla_in_w", [1, 1024, 3072]); gla_w1 = IN("gla_w1", [1, 1024, 128]); gla_w2 = IN("gla_w2", [1, 64, 1024]); gla_gb = IN("gla_gb", [1, 1, 1024])
    gla_nw = IN("gla_nw", [128, 1, 2]); gla_out_w = IN("gla_out_w", [1, 1024, 1024])
    gdn_in_w = IN("gdn_in_w", [1, 1024, 6176]); gdn_cw = IN("gdn_cw", [128, 1, 32, 3])
    gdn_dtb = IN("gdn_dtb", [1, 1, 16]); gdn_alog = IN("gdn_alog", [1, 1, 16]); gdn_nw = IN("gdn_nw", [128, 1, 2])
    gdn_out_w = IN("gdn_out_w", [1, 2048, 1024])
    lmk = IN("lmask", [128, 14, 128])
    st_ssm = IN("st_ssm", [2, 2, 128, 2048]); st_gla = IN("st_gla", [1, 2, 4, 128, 256]); st_gdn = IN("st_gdn", [1, 2, 8, 128, 256])
    yT = OUT("yT", [1024, T])
    ns_ssm = OUT("ns_ssm", [max(NP, 1), 2, 2, 128, 2048]); ns_gla = OUT("ns_gla", [max(NP, 1), 1, 2, 4, 128, 256])
    ns_gdn = OUT("ns_gdn", [max(NP, 1), 1, 2, 8, 128, 256])
    X = k.dram("X", [1024, T], F32, kind=("ExternalOutput" if "X" in dbgset else None))
    H = SCR("H", [1024, T])
    S1 = SCR("S1", [4096, T])
    S2 = SCR("S2", [4096, T])
    S3 = SCR("S3", [T, 2048])
    S4 = SCR("S4", [T, 2048])
    S5 = SCR("S5", [2816, T])
    S6 = SCR("S6", [2048, T])
    S7 = SCR("S7", [T, 1024])

    cmb = k.sb("cmb", [128, 10, 128], BF16)
    k.ld(cmb, cmb[:], cm, cm.t[:, :, :], q='pool')
    Uf, Ub, Vf, Vb, ID, ONES, UGf, UGb, VGf, VGb = range(10)
    UG_ = {0: UGf, 1: UGb}; VG_ = {0: VGf, 1: VGb}
    U_ = {0: Uf, 1: Ub}; V_ = {0: Vf, 1: Vb}
    nmw_s = k.sb("nmw_s", [128, 4, 8], F32); nfw_s = k.sb("nfw_s", [128, 4, 8], F32); fnw_s = k.sb("fnw_s", [128, 8], F32)
    k.ld(nmw_s, nmw_s[:], nmw, nmw.t[:, :, :]); k.ld(nfw_s, nfw_s[:], nfw, nfw.t[:, :, :]); k.ld(fnw_s, fnw_s[:], fnw, fnw.t[:, :])
    adab_s = k.sb("adab_s", [128, 4, 48], F32)
    k.ld(adab_s, adab_s[:], ada_b, ada_b.t[:, :, :])
    mod = k.sb("mod", [128, 4, 48, 2], F32)
    gm = k.sb("gm", [128, 4, 2, 8, 2], F32)

    xbuf = [k.sb("xb%d" % i, [128, 8, 512], F32) for i in range(2)]
    for t in range(NT):
        xb = xbuf[t % 2]
        k.ld(xb, xb[:], xT, xT.t[:, t * 512:(t + 1) * 512].rearrange("(c p) t -> p c t", p=128))
        k.stq(X, X.t[:, t * 512:(t + 1) * 512].rearrange("(c p) t -> p c t", p=128), xb, xb[:])
    cnd = k.sb("cnd", [128, 8, 2], F32); cndb = k.sb("cndb", [128, 8, 2], BF16)
    k.ld(cnd, cnd[:], condT, condT.t[:, :, :])
    k.A(lambda e: e.activation(out=cndb[:], in_=cnd[:], func=AF.Silu), r=[cnd], w=[cndb])
    wbuf = [k.sb("wb%d" % i, [128, 12288], BF16) for i in range(2)]
    wi = [0]

    def load_w(src, src_ap3, kc, ncols, after=()):
        wb = wbuf[wi[0] % 2]; wi[0] += 1
        view = wb[:, 0:kc * ncols].rearrange("p (c n) -> p c n", c=kc)
        key = k._key('ld:' + wb.name)
        for c in range(kc):
            k.P.emit('pool', lambda e: e.dma_start(out=view[:, c, :], in_=src_ap3[c * 128:(c + 1) * 128, :]), [src.b, k.tr_tok] + [a.b for a in after], [wb.b, k.cast_tok], dma_key=key)
        return wb, view

    for (li, kind, j) in LAYERS:
        for qd in range(6):
            wb, wv = load_w(ada_w, ada_w.t[li, :, qd * 1024:(qd + 1) * 1024], 8, 1024)
            ps = k.psum()
            for cb in range(8):
                for c in range(8):
                    k.M(lambda e, c=c, cb=cb, wv=wv, ps=ps: e.matmul(ps[:, cb * 2:cb * 2 + 2], wv[:, c, cb * 128:(cb + 1) * 128], cndb[:, c, :],
                                                                       start=(c == 0), stop=(c == 7)), r=[wb, cndb], w=[ps])
            k.V(lambda e, ps=ps, qd=qd, li=li: e.tensor_tensor(out=mod[:, li, qd * 8:(qd + 1) * 8, :], in0=ps[:, 0:16].rearrange("p (a b) -> p a b", b=2),
                                                                in1=bc(adab_s[:, li, qd * 8:(qd + 1) * 8], [128, 8, 2], 2), op=ALU.add), r=[ps, adab_s], w=[mod])
        for sub, nw in ((0, nmw_s), (1, nfw_s)):
            sc = 1 + 3 * sub
            k.V(lambda e, sub=sub, li=li, sc=sc: e.tensor_scalar(out=gm[:, li, sub], in0=mod[:, li, sc * 8:(sc + 1) * 8, :], scalar1=1.0, scalar2=None, op0=ALU.add), r=[mod], w=[gm])
            k.V(lambda e, sub=sub, li=li, nw=nw: e.tensor_tensor(out=gm[:, li, sub], in0=gm[:, li, sub], in1=bc(nw[:, li, :], [128, 8, 2], 2), op=ALU.mult), r=[gm, nw], w=[gm])

    def ci_of_tile(t):
        off = t * 512
        for (o, L, kd) in SEQS:
            if o <= off < o + L:
                return 0 if kd == 'p' else 1
        raise ValueError

    sqb = k.sb("sqb", [128, 8, 512], BF16)
    rsb = k.sb("rsb", [128, 512], F32)
    inb = [k.sb("inb%d" % i, [128, 22 * 512], BF16) for i in range(2)]
    hb = [k.view(inb[i], inb[i][:, 0:4096].rearrange("p (c n) -> p c n", c=8)) for i in range(2)]
    tmpf = k.sb("tmpf", [128, 512], F32)

    def norm_stage(li, sub, final=False):
        def nload(t):
            k.ld(xbuf[t % 2], xbuf[t % 2][:], X, X.t[:, t * 512:(t + 1) * 512].rearrange("(c p) t -> p c t", p=128))
        nload(0)
        for t in range(NT):
            ci = ci_of_tile(t)
            xb = xbuf[t % 2]
            sl = slice(t * 512, (t + 1) * 512)
            if t + 1 < NT:
                nload(t + 1)
            k.A(lambda e, xb=xb: e.activation(out=sqb[:], in_=xb[:], func=AF.Square), r=[xb], w=[sqb])
            ps = k.psum()
            for c in range(8):
                k.M(lambda e, c=c, ps=ps: e.matmul(ps[:], cmb[:, ONES, :], sqb[:, c, :], start=(c == 0), stop=(c == 7)), r=[cmb, sqb], w=[ps])
            k.A(lambda e, ps=ps: e.activation(out=rsb[:], in_=ps[:], func=AF.Sqrt, bias=EPS, scale=1.0 / 1024), r=[ps], w=[rsb])
            k.V(lambda e: e.reciprocal(out=rsb[:], in_=rsb[:]), r=[rsb], w=[rsb])
            if final:
                for c in range(8):
                    k.V(lambda e, c=c, xb=xb: e.scalar_tensor_tensor(out=xb[:, c, :], in0=xb[:, c, :], scalar=fnw_s[:, c:c + 1], in1=rsb[:], op0=ALU.mult, op1=ALU.mult),
                        r=[xb, fnw_s, rsb], w=[xb])
                k.stq(yT, yT.t[:, sl].rearrange("(c p) t -> p c t", p=128), xb, xb[:])
            else:
                h = hb[t % 2]
                shq = 0 if sub == 0 else 3
                for c in range(8):
                    k.V(lambda e, c=c, xb=xb: e.scalar_tensor_tensor(out=tmpf[:], in0=xb[:, c, :], scalar=gm[:, li, sub, c, ci:ci + 1], in1=rsb[:], op0=ALU.mult, op1=ALU.mult),
                        r=[xb, gm, rsb], w=[tmpf])
                    k.A(lambda e, c=c, h=h: e.activation(out=h[:, c, :], in_=tmpf[:], func=AF.Identity, bias=mod[:, li, shq * 8 + c, ci:ci + 1], scale=1.0),
                        r=[tmpf, mod], w=[h])
                k.stq(H, H.t[:, sl].rearrange("(c p) t -> p c t", p=128), h, h[:])

    ini = [0]

    def gemm(src, src_tm, kc, W, Wap, ncols, out_mode, epi, gw=None, pre=None, post=None):
        if gw is None:
            gw = max(128, (12288 // kc) // 128 * 128)
            gw = min(gw, 2048)
        seq = [(g0, min(gw, ncols - g0), t) for g0 in range(0, ncols, gw) for t in range(NT)]
        loaded = {}

        def emit_load(i, wb_dep):
            g0, gn, t = seq[i]
            ib = inb[ini[0] % 2]; ini[0] += 1
            iv = ib[:, 0:kc * 512].rearrange("p (c n) -> p c n", c=kc)
            sl = slice(t * 512, (t + 1) * 512)
            if src_tm:
                key = k._key('ld:' + ib.name)
                for c in range(kc):
                    k.P.emit('sp', lambda e: e.dma_start_transpose(out=iv[:, c, :], in_=src.t[sl, c * 128:(c + 1) * 128]), [src.b, k.cast_tok] + ([wb_dep.b] if wb_dep is not None else []), [ib.b, k.tr_tok], dma_key=key)
            else:
                k.ld(ib, iv[:], src, src.t[0:kc * 128, sl].rearrange("(c p) t -> p c t", p=128))
            loaded[i] = (ib, iv)
        wb = wv = None
        emit_load(0, None)
        for i, (g0, gn, t) in enumerate(seq):
            if t == 0:
                wb, wv = load_w(W, Wap[:, g0:g0 + gn], kc, gn, after=(inb if src_tm else ()))
            if i + 1 < len(seq):
                emit_load(i + 1, wb)
            ib, iv = loaded.pop(i)
            ncb = gn // 128
            if pre is not None:
                pre(t, g0 // 128, g0 // 128 + ncb)
            if out_mode == 'fm':
                for cb in range(ncb):
                    ps = k.psum()
                    for c in range(kc):
                        k.M(lambda e: e.matmul(ps[:], wv[:, c, cb * 128:(cb + 1) * 128], iv[:, c, :], start=(c == 0), stop=(c == kc - 1)), r=[wb, ib], w=[ps])
                    epi(ps, (g0 // 128) + cb, t, None, cb == ncb - 1)
            else:
                for sbk in range(4):
                    for p0 in range(0, gn, 512):
                        pw = min(512, gn - p0)
                        ps = k.psum()
                        for c in range(kc):
                            k.M(lambda e: e.matmul(ps[:, 0:pw], iv[:, c, sbk * 128:(sbk + 1) * 128], wv[:, c, p0:p0 + pw], start=(c == 0), stop=(c == kc - 1)), r=[wb, ib], w=[ps])
                        epi(ps, g0 + p0, t, sbk, False)
            if post is not None:
                post(t, g0 // 128, g0 // 128 + ncb)

    stg = [k.sb("stg%d" % i, [128, 512], BF16) for i in range(3)]
    sti = [0]

    stgb = [k.sb("stgb%d" % i, [128, 4, 512], BF16) for i in range(2)]
    stgi = [0]

    def epi_store_fm(dst, row0, func=None, scale=1.0):
        state = {'n': 0, 'cb0': None, 'buf': None}

        def epi(ps, cb, t, sub, last):
            if state['n'] == 0:
                state['buf'] = stgb[stgi[0] % 2]; stgi[0] += 1
                state['cb0'] = cb
            s_ = state['buf']; n = state['n']
            if func is None:
                if cb % 2 == 0:
                    k.V(lambda e: e.tensor_copy(out=s_[:, n, :], in_=ps[:]), r=[ps], w=[s_])
                else:
                    k.A(lambda e: e.activation(out=s_[:, n, :], in_=ps[:], func=AF.Copy), r=[ps], w=[s_])
            else:
                k.A(lambda e: e.activation(out=s_[:, n, :], in_=ps[:], func=func, scale=scale), r=[ps], w=[s_])
            state['n'] += 1
            if state['n'] == 4 or last:
                n_ = state['n']; c0 = state['cb0']
                k.stq(dst, dst.t[row0 + c0 * 128:row0 + (c0 + n_) * 128, t * 512:(t + 1) * 512].rearrange("(c p) t -> p c t", p=128), s_, s_[:, 0:n_, :])
                state['n'] = 0
        return epi

    def epi_store_tm(dst, col0, func=None):
        def epi(ps, c0, t, sub, last):
            s = stg[sti[0] % 3]; sti[0] += 1
            if func is None:
                k.V(lambda e, s=s, ps=ps: e.tensor_copy(out=s[:], in_=ps[:]), r=[ps], w=[s])
            else:
                k.A(lambda e, s=s, ps=ps: e.activation(out=s[:], in_=ps[:], func=func), r=[ps], w=[s])
            r0 = t * 512 + sub * 128
            k.stq(dst, dst.t[r0:r0 + 128, col0 + c0:col0 + c0 + 512], s, s[:])
        return epi

    def resid_hooks(li, gq):
        def pre(t, c_lo, c_hi):
            xb = xbuf[t % 2]
            k.ld(xb, xb[:, c_lo:c_hi, :], X, X.t[c_lo * 128:c_hi * 128, t * 512:(t + 1) * 512].rearrange("(c p) t -> p c t", p=128))

        def epi(ps, cb, t, sub, last):
            ci = ci_of_tile(t)
            xb = xbuf[t % 2]
            k.V(lambda e: e.scalar_tensor_tensor(out=xb[:, cb, :], in0=ps[:], scalar=mod[:, li, gq * 8 + cb, ci:ci + 1], in1=xb[:, cb, :], op0=ALU.mult, op1=ALU.add),
                r=[ps, mod, xb], w=[xb])

        def post(t, c_lo, c_hi):
            xb = xbuf[t % 2]
            k.stq(X, X.t[c_lo * 128:c_hi * 128, t * 512:(t + 1) * 512].rearrange("(c p) t -> p c t", p=128), xb, xb[:, c_lo:c_hi, :])
        return dict(epi=epi, pre=pre, post=post)

    cvin = [k.sb("cvin%d" % i, [128, 10, 66], BF16) for i in range(2)]
    dgb = [k.sb("dgb%d" % i, [128, 9, 128], BF16) for i in range(2)]
    cvout = [k.sb("cvout%d" % i, [128, 512], BF16) for i in range(2)]
    cvi = [0]

    def seq_of(tok):
        for (o, L, kd) in SEQS:
            if o <= tok < o + L:
                return (o, L, kd)
        raise ValueError

    def conv_stage(src, dst, nblk, wfn, bfn, grid_for_sample, wdeps, l2blocks=(), qscale_blocks=()):
        items = [(blk, t) for blk in range(nblk) for t in range(NT)]
        info = {}

        def conv_load(i):
            blk, t = items[i]
            rs_ = slice(blk * 128, (blk + 1) * 128)
            t0 = t * 512
            (o, L, kd) = seq_of(t0)
            ci_ = cvin[cvi[0] % 2]; co = cvout[cvi[0] % 2]; cvi[0] += 1
            grid = grid_for_sample and kd == 's'
            k.G(lambda e: e.memset(ci_[:], 0.0), w=[ci_])
            pieces = []
            if grid:
                ROWS_ = L // 64
                r0 = (t0 - o) // 64
                lo = max(0, r0 - 1); hi = min(ROWS_, r0 + 9)
                k.ld(ci_, ci_[:, lo - (r0 - 1):hi - (r0 - 1), 1:65], src, src.t[rs_, o + lo * 64:o + hi * 64].rearrange("p (r c) -> p r c", c=64))
            else:
                flat = ci_[:].rearrange("p r c -> p (r c)")
                pos = t0
                base = 0
                while pos < t0 + 512:
                    (o2, L2, _) = seq_of(pos)
                    pe_ = min(t0 + 512, o2 + L2)
                    pa, pb = pos, pe_
                    lo = max(o2, pa - 1); hi = min(o2 + L2, pb + 1)
                    k.ld(ci_, flat[:, base + (lo - (pa - 1)):base + (hi - (pa - 1))], src, src.t[rs_, lo:hi])
                    pieces.append((pa, pb, base)); base += (pb - pa) + 2
                    pos = pe_
            info[i] = (ci_, co, grid, pieces)

        conv_load(0)
        for i, (blk, t) in enumerate(items):
            rs_ = slice(blk * 128, (blk + 1) * 128)
            t0 = t * 512
            if t == 0:
                dg = dgb[blk % 2]
                for tap in (range(9) if grid_for_sample else range(3, 6)):
                    k.V(lambda e: e.tensor_scalar(out=dg[:, tap, :], in0=cmb[:, ID, :], scalar1=wfn(blk, tap), scalar2=None, op0=ALU.mult), r=[cmb] + wdeps, w=[dg])
            if i + 1 < len(items):
                conv_load(i + 1)
            ci_, co, grid, pieces = info.pop(i)
            ps = k.psum()
            if grid:
                pv = ps[:].rearrange("p (r c) -> p r c", c=64)
                ti = 0
                for dy in range(3):
                    for dx in range(3):
                        k.M(lambda e: e.matmul(pv, dg[:, dy * 3 + dx, :], ci_[:, dy:dy + 8, dx:dx + 64], start=(ti == 0), stop=(ti == 8)), r=[dg, ci_], w=[ps])
                        ti += 1
            else:
                flat = ci_[:].rearrange("p r c -> p (r c)")
                for (pa, pb, base) in pieces:
                    n_ = pb - pa
                    for dx in range(3):
                        k.M(lambda e: e.matmul(ps[:, pa - t0:pb - t0], dg[:, 3 + dx, :], flat[:, base + dx:base + dx + n_], start=(dx == 0), stop=(dx == 2)), r=[dg, ci_], w=[ps])
            bap = bfn(blk) if bfn is not None else None
            if bap is not None:
                k.A(lambda e: e.activation(out=co[:], in_=ps[:], func=AF.Silu, bias=bap, scale=1.0), r=[ps] + wdeps, w=[co])
            else:
                k.A(lambda e: e.activation(out=co[:], in_=ps[:], func=AF.Silu), r=[ps], w=[co])
            if blk in l2blocks:
                k.V(lambda e: e.tensor_tensor(out=sqb[:, 0, :], in0=co[:], in1=co[:], op=ALU.mult), r=[co], w=[sqb])
                ps2 = k.psum()
                k.M(lambda e: e.matmul(ps2[:], cmb[:, ONES, :], sqb[:, 0, :], start=True, stop=True), r=[cmb, sqb], w=[ps2])
                k.A(lambda e: e.activation(out=rsb[:], in_=ps2[:], func=AF.Sqrt, bias=1e-6, scale=1.0), r=[ps2], w=[rsb])
                k.V(lambda e: e.reciprocal(out=rsb[:], in_=rsb[:]), r=[rsb], w=[rsb])
                qs = (128.0 ** -0.5) if blk in qscale_blocks else 1.0
                k.V(lambda e: e.scalar_tensor_tensor(out=co[:], in0=co[:], scalar=qs, in1=rsb[:], op0=ALU.mult, op1=ALU.mult), r=[co, rsb], w=[co])
            k.stq(dst, dst.t[rs_, t0:t0 + 512], co, co[:])

    fcw_s = k.sb("fcw_s", [128, 4, 22, 9], F32); fcb_s = k.sb("fcb_s", [128, 4, 22], F32)
    k.ld(fcw_s, fcw_s[:], ffn_cw, ffn_cw.t[:, :, :, :]); k.ld(fcb_s, fcb_s[:], ffn_cb, ffn_cb.t[:, :, :])
    aab = [k.sb("aab%d" % i, [128, 512], BF16) for i in range(2)]
    aai = [0]

    def ffn(li):
        norm_stage(li, 1)
        gemm(H, False, 8, ffn_in_w, ffn_in_w.t[li, :, 0:2816], 2816, 'fm', epi_store_fm(S1, 0))
        conv_stage(S1, S2, 22, lambda blk, tap: fcw_s[:, li, blk, tap:tap + 1], lambda blk: fcb_s[:, li, blk:blk + 1], True, [fcw_s, fcb_s])

        def epi_v(ps, cb, t, sub, last):
            a = aab[aai[0] % 2]; aai[0] += 1
            k.ld(a, a[:], S2, S2.t[cb * 128:(cb + 1) * 128, t * 512:(t + 1) * 512])
            s = stg[sti[0] % 3]; sti[0] += 1
            k.V(lambda e, s=s, ps=ps, a=a: e.tensor_tensor(out=s[:], in0=ps[:], in1=a[:], op=ALU.mult), r=[ps, a], w=[s])
            k.stq(S5, S5.t[cb * 128:(cb + 1) * 128, t * 512:(t + 1) * 512], s, s[:])
        gemm(H, False, 8, ffn_in_w, ffn_in_w.t[li, :, 2816:5632], 2816, 'fm', epi_v)
        gemm(S5, False, 22, ffn_out_w, ffn_out_w.t[li, :, :], 1024, 'fm', gw=512, **resid_hooks(li, 5))

    scw_s = k.sb("scw_s", [128, 2, 32, 3], F32); scb_s = k.sb("scb_s", [128, 2, 32], F32)
    k.ld(scw_s, scw_s[:], ssm_cw, ssm_cw.t[:, :, :, :]); k.ld(scb_s, scb_s[:], ssm_cb, ssm_cb.t[:, :, :])
    dt_sb = k.view(xbuf[0], xbuf[0][:].rearrange("p c n -> p (c n)")[:, 0:NCH * 64].rearrange("p (c n) -> p c n", n=64), group='ssdx')
    la_sb = k.view(xbuf[1], xbuf[1][:].rearrange("p c n -> p (c n)")[:, 0:NCH * 64].rearrange("p (c n) -> p c n", n=64), group='ssdx')
    lab_sb = k.sb("lab_sb", [128, 1, 64], BF16)
    bc64 = k.sb("bc64", [128, 64], F32); nega = k.sb("nega", [128, 64], F32); dsk = k.sb("dsk", [128, 32], F32)
    nwb = k.sb("nwb", [128, 2048], BF16)
    W0, W1 = wbuf[0], wbuf[1]
    Ebuf = k.view(W0, W0[:, 0:4096], group='ssd'); DTb = k.view(W0, W0[:, 4096:8192], group='ssd')
    xv = k.view(W0, W0[:, 8192:10240], group='ssd'); xw = k.view(W0, W0[:, 10240:12288], group='ssd')
    Sbf = k.view(W1, W1[:, 0:2048], group='ssd'); STb = k.view(W1, W1[:, 2048:3072], group='ssd')
    yo = [k.view(W1, W1[:, 3072 + i * 2048:3072 + (i + 1) * 2048], group='ssd') for i in range(2)]
    xs_tm = [k.view(inb[i], inb[i][:, 0:2048], group='ssd') for i in range(2)]
    B_tm = [k.view(inb[i], inb[i][:, 2048:3072], group='ssd') for i in range(2)]
    BC_fm = [k.view(inb[i], inb[i][:, 3072:5120].rearrange("p (g t) -> p g t", t=128), group='ssd') for i in range(2)]
    yfb = [k.view(inb[i], inb[i][:, 5120:7168], group='ssd') for i in range(2)]
    zsb = [k.view(inb[i], inb[i][:, 7168:9216], group='ssd') for i in range(2)]
    ecs = k.sb("ecs", [128, 64], F32)
    Sst = k.sb("Sst", [128, 2048], F32)
    yc = k.sb("yc", [128, 2048], F32)
    ytmp = tmpf
    ss8 = k.sb("ss8", [128, 16], F32)

    def chunk_order(L, d):
        n = L // 128
        return list(range(n)) if d == 0 else list(range(n - 1, -1, -1))

    def ssd(li, j):
        norm_stage(li, 0)
        W = ssm_in_w
        gemm(H, False, 8, W, W.t[j, :, 0:2048], 2048, 'tm', epi_store_tm(S3, 0, AF.Silu))
        gemm(H, False, 8, W, W.t[j, :, 2048:6144], 4096, 'fm', epi_store_fm(S1, 0))
        if DBG_STOP == 'gemm':
            return
        conv_stage(S1, S2, 32, lambda blk, tap: scw_s[:, j, blk, tap - 3:tap - 2], lambda blk: scb_s[:, j, blk:blk + 1], False, [scw_s, scb_s])
        if DBG_STOP == 'conv':
            return
        k.ld(bc64, bc64[:], ssm_dtb, ssm_dtb.t[j].partition_broadcast(128))
        k.ld(nega, nega[:], ssm_alog, ssm_alog.t[j].partition_broadcast(128))
        k.A(lambda e: e.activation(out=nega[:], in_=nega[:], func=AF.Exp), r=[nega], w=[nega])
        k.V(lambda e: e.tensor_scalar(out=nega[:], in0=nega[:], scalar1=-1.0, scalar2=None, op0=ALU.mult), r=[nega], w=[nega])
        k.ld(dsk, dsk[:], ssm_d, ssm_d.t[j].partition_broadcast(128))
        k.ld(nwb, nwb[:], ssm_nw, ssm_nw.t[j].partition_broadcast(128), q='pool')

        def epi_dt(ps, c0, t, sub, last):
            ch = t * 4 + sub
            k.V(lambda e, ps=ps, ch=ch: e.tensor_tensor(out=dt_sb[:, ch, :], in0=ps[:, 0:64], in1=bc64[:], op=ALU.add), r=[ps, bc64], w=[dt_sb])
            k.A(lambda e, ch=ch: e.activation(out=dt_sb[:, ch, :], in_=dt_sb[:, ch, :], func=AF.Exp), r=[dt_sb], w=[dt_sb])
            k.A(lambda e, ch=ch: e.activation(out=dt_sb[:, ch, :], in_=dt_sb[:, ch, :], func=AF.Ln, bias=1.0, scale=1.0), r=[dt_sb], w=[dt_sb])
            k.V(lambda e, ch=ch: e.tensor_tensor(out=la_sb[:, ch, :], in0=dt_sb[:, ch, :], in1=nega[:], op=ALU.mult), r=[dt_sb, nega], w=[la_sb])
        k.enter('ssdx')
        gemm(H, False, 8, W, W.t[j, :, 6144:6208], 64, 'tm', epi_dt)
        if DBG_STOP == 'dt':
            return
        k.enter('ssd')
        pidx = 0
        for (o, L, kd) in (SEQS[:1] if DBG_STOP in ('scan1', 'scan1b') else SEQS):
            for d in ((0,) if DBG_STOP == 'scan1' else (0, 1)):
                if kd == 'p':
                    k.V(lambda e: e.memset(Sst[:], 0.0), w=[Sst])
                    k.V(lambda e: e.memset(Sbf[:], 0.0), w=[Sbf])
                else:
                    k.ld(Sst, Sst[:], st_ssm, st_ssm.t[j, d])
                    k.A(lambda e: e.activation(out=Sbf[:], in_=Sst[:], func=AF.Copy), r=[Sst], w=[Sbf])
                ie = 127 if d == 0 else 0
                order = chunk_order(L, d)

                def ssd_loads(n_, c):
                    t0 = o + c * 128
                    tsl = slice(t0, t0 + 128)
                    xs = xs_tm[n_ % 2]; Bt = B_tm[n_ % 2]; BC = BC_fm[n_ % 2]
                    for q4 in range(4):
                        k.ld(xs, xs[:, q4 * 512:(q4 + 1) * 512], S2, S2.t[q4 * 512:(q4 + 1) * 512, tsl], tr=True)
                    for q4 in range(2):
                        k.ld(Bt, Bt[:, q4 * 512:(q4 + 1) * 512], S2, S2.t[2048 + q4 * 512:2048 + (q4 + 1) * 512, tsl], tr=True)
                    k.ld(BC, BC[:], S2, S2.t[2048:4096, tsl].rearrange("(g n) t -> n g t", n=128))
                    if d == 1:
                        k.ld(yfb[n_ % 2], yfb[n_ % 2][:], S4, S4.t[tsl, :])
                        k.ld(zsb[n_ % 2], zsb[n_ % 2][:], S3, S3.t[tsl, :])
                ssd_loads(0, order[0])
                for n_, c in enumerate(order):
                    if n_ + 1 < len(order):
                        ssd_loads(n_ + 1, order[n_ + 1])
                    t0 = o + c * 128; ch = t0 // 128
                    tsl = slice(t0, t0 + 128)
                    xs = xs_tm[n_ % 2]; Bt = B_tm[n_ % 2]; BC = BC_fm[n_ % 2]
                    la = la_sb[:, ch, d * 32:(d + 1) * 32]; lab = lab_sb[:, 0, 0:32]; dtc = dt_sb[:, ch, d * 32:(d + 1) * 32]
                    k.V(lambda e: e.tensor_copy(out=lab_sb[:, 0, 0:32], in_=la), r=[la_sb], w=[lab_sb])
                    E3 = Ebuf[:].rearrange("p (h i) -> p h i", i=128)
                    D3 = DTb[:].rearrange("p (h i) -> p h i", i=128)
                    k.G(lambda e, la=la: e.tensor_tensor(out=E3, in0=bc(la, [128, 32, 128], 2), in1=bc(cmb[:, U_[d], :], [128, 32, 128], 1), op=ALU.mult), r=[la_sb, cmb], w=[Ebuf])
                    for q8 in range(8):
                        ps = k.psum()
                        k.M(lambda e, ps=ps, q8=q8: e.matmul(ps[:], cmb[:, V_[d], :], Ebuf[:, q8 * 512:(q8 + 1) * 512], start=True, stop=True), r=[cmb, Ebuf], w=[ps])
                        k.A(lambda e, ps=ps, q8=q8: e.activation(out=DTb[:, q8 * 512:(q8 + 1) * 512], in_=ps[:], func=AF.Exp), r=[ps], w=[DTb])
                    for half in range(2):
                        ps = k.psum()
                        for g4 in range(4):
                            g = half * 4 + g4
                            k.M(lambda e, ps=ps, g=g, g4=g4, BC=BC: e.matmul(ps[:, g4 * 128:(g4 + 1) * 128], BC[:, g, :], BC[:, 8 + g, :], start=True, stop=True), r=[BC], w=[ps])
                        k.V(lambda e, ps=ps, half=half: e.tensor_tensor(out=STb[:, half * 512:(half + 1) * 512].rearrange("p (g i) -> p g i", i=128), in0=ps[:].rearrange("p (g i) -> p g i", i=128),
                                                                        in1=bc(cmb[:, U_[d], :], [128, 4, 128], 1), op=ALU.mult), r=[ps, cmb], w=[STb])
                    k.V(lambda e: e.tensor_copy(out=wdec[:], in_=DTb[:].rearrange("p (h i) -> p h i", i=128)[:, :, ie]), r=[DTb], w=[wdec])
                    k.V(lambda e: e.tensor_tensor(out=DTb[:].rearrange("p (g r i) -> p g r i", r=4, i=128), in0=DTb[:].rearrange("p (g r i) -> p g r i", r=4, i=128),
                                                   in1=bc(STb[:].rearrange("p (g i) -> p g i", i=128), [128, 8, 4, 128], 2), op=ALU.mult), r=[DTb, STb], w=[DTb])
                    k.G(lambda e, xs=xs, dtc=dtc: e.tensor_tensor(out=xv[:].rearrange("p (h q) -> p h q", q=64), in0=xs[:].rearrange("p (h q) -> p h q", q=64),
                                                                   in1=bc(dtc, [128, 32, 64], 2), op=ALU.mult), r=[xs, dt_sb], w=[xv])
                    k.G(lambda e: e.tensor_tensor(out=xw[:].rearrange("p (h q) -> p h q", q=64), in0=xv[:].rearrange("p (h q) -> p h q", q=64),
                                                   in1=bc(wdec[:], [128, 32, 64], 2), op=ALU.mult), r=[xv, wdec], w=[xw])
                    ps = k.psum()
                    k.M(lambda e, ps=ps, lab=lab: e.matmul(ps[:, 0:32], cmb[:, U_[d], :], lab, start=True, stop=True), r=[cmb, lab_sb], w=[ps])
                    k.M(lambda e, ps=ps, lab=lab: e.matmul(ps[:, 32:64], cmb[:, ONES, :], lab, start=True, stop=True), r=[cmb, lab_sb], w=[ps])
                    k.A(lambda e, ps=ps: e.activation(out=ecs[:], in_=ps[:, 0:64], func=AF.Exp), r=[ps], w=[ecs])
                    for pr in range(4):
                        psA = k.psum(); psB = k.psum()
                        for gg in range(2):
                            g = pr * 2 + gg
                            for r_ in range(4):
                                h = g * 4 + r_
                                k.M(lambda e, psA=psA, h=h, gg=gg, r_=r_: e.matmul(psA[:, gg * 256 + r_ * 64:gg * 256 + (r_ + 1) * 64], DTb[:, h * 128:(h + 1) * 128], xv[:, h * 64:(h + 1) * 64],
                                                                                       start=True, stop=True), r=[DTb, xv], w=[psA])
                            k.M(lambda e, psB=psB, g=g, gg=gg, BC=BC: e.matmul(psB[:, gg * 256:(gg + 1) * 256], BC[:, 8 + g, :], Sbf[:, g * 256:(g + 1) * 256], start=True, stop=True), r=[BC, Sbf], w=[psB])
                        k.V(lambda e, psB=psB, pr=pr: e.tensor_tensor(out=ytmp[:].rearrange("p (h q) -> p h q", q=64), in0=psB[:].rearrange("p (h q) -> p h q", q=64),
                                                                       in1=bc(ecs[:, pr * 8:(pr + 1) * 8], [128, 8, 64], 2), op=ALU.mult), r=[psB, ecs], w=[ytmp])
                        k.V(lambda e, psA=psA, pr=pr: e.tensor_tensor(out=yc[:, pr * 512:(pr + 1) * 512], in0=psA[:], in1=ytmp[:], op=ALU.add), r=[psA, ytmp], w=[yc])
                    k.G(lambda e: e.tensor_tensor(out=Sst[:].rearrange("p (h q) -> p h q", q=64), in0=Sst[:].rearrange("p (h q) -> p h q", q=64),
                                                   in1=bc(ecs[:, 32:64], [128, 32, 64], 2), op=ALU.mult), r=[Sst, ecs], w=[Sst])
                    for pr in range(4):
                        ps = k.psum()
                        for gg in range(2):
                            g = pr * 2 + gg
                            k.M(lambda e, ps=ps, g=g, gg=gg, Bt=Bt: e.matmul(ps[:, gg * 256:(gg + 1) * 256], Bt[:, g * 128:(g + 1) * 128], xw[:, g * 256:(g + 1) * 256], start=True, stop=True), r=[Bt, xw], w=[ps])
                        k.V(lambda e, ps=ps, pr=pr: e.tensor_tensor(out=Sst[:, pr * 512:(pr + 1) * 512], in0=ps[:], in1=Sst[:, pr * 512:(pr + 1) * 512], op=ALU.add), r=[ps, Sst], w=[Sst])
                    k.A(lambda e: e.activation(out=Sbf[:], in_=Sst[:], func=AF.Copy), r=[Sst], w=[Sbf])
                    if d == 0:
                        y_ = yo[n_ % 2]
                        k.A(lambda e, y_=y_: e.activation(out=y_[:], in_=yc[:], func=AF.Copy), r=[yc], w=[y_])
                        k.stq(S4, S4.t[tsl, :], y_, y_[:])
                    else:
                        yf_ = yfb[n_ % 2]; zs_ = zsb[n_ % 2]; y_ = yo[n_ % 2]
                        k.V(lambda e, yf_=yf_: e.tensor_tensor(out=yc[:], in0=yc[:], in1=yf_[:], op=ALU.add), r=[yc, yf_], w=[yc])
                        k.G(lambda e, xs=xs: e.tensor_tensor(out=xv[:].rearrange("p (h q) -> p h q", q=64), in0=xs[:].rearrange("p (h q) -> p h q", q=64),
                                                              in1=bc(dsk[:], [128, 32, 64], 2), op=ALU.mult), r=[xs, dsk], w=[xv])
                        k.V(lambda e: e.tensor_tensor(out=yc[:], in0=yc[:], in1=xv[:], op=ALU.add), r=[yc, xv], w=[yc])
                        k.V(lambda e, zs_=zs_: e.tensor_tensor(out=yc[:], in0=yc[:], in1=zs_[:], op=ALU.mult), r=[yc, zs_], w=[yc])
                        k.G(lambda e: e.tensor_tensor(out=Ebuf[:, 0:2048], in0=yc[:], in1=yc[:], op=ALU.mult), r=[yc], w=[Ebuf])
                        k.V(lambda e: e.tensor_reduce(out=ss8[:, 0:8], in_=Ebuf[:, 0:2048].rearrange("p (g q) -> p g q", q=256), axis=AX.X, op=ALU.add), r=[Ebuf], w=[ss8])
                        k.A(lambda e: e.activation(out=ss8[:, 0:8], in_=ss8[:, 0:8], func=AF.Sqrt, bias=EPS, scale=1.0 / 256), r=[ss8], w=[ss8])
                        k.V(lambda e: e.reciprocal(out=ss8[:, 0:8], in_=ss8[:, 0:8]), r=[ss8], w=[ss8])
                        k.V(lambda e: e.tensor_tensor(out=yc[:].rearrange("p (g q) -> p g q", q=256), in0=yc[:].rearrange("p (g q) -> p g q", q=256),
                                                       in1=bc(ss8[:, 0:8], [128, 8, 256], 2), op=ALU.mult), r=[yc, ss8], w=[yc])
                        k.V(lambda e, y_=y_: e.tensor_tensor(out=y_[:], in0=yc[:], in1=nwb[:], op=ALU.mult), r=[yc, nwb], w=[y_])
                        k.stq(S4, S4.t[tsl, :], y_, y_[:])
                if kd == 'p':
                    k.stq(ns_ssm, ns_ssm.t[pidx, j, d], Sst, Sst[:])
            if kd == 'p':
                pidx += 1
        k.leave('ssd'); k.leave('ssdx')
        if DBG_STOP in ('scan1', 'scan1b', 'scan'):
            return
        gemm(S4, True, 16, ssm_out_w, ssm_out_w.t[j, :, :], 1024, 'fm', **resid_hooks(li, 2))

    HAS_GLA = any(kd_ == 'gla' for _, kd_, _ in LAYERS)
    lrT = k.sb("lrT", [64, 512], BF16)
    gnw_s = k.sb("gnw_s", [128, 1, 2], F32)
    k.ld(gnw_s, gnw_s[:], gla_nw, gla_nw.t[:, :, :])
    w2s = k.sb("w2s", [64, 1024 if HAS_GLA else 8], BF16)
    gbb = k.sb("gbb", [128, 1024 if HAS_GLA else 8], BF16)
    gq_fm = [k.view(inb[i], inb[i][:, 0:512].rearrange("p (h t) -> p h t", t=128), group='gla') for i in range(2)]
    gk_fm = [k.view(inb[i], inb[i][:, 512:1024].rearrange("p (h t) -> p h t", t=128), group='gla') for i in range(2)]
    gk_tm = [k.view(inb[i], inb[i][:, 1024:1536], group='gla') for i in range(2)]
    gv_tm = [k.view(inb[i], inb[i][:, 1536:2560], group='gla') for i in range(2)]
    gsp_tm = [k.view(inb[i], inb[i][:, 2560:3072], group='gla') for i in range(2)]
    gof = [k.view(inb[i], inb[i][:, 3072:4096].rearrange("p (b t) -> p b t", t=128), group='gla') for i in range(2)]
    grs = [k.view(inb[i], inb[i][:, 4096:5120].rearrange("p (b t) -> p b t", t=128), group='gla') for i in range(2)]
    gqg = k.view(W0, W0[:, 0:512].rearrange("p (h t) -> p h t", t=128), group='gla')
    gkg = k.view(W0, W0[:, 512:1024].rearrange("p (h t) -> p h t", t=128), group='gla')
    gAT = k.view(W0, W0[:, 1024:1536].rearrange("p (h t) -> p h t", t=128), group='gla')
    gkw = k.view(W0, W0[:, 1536:2048], group='gla')
    gsq = k.view(W0, W0[:, 2048:3072].rearrange("p (b t) -> p b t", t=128), group='gla')
    gon = [k.view(W1, W1[:, 3072 + i * 1024:3072 + (i + 1) * 1024].rearrange("p (b t) -> p b t", t=128), group='gla') for i in range(2)]
    gsps = [k.sb("gsps%d" % i, [128, 1024 if HAS_GLA else 8], BF16) for i in range(2)]
    Sg = k.view(Sst, Sst[:, 0:1024].rearrange("p (h e) -> p h e", e=256), group='gla')
    Sgb = k.view(W1, W1[:, 0:1024].rearrange("p (h e) -> p h e", e=256), group='gla')
    gob = k.view(yc, yc[:, 0:1024].rearrange("p (b t) -> p b t", t=128), group='gla')

    def gla(li, j):
        norm_stage(li, 0)
        W = gla_in_w
        gemm(H, False, 8, W, W.t[j, :, 0:512], 512, 'fm', epi_store_fm(S1, 0, AF.Copy, 128.0 ** -0.5))
        gemm(H, False, 8, W, W.t[j, :, 512:1024], 512, 'fm', epi_store_fm(S1, 512))
        gemm(H, False, 8, W, W.t[j, :, 1024:2048], 1024, 'tm', epi_store_tm(S3, 0))
        gemm(H, False, 8, W, W.t[j, :, 2048:3072], 1024, 'fm', epi_store_fm(S6, 0, AF.Silu))

        k.ld(w2s, w2s[:], gla_w2, gla_w2.t[j], q='pool')
        k.ld(gbb, gbb[:], gla_gb, gla_gb.t[j].partition_broadcast(128), q='pool')

        def epi_lr(ps, cb, t, sub, last):
            k.V(lambda e: e.tensor_copy(out=lrT[:], in_=ps[0:64, :]), r=[ps], w=[lrT])
            for sbk in range(4):
                ch = t * 4 + sbk
                tsl = slice(ch * 128, (ch + 1) * 128)
                sp_ = gsps[ch % 2]
                for hf in range(2):
                    ps2 = k.psum()
                    k.M(lambda e: e.matmul(ps2[:], lrT[:, sbk * 128:(sbk + 1) * 128], w2s[:, hf * 512:(hf + 1) * 512], start=True, stop=True), r=[lrT, w2s], w=[ps2])
                    k.V(lambda e: e.tensor_tensor(out=rsb[:], in0=ps2[:], in1=gbb[:, hf * 512:(hf + 1) * 512], op=ALU.add), r=[ps2, gbb], w=[rsb])
                    k.A(lambda e: e.activation(out=rsb[:], in_=rsb[:], func=AF.Exp, scale=-1.0), r=[rsb], w=[rsb])
                    k.A(lambda e: e.activation(out=sp_[:, hf * 512:(hf + 1) * 512], in_=rsb[:], func=AF.Ln, bias=1.0, scale=1.0), r=[rsb], w=[sp_])
                k.stq(S7, S7.t[tsl, :], sp_, sp_[:])
        gemm(H, False, 8, gla_w1, gla_w1.t[j, :, :], 128, 'fm', epi_lr)
        k.enter('gla')
        pidx = 0
        for (o, L, kd) in SEQS:
            for d in (0, 1):
                if kd == 'p':
                    k.V(lambda e: e.memset(Sg[:], 0.0), w=[Sg])
                    k.V(lambda e: e.memset(Sgb[:], 0.0), w=[Sgb])
                else:
                    k.ld(Sg, Sg[:], st_gla, st_gla.t[j, d].rearrange("h d e -> d h e"))
                    k.A(lambda e: e.activation(out=Sgb[:], in_=Sg[:], func=AF.Copy), r=[Sg], w=[Sgb])
                ie = 127 if d == 0 else 0
                order = chunk_order(L, d)

                def gla_loads(n_, c):
                    t0 = o + c * 128
                    tsl = slice(t0, t0 + 128)
                    q_ = gq_fm[n_ % 2]; k_ = gk_fm[n_ % 2]; kt = gk_tm[n_ % 2]; v_ = gv_tm[n_ % 2]; sp_ = gsp_tm[n_ % 2]
                    k.ld(q_, q_[:], S1, S1.t[0:512, tsl].rearrange("(h d) t -> d h t", d=128))
                    k.ld(k_, k_[:], S1, S1.t[512:1024, tsl].rearrange("(h d) t -> d h t", d=128))
                    k.ld(kt, kt[:], S1, S1.t[512:1024, tsl], tr=True)
                    k.ld(v_, v_[:], S3, S3.t[tsl, 0:1024])
                    k.ld(sp_, sp_[:], S7, S7.t[tsl, d * 512:(d + 1) * 512])
                    if d == 1:
                        k.ld(gof[n_ % 2], gof[n_ % 2][:], S5, S5.t[0:1024, tsl].rearrange("(b p) t -> p b t", p=128))
                        k.ld(grs[n_ % 2], grs[n_ % 2][:], S6, S6.t[0:1024, tsl].rearrange("(b p) t -> p b t", p=128))
                gla_loads(0, order[0])
                for n_, c in enumerate(order):
                    if n_ + 1 < len(order):
                        gla_loads(n_ + 1, order[n_ + 1])
                    t0 = o + c * 128
                    tsl = slice(t0, t0 + 128)
                    q_ = gq_fm[n_ % 2]; k_ = gk_fm[n_ % 2]; kt = gk_tm[n_ % 2]; v_ = gv_tm[n_ % 2]; sp_ = gsp_tm[n_ % 2]
                    ps = k.psum()
                    k.M(lambda e: e.matmul(ps[:], cmb[:, VG_[d], :], sp_[:], start=True, stop=True), r=[cmb, sp_], w=[ps])
                    k.A(lambda e: e.activation(out=rsb[:], in_=ps[:], func=AF.Exp), r=[ps], w=[rsb])
                    k.V(lambda e: e.tensor_tensor(out=gkw[:], in0=kt[:], in1=rsb[:], op=ALU.mult), r=[kt, rsb], w=[gkw])
                    psc = k.psum()
                    for h in range(4):
                        k.M(lambda e: e.matmul(psc[:, h * 128:(h + 1) * 128], sp_[:, h * 128:(h + 1) * 128], cmb[:, UG_[d], :], start=True, stop=True), r=[sp_, cmb], w=[psc])
                    k.A(lambda e: e.activation(out=tmpf[:], in_=psc[:], func=AF.Exp), r=[psc], w=[tmpf])
                    k.V(lambda e: e.tensor_tensor(out=gqg[:], in0=q_[:], in1=tmpf[:].rearrange("p (h t) -> p h t", t=128), op=ALU.mult), r=[q_, tmpf], w=[gqg])
                    k.A(lambda e: e.activation(out=tmpf[:], in_=psc[:], func=AF.Exp, scale=-1.0), r=[psc], w=[tmpf])
                    k.V(lambda e: e.tensor_tensor(out=gkg[:], in0=k_[:], in1=tmpf[:].rearrange("p (h t) -> p h t", t=128), op=ALU.mult), r=[k_, tmpf], w=[gkg])
                    k.A(lambda e: e.activation(out=ecs[:, 0:4], in_=psc[:].rearrange("p (h t) -> p h t", t=128)[:, :, ie], func=AF.Exp), r=[psc], w=[ecs])
                    pss = k.psum()
                    for h in range(4):
                        k.M(lambda e: e.matmul(pss[:, h * 128:(h + 1) * 128], gkg[:, h, :], gqg[:, h, :], start=True, stop=True), r=[gkg, gqg], w=[pss])
                    k.V(lambda e: e.tensor_tensor(out=gAT[:], in0=pss[:].rearrange("p (h t) -> p h t", t=128), in1=bc(cmb[:, U_[d], :], [128, 4, 128], 1), op=ALU.mult), r=[pss, cmb], w=[gAT])
                    pso = [k.psum(), k.psum()]
                    for h in range(4):
                        for eb in range(2):
                            blk = h * 2 + eb
                            po = pso[blk // 4]
                            k.M(lambda e: e.matmul(po[:, (blk % 4) * 128:(blk % 4 + 1) * 128], v_[:, h * 256 + eb * 128:h * 256 + (eb + 1) * 128], gAT[:, h, :], start=True, stop=False), r=[v_, gAT], w=[po])
                            k.M(lambda e: e.matmul(po[:, (blk % 4) * 128:(blk % 4 + 1) * 128], Sgb[:, h, eb * 128:(eb + 1) * 128], gqg[:, h, :], start=False, stop=True), r=[Sgb, gqg], w=[po])
                    for h in range(4):
                        pst = k.psum()
                        k.M(lambda e: e.matmul(pst[:, 0:256], gkw[:, h * 128:(h + 1) * 128], v_[:, h * 256:(h + 1) * 256], start=True, stop=True), r=[gkw, v_], w=[pst])
                        k.V(lambda e: e.scalar_tensor_tensor(out=Sg[:, h, :], in0=Sg[:, h, :], scalar=ecs[:, h:h + 1], in1=pst[:, 0:256], op0=ALU.mult, op1=ALU.add), r=[Sg, ecs, pst], w=[Sg])
                    on_ = gon[n_ % 2]
                    if d == 0:
                        for hf in range(2):
                            k.A(lambda e: e.activation(out=on_[:, hf * 4:(hf + 1) * 4, :], in_=pso[hf][:].rearrange("p (b t) -> p b t", t=128), func=AF.Copy), r=[pso[hf]], w=[on_])
                        k.stq(S5, S5.t[0:1024, tsl].rearrange("(b p) t -> p b t", p=128), on_, on_[:])
                    else:
                        of_ = gof[n_ % 2]; rs_ = grs[n_ % 2]
                        for hf in range(2):
                            k.V(lambda e: e.tensor_tensor(out=gob[:, hf * 4:(hf + 1) * 4, :], in0=pso[hf][:].rearrange("p (b t) -> p b t", t=128), in1=of_[:, hf * 4:(hf + 1) * 4, :], op=ALU.add), r=[pso[hf], of_], w=[gob])
                        k.G(lambda e: e.tensor_tensor(out=gsq[:], in0=gob[:], in1=gob[:], op=ALU.mult), r=[gob], w=[gsq])
                        psn = k.psum()
                        for h in range(4):
                            for eb in range(2):
                                k.M(lambda e: e.matmul(psn[:, h * 128:(h + 1) * 128], cmb[:, ONES, :], gsq[:, h * 2 + eb, :], start=(eb == 0), stop=(eb == 1)), r=[cmb, gsq], w=[psn])
                        k.A(lambda e: e.activation(out=rsb[:], in_=psn[:], func=AF.Sqrt, bias=EPS, scale=1.0 / 256), r=[psn], w=[rsb])
                        k.V(lambda e: e.reciprocal(out=rsb[:], in_=rsb[:]), r=[rsb], w=[rsb])
                        for eb in range(2):
                            gv_ = gob[:].rearrange("p (h b) t -> p h b t", b=2)[:, :, eb, :]
                            k.V(lambda e: e.scalar_tensor_tensor(out=gv_, in0=gv_, scalar=gnw_s[:, j, eb:eb + 1], in1=rsb[:].rearrange("p (h t) -> p h t", t=128), op0=ALU.mult, op1=ALU.mult), r=[gob, gnw_s, rsb], w=[gob])
                        k.V(lambda e: e.tensor_tensor(out=on_[:], in0=gob[:], in1=rs_[:], op=ALU.mult), r=[gob, rs_], w=[on_])
                        k.stq(S5, S5.t[0:1024, tsl].rearrange("(b p) t -> p b t", p=128), on_, on_[:])
                    k.A(lambda e: e.activation(out=Sgb[:], in_=Sg[:], func=AF.Copy), r=[Sg], w=[Sgb])
                if kd == 'p':
                    k.stq(ns_gla, ns_gla.t[pidx, j, d].rearrange("h d e -> d h e"), Sg, Sg[:])
            if kd == 'p':
                pidx += 1
        k.leave('gla')
        gemm(S5, False, 8, gla_out_w, gla_out_w.t[j, :, :], 1024, 'fm', **resid_hooks(li, 2))

    gcw_s = k.sb("gcw_s", [128, 1, 32, 3], F32)
    k.ld(gcw_s, gcw_s[:], gdn_cw, gdn_cw.t[:, :, :, :])
    dnw_s = k.sb("dnw_s", [128, 1, 2], F32)
    k.ld(dnw_s, dnw_s[:], gdn_nw, gdn_nw.t[:, :, :])
    xb0f = xbuf[0][:].rearrange("p c n -> p (c n)")
    d_beta = k.view(xbuf[0], xb0f[:, 0:NCH * 16].rearrange("p (c n) -> p c n", n=16), group='gdnx')
    d_nbeta = k.view(xbuf[0], xb0f[:, NCH * 16:2 * NCH * 16].rearrange("p (c n) -> p c n", n=16), group='gdnx')
    d_lg = k.view(xbuf[0], xb0f[:, 2 * NCH * 16:3 * NCH * 16].rearrange("p (c n) -> p c n", n=16), group='gdnx')
    d_lgb = k.sb("d_lgb", [128, NCH, 16], BF16)
    dbc16 = k.sb("dbc16", [128, 16], F32); dnega = k.sb("dnega", [128, 16], F32)
    h3 = lambda ap: ap.rearrange("p (h t) -> p h t", t=128)
    dq_fm = [k.view(inb[i], h3(inb[i][:, 0:1024]), group='gdn') for i in range(2)]
    dk_fm = [k.view(inb[i], h3(inb[i][:, 1024:2048]), group='gdn') for i in range(2)]
    dk_tm = [k.view(inb[i], h3(inb[i][:, 2048:3072]), group='gdn') for i in range(2)]
    dv_tm = [k.view(inb[i], inb[i][:, 3072:5120].rearrange("p (h e) -> p h e", e=256), group='gdn') for i in range(2)]
    dof = [k.view(inb[i], h3(inb[i][:, 5120:7168]), group='gdn') for i in range(2)]
    dzs = [k.view(inb[i], h3(inb[i][:, 7168:9216]), group='gdn') for i in range(2)]
    dET = k.view(W0, h3(W0[:, 0:1024]), group='gdn'); dEN = k.view(W0, h3(W0[:, 1024:2048]), group='gdn')
    dLT = k.view(W0, h3(W0[:, 2048:3072]), group='gdn'); dLTm = k.view(W0, h3(W0[:, 3072:4096]), group='gdn'); dLms = k.view(W0, h3(W0[:, 4096:5120]), group='gdn')
    dB = [k.view(W0, h3(W0[:, 5120 + i * 1024:6144 + i * 1024]), group='gdn') for i in range(2)]
    dA = [k.view(W0, h3(W0[:, 7168 + i * 1024:8192 + i * 1024]), group='gdn') for i in range(2)]
    dRTb = k.view(W0, h3(W0[:, 9216:10240]), group='gdn'); daT = k.view(W0, h3(W0[:, 10240:11264]), group='gdn'); dqd = k.view(W0, h3(W0[:, 11264:12288]), group='gdn')
    dSb = k.view(W1, W1[:, 0:2048].rearrange("p (h e) -> p h e", e=256), group='gdn')
    dvb = k.view(W1, W1[:, 2048:4096].rearrange("p (h e) -> p h e", e=256), group='gdn')
    dsq = k.view(W1, h3(W1[:, 2048:4096]), group='gdn', share=dvb)
    dkbe = k.view(W1, h3(W1[:, 4096:5120]), group='gdn'); dnwk = k.view(W1, h3(W1[:, 5120:6144]), group='gdn')
    dvn = k.view(W1, W1[:, 6144:8192].rearrange("p (h e) -> p h e", e=256), group='gdn')
    dkd = k.view(W1, h3(W1[:, 8192:9216]), group='gdn')
    don = k.view(W1, h3(W1[:, 9216:11264]), group='gdn')
    dY = k.view(W1, h3(W1[:, 11264:12288]), group='gdn'); dY2 = k.view(W1, h3(W1[:, 5120:6144]), group='gdn', share=dnwk)
    lmask = k.sb("lmask_s", [128, 14, 128], BF16)
    k.ld(lmask, lmask[:], lmk, lmk.t[:, :, :], q='pool')
    dS = k.view(Sst, Sst[:].rearrange("p (h e) -> p h e", e=256), group='gdn')
    dRTf = k.view(yc, h3(yc[:, 0:1024]), group='gdn')
    dob = k.view(yc, h3(yc[:]), group='gdn', share=dRTf)

    def gdn(li, j):
        norm_stage(li, 0)
        W = gdn_in_w
        gemm(H, False, 8, W, W.t[j, :, 0:4096], 4096, 'fm', epi_store_fm(S1, 0))
        conv_stage(S1, S2, 32, lambda blk, tap: gcw_s[:, j, blk, tap - 3:tap - 2], None, False, [gcw_s], l2blocks=tuple(range(16)), qscale_blocks=tuple(range(8)))
        gemm(H, False, 8, W, W.t[j, :, 4096:6144], 2048, 'fm', epi_store_fm(S6, 0, AF.Silu))
        k.ld(dbc16, dbc16[:], gdn_dtb, gdn_dtb.t[j].partition_broadcast(128))
        k.ld(dnega, dnega[:], gdn_alog, gdn_alog.t[j].partition_broadcast(128))
        k.A(lambda e: e.activation(out=dnega[:], in_=dnega[:], func=AF.Exp), r=[dnega], w=[dnega])
        k.V(lambda e: e.tensor_scalar(out=dnega[:], in0=dnega[:], scalar1=-1.0, scalar2=None, op0=ALU.mult), r=[dnega], w=[dnega])

        def epi_ab(ps, c0, t, sub, last):
            ch = t * 4 + sub
            k.A(lambda e: e.activation(out=d_beta[:, ch, :], in_=ps[:, 16:32], func=AF.Sigmoid), r=[ps], w=[d_beta])
            k.V(lambda e: e.tensor_scalar(out=d_nbeta[:, ch, :], in0=d_beta[:, ch, :], scalar1=-1.0, scalar2=None, op0=ALU.mult), r=[d_beta], w=[d_nbeta])
            k.V(lambda e: e.tensor_tensor(out=d_lg[:, ch, :], in0=ps[:, 0:16], in1=dbc16[:], op=ALU.add), r=[ps, dbc16], w=[d_lg])
            k.A(lambda e: e.activation(out=d_lg[:, ch, :], in_=d_lg[:, ch, :], func=AF.Exp), r=[d_lg], w=[d_lg])
            k.A(lambda e: e.activation(out=d_lg[:, ch, :], in_=d_lg[:, ch, :], func=AF.Ln, bias=1.0, scale=1.0), r=[d_lg], w=[d_lg])
            k.V(lambda e: e.tensor_tensor(out=d_lg[:, ch, :], in0=d_lg[:, ch, :], in1=dnega[:], op=ALU.mult), r=[d_lg, dnega], w=[d_lg])
            k.V(lambda e: e.tensor_copy(out=d_lgb[:, ch, :], in_=d_lg[:, ch, :]), r=[d_lg], w=[d_lgb])
        k.enter('gdnx')
        gemm(H, False, 8, W, W.t[j, :, 6144:6176], 32, 'tm', epi_ab)
        k.enter('gdn')
        pidx = 0
        for (o, L, kd) in SEQS:
            for d in (0, 1):
                if kd == 'p':
                    k.V(lambda e: e.memset(dS[:], 0.0), w=[dS])
                    k.V(lambda e: e.memset(dSb[:], 0.0), w=[dSb])
                else:
                    k.ld(dS, dS[:], st_gdn, st_gdn.t[j, d].rearrange("h d e -> d h e"))
                    k.A(lambda e: e.activation(out=dSb[:], in_=dS[:], func=AF.Copy), r=[dS], w=[dSb])
                ie = 127 if d == 0 else 0
                hs = slice(d * 8, (d + 1) * 8)
                order = chunk_order(L, d)

                def gdn_loads(n_, c):
                    t0 = o + c * 128
                    tsl = slice(t0, t0 + 128)
                    q_ = dq_fm[n_ % 2]; k_ = dk_fm[n_ % 2]; kt = dk_tm[n_ % 2]; v_ = dv_tm[n_ % 2]
                    k.ld(q_, q_[:], S2, S2.t[0:1024, tsl].rearrange("(h d) t -> d h t", d=128))
                    k.ld(k_, k_[:], S2, S2.t[1024:2048, tsl].rearrange("(h d) t -> d h t", d=128))
                    for q2 in range(2):
                        k.ld(kt, kt[:, q2 * 4:(q2 + 1) * 4, :].rearrange("p h t -> p (h t)"), S2, S2.t[1024 + q2 * 512:1024 + (q2 + 1) * 512, tsl], tr=True)
                    for q4 in range(4):
                        k.ld(v_, v_[:, q4 * 2:(q4 + 1) * 2, :].rearrange("p h e -> p (h e)"), S2, S2.t[2048 + q4 * 512:2048 + (q4 + 1) * 512, tsl], tr=True)
                    if d == 1:
                        k.ld(dof[n_ % 2], dof[n_ % 2][:], S5, S5.t[0:2048, tsl].rearrange("(b p) t -> p b t", p=128))
                        k.ld(dzs[n_ % 2], dzs[n_ % 2][:], S6, S6.t[0:2048, tsl].rearrange("(b p) t -> p b t", p=128))
                gdn_loads(0, order[0])
                for n_, c in enumerate(order):
                    if n_ + 1 < len(order):
                        gdn_loads(n_ + 1, order[n_ + 1])
                    t0 = o + c * 128; ch = t0 // 128
                    tsl = slice(t0, t0 + 128)
                    q_ = dq_fm[n_ % 2]; k_ = dk_fm[n_ % 2]; kt = dk_tm[n_ % 2]; v_ = dv_tm[n_ % 2]
                    lg = d_lg[:, ch, hs]; lgb = d_lgb[:, ch, hs]; beta = d_beta[:, ch, hs]; nbeta = d_nbeta[:, ch, hs]
                    k.G(lambda e: e.tensor_tensor(out=dET[:], in0=bc(lg, [128, 8, 128], 2), in1=bc(cmb[:, U_[d], :], [128, 8, 128], 1), op=ALU.mult), r=[d_lg, cmb], w=[dET])
                    k.G(lambda e: e.tensor_tensor(out=dEN[:], in0=bc(lg, [128, 8, 128], 2), in1=bc(cmb[:, V_[d], :], [128, 8, 128], 1), op=ALU.mult), r=[d_lg, cmb], w=[dEN])
                    for hf in range(2):
                        ps = k.psum()
                        k.M(lambda e: e.matmul(ps[:], cmb[:, V_[d], :], dET[:, hf * 4:(hf + 1) * 4, :].rearrange("p h t -> p (h t)"), start=True, stop=True), r=[cmb, dET], w=[ps])
                        k.A(lambda e: e.activation(out=dLT[:, hf * 4:(hf + 1) * 4, :], in_=h3(ps[:]), func=AF.Exp), r=[ps], w=[dLT])
                        ps2 = k.psum()
                        k.M(lambda e: e.matmul(ps2[:], cmb[:, U_[d], :], dEN[:, hf * 4:(hf + 1) * 4, :].rearrange("p h t -> p (h t)"), start=True, stop=True), r=[cmb, dEN], w=[ps2])
                        k.A(lambda e: e.activation(out=dLms[:, hf * 4:(hf + 1) * 4, :], in_=h3(ps2[:]), func=AF.Exp), r=[ps2], w=[dLms])
                        ps3 = k.psum()
                        k.M(lambda e: e.matmul(ps3[:], cmb[:, ONES, :], dET[:, hf * 4:(hf + 1) * 4, :].rearrange("p h t -> p (h t)"), start=True, stop=True), r=[cmb, dET], w=[ps3])
                        k.A(lambda e: e.activation(out=tmpf[:], in_=ps3[:], func=AF.Exp), r=[ps3], w=[tmpf])
                        k.V(lambda e: e.tensor_tensor(out=dqd[:, hf * 4:(hf + 1) * 4, :], in0=q_[:, hf * 4:(hf + 1) * 4, :], in1=h3(tmpf[:]), op=ALU.mult), r=[q_, tmpf], w=[dqd])
                    k.V(lambda e: e.tensor_copy(out=ecs[:, 24:32], in_=dLT[:, :, ie]), r=[dLT], w=[ecs])
                    k.V(lambda e: e.tensor_tensor(out=dLTm[:], in0=dLT[:], in1=bc(cmb[:, U_[d], :], [128, 8, 128], 1), op=ALU.mult), r=[dLT, cmb], w=[dLTm])
                    k.V(lambda e: e.tensor_tensor(out=dLms[:], in0=dLms[:], in1=bc(cmb[:, V_[d], :], [128, 8, 128], 1), op=ALU.mult), r=[dLms, cmb], w=[dLms])
                    ps = k.psum()
                    k.M(lambda e: e.matmul(ps[:, 0:8], cmb[:, U_[d], :], lgb, start=True, stop=True), r=[cmb, d_lgb], w=[ps])
                    k.M(lambda e: e.matmul(ps[:, 8:16], cmb[:, ONES, :], lgb, start=True, stop=True), r=[cmb, d_lgb], w=[ps])
                    k.A(lambda e: e.activation(out=ecs[:, 0:16], in_=ps[:, 0:16], func=AF.Exp), r=[ps], w=[ecs])
                    k.V(lambda e: e.tensor_tensor(out=ecs[:, 16:24], in0=ecs[:, 0:8], in1=beta, op=ALU.mult), r=[ecs, d_beta], w=[ecs])
                    mN, mT, OffN, OffT = dB[0], dA[0], dB[1], dA[1]
                    dD = dLms
                    for hf in range(2):
                        ps = k.psum()
                        for h4 in range(4):
                            h = hf * 4 + h4
                            k.M(lambda e: e.matmul(ps[:, h4 * 128:(h4 + 1) * 128], k_[:, h, :], k_[:, h, :], start=True, stop=True), r=[k_], w=[ps])
                        for h4 in range(4):
                            h = hf * 4 + h4
                            k.V(lambda e: e.scalar_tensor_tensor(out=mN[:, h, :], in0=ps[:, h4 * 128:(h4 + 1) * 128], scalar=d_beta[:, ch, d * 8 + h:d * 8 + h + 1], in1=dLms[:, h, :], op0=ALU.mult, op1=ALU.mult),
                                r=[ps, d_beta, dLms], w=[mN])
                    for hf in range(2):
                        ps = k.psum()
                        for h4 in range(4):
                            h = hf * 4 + h4
                            k.M(lambda e: e.matmul(ps[:, h4 * 128:(h4 + 1) * 128], mN[:, h, :], cmb[:, ID, :], start=True, stop=True), r=[mN, cmb], w=[ps])
                        k.A(lambda e: e.activation(out=mT[:, hf * 4:(hf + 1) * 4, :], in_=h3(ps[:]), func=AF.Copy), r=[ps], w=[mT])
                    MN = lambda lv: lmask[:, (lv if d == 0 else 7 + lv), :]
                    MT_ = lambda lv: lmask[:, (7 + lv if d == 0 else lv), :]
                    k.V(lambda e: e.tensor_tensor(out=OffN[:], in0=mN[:], in1=bc(MN(0), [128, 8, 128], 1), op=ALU.mult), r=[mN, lmask], w=[OffN])
                    k.V(lambda e: e.tensor_tensor(out=dD[:], in0=bc(cmb[:, ID, :], [128, 8, 128], 1), in1=OffN[:], op=ALU.subtract), r=[cmb, OffN], w=[dD])
                    k.V(lambda e: e.tensor_tensor(out=OffT[:], in0=mT[:], in1=bc(MT_(0), [128, 8, 128], 1), op=ALU.mult), r=[mT, lmask], w=[OffT])
                    k.V(lambda e: e.tensor_tensor(out=dRTb[:], in0=bc(cmb[:, ID, :], [128, 8, 128], 1), in1=OffT[:], op=ALU.subtract), r=[cmb, OffT], w=[dRTb])
                    for lv in range(1, 7):
                        k.V(lambda e: e.tensor_tensor(out=OffN[:], in0=mN[:], in1=bc(MN(lv), [128, 8, 128], 1), op=ALU.mult), r=[mN, lmask], w=[OffN])
                        k.G(lambda e: e.tensor_tensor(out=OffT[:], in0=mT[:], in1=bc(MT_(lv), [128, 8, 128], 1), op=ALU.mult), r=[mT, lmask], w=[OffT])
                        for hf in range(2):
                            ps = k.psum()
                            for h4 in range(4):
                                h = hf * 4 + h4
                                k.M(lambda e: e.matmul(ps[:, h4 * 128:(h4 + 1) * 128], OffT[:, h, :], dD[:, h, :], start=True, stop=True), r=[OffT, dD], w=[ps])
                            k.A(lambda e: e.activation(out=dY[:, hf * 4:(hf + 1) * 4, :], in_=h3(ps[:]), func=AF.Copy), r=[ps], w=[dY])
                            ps2 = k.psum()
                            for h4 in range(4):
                                h = hf * 4 + h4
                                k.M(lambda e: e.matmul(ps2[:, h4 * 128:(h4 + 1) * 128], OffN[:, h, :], dRTb[:, h, :], start=True, stop=True), r=[OffN, dRTb], w=[ps2])
                            k.V(lambda e: e.tensor_copy(out=dY2[:, hf * 4:(hf + 1) * 4, :], in_=h3(ps2[:])), r=[ps2], w=[dY2])
                        pX = [k.psum(), k.psum()]; pX2 = [k.psum(), k.psum()]
                        for hf in range(2):
                            for h4 in range(4):
                                h = hf * 4 + h4
                                k.M(lambda e: e.matmul(pX[hf][:, h4 * 128:(h4 + 1) * 128], dRTb[:, h, :], dY[:, h, :], start=True, stop=True), r=[dRTb, dY], w=[pX[hf]])
                            for h4 in range(4):
                                h = hf * 4 + h4
                                k.M(lambda e: e.matmul(pX2[hf][:, h4 * 128:(h4 + 1) * 128], dD[:, h, :], dY2[:, h, :], start=True, stop=True), r=[dD, dY2], w=[pX2[hf]])
                        for hf in range(2):
                            k.V(lambda e: e.tensor_tensor(out=dD[:, hf * 4:(hf + 1) * 4, :], in0=dD[:, hf * 4:(hf + 1) * 4, :], in1=h3(pX[hf][:]), op=ALU.subtract), r=[dD, pX[hf]], w=[dD])
                            k.V(lambda e: e.tensor_tensor(out=dRTb[:, hf * 4:(hf + 1) * 4, :], in0=dRTb[:, hf * 4:(hf + 1) * 4, :], in1=h3(pX2[hf][:]), op=ALU.subtract), r=[dRTb, pX2[hf]], w=[dRTb])
                    k.G(lambda e: e.tensor_tensor(out=dvb[:], in0=v_[:], in1=bc(beta, [128, 8, 256], 2), op=ALU.mult), r=[v_, d_beta], w=[dvb])
                    k.G(lambda e: e.tensor_tensor(out=dkbe[:], in0=kt[:], in1=bc(ecs[:, 16:24], [128, 8, 128], 2), op=ALU.mult), r=[kt, ecs], w=[dkbe])
                    k.G(lambda e: e.tensor_tensor(out=dkd[:], in0=kt[:], in1=bc(ecs[:, 24:32], [128, 8, 128], 2), op=ALU.mult), r=[kt, ecs], w=[dkd])
                    for hf in range(2):
                        ps = k.psum()
                        for h4 in range(4):
                            h = hf * 4 + h4
                            k.M(lambda e: e.matmul(ps[:, h4 * 128:(h4 + 1) * 128], dkbe[:, h, :], dRTb[:, h, :], start=True, stop=True), r=[dkbe, dRTb], w=[ps])
                        k.A(lambda e: e.activation(out=dnwk[:, hf * 4:(hf + 1) * 4, :], in_=h3(ps[:]), func=AF.Copy, scale=-1.0), r=[ps], w=[dnwk])
                    for hp in range(4):
                        ps = k.psum()
                        for h2 in range(2):
                            h = hp * 2 + h2
                            k.M(lambda e: e.matmul(ps[:, h2 * 256:(h2 + 1) * 256], dRTb[:, h, :], dvb[:, h, :], start=True, stop=False), r=[dRTb, dvb], w=[ps])
                            k.M(lambda e: e.matmul(ps[:, h2 * 256:(h2 + 1) * 256], dnwk[:, h, :], dSb[:, h, :], start=False, stop=True), r=[dnwk, dSb], w=[ps])
                        k.A(lambda e: e.activation(out=dvn[:, hp * 2:(hp + 1) * 2, :], in_=ps[:].rearrange("p (h e) -> p h e", e=256), func=AF.Copy), r=[ps], w=[dvn])
                    for hf in range(2):
                        ps = k.psum()
                        for h4 in range(4):
                            h = hf * 4 + h4
                            k.M(lambda e: e.matmul(ps[:, h4 * 128:(h4 + 1) * 128], k_[:, h, :], q_[:, h, :], start=True, stop=True), r=[k_, q_], w=[ps])
                        k.V(lambda e: e.tensor_tensor(out=daT[:, hf * 4:(hf + 1) * 4, :], in0=h3(ps[:]), in1=dLTm[:, hf * 4:(hf + 1) * 4, :], op=ALU.mult), r=[ps, dLTm], w=[daT])
                    pso = [k.psum() for _ in range(4)]
                    for h in range(8):
                        for eb in range(2):
                            blk = h * 2 + eb
                            po = pso[blk // 4]; cs_ = slice((blk % 4) * 128, (blk % 4 + 1) * 128)
                            k.M(lambda e: e.matmul(po[:, cs_], dSb[:, h, eb * 128:(eb + 1) * 128], dqd[:, h, :], start=True, stop=False), r=[dSb, dqd], w=[po])
                            k.M(lambda e: e.matmul(po[:, cs_], dvn[:, h, eb * 128:(eb + 1) * 128], daT[:, h, :], start=False, stop=True), r=[dvn, daT], w=[po])
                    if d == 0:
                        for q4 in range(4):
                            k.A(lambda e: e.activation(out=don[:, q4 * 4:(q4 + 1) * 4, :], in_=h3(pso[q4][:]), func=AF.Copy), r=[pso[q4]], w=[don])
                        k.stq(S5, S5.t[0:2048, tsl].rearrange("(b p) t -> p b t", p=128), don, don[:])
                    else:
                        of_ = dof[n_ % 2]; zs_ = dzs[n_ % 2]
                        for q4 in range(4):
                            k.V(lambda e: e.tensor_tensor(out=dob[:, q4 * 4:(q4 + 1) * 4, :], in0=h3(pso[q4][:]), in1=of_[:, q4 * 4:(q4 + 1) * 4, :], op=ALU.add), r=[pso[q4], of_], w=[dob])
                    for h in range(8):
                        pst = k.psum()
                        k.M(lambda e: e.matmul(pst[:, 0:256], dkd[:, h, :], dvn[:, h, :], start=True, stop=True), r=[dkd, dvn], w=[pst])
                        k.V(lambda e: e.scalar_tensor_tensor(out=dS[:, h, :], in0=dS[:, h, :], scalar=ecs[:, 8 + h:9 + h], in1=pst[:, 0:256], op0=ALU.mult, op1=ALU.add), r=[dS, ecs, pst], w=[dS])
                    k.A(lambda e: e.activation(out=dSb[:], in_=dS[:], func=AF.Copy), r=[dS], w=[dSb])
                    if d == 1:
                        k.G(lambda e: e.tensor_tensor(out=dsq[:], in0=dob[:], in1=dob[:], op=ALU.mult), r=[dob], w=[dsq])
                        for hf in range(2):
                            psn = k.psum()
                            for h4 in range(4):
                                h = hf * 4 + h4
                                for eb in range(2):
                                    k.M(lambda e: e.matmul(psn[:, h4 * 128:(h4 + 1) * 128], cmb[:, ONES, :], dsq[:, h * 2 + eb, :], start=(eb == 0), stop=(eb == 1)), r=[cmb, dsq], w=[psn])
                            k.A(lambda e: e.activation(out=rsb[:], in_=psn[:], func=AF.Sqrt, bias=EPS, scale=1.0 / 256), r=[psn], w=[rsb])
                            k.V(lambda e: e.reciprocal(out=rsb[:], in_=rsb[:]), r=[rsb], w=[rsb])
                            for eb in range(2):
                                gv_ = dob[:, hf * 8:(hf + 1) * 8, :].rearrange("p (h b) t -> p h b t", b=2)[:, :, eb, :]
                                k.V(lambda e: e.scalar_tensor_tensor(out=gv_, in0=gv_, scalar=dnw_s[:, j, eb:eb + 1], in1=h3(rsb[:]), op0=ALU.mult, op1=ALU.mult), r=[dob, dnw_s, rsb], w=[dob])
                        k.V(lambda e: e.tensor_tensor(out=don[:], in0=dob[:], in1=zs_[:], op=ALU.mult), r=[dob, zs_], w=[don])
                        k.stq(S5, S5.t[0:2048, tsl].rearrange("(b p) t -> p b t", p=128), don, don[:])
                if kd == 'p':
                    k.stq(ns_gdn, ns_gdn.t[pidx, j, d].rearrange("h d e -> d h e"), dS, dS[:])
            if kd == 'p':
                pidx += 1
        k.leave('gdn'); k.leave('gdnx')
        gemm(S5, False, 16, gdn_out_w, gdn_out_w.t[j, :, :], 1024, 'fm', **resid_hooks(li, 2))

    wdec = k.sb("wdec", [128, 32], F32)

    for (li, kind, j) in LAYERS:
        if kind == 'ssd':
            ssd(li, j)
        elif kind == 'gla':
            gla(li, j)
        elif kind == 'gdn':
            gdn(li, j)
        elif kind == 'none':
            pass
        else:
            raise NotImplementedError(kind)
        if kind != 'none' or True:
            ffn(li)
    norm_stage(0, 0, final=True)
    outs = [yT, X, ns_ssm, ns_gla, ns_gdn, S1, S2, S3, S4, S5, H]
    k.P.wait_all('sp', [o.b for o in outs])
    k.P.run()
    st.close()
    return nc, k.P.ninst


def _consts():
    i = np.arange(128)
    kk, ii = np.meshgrid(i, i, indexing='ij')
    cm = np.zeros((128, 10, 128), np.float32)
    cm[:, 0] = (kk <= ii); cm[:, 1] = (kk >= ii); cm[:, 2] = (kk > ii); cm[:, 3] = (kk < ii)
    cm[:, 4] = (kk == ii); cm[:, 5] = 1.0
    cm[:, 6] = (kk <= ii) / -16.0; cm[:, 7] = (kk >= ii) / -16.0; cm[:, 8] = (kk > ii) / -16.0; cm[:, 9] = (kk < ii) / -16.0
    return cm


def _fm(v, nblk):
    v = np.asarray(v, np.float32)
    lead = v.shape[:-1]
    v = v.reshape(*lead, nblk, 128)
    return np.ascontiguousarray(np.moveaxis(v, -1, 0))


def prep_common(w):
    d = {}
    d['cmask'] = _consts()
    ii, jj = np.meshgrid(np.arange(128), np.arange(128), indexing='ij')
    lm = np.zeros((128, 14, 128), np.float32)
    for lv in range(7):
        b = 1 << lv
        m = ((ii // (2 * b)) == (jj // (2 * b))) & ((ii % (2 * b)) >= b) & ((jj % (2 * b)) < b)
        lm[:, lv] = m; lm[:, 7 + lv] = m.T
    d['lmask'] = lm
    d['nmw'] = _fm(w['norm_mix_w'], 8); d['nfw'] = _fm(w['norm_ffn_w'], 8); d['fnw'] = _fm(w['final_norm_w'], 8)
    d['ada_w'] = w['ada_w']; d['ada_b'] = _fm(w['ada_b'], 48)
    d['ffn_in_w'] = w['ffn_in_w']; d['ffn_out_w'] = w['ffn_out_w']
    d['ffn_cw'] = _fm(np.moveaxis(np.asarray(w['ffn_conv_w']).reshape(4, 9, 2816), 1, 2).reshape(4, 2816 * 9).reshape(4, 22, 128, 9).transpose(0, 1, 3, 2).reshape(4, 22, 9, 128), 1)[..., 0] if False else None
    cw = np.asarray(w['ffn_conv_w'], np.float32).reshape(4, 9, 22, 128)
    d['ffn_cw'] = np.ascontiguousarray(cw.transpose(3, 0, 2, 1))
    d['ffn_cb'] = _fm(w['ffn_conv_b'], 22)
    d['ssm_in_w'] = w['ssm_in_w']
    cw = np.asarray(w['ssm_conv_w'], np.float32).reshape(2, 3, 32, 128)
    d['ssm_cw'] = np.ascontiguousarray(cw.transpose(3, 0, 2, 1))
    d['ssm_cb'] = _fm(w['ssm_conv_b'], 32)
    d['ssm_dtb'] = np.asarray(w['ssm_dt_bias'], np.float32).reshape(2, 1, 64)
    d['ssm_alog'] = np.asarray(w['ssm_a_log'], np.float32).reshape(2, 1, 64)
    d['ssm_d'] = np.asarray(w['ssm_d'], np.float32).reshape(2, 1, 32)
    d['ssm_nw'] = np.asarray(w['ssm_norm_w'], np.float32).reshape(2, 1, 2048)
    d['ssm_out_w'] = w['ssm_out_w']
    d['gla_in_w'] = w['gla_in_w']
    w1 = np.zeros((1, 1024, 128), np.float32)
    g1 = np.asarray(w['gla_gate_w1'], np.float32)
    w1[:, :, 0:16] = g1[:, 0]; w1[:, :, 32:48] = g1[:, 1]
    d['gla_w1'] = w1
    w2 = np.zeros((1, 64, 2, 512), np.float32)
    g2 = np.asarray(w['gla_gate_w2'], np.float32)
    w2[:, 0:16, 0] = g2[:, 0]; w2[:, 32:48, 1] = g2[:, 1]
    d['gla_w2'] = w2.reshape(1, 64, 1024)
    d['gla_gb'] = np.asarray(w['gla_gate_b'], np.float32).reshape(1, 1, 1024)
    d['gla_nw'] = _fm(w['gla_norm_w'], 2); d['gla_out_w'] = w['gla_out_w']
    d['gdn_in_w'] = w['gdn_in_w']
    cw = np.asarray(w['gdn_conv_w'], np.float32).reshape(1, 3, 32, 128)
    d['gdn_cw'] = np.ascontiguousarray(cw.transpose(3, 0, 2, 1))
    d['gdn_dtb'] = np.asarray(w['gdn_dt_bias'], np.float32).reshape(1, 1, 16)
    d['gdn_alog'] = np.asarray(w['gdn_a_log'], np.float32).reshape(1, 1, 16)
    d['gdn_nw'] = _fm(w['gdn_norm_w'], 2); d['gdn_out_w'] = w['gdn_out_w']
    return {k_: np.ascontiguousarray(np.asarray(v, np.float32)) for k_, v in d.items()}


FULL_SEQS = [(0, 256, 'p'), (256, 256, 'p'), (512, 4096, 's')]
FULL_LAYERS = [(0, 'ssd', 0), (1, 'gla', 0), (2, 'gdn', 0), (3, 'ssd', 1)]


def kernel(**inp):
    w = {k_: np.asarray(v) for k_, v in inp.items()}
    common = prep_common(w)
    nc, ninst = build_program(FULL_SEQS, FULL_LAYERS)
    xp = np.asarray(w['x_prompt'], np.float32); xs = np.asarray(w['x_sample'], np.float32)
    in_maps = []
    for core in range(8):
        b = core % 2
        xcat = np.concatenate([xp[2 * core], xp[2 * core + 1], xs[b]], axis=0)
        m = dict(common)
        m['xT'] = np.ascontiguousarray(xcat.T)
        cond = np.stack([np.asarray(w['c_ctx'], np.float32), np.asarray(w['c'], np.float32)[b]], axis=1)
        m['condT'] = np.ascontiguousarray(cond.reshape(8, 128, 2).transpose(1, 0, 2))
        ss = np.asarray(w['state_ssm'], np.float32)[b]
        m['st_ssm'] = np.ascontiguousarray(ss.transpose(0, 1, 3, 2, 4).reshape(2, 2, 128, 2048))
        m['st_gla'] = np.ascontiguousarray(np.asarray(w['state_gla'], np.float32)[b])
        m['st_gdn'] = np.ascontiguousarray(np.asarray(w['state_gdn'], np.float32)[b])
        in_maps.append(m)
    res = run_bass_kernel_spmd(nc, in_maps, core_ids=list(range(8)))
    y_prompt = np.zeros((16, 256, 1024), np.float32); y_sample = np.zeros((2, 4096, 1024), np.float32)
    n_ssm = np.zeros((16, 2, 2, 32, 128, 64), np.float32); n_gla = np.zeros((16, 1, 2, 4, 128, 256), np.float32)
    n_gdn = np.zeros((16, 1, 2, 8, 128, 256), np.float32)
    for core in range(8):
        r = res.results[core]
        y = np.asarray(r['yT']).T
        y_prompt[2 * core] = y[0:256]; y_prompt[2 * core + 1] = y[256:512]
        if core < 2:
            y_sample[core] = y[512:]
        s_ = np.asarray(r['ns_ssm']).reshape(2, 2, 2, 128, 32, 64).transpose(0, 1, 2, 4, 3, 5)
        n_ssm[2 * core:2 * core + 2] = s_
        n_gla[2 * core:2 * core + 2] = np.asarray(r['ns_gla'])
        n_gdn[2 * core:2 * core + 2] = np.asarray(r['ns_gdn'])
    return (y_prompt, y_sample, n_ssm, n_gla, n_gdn)
```

```python
import contextlib
import numpy as np
import concourse.bass as bass
import concourse.mybir as mybir
from concourse.bass_utils import run_bass_kernel_spmd

F32 = mybir.dt.float32
BF16 = mybir.dt.bfloat16
AF = mybir.ActivationFunctionType
ALU = mybir.AluOpType
AX = mybir.AxisListType
ENGS = ['pe', 'act', 'dve', 'pool', 'sp']
EPS = 1e-6
DBG_STOP = None


class Buf:
    __slots__ = ('name', 'lw', 'rd', 'acc')

    def __init__(self, name, acc=False):
        self.name = name
        self.lw = {}
        self.rd = {}
        self.acc = acc


class _Rec:
    def __init__(self):
        self.call = None

    def __getattr__(self, name):
        def f(*a, **kw):
            self.call = (name, a, kw)
            return self
        return f


class Prog:
    def __init__(self, nc):
        self.nc = nc
        self.q = {e: [] for e in ENGS}
        self.cnt = {e: 0 for e in ENGS}
        self.known = {e: {} for e in ENGS}
        self.dma_cnt = {}
        self.ninst = 0

    def _need(self, eng, key, val, src, waits):
        if src == eng and eng == 'pe':
            return
        if self.known[eng].get(key, 0) >= val:
            return
        if waits.get(key, 0) < val:
            waits[key] = val

    def emit(self, eng, fn, reads=(), writes=(), dma_key=None):
        rec = _Rec()
        fn(rec)
        _c = rec.call
        fn = lambda e, _c=_c: getattr(e, _c[0])(*_c[1], **_c[2])
        waits = {}
        for b in reads:
            for k, (v, s) in b.lw.items():
                self._need(eng, k, v, s, waits)
        for b in writes:
            if not b.acc:
                for k, (v, s) in b.lw.items():
                    self._need(eng, k, v, s, waits)
            for k, (v, s) in b.rd.items():
                self._need(eng, k, v, s, waits)
        for k, v in waits.items():
            self.known[eng][k] = v
        if dma_key is None:
            self.cnt[eng] += 1
            key, val, src, inc = eng, self.cnt[eng], eng, 1
        else:
            self.dma_cnt[dma_key] = self.dma_cnt.get(dma_key, 0) + 16
            key, val, src, inc = dma_key, self.dma_cnt[dma_key], None, 16
        for b in reads:
            if b.rd.get(key, (0, None))[0] < val:
                b.rd[key] = (val, src)
        for b in writes:
            if b.acc:
                b.lw[key] = (val, src)
            else:
                b.lw = {key: (val, src)}
            b.rd = {}
        self.q[eng].append((list(waits.items()), fn, (key, inc)))
        self.ninst += 1
        return (key, val, src)

    def wait_all(self, eng, bufs):
        waits = {}
        for b in bufs:
            for k, (v, s) in b.lw.items():
                self._need(eng, k, v, s, waits)
        self.q[eng].append((list(waits.items()), None, None))

    def run(self):
        nc = self.nc
        keys = list(ENGS) + sorted(self.dma_cnt.keys())
        with contextlib.ExitStack() as st:
            sems = {k: st.enter_context(nc.semaphore("s%d" % i)) for i, k in enumerate(keys)}
            block = st.enter_context(nc.Block())

            def mk(engname):
                def body(e):
                    for waits, fn, inc in self.q[engname]:
                        for k, v in waits:
                            e.wait_ge(sems[k], v)
                        if fn is not None:
                            fn(e).then_inc(sems[inc[0]], inc[1])
                return body
            block.tensor(mk('pe'))
            block.scalar(mk('act'))
            block.vector(mk('dve'))
            block.gpsimd(mk('pool'))
            block.sync(mk('sp'))


class Tl:
    def __init__(self, t, name, acc=False):
        self.t = t
        self.b = Buf(name, acc)
        self.name = name

    def __getitem__(self, k):
        return self.t[k]


class KB:
    def __init__(self, nc, st):
        self.nc = nc
        self.st = st
        self.P = Prog(nc)
        self.ps = [Tl(st.enter_context(nc.psum_tensor("ps%d" % i, [128, 512], F32)), "ps%d" % i) for i in range(8)]
        self.psi = 0
        self.nkey = 0
        self.keymap = {}
        self.groups = {}
        self.nview = 0
        self.cast_tok = Buf('castdma', acc=True)
        self.tr_tok = Buf('trdma', acc=True)

    def sb(self, name, shape, dt):
        return Tl(self.st.enter_context(self.nc.sbuf_tensor(name, shape, dt)), name)

    def dram(self, name, shape, dt, kind=None):
        if kind is None:
            t = self.nc.dram_tensor(name, shape, dt)
        else:
            t = self.nc.dram_tensor(name, shape, dt, kind=kind)
        return Tl(t.ap(), name, acc=True)

    def view(self, parent, ap, group=None, share=None):
        v = Tl(ap, parent.name)
        if group is None:
            v.b = parent.b
        else:
            self.nview += 1
            v.name = "%s#%d" % (parent.name, self.nview)
            if share is not None:
                v.b = share.b
                v.name = share.name
            self.groups.setdefault(group, []).append((parent, v))
        return v

    @staticmethod
    def _union(dicts):
        out = {}
        for d_ in dicts:
            for k_, (v_, s_) in d_.items():
                if out.get(k_, (0, None))[0] < v_:
                    out[k_] = (v_, s_)
        return out

    def enter(self, group):
        for parent, v in self.groups.get(group, []):
            v.b.lw = self._union([parent.b.lw, parent.b.rd])
            v.b.rd = {}

    def leave(self, group):
        parents = {}
        for parent, v in self.groups.get(group, []):
            parents.setdefault(id(parent), (parent, []))[1].append(v)
        for parent, vs in parents.values():
            parent.b.lw = self._union([parent.b.lw, parent.b.rd] + [v.b.lw for v in vs] + [v.b.rd for v in vs])
            parent.b.rd = {}

    def psum(self):
        p = self.ps[self.psi % 8]
        self.psi += 1
        return p

    def _b(self, l):
        return [x.b for x in l]

    def M(self, fn, r=(), w=()):
        return self.P.emit('pe', fn, self._b(r), self._b(w))

    def A(self, fn, r=(), w=()):
        return self.P.emit('act', fn, self._b(r), self._b(w))

    def V(self, fn, r=(), w=()):
        return self.P.emit('dve', fn, self._b(r), self._b(w))

    def G(self, fn, r=(), w=()):
        return self.P.emit('pool', fn, self._b(r), self._b(w))

    def _key(self, s):
        if s not in self.keymap:
            self.keymap[s] = "d%04d" % len(self.keymap)
        return self.keymap[s]

    def ld(self, dst, dst_ap, src, src_ap, q='sp', tr=False):
        key = self._key('ld:' + dst.name)
        if tr:
            fn = lambda e: e.dma_start_transpose(out=dst_ap, in_=src_ap)
        else:
            fn = lambda e: e.dma_start(out=dst_ap, in_=src_ap)
        if tr:
            return self.P.emit(q, fn, [src.b, self.cast_tok], [dst.b, self.tr_tok], dma_key=key)
        if q == 'pool':
            return self.P.emit(q, fn, [src.b, self.tr_tok], [dst.b, self.cast_tok], dma_key=key)
        return self.P.emit(q, fn, [src.b], [dst.b], dma_key=key)

    def stq(self, dst, dst_ap, src, src_ap, q='sp'):
        key = self._key('st:' + src.name)
        return self.P.emit(q, lambda e: e.dma_start(out=dst_ap, in_=src_ap), [src.b], [dst.b], dma_key=key)


def bc(ap, shape, axis):
    return ap.unsqueeze(axis).broadcast_to(shape)


def build_program(SEQS, LAYERS, dbg=()):
    T = sum(L for _, L, _ in SEQS)
    NT = T // 512
    assert T % 512 == 0
    NCH = T // 128
    NP = sum(1 for s in SEQS if s[2] == 'p')
    nc = bass.Bass("TRN2", target_bir_lowering=False)
    st = contextlib.ExitStack()
    k = KB(nc, st)
    IN = lambda name, shape, dt=F32: k.dram(name, shape, dt, kind="ExternalInput")
    OUT = lambda name, shape, dt=F32: k.dram(name, shape, dt, kind="ExternalOutput")
    dbgset = set(dbg)
    SCR = lambda name, shape, dt=BF16: k.dram(name, shape, dt, kind=("ExternalOutput" if name in dbgset else None))

    xT = IN("xT", [1024, T])
    condT = IN("condT", [128, 8, 2])
    cm = IN("cmask", [128, 10, 128])
    nmw = IN("nmw", [128, 4, 8]); nfw = IN("nfw", [128, 4, 8]); fnw = IN("fnw", [128, 8])
    ada_w = IN("ada_w", [4, 1024, 6144]); ada_b = IN("ada_b", [128, 4, 48])
    ffn_in_w = IN("ffn_in_w", [4, 1024, 5632]); ffn_cw = IN("ffn_cw", [128, 4, 22, 9]); ffn_cb = IN("ffn_cb", [128, 4, 22])
    ffn_out_w = IN("ffn_out_w", [4, 2816, 1024])
    ssm_in_w = IN("ssm_in_w", [2, 1024, 6208]); ssm_cw = IN("ssm_cw", [128, 2, 32, 3]); ssm_cb = IN("ssm_cb", [128, 2, 32])
    ssm_dtb = IN("ssm_dtb", [2, 1, 64]); ssm_alog = IN("ssm_alog", [2, 1, 64]); ssm_d = IN("ssm_d", [2, 1, 32])
    ssm_nw = IN("ssm_nw", [2, 1, 2048]); ssm_out_w = IN("ssm_out_w", [2, 2048, 1024])
    gla_in_w = IN("gla_in_w", [1, 1024, 3072]); gla_w1 = IN("gla_w1", [1, 1024, 128]); gla_w2 = IN("gla_w2", [1, 64, 1024]); gla_gb = IN("gla_gb", [1, 1, 1024])
    gla_nw = IN("gla_nw", [128, 1, 2]); gla_out_w = IN("gla_out_w", [1, 1024, 1024])
    gdn_in_w = IN("gdn_in_w", [1, 1024, 6176]); gdn_cw = IN("gdn_cw", [128, 1, 32, 3])
    gdn_dtb = IN("gdn_dtb", [1, 1, 16]); gdn_alog = IN("gdn_alog", [1, 1, 16]); gdn_nw = IN("gdn_nw", [128, 1, 2])
    gdn_out_w = IN("gdn_out_w", [1, 2048, 1024])
    lmk = IN("lmask", [128, 14, 128])
    st_ssm = IN("st_ssm", [2, 2, 128, 2048]); st_gla = IN("st_gla", [1, 2, 4, 128, 256]); st_gdn = IN("st_gdn", [1, 2, 8, 128, 256])
    yT = OUT("yT", [1024, T])
    ns_ssm = OUT("ns_ssm", [max(NP, 1), 2, 2, 128, 2048]); ns_gla = OUT("ns_gla", [max(NP, 1), 1, 2, 4, 128, 256])
    ns_gdn = OUT("ns_gdn", [max(NP, 1), 1, 2, 8, 128, 256])
    X = k.dram("X", [1024, T], F32, kind=("ExternalOutput" if "X" in dbgset else None))
    H = SCR("H", [1024, T])
    S1 = SCR("S1", [4096, T])
    S2 = SCR("S2", [4096, T])
    S3 = SCR("S3", [T, 2048])
    S4 = SCR("S4", [T, 2048])
    S5 = SCR("S5", [2816, T])
    S6 = SCR("S6", [2048, T])
    S7 = SCR("S7", [T, 1024])

    cmb = k.sb("cmb", [128, 10, 128], BF16)
    k.ld(cmb, cmb[:], cm, cm.t[:, :, :], q='pool')
    Uf, Ub, Vf, Vb, ID, ONES, UGf, UGb, VGf, VGb = range(10)
    UG_ = {0: UGf, 1: UGb}; VG_ = {0: VGf, 1: VGb}
    U_ = {0: Uf, 1: Ub}; V_ = {0: Vf, 1: Vb}
    nmw_s = k.sb("nmw_s", [128, 4, 8], F32); nfw_s = k.sb("nfw_s", [128, 4, 8], F32); fnw_s = k.sb("fnw_s", [128, 8], F32)
    k.ld(nmw_s, nmw_s[:], nmw, nmw.t[:, :, :]); k.ld(nfw_s, nfw_s[:], nfw, nfw.t[:, :, :]); k.ld(fnw_s, fnw_s[:], fnw, fnw.t[:, :])
    adab_s = k.sb("adab_s", [128, 4, 48], F32)
    k.ld(adab_s, adab_s[:], ada_b, ada_b.t[:, :, :])
    mod = k.sb("mod", [128, 4, 48, 2], F32)
    gm = k.sb("gm", [128, 4, 2, 8, 2], F32)

    xbuf = [k.sb("xb%d" % i, [128, 8, 512], F32) for i in range(2)]
    for t in range(NT):
        xb = xbuf[t % 2]
        k.ld(xb, xb[:], xT, xT.t[:, t * 512:(t + 1) * 512].rearrange("(c p) t -> p c t", p=128))
        k.stq(X, X.t[:, t * 512:(t + 1) * 512].rearrange("(c p) t -> p c t", p=128), xb, xb[:])
    cnd = k.sb("cnd", [128, 8, 2], F32); cndb = k.sb("cndb", [128, 8, 2], BF16)
    k.ld(cnd, cnd[:], condT, condT.t[:, :, :])
    k.A(lambda e: e.activation(out=cndb[:], in_=cnd[:], func=AF.Silu), r=[cnd], w=[cndb])
    wbuf = [k.sb("wb%d" % i, [128, 12288], BF16) for i in range(2)]
    wi = [0]

    def load_w(src, src_ap3, kc, ncols, after=()):
        wb = wbuf[wi[0] % 2]; wi[0] += 1
        view = wb[:, 0:kc * ncols].rearrange("p (c n) -> p c n", c=kc)
        key = k._key('ld:' + wb.name)
        for c in range(kc):
            k.P.emit('pool', lambda e: e.dma_start(out=view[:, c, :], in_=src_ap3[c * 128:(c + 1) * 128, :]), [src.b, k.tr_tok] + [a.b for a in after], [wb.b, k.cast_tok], dma_key=key)
        return wb, view

    for (li, kind, j) in LAYERS:
        for qd in range(6):
            wb, wv = load_w(ada_w, ada_w.t[li, :, qd * 1024:(qd + 1) * 1024], 8, 1024)
            ps = k.psum()
            for cb in range(8):
                for c in range(8):
                    k.M(lambda e, c=c, cb=cb, wv=wv, ps=ps: e.matmul(ps[:, cb * 2:cb * 2 + 2], wv[:, c, cb * 128:(cb + 1) * 128], cndb[:, c, :],
                                                                       start=(c == 0), stop=(c == 7)), r=[wb, cndb], w=[ps])
            k.V(lambda e, ps=ps, qd=qd, li=li: e.tensor_tensor(out=mod[:, li, qd * 8:(qd + 1) * 8, :], in0=ps[:, 0:16].rearrange("p (a b) -> p a b", b=2),
                                                                in1=bc(adab_s[:, li, qd * 8:(qd + 1) * 8], [128, 8, 2], 2), op=ALU.add), r=[ps, adab_s], w=[mod])
        for sub, nw in ((0, nmw_s), (1, nfw_s)):
            sc = 1 + 3 * sub
            k.V(lambda e, sub=sub, li=li, sc=sc: e.tensor_scalar(out=gm[:, li, sub], in0=mod[:, li, sc * 8:(sc + 1) * 8, :], scalar1=1.0, scalar2=None, op0=ALU.add), r=[mod], w=[gm])
            k.V(lambda e, sub=sub, li=li, nw=nw: e.tensor_tensor(out=gm[:, li, sub], in0=gm[:, li, sub], in1=bc(nw[:, li, :], [128, 8, 2], 2), op=ALU.mult), r=[gm, nw], w=[gm])

    def ci_of_tile(t):
        off = t * 512
        for (o, L, kd) in SEQS:
            if o <= off < o + L:
                return 0 if kd == 'p' else 1
        raise ValueError

    sqb = k.sb("sqb", [128, 8, 512], BF16)
    rsb = k.sb("rsb", [128, 512], F32)
    inb = [k.sb("inb%d" % i, [128, 22 * 512], BF16) for i in range(2)]
    hb = [k.view(inb[i], inb[i][:, 0:4096].rearrange("p (c n) -> p c n", c=8)) for i in range(2)]
    tmpf = k.sb("tmpf", [128, 512], F32)

    def norm_stage(li, sub, final=False):
        def nload(t):
            k.ld(xbuf[t % 2], xbuf[t % 2][:], X, X.t[:, t * 512:(t + 1) * 512].rearrange("(c p) t -> p c t", p=128))
        nload(0)
        for t in range(NT):
            ci = ci_of_tile(t)
            xb = xbuf[t % 2]
            sl = slice(t * 512, (t + 1) * 512)
            if t + 1 < NT:
                nload(t + 1)
            k.A(lambda e, xb=xb: e.activation(out=sqb[:], in_=xb[:], func=AF.Square), r=[xb], w=[sqb])
            ps = k.psum()
            for c in range(8):
                k.M(lambda e, c=c, ps=ps: e.matmul(ps[:], cmb[:, ONES, :], sqb[:, c, :], start=(c == 0), stop=(c == 7)), r=[cmb, sqb], w=[ps])
            k.A(lambda e, ps=ps: e.activation(out=rsb[:], in_=ps[:], func=AF.Sqrt, bias=EPS, scale=1.0 / 1024), r=[ps], w=[rsb])
            k.V(lambda e: e.reciprocal(out=rsb[:], in_=rsb[:]), r=[rsb], w=[rsb])
            if final:
                for c in range(8):
                    k.V(lambda e, c=c, xb=xb: e.scalar_tensor_tensor(out=xb[:, c, :], in0=xb[:, c, :], scalar=fnw_s[:, c:c + 1], in1=rsb[:], op0=ALU.mult, op1=ALU.mult),
                        r=[xb, fnw_s, rsb], w=[xb])
                k.stq(yT, yT.t[:, sl].rearrange("(c p) t -> p c t", p=128), xb, xb[:])
            else:
                h = hb[t % 2]
                shq = 0 if sub == 0 else 3
                for c in range(8):
                    k.V(lambda e, c=c, xb=xb: e.scalar_tensor_tensor(out=tmpf[:], in0=xb[:, c, :], scalar=gm[:, li, sub, c, ci:ci + 1], in1=rsb[:], op0=ALU.mult, op1=ALU.mult),
                        r=[xb, gm, rsb], w=[tmpf])
                    k.A(lambda e, c=c, h=h: e.activation(out=h[:, c, :], in_=tmpf[:], func=AF.Identity, bias=mod[:, li, shq * 8 + c, ci:ci + 1], scale=1.0),
                        r=[tmpf, mod], w=[h])
                k.stq(H, H.t[:, sl].rearrange("(c p) t -> p c t", p=128), h, h[:])

    ini = [0]

    def gemm(src, src_tm, kc, W, Wap, ncols, out_mode, epi, gw=None, pre=None, post=None):
        if gw is None:
            gw = max(128, (12288 // kc) // 128 * 128)
            gw = min(gw, 2048)
        seq = [(g0, min(gw, ncols - g0), t) for g0 in range(0, ncols, gw) for t in range(NT)]
        loaded = {}

        def emit_load(i, wb_dep):
            g0, gn, t = seq[i]
            ib = inb[ini[0] % 2]; ini[0] += 1
            iv = ib[:, 0:kc * 512].rearrange("p (c n) -> p c n", c=kc)
            sl = slice(t * 512, (t + 1) * 512)
            if src_tm:
                key = k._key('ld:' + ib.name)
                for c in range(kc):
                    k.P.emit('sp', lambda e: e.dma_start_transpose(out=iv[:, c, :], in_=src.t[sl, c * 128:(c + 1) * 128]), [src.b, k.cast_tok] + ([wb_dep.b] if wb_dep is not None else []), [ib.b, k.tr_tok], dma_key=key)
            else:
                k.ld(ib, iv[:], src, src.t[0:kc * 128, sl].rearrange("(c p) t -> p c t", p=128))
            loaded[i] = (ib, iv)
        wb = wv = None
        emit_load(0, None)
        groups_ = [(g0, min(gw, ncols - g0)) for g0 in range(0, ncols, gw)]
        wq = {}

        def wload(gi):
            g0_, gn_ = groups_[gi]
            wq[gi] = load_w(W, Wap[:, g0_:g0_ + gn_], kc, gn_, after=(inb if src_tm else ()))
        wload(0)
        for i, (g0, gn, t) in enumerate(seq):
            if t == 0:
                gi = g0 // gw
                if gi not in wq:
                    wload(gi)
                wb, wv = wq.pop(gi)
                if gi + 1 < len(groups_) and not src_tm:
                    wload(gi + 1)
            elif t == 1 and src_tm and (g0 // gw) + 1 < len(groups_) and False:
                pass
            if t == 0 and src_tm and (g0 // gw) + 1 < len(groups_):
                pass
            if i + 1 < len(seq):
                emit_load(i + 1, wb)
            ib, iv = loaded.pop(i)
            ncb = gn // 128
            if pre is not None:
                pre(t, g0 // 128, g0 // 128 + ncb)
            if out_mode == 'fm':
                for cb in range(ncb):
                    ps = k.psum()
                    for c in range(kc):
                        k.M(lambda e: e.matmul(ps[:], wv[:, c, cb * 128:(cb + 1) * 128], iv[:, c, :], start=(c == 0), stop=(c == kc - 1)), r=[wb, ib], w=[ps])
                    epi(ps, (g0 // 128) + cb, t, None, cb == ncb - 1)
            else:
                for sbk in range(4):
                    for p0 in range(0, gn, 512):
                        pw = min(512, gn - p0)
                        ps = k.psum()
                        for c in range(kc):
                            k.M(lambda e: e.matmul(ps[:, 0:pw], iv[:, c, sbk * 128:(sbk + 1) * 128], wv[:, c, p0:p0 + pw], start=(c == 0), stop=(c == kc - 1)), r=[wb, ib], w=[ps])
                        epi(ps, g0 + p0, t, sbk, False)
            if post is not None:
                post(t, g0 // 128, g0 // 128 + ncb)

    stg = [k.sb("stg%d" % i, [128, 512], BF16) for i in range(3)]
    sti = [0]

    stgb = [k.sb("stgb%d" % i, [128, 4, 512], BF16) for i in range(2)]
    stgi = [0]

    def epi_store_fm(dst, row0, func=None, scale=1.0):
        state = {'n': 0, 'cb0': None, 'buf': None}

        def epi(ps, cb, t, sub, last):
            if state['n'] == 0:
                state['buf'] = stgb[stgi[0] % 2]; stgi[0] += 1
                state['cb0'] = cb
            s_ = state['buf']; n = state['n']
            if func is None:
                if cb % 2 == 0:
                    k.V(lambda e: e.tensor_copy(out=s_[:, n, :], in_=ps[:]), r=[ps], w=[s_])
                else:
                    k.A(lambda e: e.activation(out=s_[:, n, :], in_=ps[:], func=AF.Copy), r=[ps], w=[s_])
            else:
                k.A(lambda e: e.activation(out=s_[:, n, :], in_=ps[:], func=func, scale=scale), r=[ps], w=[s_])
            state['n'] += 1
            if state['n'] == 4 or last:
                n_ = state['n']; c0 = state['cb0']
                k.stq(dst, dst.t[row0 + c0 * 128:row0 + (c0 + n_) * 128, t * 512:(t + 1) * 512].rearrange("(c p) t -> p c t", p=128), s_, s_[:, 0:n_, :])
                state['n'] = 0
        return epi

    def epi_store_tm(dst, col0, func=None):
        def epi(ps, c0, t, sub, last):
            s = stg[sti[0] % 3]; sti[0] += 1
            if func is None:
                k.V(lambda e, s=s, ps=ps: e.tensor_copy(out=s[:], in_=ps[:]), r=[ps], w=[s])
            else:
                k.A(lambda e, s=s, ps=ps: e.activation(out=s[:], in_=ps[:], func=func), r=[ps], w=[s])
            r0 = t * 512 + sub * 128
            k.stq(dst, dst.t[r0:r0 + 128, col0 + c0:col0 + c0 + 512], s, s[:])
        return epi

    def resid_hooks(li, gq):
        def pre(t, c_lo, c_hi):
            xb = xbuf[t % 2]
            k.ld(xb, xb[:, c_lo:c_hi, :], X, X.t[c_lo * 128:c_hi * 128, t * 512:(t + 1) * 512].rearrange("(c p) t -> p c t", p=128))

        def epi(ps, cb, t, sub, last):
            ci = ci_of_tile(t)
            xb = xbuf[t % 2]
            k.V(lambda e: e.scalar_tensor_tensor(out=xb[:, cb, :], in0=ps[:], scalar=mod[:, li, gq * 8 + cb, ci:ci + 1], in1=xb[:, cb, :], op0=ALU.mult, op1=ALU.add),
                r=[ps, mod, xb], w=[xb])

        def post(t, c_lo, c_hi):
            xb = xbuf[t % 2]
            k.stq(X, X.t[c_lo * 128:c_hi * 128, t * 512:(t + 1) * 512].rearrange("(c p) t -> p c t", p=128), xb, xb[:, c_lo:c_hi, :])
        return dict(epi=epi, pre=pre, post=post)

    cvin = [k.sb("cvin%d" % i, [128, 10, 66], BF16) for i in range(2)]
    dgb = [k.sb("dgb%d" % i, [128, 9, 128], BF16) for i in range(2)]
    cvout = [k.sb("cvout%d" % i, [128, 512], BF16) for i in range(2)]
    cvi = [0]

    def seq_of(tok):
        for (o, L, kd) in SEQS:
            if o <= tok < o + L:
                return (o, L, kd)
        raise ValueError

    def conv_stage(src, dst, nblk, wfn, bfn, grid_for_sample, wdeps, l2blocks=(), qscale_blocks=()):
        items = [(blk, t) for blk in range(nblk) for t in range(NT)]
        info = {}

        def conv_load(i):
            blk, t = items[i]
            rs_ = slice(blk * 128, (blk + 1) * 128)
            t0 = t * 512
            (o, L, kd) = seq_of(t0)
            ci_ = cvin[cvi[0] % 2]; co = cvout[cvi[0] % 2]; cvi[0] += 1
            grid = grid_for_sample and kd == 's'
            k.G(lambda e: e.memset(ci_[:], 0.0), w=[ci_])
            pieces = []
            if grid:
                ROWS_ = L // 64
                r0 = (t0 - o) // 64
                lo = max(0, r0 - 1); hi = min(ROWS_, r0 + 9)
                k.ld(ci_, ci_[:, lo - (r0 - 1):hi - (r0 - 1), 1:65], src, src.t[rs_, o + lo * 64:o + hi * 64].rearrange("p (r c) -> p r c", c=64))
            else:
                flat = ci_[:].rearrange("p r c -> p (r c)")
                pos = t0
                base = 0
                while pos < t0 + 512:
                    (o2, L2, _) = seq_of(pos)
                    pe_ = min(t0 + 512, o2 + L2)
                    pa, pb = pos, pe_
                    lo = max(o2, pa - 1); hi = min(o2 + L2, pb + 1)
                    k.ld(ci_, flat[:, base + (lo - (pa - 1)):base + (hi - (pa - 1))], src, src.t[rs_, lo:hi])
                    pieces.append((pa, pb, base)); base += (pb - pa) + 2
                    pos = pe_
            info[i] = (ci_, co, grid, pieces)

        conv_load(0)
        for i, (blk, t) in enumerate(items):
            rs_ = slice(blk * 128, (blk + 1) * 128)
            t0 = t * 512
            if t == 0:
                dg = dgb[blk % 2]
                for tap in (range(9) if grid_for_sample else range(3, 6)):
                    k.V(lambda e: e.tensor_scalar(out=dg[:, tap, :], in0=cmb[:, ID, :], scalar1=wfn(blk, tap), scalar2=None, op0=ALU.mult), r=[cmb] + wdeps, w=[dg])
            if i + 1 < len(items):
                conv_load(i + 1)
            ci_, co, grid, pieces = info.pop(i)
            ps = k.psum()
            if grid:
                pv = ps[:].rearrange("p (r c) -> p r c", c=64)
                ti = 0
                for dy in range(3):
                    for dx in range(3):
                        k.M(lambda e: e.matmul(pv, dg[:, dy * 3 + dx, :], ci_[:, dy:dy + 8, dx:dx + 64], start=(ti == 0), stop=(ti == 8)), r=[dg, ci_], w=[ps])
                        ti += 1
            else:
                flat = ci_[:].rearrange("p r c -> p (r c)")
                for (pa, pb, base) in pieces:
                    n_ = pb - pa
                    for dx in range(3):
                        k.M(lambda e: e.matmul(ps[:, pa - t0:pb - t0], dg[:, 3 + dx, :], flat[:, base + dx:base + dx + n_], start=(dx == 0), stop=(dx == 2)), r=[dg, ci_], w=[ps])
            bap = bfn(blk) if bfn is not None else None
            if bap is not None:
                k.A(lambda e: e.activation(out=co[:], in_=ps[:], func=AF.Silu, bias=bap, scale=1.0), r=[ps] + wdeps, w=[co])
            else:
                k.A(lambda e: e.activation(out=co[:], in_=ps[:], func=AF.Silu), r=[ps], w=[co])
            if blk in l2blocks:
                k.V(lambda e: e.tensor_tensor(out=sqb[:, 0, :], in0=co[:], in1=co[:], op=ALU.mult), r=[co], w=[sqb])
                ps2 = k.psum()
                k.M(lambda e: e.matmul(ps2[:], cmb[:, ONES, :], sqb[:, 0, :], start=True, stop=True), r=[cmb, sqb], w=[ps2])
                k.A(lambda e: e.activation(out=rsb[:], in_=ps2[:], func=AF.Sqrt, bias=1e-6, scale=1.0), r=[ps2], w=[rsb])
                k.V(lambda e: e.reciprocal(out=rsb[:], in_=rsb[:]), r=[rsb], w=[rsb])
                qs = (128.0 ** -0.5) if blk in qscale_blocks else 1.0
                k.V(lambda e: e.scalar_tensor_tensor(out=co[:], in0=co[:], scalar=qs, in1=rsb[:], op0=ALU.mult, op1=ALU.mult), r=[co, rsb], w=[co])
            k.stq(dst, dst.t[rs_, t0:t0 + 512], co, co[:])

    fcw_s = k.sb("fcw_s", [128, 4, 22, 9], F32); fcb_s = k.sb("fcb_s", [128, 4, 22], F32)
    k.ld(fcw_s, fcw_s[:], ffn_cw, ffn_cw.t[:, :, :, :]); k.ld(fcb_s, fcb_s[:], ffn_cb, ffn_cb.t[:, :, :])
    avb = [k.view(wbuf[i], wbuf[i][:, 8192:12288].rearrange("p (c n) -> p c n", c=8), group='ffnv') for i in range(2)]
    aai = [0]

    def ffn(li):
        norm_stage(li, 1)
        gemm(H, False, 8, ffn_in_w, ffn_in_w.t[li, :, 0:2816], 2816, 'fm', epi_store_fm(S1, 0))
        conv_stage(S1, S2, 22, lambda blk, tap: fcw_s[:, li, blk, tap:tap + 1], lambda blk: fcb_s[:, li, blk:blk + 1], True, [fcw_s, fcb_s])

        vstate = {}

        def pre_v(t, c_lo, c_hi):
            a = avb[aai[0] % 2]; aai[0] += 1
            k.ld(a, a[:, 0:c_hi - c_lo, :], S2, S2.t[c_lo * 128:c_hi * 128, t * 512:(t + 1) * 512].rearrange("(c p) t -> p c t", p=128))
            vstate['a'] = a; vstate['lo'] = c_lo

        def epi_v(ps, cb, t, sub, last):
            a = vstate['a']; n = cb - vstate['lo']
            k.V(lambda e: e.tensor_tensor(out=a[:, n, :], in0=ps[:], in1=a[:, n, :], op=ALU.mult), r=[ps, a], w=[a])

        def post_v(t, c_lo, c_hi):
            a = vstate['a']
            k.stq(S5, S5.t[c_lo * 128:c_hi * 128, t * 512:(t + 1) * 512].rearrange("(c p) t -> p c t", p=128), a, a[:, 0:c_hi - c_lo, :])
        k.enter('ffnv')
        gemm(H, False, 8, ffn_in_w, ffn_in_w.t[li, :, 2816:5632], 2816, 'fm', epi_v, gw=1024, pre=pre_v, post=post_v)
        k.leave('ffnv')
        gemm(S5, False, 22, ffn_out_w, ffn_out_w.t[li, :, :], 1024, 'fm', gw=512, **resid_hooks(li, 5))

    scw_s = k.sb("scw_s", [128, 2, 32, 3], F32); scb_s = k.sb("scb_s", [128, 2, 32], F32)
    k.ld(scw_s, scw_s[:], ssm_cw, ssm_cw.t[:, :, :, :]); k.ld(scb_s, scb_s[:], ssm_cb, ssm_cb.t[:, :, :])
    dt_sb = k.view(xbuf[0], xbuf[0][:].rearrange("p c n -> p (c n)")[:, 0:NCH * 64].rearrange("p (c n) -> p c n", n=64), group='ssdx')
    la_sb = k.view(xbuf[1], xbuf[1][:].rearrange("p c n -> p (c n)")[:, 0:NCH * 64].rearrange("p (c n) -> p c n", n=64), group='ssdx')
    lab_sb = k.sb("lab_sb", [128, 1, 64], BF16)
    bc64 = k.sb("bc64", [128, 64], F32); nega = k.sb("nega", [128, 64], F32); dsk = k.sb("dsk", [128, 32], F32)
    nwb = k.sb("nwb", [128, 2048], BF16)
    W0, W1 = wbuf[0], wbuf[1]
    Ebuf = k.view(W0, W0[:, 0:4096], group='ssd'); DTb = k.view(W0, W0[:, 4096:8192], group='ssd')
    xv = k.view(W0, W0[:, 8192:10240], group='ssd'); xw = k.view(W0, W0[:, 10240:12288], group='ssd')
    Sbf = k.view(W1, W1[:, 0:2048], group='ssd'); STb = k.view(W1, W1[:, 2048:3072], group='ssd')
    yo = [k.view(W1, W1[:, 3072 + i * 2048:3072 + (i + 1) * 2048], group='ssd') for i in range(2)]
    xs_tm = [k.view(inb[i], inb[i][:, 0:2048], group='ssd') for i in range(2)]
    B_tm = [k.view(inb[i], inb[i][:, 2048:3072], group='ssd') for i in range(2)]
    BC_fm = [k.view(inb[i], inb[i][:, 3072:5120].rearrange("p (g t) -> p g t", t=128), group='ssd') for i in range(2)]
    yfb = [k.view(inb[i], inb[i][:, 5120:7168], group='ssd') for i in range(2)]
    zsb = [k.view(inb[i], inb[i][:, 7168:9216], group='ssd') for i in range(2)]
    ecs = k.sb("ecs", [128, 64], F32)
    Sst = k.sb("Sst", [128, 2048], F32)
    yc = k.sb("yc", [128, 2048], F32)
    ytmp = tmpf
    ss8 = k.sb("ss8", [128, 16], F32)

    def chunk_order(L, d):
        n = L // 128
        return list(range(n)) if d == 0 else list(range(n - 1, -1, -1))

    def ssd(li, j):
        norm_stage(li, 0)
        W = ssm_in_w
        gemm(H, False, 8, W, W.t[j, :, 0:2048], 2048, 'tm', epi_store_tm(S3, 0, AF.Silu))
        gemm(H, False, 8, W, W.t[j, :, 2048:6144], 4096, 'fm', epi_store_fm(S1, 0))
        if DBG_STOP == 'gemm':
            return
        conv_stage(S1, S2, 32, lambda blk, tap: scw_s[:, j, blk, tap - 3:tap - 2], lambda blk: scb_s[:, j, blk:blk + 1], False, [scw_s, scb_s])
        if DBG_STOP == 'conv':
            return
        k.ld(bc64, bc64[:], ssm_dtb, ssm_dtb.t[j].partition_broadcast(128))
        k.ld(nega, nega[:], ssm_alog, ssm_alog.t[j].partition_broadcast(128))
        k.A(lambda e: e.activation(out=nega[:], in_=nega[:], func=AF.Exp), r=[nega], w=[nega])
        k.V(lambda e: e.tensor_scalar(out=nega[:], in0=nega[:], scalar1=-1.0, scalar2=None, op0=ALU.mult), r=[nega], w=[nega])
        k.ld(dsk, dsk[:], ssm_d, ssm_d.t[j].partition_broadcast(128))
        k.ld(nwb, nwb[:], ssm_nw, ssm_nw.t[j].partition_broadcast(128), q='pool')

        def epi_dt(ps, c0, t, sub, last):
            ch = t * 4 + sub
            k.V(lambda e, ps=ps, ch=ch: e.tensor_tensor(out=dt_sb[:, ch, :], in0=ps[:, 0:64], in1=bc64[:], op=ALU.add), r=[ps, bc64], w=[dt_sb])
            k.A(lambda e, ch=ch: e.activation(out=dt_sb[:, ch, :], in_=dt_sb[:, ch, :], func=AF.Exp), r=[dt_sb], w=[dt_sb])
            k.A(lambda e, ch=ch: e.activation(out=dt_sb[:, ch, :], in_=dt_sb[:, ch, :], func=AF.Ln, bias=1.0, scale=1.0), r=[dt_sb], w=[dt_sb])
            k.V(lambda e, ch=ch: e.tensor_tensor(out=la_sb[:, ch, :], in0=dt_sb[:, ch, :], in1=nega[:], op=ALU.mult), r=[dt_sb, nega], w=[la_sb])
        k.enter('ssdx')
        gemm(H, False, 8, W, W.t[j, :, 6144:6208], 64, 'tm', epi_dt)
        if DBG_STOP == 'dt':
            return
        k.enter('ssd')
        pidx = 0
        for (o, L, kd) in (SEQS[:1] if DBG_STOP in ('scan1', 'scan1b') else SEQS):
            for d in ((0,) if DBG_STOP == 'scan1' else (0, 1)):
                if kd == 'p':
                    k.V(lambda e: e.memset(Sst[:], 0.0), w=[Sst])
                    k.V(lambda e: e.memset(Sbf[:], 0.0), w=[Sbf])
                else:
                    k.ld(Sst, Sst[:], st_ssm, st_ssm.t[j, d])
                    k.A(lambda e: e.activation(out=Sbf[:], in_=Sst[:], func=AF.Copy), r=[Sst], w=[Sbf])
                ie = 127 if d == 0 else 0
                order = chunk_order(L, d)

                def ssd_loads(n_, c):
                    t0 = o + c * 128
                    tsl = slice(t0, t0 + 128)
                    xs = xs_tm[n_ % 2]; Bt = B_tm[n_ % 2]; BC = BC_fm[n_ % 2]
                    for q4 in range(4):
                        k.ld(xs, xs[:, q4 * 512:(q4 + 1) * 512], S2, S2.t[q4 * 512:(q4 + 1) * 512, tsl], tr=True)
                    for q4 in range(2):
                        k.ld(Bt, Bt[:, q4 * 512:(q4 + 1) * 512], S2, S2.t[2048 + q4 * 512:2048 + (q4 + 1) * 512, tsl], tr=True)
                    k.ld(BC, BC[:], S2, S2.t[2048:4096, tsl].rearrange("(g n) t -> n g t", n=128))
                    if d == 1:
                        k.ld(yfb[n_ % 2], yfb[n_ % 2][:], S4, S4.t[tsl, :])
                        k.ld(zsb[n_ % 2], zsb[n_ % 2][:], S3, S3.t[tsl, :])
                ssd_loads(0, order[0])
                for n_, c in enumerate(order):
                    if n_ + 1 < len(order):
                        ssd_loads(n_ + 1, order[n_ + 1])
                    t0 = o + c * 128; ch = t0 // 128
                    tsl = slice(t0, t0 + 128)
                    xs = xs_tm[n_ % 2]; Bt = B_tm[n_ % 2]; BC = BC_fm[n_ % 2]
                    la = la_sb[:, ch, d * 32:(d + 1) * 32]; lab = lab_sb[:, 0, 0:32]; dtc = dt_sb[:, ch, d * 32:(d + 1) * 32]
                    k.V(lambda e: e.tensor_copy(out=lab_sb[:, 0, 0:32], in_=la), r=[la_sb], w=[lab_sb])
                    E3 = Ebuf[:].rearrange("p (h i) -> p h i", i=128)
                    D3 = DTb[:].rearrange("p (h i) -> p h i", i=128)
                    k.G(lambda e, la=la: e.tensor_tensor(out=E3, in0=bc(la, [128, 32, 128], 2), in1=bc(cmb[:, U_[d], :], [128, 32, 128], 1), op=ALU.mult), r=[la_sb, cmb], w=[Ebuf])
                    for q8 in range(8):
                        ps = k.psum()
                        k.M(lambda e, ps=ps, q8=q8: e.matmul(ps[:], cmb[:, V_[d], :], Ebuf[:, q8 * 512:(q8 + 1) * 512], start=True, stop=True), r=[cmb, Ebuf], w=[ps])
                        k.A(lambda e, ps=ps, q8=q8: e.activation(out=DTb[:, q8 * 512:(q8 + 1) * 512], in_=ps[:], func=AF.Exp), r=[ps], w=[DTb])
                    for half in range(2):
                        ps = k.psum()
                        for g4 in range(4):
                            g = half * 4 + g4
                            k.M(lambda e, ps=ps, g=g, g4=g4, BC=BC: e.matmul(ps[:, g4 * 128:(g4 + 1) * 128], BC[:, g, :], BC[:, 8 + g, :], start=True, stop=True), r=[BC], w=[ps])
                        k.V(lambda e, ps=ps, half=half: e.tensor_tensor(out=STb[:, half * 512:(half + 1) * 512].rearrange("p (g i) -> p g i", i=128), in0=ps[:].rearrange("p (g i) -> p g i", i=128),
                                                                        in1=bc(cmb[:, U_[d], :], [128, 4, 128], 1), op=ALU.mult), r=[ps, cmb], w=[STb])
                    k.V(lambda e: e.tensor_copy(out=wdec[:], in_=DTb[:].rearrange("p (h i) -> p h i", i=128)[:, :, ie]), r=[DTb], w=[wdec])
                    k.V(lambda e: e.tensor_tensor(out=DTb[:].rearrange("p (g r i) -> p g r i", r=4, i=128), in0=DTb[:].rearrange("p (g r i) -> p g r i", r=4, i=128),
                                                   in1=bc(STb[:].rearrange("p (g i) -> p g i", i=128), [128, 8, 4, 128], 2), op=ALU.mult), r=[DTb, STb], w=[DTb])
                    k.G(lambda e, xs=xs, dtc=dtc: e.tensor_tensor(out=xv[:].rearrange("p (h q) -> p h q", q=64), in0=xs[:].rearrange("p (h q) -> p h q", q=64),
                                                                   in1=bc(dtc, [128, 32, 64], 2), op=ALU.mult), r=[xs, dt_sb], w=[xv])
                    k.G(lambda e: e.tensor_tensor(out=xw[:].rearrange("p (h q) -> p h q", q=64), in0=xv[:].rearrange("p (h q) -> p h q", q=64),
                                                   in1=bc(wdec[:], [128, 32, 64], 2), op=ALU.mult), r=[xv, wdec], w=[xw])
                    ps = k.psum()
                    k.M(lambda e, ps=ps, lab=lab: e.matmul(ps[:, 0:32], cmb[:, U_[d], :], lab, start=True, stop=True), r=[cmb, lab_sb], w=[ps])
                    k.M(lambda e, ps=ps, lab=lab: e.matmul(ps[:, 32:64], cmb[:, ONES, :], lab, start=True, stop=True), r=[cmb, lab_sb], w=[ps])
                    k.A(lambda e, ps=ps: e.activation(out=ecs[:], in_=ps[:, 0:64], func=AF.Exp), r=[ps], w=[ecs])
                    for pr in range(4):
                        psA = k.psum(); psB = k.psum()
                        for gg in range(2):
                            g = pr * 2 + gg
                            for r_ in range(4):
                                h = g * 4 + r_
                                k.M(lambda e, psA=psA, h=h, gg=gg, r_=r_: e.matmul(psA[:, gg * 256 + r_ * 64:gg * 256 + (r_ + 1) * 64], DTb[:, h * 128:(h + 1) * 128], xv[:, h * 64:(h + 1) * 64],
                                                                                       start=True, stop=True), r=[DTb, xv], w=[psA])
                            k.M(lambda e, psB=psB, g=g, gg=gg, BC=BC: e.matmul(psB[:, gg * 256:(gg + 1) * 256], BC[:, 8 + g, :], Sbf[:, g * 256:(g + 1) * 256], start=True, stop=True), r=[BC, Sbf], w=[psB])
                        k.V(lambda e, psB=psB, pr=pr: e.tensor_tensor(out=ytmp[:].rearrange("p (h q) -> p h q", q=64), in0=psB[:].rearrange("p (h q) -> p h q", q=64),
                                                                       in1=bc(ecs[:, pr * 8:(pr + 1) * 8], [128, 8, 64], 2), op=ALU.mult), r=[psB, ecs], w=[ytmp])
                        k.V(lambda e, psA=psA, pr=pr: e.tensor_tensor(out=yc[:, pr * 512:(pr + 1) * 512], in0=psA[:], in1=ytmp[:], op=ALU.add), r=[psA, ytmp], w=[yc])
                    k.G(lambda e: e.tensor_tensor(out=Sst[:].rearrange("p (h q) -> p h q", q=64), in0=Sst[:].rearrange("p (h q) -> p h q", q=64),
                                                   in1=bc(ecs[:, 32:64], [128, 32, 64], 2), op=ALU.mult), r=[Sst, ecs], w=[Sst])
                    for pr in range(4):
                        ps = k.psum()
                        for gg in range(2):
                            g = pr * 2 + gg
                            k.M(lambda e, ps=ps, g=g, gg=gg, Bt=Bt: e.matmul(ps[:, gg * 256:(gg + 1) * 256], Bt[:, g * 128:(g + 1) * 128], xw[:, g * 256:(g + 1) * 256], start=True, stop=True), r=[Bt, xw], w=[ps])
                        k.V(lambda e, ps=ps, pr=pr: e.tensor_tensor(out=Sst[:, pr * 512:(pr + 1) * 512], in0=ps[:], in1=Sst[:, pr * 512:(pr + 1) * 512], op=ALU.add), r=[ps, Sst], w=[Sst])
                    k.A(lambda e: e.activation(out=Sbf[:], in_=Sst[:], func=AF.Copy), r=[Sst], w=[Sbf])
                    if d == 0:
                        y_ = yo[n_ % 2]
                        k.A(lambda e, y_=y_: e.activation(out=y_[:], in_=yc[:], func=AF.Copy), r=[yc], w=[y_])
                        k.stq(S4, S4.t[tsl, :], y_, y_[:])
                    else:
                        yf_ = yfb[n_ % 2]; zs_ = zsb[n_ % 2]; y_ = yo[n_ % 2]
                        k.V(lambda e, yf_=yf_: e.tensor_tensor(out=yc[:], in0=yc[:], in1=yf_[:], op=ALU.add), r=[yc, yf_], w=[yc])
                        k.G(lambda e, xs=xs: e.tensor_tensor(out=xv[:].rearrange("p (h q) -> p h q", q=64), in0=xs[:].rearrange("p (h q) -> p h q", q=64),
                                                              in1=bc(dsk[:], [128, 32, 64], 2), op=ALU.mult), r=[xs, dsk], w=[xv])
                        k.V(lambda e: e.tensor_tensor(out=yc[:], in0=yc[:], in1=xv[:], op=ALU.add), r=[yc, xv], w=[yc])
                        k.V(lambda e, zs_=zs_: e.tensor_tensor(out=yc[:], in0=yc[:], in1=zs_[:], op=ALU.mult), r=[yc, zs_], w=[yc])
                        k.G(lambda e: e.tensor_tensor(out=Ebuf[:, 0:2048], in0=yc[:], in1=yc[:], op=ALU.mult), r=[yc], w=[Ebuf])
                        k.V(lambda e: e.tensor_reduce(out=ss8[:, 0:8], in_=Ebuf[:, 0:2048].rearrange("p (g q) -> p g q", q=256), axis=AX.X, op=ALU.add), r=[Ebuf], w=[ss8])
                        k.A(lambda e: e.activation(out=ss8[:, 0:8], in_=ss8[:, 0:8], func=AF.Sqrt, bias=EPS, scale=1.0 / 256), r=[ss8], w=[ss8])
                        k.V(lambda e: e.reciprocal(out=ss8[:, 0:8], in_=ss8[:, 0:8]), r=[ss8], w=[ss8])
                        k.V(lambda e: e.tensor_tensor(out=yc[:].rearrange("p (g q) -> p g q", q=256), in0=yc[:].rearrange("p (g q) -> p g q", q=256),
                                                       in1=bc(ss8[:, 0:8], [128, 8, 256], 2), op=ALU.mult), r=[yc, ss8], w=[yc])
                        k.V(lambda e, y_=y_: e.tensor_tensor(out=y_[:], in0=yc[:], in1=nwb[:], op=ALU.mult), r=[yc, nwb], w=[y_])
                        k.stq(S4, S4.t[tsl, :], y_, y_[:])
                if kd == 'p':
                    k.stq(ns_ssm, ns_ssm.t[pidx, j, d], Sst, Sst[:])
            if kd == 'p':
                pidx += 1
        k.leave('ssd'); k.leave('ssdx')
        if DBG_STOP in ('scan1', 'scan1b', 'scan'):
            return
        gemm(S4, True, 16, ssm_out_w, ssm_out_w.t[j, :, :], 1024, 'fm', **resid_hooks(li, 2))

    HAS_GLA = any(kd_ == 'gla' for _, kd_, _ in LAYERS)
    lrT = k.sb("lrT", [64, 512], BF16)
    gnw_s = k.sb("gnw_s", [128, 1, 2], F32)
    k.ld(gnw_s, gnw_s[:], gla_nw, gla_nw.t[:, :, :])
    w2s = k.sb("w2s", [64, 1024 if HAS_GLA else 8], BF16)
    gbb = k.sb("gbb", [128, 1024 if HAS_GLA else 8], BF16)
    gq_fm = [k.view(inb[i], inb[i][:, 0:512].rearrange("p (h t) -> p h t", t=128), group='gla') for i in range(2)]
    gk_fm = [k.view(inb[i], inb[i][:, 512:1024].rearrange("p (h t) -> p h t", t=128), group='gla') for i in range(2)]
    gk_tm = [k.view(inb[i], inb[i][:, 1024:1536], group='gla') for i in range(2)]
    gv_tm = [k.view(inb[i], inb[i][:, 1536:2560], group='gla') for i in range(2)]
    gsp_tm = [k.view(inb[i], inb[i][:, 2560:3072], group='gla') for i in range(2)]
    gof = [k.view(inb[i], inb[i][:, 3072:4096].rearrange("p (b t) -> p b t", t=128), group='gla') for i in range(2)]
    grs = [k.view(inb[i], inb[i][:, 4096:5120].rearrange("p (b t) -> p b t", t=128), group='gla') for i in range(2)]
    gqg = k.view(W0, W0[:, 0:512].rearrange("p (h t) -> p h t", t=128), group='gla')
    gkg = k.view(W0, W0[:, 512:1024].rearrange("p (h t) -> p h t", t=128), group='gla')
    gAT = k.view(W0, W0[:, 1024:1536].rearrange("p (h t) -> p h t", t=128), group='gla')
    gkw = k.view(W0, W0[:, 1536:2048], group='gla')
    gsq = k.view(W0, W0[:, 2048:3072].rearrange("p (b t) -> p b t", t=128), group='gla')
    gon = [k.view(W1, W1[:, 3072 + i * 1024:3072 + (i + 1) * 1024].rearrange("p (b t) -> p b t", t=128), group='gla') for i in range(2)]
    gsps = [k.sb("gsps%d" % i, [128, 1024 if HAS_GLA else 8], BF16) for i in range(2)]
    Sg = k.view(Sst, Sst[:, 0:1024].rearrange("p (h e) -> p h e", e=256), group='gla')
    Sgb = k.view(W1, W1[:, 0:1024].rearrange("p (h e) -> p h e", e=256), group='gla')
    gob = k.view(yc, yc[:, 0:1024].rearrange("p (b t) -> p b t", t=128), group='gla')

    def gla(li, j):
        norm_stage(li, 0)
        W = gla_in_w
        gemm(H, False, 8, W, W.t[j, :, 0:512], 512, 'fm', epi_store_fm(S1, 0, AF.Copy, 128.0 ** -0.5))
        gemm(H, False, 8, W, W.t[j, :, 512:1024], 512, 'fm', epi_store_fm(S1, 512))
        gemm(H, False, 8, W, W.t[j, :, 1024:2048], 1024, 'tm', epi_store_tm(S3, 0))
        gemm(H, False, 8, W, W.t[j, :, 2048:3072], 1024, 'fm', epi_store_fm(S6, 0, AF.Silu))

        k.ld(w2s, w2s[:], gla_w2, gla_w2.t[j], q='pool')
        k.ld(gbb, gbb[:], gla_gb, gla_gb.t[j].partition_broadcast(128), q='pool')

        def epi_lr(ps, cb, t, sub, last):
            k.V(lambda e: e.tensor_copy(out=lrT[:], in_=ps[0:64, :]), r=[ps], w=[lrT])
            for sbk in range(4):
                ch = t * 4 + sbk
                tsl = slice(ch * 128, (ch + 1) * 128)
                sp_ = gsps[ch % 2]
                for hf in range(2):
                    ps2 = k.psum()
                    k.M(lambda e: e.matmul(ps2[:], lrT[:, sbk * 128:(sbk + 1) * 128], w2s[:, hf * 512:(hf + 1) * 512], start=True, stop=True), r=[lrT, w2s], w=[ps2])
                    k.V(lambda e: e.tensor_tensor(out=rsb[:], in0=ps2[:], in1=gbb[:, hf * 512:(hf + 1) * 512], op=ALU.add), r=[ps2, gbb], w=[rsb])
                    k.A(lambda e: e.activation(out=rsb[:], in_=rsb[:], func=AF.Exp, scale=-1.0), r=[rsb], w=[rsb])
                    k.A(lambda e: e.activation(out=sp_[:, hf * 512:(hf + 1) * 512], in_=rsb[:], func=AF.Ln, bias=1.0, scale=1.0), r=[rsb], w=[sp_])
                k.stq(S7, S7.t[tsl, :], sp_, sp_[:])
        gemm(H, False, 8, gla_w1, gla_w1.t[j, :, :], 128, 'fm', epi_lr)
        k.enter('gla')
        pidx = 0
        for (o, L, kd) in SEQS:
            for d in (0, 1):
                if kd == 'p':
                    k.V(lambda e: e.memset(Sg[:], 0.0), w=[Sg])
                    k.V(lambda e: e.memset(Sgb[:], 0.0), w=[Sgb])
                else:
                    k.ld(Sg, Sg[:], st_gla, st_gla.t[j, d].rearrange("h d e -> d h e"))
                    k.A(lambda e: e.activation(out=Sgb[:], in_=Sg[:], func=AF.Copy), r=[Sg], w=[Sgb])
                ie = 127 if d == 0 else 0
                order = chunk_order(L, d)

                def gla_loads(n_, c):
                    t0 = o + c * 128
                    tsl = slice(t0, t0 + 128)
                    q_ = gq_fm[n_ % 2]; k_ = gk_fm[n_ % 2]; kt = gk_tm[n_ % 2]; v_ = gv_tm[n_ % 2]; sp_ = gsp_tm[n_ % 2]
                    k.ld(q_, q_[:], S1, S1.t[0:512, tsl].rearrange("(h d) t -> d h t", d=128))
                    k.ld(k_, k_[:], S1, S1.t[512:1024, tsl].rearrange("(h d) t -> d h t", d=128))
                    k.ld(kt, kt[:], S1, S1.t[512:1024, tsl], tr=True)
                    k.ld(v_, v_[:], S3, S3.t[tsl, 0:1024])
                    k.ld(sp_, sp_[:], S7, S7.t[tsl, d * 512:(d + 1) * 512])
                    if d == 1:
                        k.ld(gof[n_ % 2], gof[n_ % 2][:], S5, S5.t[0:1024, tsl].rearrange("(b p) t -> p b t", p=128))
                        k.ld(grs[n_ % 2], grs[n_ % 2][:], S6, S6.t[0:1024, tsl].rearrange("(b p) t -> p b t", p=128))
                gla_loads(0, order[0])
                for n_, c in enumerate(order):
                    if n_ + 1 < len(order):
                        gla_loads(n_ + 1, order[n_ + 1])
                    t0 = o + c * 128
                    tsl = slice(t0, t0 + 128)
                    q_ = gq_fm[n_ % 2]; k_ = gk_fm[n_ % 2]; kt = gk_tm[n_ % 2]; v_ = gv_tm[n_ % 2]; sp_ = gsp_tm[n_ % 2]
                    ps = k.psum()
                    k.M(lambda e: e.matmul(ps[:], cmb[:, VG_[d], :], sp_[:], start=True, stop=True), r=[cmb, sp_], w=[ps])
                    k.A(lambda e: e.activation(out=rsb[:], in_=ps[:], func=AF.Exp), r=[ps], w=[rsb])
                    k.V(lambda e: e.tensor_tensor(out=gkw[:], in0=kt[:], in1=rsb[:], op=ALU.mult), r=[kt, rsb], w=[gkw])
                    psc = k.psum()
                    for h in range(4):
                        k.M(lambda e: e.matmul(psc[:, h * 128:(h + 1) * 128], sp_[:, h * 128:(h + 1) * 128], cmb[:, UG_[d], :], start=True, stop=True), r=[sp_, cmb], w=[psc])
                    k.A(lambda e: e.activation(out=tmpf[:], in_=psc[:], func=AF.Exp), r=[psc], w=[tmpf])
                    k.V(lambda e: e.tensor_tensor(out=gqg[:], in0=q_[:], in1=tmpf[:].rearrange("p (h t) -> p h t", t=128), op=ALU.mult), r=[q_, tmpf], w=[gqg])
                    k.A(lambda e: e.activation(out=tmpf[:], in_=psc[:], func=AF.Exp, scale=-1.0), r=[psc], w=[tmpf])
                    k.V(lambda e: e.tensor_tensor(out=gkg[:], in0=k_[:], in1=tmpf[:].rearrange("p (h t) -> p h t", t=128), op=ALU.mult), r=[k_, tmpf], w=[gkg])
                    k.A(lambda e: e.activation(out=ecs[:, 0:4], in_=psc[:].rearrange("p (h t) -> p h t", t=128)[:, :, ie], func=AF.Exp), r=[psc], w=[ecs])
                    pss = k.psum()
                    for h in range(4):
                        k.M(lambda e: e.matmul(pss[:, h * 128:(h + 1) * 128], gkg[:, h, :], gqg[:, h, :], start=True, stop=True), r=[gkg, gqg], w=[pss])
                    k.V(lambda e: e.tensor_tensor(out=gAT[:], in0=pss[:].rearrange("p (h t) -> p h t", t=128), in1=bc(cmb[:, U_[d], :], [128, 4, 128], 1), op=ALU.mult), r=[pss, cmb], w=[gAT])
                    pso = [k.psum(), k.psum()]
                    for h in range(4):
                        for eb in range(2):
                            blk = h * 2 + eb
                            po = pso[blk // 4]
                            k.M(lambda e: e.matmul(po[:, (blk % 4) * 128:(blk % 4 + 1) * 128], v_[:, h * 256 + eb * 128:h * 256 + (eb + 1) * 128], gAT[:, h, :], start=True, stop=False), r=[v_, gAT], w=[po])
                            k.M(lambda e: e.matmul(po[:, (blk % 4) * 128:(blk % 4 + 1) * 128], Sgb[:, h, eb * 128:(eb + 1) * 128], gqg[:, h, :], start=False, stop=True), r=[Sgb, gqg], w=[po])
                    for h in range(4):
                        pst = k.psum()
                        k.M(lambda e: e.matmul(pst[:, 0:256], gkw[:, h * 128:(h + 1) * 128], v_[:, h * 256:(h + 1) * 256], start=True, stop=True), r=[gkw, v_], w=[pst])
                        k.V(lambda e: e.scalar_tensor_tensor(out=Sg[:, h, :], in0=Sg[:, h, :], scalar=ecs[:, h:h + 1], in1=pst[:, 0:256], op0=ALU.mult, op1=ALU.add), r=[Sg, ecs, pst], w=[Sg])
                    on_ = gon[n_ % 2]
                    if d == 0:
                        for hf in range(2):
                            k.A(lambda e: e.activation(out=on_[:, hf * 4:(hf + 1) * 4, :], in_=pso[hf][:].rearrange("p (b t) -> p b t", t=128), func=AF.Copy), r=[pso[hf]], w=[on_])
                        k.stq(S5, S5.t[0:1024, tsl].rearrange("(b p) t -> p b t", p=128), on_, on_[:])
                    else:
                        of_ = gof[n_ % 2]; rs_ = grs[n_ % 2]
                        for hf in range(2):
                            k.V(lambda e: e.tensor_tensor(out=gob[:, hf * 4:(hf + 1) * 4, :], in0=pso[hf][:].rearrange("p (b t) -> p b t", t=128), in1=of_[:, hf * 4:(hf + 1) * 4, :], op=ALU.add), r=[pso[hf], of_], w=[gob])
                        k.G(lambda e: e.tensor_tensor(out=gsq[:], in0=gob[:], in1=gob[:], op=ALU.mult), r=[gob], w=[gsq])
                        psn = k.psum()
                        for h in range(4):
                            for eb in range(2):
                                k.M(lambda e: e.matmul(psn[:, h * 128:(h + 1) * 128], cmb[:, ONES, :], gsq[:, h * 2 + eb, :], start=(eb == 0), stop=(eb == 1)), r=[cmb, gsq], w=[psn])
                        k.A(lambda e: e.activation(out=rsb[:], in_=psn[:], func=AF.Sqrt, bias=EPS, scale=1.0 / 256), r=[psn], w=[rsb])
                        k.V(lambda e: e.reciprocal(out=rsb[:], in_=rsb[:]), r=[rsb], w=[rsb])
                        for eb in range(2):
                            gv_ = gob[:].rearrange("p (h b) t -> p h b t", b=2)[:, :, eb, :]
                            k.V(lambda e: e.scalar_tensor_tensor(out=gv_, in0=gv_, scalar=gnw_s[:, j, eb:eb + 1], in1=rsb[:].rearrange("p (h t) -> p h t", t=128), op0=ALU.mult, op1=ALU.mult), r=[gob, gnw_s, rsb], w=[gob])
                        k.V(lambda e: e.tensor_tensor(out=on_[:], in0=gob[:], in1=rs_[:], op=ALU.mult), r=[gob, rs_], w=[on_])
                        k.stq(S5, S5.t[0:1024, tsl].rearrange("(b p) t -> p b t", p=128), on_, on_[:])
                    k.A(lambda e: e.activation(out=Sgb[:], in_=Sg[:], func=AF.Copy), r=[Sg], w=[Sgb])
                if kd == 'p':
                    k.stq(ns_gla, ns_gla.t[pidx, j, d].rearrange("h d e -> d h e"), Sg, Sg[:])
            if kd == 'p':
                pidx += 1
        k.leave('gla')
        gemm(S5, False, 8, gla_out_w, gla_out_w.t[j, :, :], 1024, 'fm', **resid_hooks(li, 2))

    gcw_s = k.sb("gcw_s", [128, 1, 32, 3], F32)
    k.ld(gcw_s, gcw_s[:], gdn_cw, gdn_cw.t[:, :, :, :])
    dnw_s = k.sb("dnw_s", [128, 1, 2], F32)
    k.ld(dnw_s, dnw_s[:], gdn_nw, gdn_nw.t[:, :, :])
    xb0f = xbuf[0][:].rearrange("p c n -> p (c n)")
    d_beta = k.view(xbuf[0], xb0f[:, 0:NCH * 16].rearrange("p (c n) -> p c n", n=16), group='gdnx')
    d_nbeta = k.view(xbuf[0], xb0f[:, NCH * 16:2 * NCH * 16].rearrange("p (c n) -> p c n", n=16), group='gdnx')
    d_lg = k.view(xbuf[0], xb0f[:, 2 * NCH * 16:3 * NCH * 16].rearrange("p (c n) -> p c n", n=16), group='gdnx')
    d_lgb = k.sb("d_lgb", [128, NCH, 16], BF16)
    dbc16 = k.sb("dbc16", [128, 16], F32); dnega = k.sb("dnega", [128, 16], F32)
    h3 = lambda ap: ap.rearrange("p (h t) -> p h t", t=128)
    dq_fm = [k.view(inb[i], h3(inb[i][:, 0:1024]), group='gdn') for i in range(2)]
    dk_fm = [k.view(inb[i], h3(inb[i][:, 1024:2048]), group='gdn') for i in range(2)]
    dk_tm = [k.view(inb[i], h3(inb[i][:, 2048:3072]), group='gdn') for i in range(2)]
    dv_tm = [k.view(inb[i], inb[i][:, 3072:5120].rearrange("p (h e) -> p h e", e=256), group='gdn') for i in range(2)]
    dof = [k.view(inb[i], h3(inb[i][:, 5120:7168]), group='gdn') for i in range(2)]
    dzs = [k.view(inb[i], h3(inb[i][:, 7168:9216]), group='gdn') for i in range(2)]
    dET = k.view(W0, h3(W0[:, 0:1024]), group='gdn'); dEN = k.view(W0, h3(W0[:, 1024:2048]), group='gdn')
    dLT = k.view(W0, h3(W0[:, 2048:3072]), group='gdn'); dLTm = k.view(W0, h3(W0[:, 3072:4096]), group='gdn'); dLms = k.view(W0, h3(W0[:, 4096:5120]), group='gdn')
    dB = [k.view(W0, h3(W0[:, 5120 + i * 1024:6144 + i * 1024]), group='gdn') for i in range(2)]
    dA = [k.view(W0, h3(W0[:, 7168 + i * 1024:8192 + i * 1024]), group='gdn') for i in range(2)]
    dRTb = k.view(W0, h3(W0[:, 9216:10240]), group='gdn'); daT = k.view(W0, h3(W0[:, 10240:11264]), group='gdn'); dqd = k.view(W0, h3(W0[:, 11264:12288]), group='gdn')
    dSb = k.view(W1, W1[:, 0:2048].rearrange("p (h e) -> p h e", e=256), group='gdn')
    dvb = k.view(W1, W1[:, 2048:4096].rearrange("p (h e) -> p h e", e=256), group='gdn')
    dsq = k.view(W1, h3(W1[:, 2048:4096]), group='gdn', share=dvb)
    dkbe = k.view(W1, h3(W1[:, 4096:5120]), group='gdn'); dnwk = k.view(W1, h3(W1[:, 5120:6144]), group='gdn')
    dvn = k.view(W1, W1[:, 6144:8192].rearrange("p (h e) -> p h e", e=256), group='gdn')
    dkd = k.view(W1, h3(W1[:, 8192:9216]), group='gdn')
    don = k.view(W1, h3(W1[:, 9216:11264]), group='gdn')
    dY = k.view(W1, h3(W1[:, 11264:12288]), group='gdn'); dY2 = k.view(W1, h3(W1[:, 5120:6144]), group='gdn', share=dnwk)
    lmask = k.sb("lmask_s", [128, 14, 128], BF16)
    k.ld(lmask, lmask[:], lmk, lmk.t[:, :, :], q='pool')
    dS = k.view(Sst, Sst[:].rearrange("p (h e) -> p h e", e=256), group='gdn')
    dRTf = k.view(yc, h3(yc[:, 0:1024]), group='gdn')
    dob = k.view(yc, h3(yc[:]), group='gdn', share=dRTf)

    def gdn(li, j):
        norm_stage(li, 0)
        W = gdn_in_w
        gemm(H, False, 8, W, W.t[j, :, 0:4096], 4096, 'fm', epi_store_fm(S1, 0))
        conv_stage(S1, S2, 32, lambda blk, tap: gcw_s[:, j, blk, tap - 3:tap - 2], None, False, [gcw_s], l2blocks=tuple(range(16)), qscale_blocks=tuple(range(8)))
        gemm(H, False, 8, W, W.t[j, :, 4096:6144], 2048, 'fm', epi_store_fm(S6, 0, AF.Silu))
        k.ld(dbc16, dbc16[:], gdn_dtb, gdn_dtb.t[j].partition_broadcast(128))
        k.ld(dnega, dnega[:], gdn_alog, gdn_alog.t[j].partition_broadcast(128))
        k.A(lambda e: e.activation(out=dnega[:], in_=dnega[:], func=AF.Exp), r=[dnega], w=[dnega])
        k.V(lambda e: e.tensor_scalar(out=dnega[:], in0=dnega[:], scalar1=-1.0, scalar2=None, op0=ALU.mult), r=[dnega], w=[dnega])

        def epi_ab(ps, c0, t, sub, last):
            ch = t * 4 + sub
            k.A(lambda e: e.activation(out=d_beta[:, ch, :], in_=ps[:, 16:32], func=AF.Sigmoid), r=[ps], w=[d_beta])
            k.V(lambda e: e.tensor_scalar(out=d_nbeta[:, ch, :], in0=d_beta[:, ch, :], scalar1=-1.0, scalar2=None, op0=ALU.mult), r=[d_beta], w=[d_nbeta])
            k.V(lambda e: e.tensor_tensor(out=d_lg[:, ch, :], in0=ps[:, 0:16], in1=dbc16[:], op=ALU.add), r=[ps, dbc16], w=[d_lg])
            k.A(lambda e: e.activation(out=d_lg[:, ch, :], in_=d_lg[:, ch, :], func=AF.Exp), r=[d_lg], w=[d_lg])
            k.A(lambda e: e.activation(out=d_lg[:, ch, :], in_=d_lg[:, ch, :], func=AF.Ln, bias=1.0, scale=1.0), r=[d_lg], w=[d_lg])
            k.V(lambda e: e.tensor_tensor(out=d_lg[:, ch, :], in0=d_lg[:, ch, :], in1=dnega[:], op=ALU.mult), r=[d_lg, dnega], w=[d_lg])
            k.V(lambda e: e.tensor_copy(out=d_lgb[:, ch, :], in_=d_lg[:, ch, :]), r=[d_lg], w=[d_lgb])
        k.enter('gdnx')
        gemm(H, False, 8, W, W.t[j, :, 6144:6176], 32, 'tm', epi_ab)
        k.enter('gdn')
        pidx = 0
        for (o, L, kd) in SEQS:
            for d in (0, 1):
                if kd == 'p':
                    k.V(lambda e: e.memset(dS[:], 0.0), w=[dS])
                    k.V(lambda e: e.memset(dSb[:], 0.0), w=[dSb])
                else:
                    k.ld(dS, dS[:], st_gdn, st_gdn.t[j, d].rearrange("h d e -> d h e"))
                    k.A(lambda e: e.activation(out=dSb[:], in_=dS[:], func=AF.Copy), r=[dS], w=[dSb])
                ie = 127 if d == 0 else 0
                hs = slice(d * 8, (d + 1) * 8)
                order = chunk_order(L, d)

                def gdn_loads(n_, c):
                    t0 = o + c * 128
                    tsl = slice(t0, t0 + 128)
                    q_ = dq_fm[n_ % 2]; k_ = dk_fm[n_ % 2]; kt = dk_tm[n_ % 2]; v_ = dv_tm[n_ % 2]
                    k.ld(q_, q_[:], S2, S2.t[0:1024, tsl].rearrange("(h d) t -> d h t", d=128))
                    k.ld(k_, k_[:], S2, S2.t[1024:2048, tsl].rearrange("(h d) t -> d h t", d=128))
                    for q2 in range(2):
                        k.ld(kt, kt[:, q2 * 4:(q2 + 1) * 4, :].rearrange("p h t -> p (h t)"), S2, S2.t[1024 + q2 * 512:1024 + (q2 + 1) * 512, tsl], tr=True)
                    for q4 in range(4):
                        k.ld(v_, v_[:, q4 * 2:(q4 + 1) * 2, :].rearrange("p h e -> p (h e)"), S2, S2.t[2048 + q4 * 512:2048 + (q4 + 1) * 512, tsl], tr=True)
                    if d == 1:
                        k.ld(dof[n_ % 2], dof[n_ % 2][:], S5, S5.t[0:2048, tsl].rearrange("(b p) t -> p b t", p=128))
                        k.ld(dzs[n_ % 2], dzs[n_ % 2][:], S6, S6.t[0:2048, tsl].rearrange("(b p) t -> p b t", p=128))
                gdn_loads(0, order[0])
                for n_, c in enumerate(order):
                    if n_ + 1 < len(order):
                        gdn_loads(n_ + 1, order[n_ + 1])
                    t0 = o + c * 128; ch = t0 // 128
                    tsl = slice(t0, t0 + 128)
                    q_ = dq_fm[n_ % 2]; k_ = dk_fm[n_ % 2]; kt = dk_tm[n_ % 2]; v_ = dv_tm[n_ % 2]
                    lg = d_lg[:, ch, hs]; lgb = d_lgb[:, ch, hs]; beta = d_beta[:, ch, hs]; nbeta = d_nbeta[:, ch, hs]
                    k.G(lambda e: e.tensor_tensor(out=dET[:], in0=bc(lg, [128, 8, 128], 2), in1=bc(cmb[:, U_[d], :], [128, 8, 128], 1), op=ALU.mult), r=[d_lg, cmb], w=[dET])
                    k.G(lambda e: e.tensor_tensor(out=dEN[:], in0=bc(lg, [128, 8, 128], 2), in1=bc(cmb[:, V_[d], :], [128, 8, 128], 1), op=ALU.mult), r=[d_lg, cmb], w=[dEN])
                    for hf in range(2):
                        ps = k.psum()
                        k.M(lambda e: e.matmul(ps[:], cmb[:, V_[d], :], dET[:, hf * 4:(hf + 1) * 4, :].rearrange("p h t -> p (h t)"), start=True, stop=True), r=[cmb, dET], w=[ps])
                        k.A(lambda e: e.activation(out=dLT[:, hf * 4:(hf + 1) * 4, :], in_=h3(ps[:]), func=AF.Exp), r=[ps], w=[dLT])
                        ps2 = k.psum()
                        k.M(lambda e: e.matmul(ps2[:], cmb[:, U_[d], :], dEN[:, hf * 4:(hf + 1) * 4, :].rearrange("p h t -> p (h t)"), start=True, stop=True), r=[cmb, dEN], w=[ps2])
                        k.A(lambda e: e.activation(out=dLms[:, hf * 4:(hf + 1) * 4, :], in_=h3(ps2[:]), func=AF.Exp), r=[ps2], w=[dLms])
                        ps3 = k.psum()
                        k.M(lambda e: e.matmul(ps3[:], cmb[:, ONES, :], dET[:, hf * 4:(hf + 1) * 4, :].rearrange("p h t -> p (h t)"), start=True, stop=True), r=[cmb, dET], w=[ps3])
                        k.A(lambda e: e.activation(out=tmpf[:], in_=ps3[:], func=AF.Exp), r=[ps3], w=[tmpf])
                        k.V(lambda e: e.tensor_tensor(out=dqd[:, hf * 4:(hf + 1) * 4, :], in0=q_[:, hf * 4:(hf + 1) * 4, :], in1=h3(tmpf[:]), op=ALU.mult), r=[q_, tmpf], w=[dqd])
                    k.V(lambda e: e.tensor_copy(out=ecs[:, 24:32], in_=dLT[:, :, ie]), r=[dLT], w=[ecs])
                    k.V(lambda e: e.tensor_tensor(out=dLTm[:], in0=dLT[:], in1=bc(cmb[:, U_[d], :], [128, 8, 128], 1), op=ALU.mult), r=[dLT, cmb], w=[dLTm])
                    k.V(lambda e: e.tensor_tensor(out=dLms[:], in0=dLms[:], in1=bc(cmb[:, V_[d], :], [128, 8, 128], 1), op=ALU.mult), r=[dLms, cmb], w=[dLms])
                    ps = k.psum()
                    k.M(lambda e: e.matmul(ps[:, 0:8], cmb[:, U_[d], :], lgb, start=True, stop=True), r=[cmb, d_lgb], w=[ps])
                    k.M(lambda e: e.matmul(ps[:, 8:16], cmb[:, ONES, :], lgb, start=True, stop=True), r=[cmb, d_lgb], w=[ps])
                    k.A(lambda e: e.activation(out=ecs[:, 0:16], in_=ps[:, 0:16], func=AF.Exp), r=[ps], w=[ecs])
                    k.V(lambda e: e.tensor_tensor(out=ecs[:, 16:24], in0=ecs[:, 0:8], in1=beta, op=ALU.mult), r=[ecs, d_beta], w=[ecs])
                    mN, mT, OffN, OffT = dB[0], dA[0], dB[1], dA[1]
                    dD = dLms
                    for hf in range(2):
                        ps = k.psum()
                        for h4 in range(4):
                            h = hf * 4 + h4
                            k.M(lambda e: e.matmul(ps[:, h4 * 128:(h4 + 1) * 128], k_[:, h, :], k_[:, h, :], start=True, stop=True), r=[k_], w=[ps])
                        for h4 in range(4):
                            h = hf * 4 + h4
                            k.V(lambda e: e.scalar_tensor_tensor(out=mN[:, h, :], in0=ps[:, h4 * 128:(h4 + 1) * 128], scalar=d_beta[:, ch, d * 8 + h:d * 8 + h + 1], in1=dLms[:, h, :], op0=ALU.mult, op1=ALU.mult),
                                r=[ps, d_beta, dLms], w=[mN])
                    for hf in range(2):
                        ps = k.psum()
                        for h4 in range(4):
                            h = hf * 4 + h4
                            k.M(lambda e: e.matmul(ps[:, h4 * 128:(h4 + 1) * 128], mN[:, h, :], cmb[:, ID, :], start=True, stop=True), r=[mN, cmb], w=[ps])
                        k.A(lambda e: e.activation(out=mT[:, hf * 4:(hf + 1) * 4, :], in_=h3(ps[:]), func=AF.Copy), r=[ps], w=[mT])
                    MN = lambda lv: lmask[:, (lv if d == 0 else 7 + lv), :]
                    MT_ = lambda lv: lmask[:, (7 + lv if d == 0 else lv), :]
                    k.V(lambda e: e.tensor_tensor(out=OffN[:], in0=mN[:], in1=bc(MN(0), [128, 8, 128], 1), op=ALU.mult), r=[mN, lmask], w=[OffN])
                    k.V(lambda e: e.tensor_tensor(out=dD[:], in0=bc(cmb[:, ID, :], [128, 8, 128], 1), in1=OffN[:], op=ALU.subtract), r=[cmb, OffN], w=[dD])
                    k.V(lambda e: e.tensor_tensor(out=OffT[:], in0=mT[:], in1=bc(MT_(0), [128, 8, 128], 1), op=ALU.mult), r=[mT, lmask], w=[OffT])
                    k.V(lambda e: e.tensor_tensor(out=dRTb[:], in0=bc(cmb[:, ID, :], [128, 8, 128], 1), in1=OffT[:], op=ALU.subtract), r=[cmb, OffT], w=[dRTb])
                    for lv in range(1, 7):
                        k.V(lambda e: e.tensor_tensor(out=OffN[:], in0=mN[:], in1=bc(MN(lv), [128, 8, 128], 1), op=ALU.mult), r=[mN, lmask], w=[OffN])
                        k.G(lambda e: e.tensor_tensor(out=OffT[:], in0=mT[:], in1=bc(MT_(lv), [128, 8, 128], 1), op=ALU.mult), r=[mT, lmask], w=[OffT])
                        for hf in range(2):
                            ps = k.psum()
                            for h4 in range(4):
                                h = hf * 4 + h4
                                k.M(lambda e: e.matmul(ps[:, h4 * 128:(h4 + 1) * 128], OffT[:, h, :], dD[:, h, :], start=True, stop=True), r=[OffT, dD], w=[ps])
                            k.A(lambda e: e.activation(out=dY[:, hf * 4:(hf + 1) * 4, :], in_=h3(ps[:]), func=AF.Copy), r=[ps], w=[dY])
                            ps2 = k.psum()
                            for h4 in range(4):
                                h = hf * 4 + h4
                                k.M(lambda e: e.matmul(ps2[:, h4 * 128:(h4 + 1) * 128], OffN[:, h, :], dRTb[:, h, :], start=True, stop=True), r=[OffN, dRTb], w=[ps2])
                            k.V(lambda e: e.tensor_copy(out=dY2[:, hf * 4:(hf + 1) * 4, :], in_=h3(ps2[:])), r=[ps2], w=[dY2])
                        pX = [k.psum(), k.psum()]; pX2 = [k.psum(), k.psum()]
                        for hf in range(2):
                            for h4 in range(4):
                                h = hf * 4 + h4
                                k.M(lambda e: e.matmul(pX[hf][:, h4 * 128:(h4 + 1) * 128], dRTb[:, h, :], dY[:, h, :], start=True, stop=True), r=[dRTb, dY], w=[pX[hf]])
                            for h4 in range(4):
                                h = hf * 4 + h4
                                k.M(lambda e: e.matmul(pX2[hf][:, h4 * 128:(h4 + 1) * 128], dD[:, h, :], dY2[:, h, :], start=True, stop=True), r=[dD, dY2], w=[pX2[hf]])
                        for hf in range(2):
                            k.V(lambda e: e.tensor_tensor(out=dD[:, hf * 4:(hf + 1) * 4, :], in0=dD[:, hf * 4:(hf + 1) * 4, :], in1=h3(pX[hf][:]), op=ALU.subtract), r=[dD, pX[hf]], w=[dD])
                            k.V(lambda e: e.tensor_tensor(out=dRTb[:, hf * 4:(hf + 1) * 4, :], in0=dRTb[:, hf * 4:(hf + 1) * 4, :], in1=h3(pX2[hf][:]), op=ALU.subtract), r=[dRTb, pX2[hf]], w=[dRTb])
                    k.G(lambda e: e.tensor_tensor(out=dvb[:], in0=v_[:], in1=bc(beta, [128, 8, 256], 2), op=ALU.mult), r=[v_, d_beta], w=[dvb])
                    k.G(lambda e: e.tensor_tensor(out=dkbe[:], in0=kt[:], in1=bc(ecs[:, 16:24], [128, 8, 128], 2), op=ALU.mult), r=[kt, ecs], w=[dkbe])
                    k.G(lambda e: e.tensor_tensor(out=dkd[:], in0=kt[:], in1=bc(ecs[:, 24:32], [128, 8, 128], 2), op=ALU.mult), r=[kt, ecs], w=[dkd])
                    for hf in range(2):
                        ps = k.psum()
                        for h4 in range(4):
                            h = hf * 4 + h4
                            k.M(lambda e: e.matmul(ps[:, h4 * 128:(h4 + 1) * 128], dkbe[:, h, :], dRTb[:, h, :], start=True, stop=True), r=[dkbe, dRTb], w=[ps])
                        k.A(lambda e: e.activation(out=dnwk[:, hf * 4:(hf + 1) * 4, :], in_=h3(ps[:]), func=AF.Copy, scale=-1.0), r=[ps], w=[dnwk])
                    for hp in range(4):
                        ps = k.psum()
                        for h2 in range(2):
                            h = hp * 2 + h2
                            k.M(lambda e: e.matmul(ps[:, h2 * 256:(h2 + 1) * 256], dRTb[:, h, :], dvb[:, h, :], start=True, stop=False), r=[dRTb, dvb], w=[ps])
                            k.M(lambda e: e.matmul(ps[:, h2 * 256:(h2 + 1) * 256], dnwk[:, h, :], dSb[:, h, :], start=False, stop=True), r=[dnwk, dSb], w=[ps])
                        k.A(lambda e: e.activation(out=dvn[:, hp * 2:(hp + 1) * 2, :], in_=ps[:].rearrange("p (h e) -> p h e", e=256), func=AF.Copy), r=[ps], w=[dvn])
                    for hf in range(2):
                        ps = k.psum()
                        for h4 in range(4):
                            h = hf * 4 + h4
                            k.M(lambda e: e.matmul(ps[:, h4 * 128:(h4 + 1) * 128], k_[:, h, :], q_[:, h, :], start=True, stop=True), r=[k_, q_], w=[ps])
                        k.V(lambda e: e.tensor_tensor(out=daT[:, hf * 4:(hf + 1) * 4, :], in0=h3(ps[:]), in1=dLTm[:, hf * 4:(hf + 1) * 4, :], op=ALU.mult), r=[ps, dLTm], w=[daT])
                    pso = [k.psum() for _ in range(4)]
                    for h in range(8):
                        for eb in range(2):
                            blk = h * 2 + eb
                            po = pso[blk // 4]; cs_ = slice((blk % 4) * 128, (blk % 4 + 1) * 128)
                            k.M(lambda e: e.matmul(po[:, cs_], dSb[:, h, eb * 128:(eb + 1) * 128], dqd[:, h, :], start=True, stop=False), r=[dSb, dqd], w=[po])
                            k.M(lambda e: e.matmul(po[:, cs_], dvn[:, h, eb * 128:(eb + 1) * 128], daT[:, h, :], start=False, stop=True), r=[dvn, daT], w=[po])
                    if d == 0:
                        for q4 in range(4):
                            k.A(lambda e: e.activation(out=don[:, q4 * 4:(q4 + 1) * 4, :], in_=h3(pso[q4][:]), func=AF.Copy), r=[pso[q4]], w=[don])
                        k.stq(S5, S5.t[0:2048, tsl].rearrange("(b p) t -> p b t", p=128), don, don[:])
                    else:
                        of_ = dof[n_ % 2]; zs_ = dzs[n_ % 2]
                        for q4 in range(4):
                            k.V(lambda e: e.tensor_tensor(out=dob[:, q4 * 4:(q4 + 1) * 4, :], in0=h3(pso[q4][:]), in1=of_[:, q4 * 4:(q4 + 1) * 4, :], op=ALU.add), r=[pso[q4], of_], w=[dob])
                    for h in range(8):
                        pst = k.psum()
                        k.M(lambda e: e.matmul(pst[:, 0:256], dkd[:, h, :], dvn[:, h, :], start=True, stop=True), r=[dkd, dvn], w=[pst])
                        k.V(lambda e: e.scalar_tensor_tensor(out=dS[:, h, :], in0=dS[:, h, :], scalar=ecs[:, 8 + h:9 + h], in1=pst[:, 0:256], op0=ALU.mult, op1=ALU.add), r=[dS, ecs, pst], w=[dS])
                    k.A(lambda e: e.activation(out=dSb[:], in_=dS[:], func=AF.Copy), r=[dS], w=[dSb])
                    if d == 1:
                        k.G(lambda e: e.tensor_tensor(out=dsq[:], in0=dob[:], in1=dob[:], op=ALU.mult), r=[dob], w=[dsq])
                        for hf in range(2):
                            psn = k.psum()
                            for h4 in range(4):
                                h = hf * 4 + h4
                                for eb in range(2):
                                    k.M(lambda e: e.matmul(psn[:, h4 * 128:(h4 + 1) * 128], cmb[:, ONES, :], dsq[:, h * 2 + eb, :], start=(eb == 0), stop=(eb == 1)), r=[cmb, dsq], w=[psn])
                            k.A(lambda e: e.activation(out=rsb[:], in_=psn[:], func=AF.Sqrt, bias=EPS, scale=1.0 / 256), r=[psn], w=[rsb])
                            k.V(lambda e: e.reciprocal(out=rsb[:], in_=rsb[:]), r=[rsb], w=[rsb])
                            for eb in range(2):
                                gv_ = dob[:, hf * 8:(hf + 1) * 8, :].rearrange("p (h b) t -> p h b t", b=2)[:, :, eb, :]
                                k.V(lambda e: e.scalar_tensor_tensor(out=gv_, in0=gv_, scalar=dnw_s[:, j, eb:eb + 1], in1=h3(rsb[:]), op0=ALU.mult, op1=ALU.mult), r=[dob, dnw_s, rsb], w=[dob])
                        k.V(lambda e: e.tensor_tensor(out=don[:], in0=dob[:], in1=zs_[:], op=ALU.mult), r=[dob, zs_], w=[don])
                        k.stq(S5, S5.t[0:2048, tsl].rearrange("(b p) t -> p b t", p=128), don, don[:])
                if kd == 'p':
                    k.stq(ns_gdn, ns_gdn.t[pidx, j, d].rearrange("h d e -> d h e"), dS, dS[:])
            if kd == 'p':
                pidx += 1
        k.leave('gdn'); k.leave('gdnx')
        gemm(S5, False, 16, gdn_out_w, gdn_out_w.t[j, :, :], 1024, 'fm', **resid_hooks(li, 2))

    wdec = k.sb("wdec", [128, 32], F32)

    for (li, kind, j) in LAYERS:
        if kind == 'ssd':
            ssd(li, j)
        elif kind == 'gla':
            gla(li, j)
        elif kind == 'gdn':
            gdn(li, j)
        elif kind == 'none':
            pass
        else:
            raise NotImplementedError(kind)
        if kind != 'none' or True:
            ffn(li)
    norm_stage(0, 0, final=True)
    outs = [yT, X, ns_ssm, ns_gla, ns_gdn, S1, S2, S3, S4, S5, H]
    k.P.wait_all('sp', [o.b for o in outs])
    k.P.run()
    st.close()
    return nc, k.P.ninst


def _consts():
    i = np.arange(128)
    kk, ii = np.meshgrid(i, i, indexing='ij')
    cm = np.zeros((128, 10, 128), np.float32)
    cm[:, 0] = (kk <= ii); cm[:, 1] = (kk >= ii); cm[:, 2] = (kk > ii); cm[:, 3] = (kk < ii)
    cm[:, 4] = (kk == ii); cm[:, 5] = 1.0
    cm[:, 6] = (kk <= ii) / -16.0; cm[:, 7] = (kk >= ii) / -16.0; cm[:, 8] = (kk > ii) / -16.0; cm[:, 9] = (kk < ii) / -16.0
    return cm


def _fm(v, nblk):
    v = np.asarray(v, np.float32)
    lead = v.shape[:-1]
    v = v.reshape(*lead, nblk, 128)
    return np.ascontiguousarray(np.moveaxis(v, -1, 0))


def prep_common(w):
    d = {}
    d['cmask'] = _consts()
    ii, jj = np.meshgrid(np.arange(128), np.arange(128), indexing='ij')
    lm = np.zeros((128, 14, 128), np.float32)
    for lv in range(7):
        b = 1 << lv
        m = ((ii // (2 * b)) == (jj // (2 * b))) & ((ii % (2 * b)) >= b) & ((jj % (2 * b)) < b)
        lm[:, lv] = m; lm[:, 7 + lv] = m.T
    d['lmask'] = lm
    d['nmw'] = _fm(w['norm_mix_w'], 8); d['nfw'] = _fm(w['norm_ffn_w'], 8); d['fnw'] = _fm(w['final_norm_w'], 8)
    d['ada_w'] = w['ada_w']; d['ada_b'] = _fm(w['ada_b'], 48)
    d['ffn_in_w'] = w['ffn_in_w']; d['ffn_out_w'] = w['ffn_out_w']
    d['ffn_cw'] = _fm(np.moveaxis(np.asarray(w['ffn_conv_w']).reshape(4, 9, 2816), 1, 2).reshape(4, 2816 * 9).reshape(4, 22, 128, 9).transpose(0, 1, 3, 2).reshape(4, 22, 9, 128), 1)[..., 0] if False else None
    cw = np.asarray(w['ffn_conv_w'], np.float32).reshape(4, 9, 22, 128)
    d['ffn_cw'] = np.ascontiguousarray(cw.transpose(3, 0, 2, 1))
    d['ffn_cb'] = _fm(w['ffn_conv_b'], 22)
    d['ssm_in_w'] = w['ssm_in_w']
    cw = np.asarray(w['ssm_conv_w'], np.float32).reshape(2, 3, 32, 128)
    d['ssm_cw'] = np.ascontiguousarray(cw.transpose(3, 0, 2, 1))
    d['ssm_cb'] = _fm(w['ssm_conv_b'], 32)
    d['ssm_dtb'] = np.asarray(w['ssm_dt_bias'], np.float32).reshape(2, 1, 64)
    d['ssm_alog'] = np.asarray(w['ssm_a_log'], np.float32).reshape(2, 1, 64)
    d['ssm_d'] = np.asarray(w['ssm_d'], np.float32).reshape(2, 1, 32)
    d['ssm_nw'] = np.asarray(w['ssm_norm_w'], np.float32).reshape(2, 1, 2048)
    d['ssm_out_w'] = w['ssm_out_w']
    d['gla_in_w'] = w['gla_in_w']
    w1 = np.zeros((1, 1024, 128), np.float32)
    g1 = np.asarray(w['gla_gate_w1'], np.float32)
    w1[:, :, 0:16] = g1[:, 0]; w1[:, :, 32:48] = g1[:, 1]
    d['gla_w1'] = w1
    w2 = np.zeros((1, 64, 2, 512), np.float32)
    g2 = np.asarray(w['gla_gate_w2'], np.float32)
    w2[:, 0:16, 0] = g2[:, 0]; w2[:, 32:48, 1] = g2[:, 1]
    d['gla_w2'] = w2.reshape(1, 64, 1024)
    d['gla_gb'] = np.asarray(w['gla_gate_b'], np.float32).reshape(1, 1, 1024)
    d['gla_nw'] = _fm(w['gla_norm_w'], 2); d['gla_out_w'] = w['gla_out_w']
    d['gdn_in_w'] = w['gdn_in_w']
    cw = np.asarray(w['gdn_conv_w'], np.float32).reshape(1, 3, 32, 128)
    d['gdn_cw'] = np.ascontiguousarray(cw.transpose(3, 0, 2, 1))
    d['gdn_dtb'] = np.asarray(w['gdn_dt_bias'], np.float32).reshape(1, 1, 16)
    d['gdn_alog'] = np.asarray(w['gdn_a_log'], np.float32).reshape(1, 1, 16)
    d['gdn_nw'] = _fm(w['gdn_norm_w'], 2); d['gdn_out_w'] = w['gdn_out_w']
    return {k_: np.ascontiguousarray(np.asarray(v, np.float32)) for k_, v in d.items()}


FULL_SEQS = [(0, 256, 'p'), (256, 256, 'p'), (512, 4096, 's')]
FULL_LAYERS = [(0, 'ssd', 0), (1, 'gla', 0), (2, 'gdn', 0), (3, 'ssd', 1)]


def kernel(**inp):
    w = {k_: np.asarray(v) for k_, v in inp.items()}
    common = prep_common(w)
    nc, ninst = build_program(FULL_SEQS, FULL_LAYERS)
    xp = np.asarray(w['x_prompt'], np.float32); xs = np.asarray(w['x_sample'], np.float32)
    in_maps = []
    for core in range(8):
        b = core % 2
        xcat = np.concatenate([xp[2 * core], xp[2 * core + 1], xs[b]], axis=0)
        m = dict(common)
        m['xT'] = np.ascontiguousarray(xcat.T)
        cond = np.stack([np.asarray(w['c_ctx'], np.float32), np.asarray(w['c'], np.float32)[b]], axis=1)
        m['condT'] = np.ascontiguousarray(cond.reshape(8, 128, 2).transpose(1, 0, 2))
        ss = np.asarray(w['state_ssm'], np.float32)[b]
        m['st_ssm'] = np.ascontiguousarray(ss.transpose(0, 1, 3, 2, 4).reshape(2, 2, 128, 2048))
        m['st_gla'] = np.ascontiguousarray(np.asarray(w['state_gla'], np.float32)[b])
        m['st_gdn'] = np.ascontiguousarray(np.asarray(w['state_gdn'], np.float32)[b])
        in_maps.append(m)
    res = run_bass_kernel_spmd(nc, in_maps, core_ids=list(range(8)))
    y_prompt = np.zeros((16, 256, 1024), np.float32); y_sample = np.zeros((2, 4096, 1024), np.float32)
    n_ssm = np.zeros((16, 2, 2, 32, 128, 64), np.float32); n_gla = np.zeros((16, 1, 2, 4, 128, 256), np.float32)
    n_gdn = np.zeros((16, 1, 2, 8, 128, 256), np.float32)
    for core in range(8):
        r = res.results[core]
        y = np.asarray(r['yT']).T
        y_prompt[2 * core] = y[0:256]; y_prompt[2 * core + 1] = y[256:512]
        if core < 2:
            y_sample[core] = y[512:]
        s_ = np.asarray(r['ns_ssm']).reshape(2, 2, 2, 128, 32, 64).transpose(0, 1, 2, 4, 3, 5)
        n_ssm[2 * core:2 * core + 2] = s_
        n_gla[2 * core:2 * core + 2] = np.asarray(r['ns_gla'])
        n_gdn[2 * core:2 * core + 2] = np.asarray(r['ns_gdn'])
    return (y_prompt, y_sample, n_ssm, n_gla, n_gdn)
```

```python
import contextlib
import numpy as np
import concourse.bass as bass
import concourse.mybir as mybir
from concourse.bass_utils import run_bass_kernel_spmd

F32 = mybir.dt.float32
BF16 = mybir.dt.bfloat16
AF = mybir.ActivationFunctionType
ALU = mybir.AluOpType
AX = mybir.AxisListType
ENGS = ['pe', 'act', 'dve', 'pool', 'sp']
EPS = 1e-6
DBG_STOP = None


class Buf:
    __slots__ = ('name', 'lw', 'rd', 'acc')

    def __init__(self, name, acc=False):
        self.name = name
        self.lw = {}
        self.rd = {}
        self.acc = acc


class _Rec:
    def __init__(self):
        self.call = None

    def __getattr__(self, name):
        def f(*a, **kw):
            self.call = (name, a, kw)
            return self
        return f


class Prog:
    def __init__(self, nc):
        self.nc = nc
        self.q = {e: [] for e in ENGS}
        self.cnt = {e: 0 for e in ENGS}
        self.known = {e: {} for e in ENGS}
        self.dma_cnt = {}
        self.ninst = 0

    def _need(self, eng, key, val, src, waits):
        if src == eng and eng == 'pe':
            return
        if self.known[eng].get(key, 0) >= val:
            return
        if waits.get(key, 0) < val:
            waits[key] = val

    def emit(self, eng, fn, reads=(), writes=(), dma_key=None):
        rec = _Rec()
        fn(rec)
        _c = rec.call
        fn = lambda e, _c=_c: getattr(e, _c[0])(*_c[1], **_c[2])
        waits = {}
        for b in reads:
            for k, (v, s) in b.lw.items():
                self._need(eng, k, v, s, waits)
        for b in writes:
            if not b.acc:
                for k, (v, s) in b.lw.items():
                    self._need(eng, k, v, s, waits)
            for k, (v, s) in b.rd.items():
                self._need(eng, k, v, s, waits)
        for k, v in waits.items():
            self.known[eng][k] = v
        if dma_key is None:
            self.cnt[eng] += 1
            key, val, src, inc = eng, self.cnt[eng], eng, 1
        else:
            self.dma_cnt[dma_key] = self.dma_cnt.get(dma_key, 0) + 16
            key, val, src, inc = dma_key, self.dma_cnt[dma_key], None, 16
        for b in reads:
            if b.rd.get(key, (0, None))[0] < val:
                b.rd[key] = (val, src)
        for b in writes:
            if b.acc:
                b.lw[key] = (val, src)
            else:
                b.lw = {key: (val, src)}
            b.rd = {}
        self.q[eng].append((list(waits.items()), fn, (key, inc)))
        self.ninst += 1
        return (key, val, src)

    def wait_all(self, eng, bufs):
        waits = {}
        for b in bufs:
            for k, (v, s) in b.lw.items():
                self._need(eng, k, v, s, waits)
        self.q[eng].append((list(waits.items()), None, None))

    def run(self):
        nc = self.nc
        keys = list(ENGS) + sorted(self.dma_cnt.keys())
        with contextlib.ExitStack() as st:
            sems = {k: st.enter_context(nc.semaphore("s%d" % i)) for i, k in enumerate(keys)}
            block = st.enter_context(nc.Block())

            def mk(engname):
                def body(e):
                    for waits, fn, inc in self.q[engname]:
                        for k, v in waits:
                            e.wait_ge(sems[k], v)
                        if fn is not None:
                            fn(e).then_inc(sems[inc[0]], inc[1])
                return body
            block.tensor(mk('pe'))
            block.scalar(mk('act'))
            block.vector(mk('dve'))
            block.gpsimd(mk('pool'))
            block.sync(mk('sp'))


class Tl:
    def __init__(self, t, name, acc=False):
        self.t = t
        self.b = Buf(name, acc)
        self.name = name

    def __getitem__(self, k):
        return self.t[k]


class KB:
    def __init__(self, nc, st):
        self.nc = nc
        self.st = st
        self.P = Prog(nc)
        self.ps = [Tl(st.enter_context(nc.psum_tensor("ps%d" % i, [128, 512], F32)), "ps%d" % i) for i in range(8)]
        self.psi = 0
        self.nkey = 0
        self.keymap = {}
        self.groups = {}
        self.nview = 0
        self.cast_tok = Buf('castdma', acc=True)
        self.tr_tok = Buf('trdma', acc=True)

    def sb(self, name, shape, dt):
        return Tl(self.st.enter_context(self.nc.sbuf_tensor(name, shape, dt)), name)

    def dram(self, name, shape, dt, kind=None):
        if kind is None:
            t = self.nc.dram_tensor(name, shape, dt)
        else:
            t = self.nc.dram_tensor(name, shape, dt, kind=kind)
        return Tl(t.ap(), name, acc=True)

    def view(self, parent, ap, group=None, share=None):
        v = Tl(ap, parent.name)
        if group is None:
            v.b = parent.b
        else:
            self.nview += 1
            v.name = "%s#%d" % (parent.name, self.nview)
            if share is not None:
                v.b = share.b
                v.name = share.name
            self.groups.setdefault(group, []).append((parent, v))
        return v

    @staticmethod
    def _union(dicts):
        out = {}
        for d_ in dicts:
            for k_, (v_, s_) in d_.items():
                if out.get(k_, (0, None))[0] < v_:
                    out[k_] = (v_, s_)
        return out

    def enter(self, group):
        for parent, v in self.groups.get(group, []):
            v.b.lw = self._union([parent.b.lw, parent.b.rd])
            v.b.rd = {}

    def leave(self, group):
        parents = {}
        for parent, v in self.groups.get(group, []):
            parents.setdefault(id(parent), (parent, []))[1].append(v)
        for parent, vs in parents.values():
            parent.b.lw = self._union([parent.b.lw, parent.b.rd] + [v.b.lw for v in vs] + [v.b.rd for v in vs])
            parent.b.rd = {}

    def psum(self):
        p = self.ps[self.psi % 8]
        self.psi += 1
        return p

    def _b(self, l):
        return [x.b for x in l]

    def M(self, fn, r=(), w=()):
        return self.P.emit('pe', fn, self._b(r), self._b(w))

    def A(self, fn, r=(), w=()):
        return self.P.emit('act', fn, self._b(r), self._b(w))

    def V(self, fn, r=(), w=()):
        return self.P.emit('dve', fn, self._b(r), self._b(w))

    def G(self, fn, r=(), w=()):
        return self.P.emit('pool', fn, self._b(r), self._b(w))

    def _key(self, s):
        if s not in self.keymap:
            self.keymap[s] = "d%04d" % len(self.keymap)
        return self.keymap[s]

    def ld(self, dst, dst_ap, src, src_ap, q='sp', tr=False):
        key = self._key('ld:' + dst.name)
        if tr:
            fn = lambda e: e.dma_start_transpose(out=dst_ap, in_=src_ap)
        else:
            fn = lambda e: e.dma_start(out=dst_ap, in_=src_ap)
        if tr:
            return self.P.emit(q, fn, [src.b, self.cast_tok], [dst.b, self.tr_tok], dma_key=key)
        if q == 'pool':
            return self.P.emit(q, fn, [src.b, self.tr_tok], [dst.b, self.cast_tok], dma_key=key)
        return self.P.emit(q, fn, [src.b], [dst.b], dma_key=key)

    def stq(self, dst, dst_ap, src, src_ap, q='sp'):
        key = self._key('st:' + src.name)
        return self.P.emit(q, lambda e: e.dma_start(out=dst_ap, in_=src_ap), [src.b], [dst.b], dma_key=key)


def bc(ap, shape, axis):
    return ap.unsqueeze(axis).broadcast_to(shape)


def build_program(SEQS, LAYERS, dbg=()):
    T = sum(L for _, L, _ in SEQS)
    NT = T // 512
    assert T % 512 == 0
    NCH = T // 128
    NP = sum(1 for s in SEQS if s[2] == 'p')
    nc = bass.Bass("TRN2", target_bir_lowering=False)
    st = contextlib.ExitStack()
    k = KB(nc, st)
    IN = lambda name, shape, dt=F32: k.dram(name, shape, dt, kind="ExternalInput")
    OUT = lambda name, shape, dt=F32: k.dram(name, shape, dt, kind="ExternalOutput")
    dbgset = set(dbg)
    SCR = lambda name, shape, dt=BF16: k.dram(name, shape, dt, kind=("ExternalOutput" if name in dbgset else None))

    xT = IN("xT", [1024, T])
    condT = IN("condT", [128, 8, 2])
    cm = IN("cmask", [128, 10, 128])
    nmw = IN("nmw", [128, 4, 8]); nfw = IN("nfw", [128, 4, 8]); fnw = IN("fnw", [128, 8])
    ada_w = IN("ada_w", [4, 1024, 6144]); ada_b = IN("ada_b", [128, 4, 48])
    ffn_in_w = IN("ffn_in_w", [4, 1024, 5632]); ffn_cw = IN("ffn_cw", [128, 4, 22, 9]); ffn_cb = IN("ffn_cb", [128, 4, 22])
    ffn_out_w = IN("ffn_out_w", [4, 2816, 1024])
    ssm_in_w = IN("ssm_in_w", [2, 1024, 6208]); ssm_cw = IN("ssm_cw", [128, 2, 32, 3]); ssm_cb = IN("ssm_cb", [128, 2, 32])
    ssm_dtb = IN("ssm_dtb", [2, 1, 64]); ssm_alog = IN("ssm_alog", [2, 1, 64]); ssm_d = IN("ssm_d", [2, 1, 32])
    ssm_nw = IN("ssm_nw", [2, 1, 2048]); ssm_out_w = IN("ssm_out_w", [2, 2048, 1024])
    gla_in_w = IN("gla_in_w", [1, 1024, 3072]); gla_w1 = IN("gla_w1", [1, 1024, 128]); gla_w2 = IN("gla_w2", [1, 64, 1024]); gla_gb = IN("gla_gb", [1, 1, 1024])
    gla_nw = IN("gla_nw", [128, 1, 2]); gla_out_w = IN("gla_out_w", [1, 1024, 1024])
    gdn_in_w = IN("gdn_in_w", [1, 1024, 6176]); gdn_cw = IN("gdn_cw", [128, 1, 32, 3])
    gdn_dtb = IN("gdn_dtb", [1, 1, 16]); gdn_alog = IN("gdn_alog", [1, 1, 16]); gdn_nw = IN("gdn_nw", [128, 1, 2])
    gdn_out_w = IN("gdn_out_w", [1, 2048, 1024])
    lmk = IN("lmask", [128, 14, 128])
    st_ssm = IN("st_ssm", [2, 2, 128, 2048]); st_gla = IN("st_gla", [1, 2, 4, 128, 256]); st_gdn = IN("st_gdn", [1, 2, 8, 128, 256])
    yT = OUT("yT", [1024, T])
    ns_ssm = OUT("ns_ssm", [max(NP, 1), 2, 2, 128, 2048]); ns_gla = OUT("ns_gla", [max(NP, 1), 1, 2, 4, 128, 256])
    ns_gdn = OUT("ns_gdn", [max(NP, 1), 1, 2, 8, 128, 256])
    X = k.dram("X", [1024, T], F32, kind=("ExternalOutput" if "X" in dbgset else None))
    H = SCR("H", [1024, T])
    S1 = SCR("S1", [4096, T])
    S2 = SCR("S2", [4096, T])
    S3 = SCR("S3", [T, 2048])
    S4 = SCR("S4", [T, 2048])
    S5 = SCR("S5", [2816, T])
    S6 = SCR("S6", [2048, T])
    S7 = SCR("S7", [T, 1024])

    cmb = k.sb("cmb", [128, 10, 128], BF16)
    k.ld(cmb, cmb[:], cm, cm.t[:, :, :], q='pool')
    Uf, Ub, Vf, Vb, ID, ONES, UGf, UGb, VGf, VGb = range(10)
    UG_ = {0: UGf, 1: UGb}; VG_ = {0: VGf, 1: VGb}
    U_ = {0: Uf, 1: Ub}; V_ = {0: Vf, 1: Vb}
    nmw_s = k.sb("nmw_s", [128, 4, 8], F32); nfw_s = k.sb("nfw_s", [128, 4, 8], F32); fnw_s = k.sb("fnw_s", [128, 8], F32)
    k.ld(nmw_s, nmw_s[:], nmw, nmw.t[:, :, :]); k.ld(nfw_s, nfw_s[:], nfw, nfw.t[:, :, :]); k.ld(fnw_s, fnw_s[:], fnw, fnw.t[:, :])
    adab_s = k.sb("adab_s", [128, 4, 48], F32)
    k.ld(adab_s, adab_s[:], ada_b, ada_b.t[:, :, :])
    mod = k.sb("mod", [128, 4, 48, 2], F32)
    gm = k.sb("gm", [128, 4, 2, 8, 2], F32)

    xbuf = [k.sb("xb%d" % i, [128, 8, 512], F32) for i in range(2)]
    for t in range(NT):
        xb = xbuf[t % 2]
        k.ld(xb, xb[:], xT, xT.t[:, t * 512:(t + 1) * 512].rearrange("(c p) t -> p c t", p=128))
        k.stq(X, X.t[:, t * 512:(t + 1) * 512].rearrange("(c p) t -> p c t", p=128), xb, xb[:])
    cnd = k.sb("cnd", [128, 8, 2], F32); cndb = k.sb("cndb", [128, 8, 2], BF16)
    k.ld(cnd, cnd[:], condT, condT.t[:, :, :])
    k.A(lambda e: e.activation(out=cndb[:], in_=cnd[:], func=AF.Silu), r=[cnd], w=[cndb])
    wbuf = [k.sb("wb%d" % i, [128, 12288], BF16) for i in range(2)]
    wi = [0]

    def load_w(src, src_ap3, kc, ncols, after=()):
        wb = wbuf[wi[0] % 2]; wi[0] += 1
        view = wb[:, 0:kc * ncols].rearrange("p (c n) -> p c n", c=kc)
        key = k._key('ld:' + wb.name)
        for c in range(kc):
            k.P.emit('pool', lambda e: e.dma_start(out=view[:, c, :], in_=src_ap3[c * 128:(c + 1) * 128, :]), [src.b, k.tr_tok] + [a.b for a in after], [wb.b, k.cast_tok], dma_key=key)
        return wb, view

    for (li, kind, j) in LAYERS:
        for qd in range(6):
            wb, wv = load_w(ada_w, ada_w.t[li, :, qd * 1024:(qd + 1) * 1024], 8, 1024)
            ps = k.psum()
            for cb in range(8):
                for c in range(8):
                    k.M(lambda e, c=c, cb=cb, wv=wv, ps=ps: e.matmul(ps[:, cb * 2:cb * 2 + 2], wv[:, c, cb * 128:(cb + 1) * 128], cndb[:, c, :],
                                                                       start=(c == 0), stop=(c == 7)), r=[wb, cndb], w=[ps])
            k.V(lambda e, ps=ps, qd=qd, li=li: e.tensor_tensor(out=mod[:, li, qd * 8:(qd + 1) * 8, :], in0=ps[:, 0:16].rearrange("p (a b) -> p a b", b=2),
                                                                in1=bc(adab_s[:, li, qd * 8:(qd + 1) * 8], [128, 8, 2], 2), op=ALU.add), r=[ps, adab_s], w=[mod])
        for sub, nw in ((0, nmw_s), (1, nfw_s)):
            sc = 1 + 3 * sub
            k.V(lambda e, sub=sub, li=li, sc=sc: e.tensor_scalar(out=gm[:, li, sub], in0=mod[:, li, sc * 8:(sc + 1) * 8, :], scalar1=1.0, scalar2=None, op0=ALU.add), r=[mod], w=[gm])
            k.V(lambda e, sub=sub, li=li, nw=nw: e.tensor_tensor(out=gm[:, li, sub], in0=gm[:, li, sub], in1=bc(nw[:, li, :], [128, 8, 2], 2), op=ALU.mult), r=[gm, nw], w=[gm])

    def ci_of_tile(t):
        off = t * 512
        for (o, L, kd) in SEQS:
            if o <= off < o + L:
                return 0 if kd == 'p' else 1
        raise ValueError

    sqb = k.sb("sqb", [128, 8, 512], BF16)
    rsb = k.sb("rsb", [128, 512], F32)
    inb = [k.sb("inb%d" % i, [128, 22 * 512], BF16) for i in range(2)]
    hb = [k.view(inb[i], inb[i][:, 0:4096].rearrange("p (c n) -> p c n", c=8)) for i in range(2)]
    tmpf = k.sb("tmpf", [128, 512], F32)

    def norm_stage(li, sub, final=False):
        def nload(t):
            k.ld(xbuf[t % 2], xbuf[t % 2][:], X, X.t[:, t * 512:(t + 1) * 512].rearrange("(c p) t -> p c t", p=128))
        nload(0)
        for t in range(NT):
            ci = ci_of_tile(t)
            xb = xbuf[t % 2]
            sl = slice(t * 512, (t + 1) * 512)
            if t + 1 < NT:
                nload(t + 1)
            k.A(lambda e, xb=xb: e.activation(out=sqb[:], in_=xb[:], func=AF.Square), r=[xb], w=[sqb])
            ps = k.psum()
            for c in range(8):
                k.M(lambda e, c=c, ps=ps: e.matmul(ps[:], cmb[:, ONES, :], sqb[:, c, :], start=(c == 0), stop=(c == 7)), r=[cmb, sqb], w=[ps])
            k.A(lambda e, ps=ps: e.activation(out=rsb[:], in_=ps[:], func=AF.Sqrt, bias=EPS, scale=1.0 / 1024), r=[ps], w=[rsb])
            k.V(lambda e: e.reciprocal(out=rsb[:], in_=rsb[:]), r=[rsb], w=[rsb])
            if final:
                for c in range(8):
                    k.V(lambda e, c=c, xb=xb: e.scalar_tensor_tensor(out=xb[:, c, :], in0=xb[:, c, :], scalar=fnw_s[:, c:c + 1], in1=rsb[:], op0=ALU.mult, op1=ALU.mult),
                        r=[xb, fnw_s, rsb], w=[xb])
                k.stq(yT, yT.t[:, sl].rearrange("(c p) t -> p c t", p=128), xb, xb[:])
            else:
                h = hb[t % 2]
                shq = 0 if sub == 0 else 3
                for c in range(8):
                    k.V(lambda e, c=c, xb=xb: e.scalar_tensor_tensor(out=tmpf[:], in0=xb[:, c, :], scalar=gm[:, li, sub, c, ci:ci + 1], in1=rsb[:], op0=ALU.mult, op1=ALU.mult),
                        r=[xb, gm, rsb], w=[tmpf])
                    k.A(lambda e, c=c, h=h: e.activation(out=h[:, c, :], in_=tmpf[:], func=AF.Identity, bias=mod[:, li, shq * 8 + c, ci:ci + 1], scale=1.0),
                        r=[tmpf, mod], w=[h])
                k.stq(H, H.t[:, sl].rearrange("(c p) t -> p c t", p=128), h, h[:])

    ini = [0]

    def gemm(src, src_tm, kc, W, Wap, ncols, out_mode, epi, gw=None, pre=None, post=None, wsplit=False):
        if wsplit:
            gw = ncols
        if gw is None:
            gw = max(128, (12288 // kc) // 128 * 128)
            gw = min(gw, 2048)
        seq = [(g0, min(gw, ncols - g0), t) for g0 in range(0, ncols, gw) for t in range(NT)]
        loaded = {}

        def emit_load(i, wb_dep):
            g0, gn, t = seq[i]
            ib = inb[ini[0] % 2]; ini[0] += 1
            iv = ib[:, 0:kc * 512].rearrange("p (c n) -> p c n", c=kc)
            sl = slice(t * 512, (t + 1) * 512)
            if src_tm:
                key = k._key('ld:' + ib.name)
                for c in range(kc):
                    k.P.emit('sp', lambda e: e.dma_start_transpose(out=iv[:, c, :], in_=src.t[sl, c * 128:(c + 1) * 128]), [src.b, k.cast_tok] + ([wb_dep.b] if wb_dep is not None else []), [ib.b, k.tr_tok], dma_key=key)
            else:
                k.ld(ib, iv[:], src, src.t[0:kc * 128, sl].rearrange("(c p) t -> p c t", p=128))
            loaded[i] = (ib, iv)
        wb = wv = None
        emit_load(0, None)
        groups_ = [(g0, min(gw, ncols - g0)) for g0 in range(0, ncols, gw)]
        wq = {}

        def wload(gi):
            g0_, gn_ = groups_[gi]
            if wsplit:
                h_ = kc // 2
                a_ = load_w(W, Wap[0:h_ * 128, g0_:g0_ + gn_], h_, gn_, after=(inb if src_tm else ()))
                b_ = load_w(W, Wap[h_ * 128:kc * 128, g0_:g0_ + gn_], h_, gn_, after=(inb if src_tm else ()))
                wq[gi] = (a_, b_)
            else:
                wq[gi] = load_w(W, Wap[:, g0_:g0_ + gn_], kc, gn_, after=(inb if src_tm else ()))
        wload(0)
        for i, (g0, gn, t) in enumerate(seq):
            if t == 0:
                gi = g0 // gw
                if gi not in wq:
                    wload(gi)
                if wsplit:
                    (wbA, wvA), (wbB, wvB) = wq.pop(gi)
                    wb = wbB
                else:
                    wb, wv = wq.pop(gi)
                if gi + 1 < len(groups_) and not src_tm:
                    wload(gi + 1)
            elif t == 1 and src_tm and (g0 // gw) + 1 < len(groups_) and False:
                pass
            if t == 0 and src_tm and (g0 // gw) + 1 < len(groups_):
                pass
            if i + 1 < len(seq):
                emit_load(i + 1, wb)
            ib, iv = loaded.pop(i)
            ncb = gn // 128
            if pre is not None:
                pre(t, g0 // 128, g0 // 128 + ncb)
            if out_mode == 'fm':
                for cb in range(ncb):
                    ps = k.psum()
                    for c in range(kc):
                        if wsplit:
                            wb_, wv_, c_ = (wbA, wvA, c) if c < kc // 2 else (wbB, wvB, c - kc // 2)
                        else:
                            wb_, wv_, c_ = wb, wv, c
                        k.M(lambda e: e.matmul(ps[:], wv_[:, c_, cb * 128:(cb + 1) * 128], iv[:, c, :], start=(c == 0), stop=(c == kc - 1)), r=[wb_, ib], w=[ps])
                    epi(ps, (g0 // 128) + cb, t, None, cb == ncb - 1)
            else:
                for sbk in range(4):
                    for p0 in range(0, gn, 512):
                        pw = min(512, gn - p0)
                        ps = k.psum()
                        for c in range(kc):
                            k.M(lambda e: e.matmul(ps[:, 0:pw], iv[:, c, sbk * 128:(sbk + 1) * 128], wv[:, c, p0:p0 + pw], start=(c == 0), stop=(c == kc - 1)), r=[wb, ib], w=[ps])
                        epi(ps, g0 + p0, t, sbk, False)
            if post is not None:
                post(t, g0 // 128, g0 // 128 + ncb)

    stg = [k.sb("stg%d" % i, [128, 512], BF16) for i in range(3)]
    sti = [0]

    stgb = [k.sb("stgb%d" % i, [128, 4, 512], BF16) for i in range(2)]
    stgi = [0]

    def epi_store_fm(dst, row0, func=None, scale=1.0):
        state = {'n': 0, 'cb0': None, 'buf': None}

        def epi(ps, cb, t, sub, last):
            if state['n'] == 0:
                state['buf'] = stgb[stgi[0] % 2]; stgi[0] += 1
                state['cb0'] = cb
            s_ = state['buf']; n = state['n']
            if func is None:
                if cb % 2 == 0:
                    k.V(lambda e: e.tensor_copy(out=s_[:, n, :], in_=ps[:]), r=[ps], w=[s_])
                else:
                    k.A(lambda e: e.activation(out=s_[:, n, :], in_=ps[:], func=AF.Copy), r=[ps], w=[s_])
            else:
                k.A(lambda e: e.activation(out=s_[:, n, :], in_=ps[:], func=func, scale=scale), r=[ps], w=[s_])
            state['n'] += 1
            if state['n'] == 4 or last:
                n_ = state['n']; c0 = state['cb0']
                k.stq(dst, dst.t[row0 + c0 * 128:row0 + (c0 + n_) * 128, t * 512:(t + 1) * 512].rearrange("(c p) t -> p c t", p=128), s_, s_[:, 0:n_, :])
                state['n'] = 0
        return epi

    def epi_store_tm(dst, col0, func=None):
        def epi(ps, c0, t, sub, last):
            s = stg[sti[0] % 3]; sti[0] += 1
            if func is None:
                k.V(lambda e, s=s, ps=ps: e.tensor_copy(out=s[:], in_=ps[:]), r=[ps], w=[s])
            else:
                k.A(lambda e, s=s, ps=ps: e.activation(out=s[:], in_=ps[:], func=func), r=[ps], w=[s])
            r0 = t * 512 + sub * 128
            k.stq(dst, dst.t[r0:r0 + 128, col0 + c0:col0 + c0 + 512], s, s[:])
        return epi

    def resid_hooks(li, gq):
        def pre(t, c_lo, c_hi):
            xb = xbuf[t % 2]
            k.ld(xb, xb[:, c_lo:c_hi, :], X, X.t[c_lo * 128:c_hi * 128, t * 512:(t + 1) * 512].rearrange("(c p) t -> p c t", p=128))

        def epi(ps, cb, t, sub, last):
            ci = ci_of_tile(t)
            xb = xbuf[t % 2]
            k.V(lambda e: e.scalar_tensor_tensor(out=xb[:, cb, :], in0=ps[:], scalar=mod[:, li, gq * 8 + cb, ci:ci + 1], in1=xb[:, cb, :], op0=ALU.mult, op1=ALU.add),
                r=[ps, mod, xb], w=[xb])

        def post(t, c_lo, c_hi):
            xb = xbuf[t % 2]
            k.stq(X, X.t[c_lo * 128:c_hi * 128, t * 512:(t + 1) * 512].rearrange("(c p) t -> p c t", p=128), xb, xb[:, c_lo:c_hi, :])
        return dict(epi=epi, pre=pre, post=post)

    cvin = [k.sb("cvin%d" % i, [128, 10, 66], BF16) for i in range(2)]
    dgb = [k.sb("dgb%d" % i, [128, 9, 128], BF16) for i in range(2)]
    cvout = [k.sb("cvout%d" % i, [128, 512], BF16) for i in range(2)]
    cvi = [0]

    def seq_of(tok):
        for (o, L, kd) in SEQS:
            if o <= tok < o + L:
                return (o, L, kd)
        raise ValueError

    def conv_stage(src, dst, nblk, wfn, bfn, grid_for_sample, wdeps, l2blocks=(), qscale_blocks=()):
        items = [(blk, t) for blk in range(nblk) for t in range(NT)]
        info = {}

        def conv_load(i):
            blk, t = items[i]
            rs_ = slice(blk * 128, (blk + 1) * 128)
            t0 = t * 512
            (o, L, kd) = seq_of(t0)
            ci_ = cvin[cvi[0] % 2]; co = cvout[cvi[0] % 2]; cvi[0] += 1
            grid = grid_for_sample and kd == 's'
            k.G(lambda e: e.memset(ci_[:], 0.0), w=[ci_])
            pieces = []
            if grid:
                ROWS_ = L // 64
                r0 = (t0 - o) // 64
                lo = max(0, r0 - 1); hi = min(ROWS_, r0 + 9)
                k.ld(ci_, ci_[:, lo - (r0 - 1):hi - (r0 - 1), 1:65], src, src.t[rs_, o + lo * 64:o + hi * 64].rearrange("p (r c) -> p r c", c=64))
            else:
                flat = ci_[:].rearrange("p r c -> p (r c)")
                pos = t0
                base = 0
                while pos < t0 + 512:
                    (o2, L2, _) = seq_of(pos)
                    pe_ = min(t0 + 512, o2 + L2)
                    pa, pb = pos, pe_
                    lo = max(o2, pa - 1); hi = min(o2 + L2, pb + 1)
                    k.ld(ci_, flat[:, base + (lo - (pa - 1)):base + (hi - (pa - 1))], src, src.t[rs_, lo:hi])
                    pieces.append((pa, pb, base)); base += (pb - pa) + 2
                    pos = pe_
            info[i] = (ci_, co, grid, pieces)

        conv_load(0)
        for i, (blk, t) in enumerate(items):
            rs_ = slice(blk * 128, (blk + 1) * 128)
            t0 = t * 512
            if t == 0:
                dg = dgb[blk % 2]
                for tap in (range(9) if grid_for_sample else range(3, 6)):
                    k.V(lambda e: e.tensor_scalar(out=dg[:, tap, :], in0=cmb[:, ID, :], scalar1=wfn(blk, tap), scalar2=None, op0=ALU.mult), r=[cmb] + wdeps, w=[dg])
            if i + 1 < len(items):
                conv_load(i + 1)
            ci_, co, grid, pieces = info.pop(i)
            ps = k.psum()
            if grid:
                pv = ps[:].rearrange("p (r c) -> p r c", c=64)
                ti = 0
                for dy in range(3):
                    for dx in range(3):
                        k.M(lambda e: e.matmul(pv, dg[:, dy * 3 + dx, :], ci_[:, dy:dy + 8, dx:dx + 64], start=(ti == 0), stop=(ti == 8)), r=[dg, ci_], w=[ps])
                        ti += 1
            else:
                flat = ci_[:].rearrange("p r c -> p (r c)")
                for (pa, pb, base) in pieces:
                    n_ = pb - pa
                    for dx in range(3):
                        k.M(lambda e: e.matmul(ps[:, pa - t0:pb - t0], dg[:, 3 + dx, :], flat[:, base + dx:base + dx + n_], start=(dx == 0), stop=(dx == 2)), r=[dg, ci_], w=[ps])
            bap = bfn(blk) if bfn is not None else None
            if bap is not None:
                k.A(lambda e: e.activation(out=co[:], in_=ps[:], func=AF.Silu, bias=bap, scale=1.0), r=[ps] + wdeps, w=[co])
            else:
                k.A(lambda e: e.activation(out=co[:], in_=ps[:], func=AF.Silu), r=[ps], w=[co])
            if blk in l2blocks:
                k.V(lambda e: e.tensor_tensor(out=sqb[:, 0, :], in0=co[:], in1=co[:], op=ALU.mult), r=[co], w=[sqb])
                ps2 = k.psum()
                k.M(lambda e: e.matmul(ps2[:], cmb[:, ONES, :], sqb[:, 0, :], start=True, stop=True), r=[cmb, sqb], w=[ps2])
                k.A(lambda e: e.activation(out=rsb[:], in_=ps2[:], func=AF.Sqrt, bias=1e-6, scale=1.0), r=[ps2], w=[rsb])
                k.V(lambda e: e.reciprocal(out=rsb[:], in_=rsb[:]), r=[rsb], w=[rsb])
                qs = (128.0 ** -0.5) if blk in qscale_blocks else 1.0
                k.V(lambda e: e.scalar_tensor_tensor(out=co[:], in0=co[:], scalar=qs, in1=rsb[:], op0=ALU.mult, op1=ALU.mult), r=[co, rsb], w=[co])
            k.stq(dst, dst.t[rs_, t0:t0 + 512], co, co[:])

    fcw_s = k.sb("fcw_s", [128, 4, 22, 9], F32); fcb_s = k.sb("fcb_s", [128, 4, 22], F32)
    k.ld(fcw_s, fcw_s[:], ffn_cw, ffn_cw.t[:, :, :, :]); k.ld(fcb_s, fcb_s[:], ffn_cb, ffn_cb.t[:, :, :])
    avb = [k.view(wbuf[i], wbuf[i][:, 8192:12288].rearrange("p (c n) -> p c n", c=8), group='ffnv') for i in range(2)]
    aai = [0]

    def ffn(li):
        norm_stage(li, 1)
        gemm(H, False, 8, ffn_in_w, ffn_in_w.t[li, :, 0:2816], 2816, 'fm', epi_store_fm(S1, 0))
        conv_stage(S1, S2, 22, lambda blk, tap: fcw_s[:, li, blk, tap:tap + 1], lambda blk: fcb_s[:, li, blk:blk + 1], True, [fcw_s, fcb_s])

        vstate = {}

        def pre_v(t, c_lo, c_hi):
            a = avb[aai[0] % 2]; aai[0] += 1
            k.ld(a, a[:, 0:c_hi - c_lo, :], S2, S2.t[c_lo * 128:c_hi * 128, t * 512:(t + 1) * 512].rearrange("(c p) t -> p c t", p=128))
            vstate['a'] = a; vstate['lo'] = c_lo

        def epi_v(ps, cb, t, sub, last):
            a = vstate['a']; n = cb - vstate['lo']
            k.V(lambda e: e.tensor_tensor(out=a[:, n, :], in0=ps[:], in1=a[:, n, :], op=ALU.mult), r=[ps, a], w=[a])

        def post_v(t, c_lo, c_hi):
            a = vstate['a']
            k.stq(S5, S5.t[c_lo * 128:c_hi * 128, t * 512:(t + 1) * 512].rearrange("(c p) t -> p c t", p=128), a, a[:, 0:c_hi - c_lo, :])
        k.enter('ffnv')
        gemm(H, False, 8, ffn_in_w, ffn_in_w.t[li, :, 2816:5632], 2816, 'fm', epi_v, gw=1024, pre=pre_v, post=post_v)
        k.leave('ffnv')
        gemm(S5, False, 22, ffn_out_w, ffn_out_w.t[li, :, :], 1024, 'fm', gw=512, **resid_hooks(li, 5))

    scw_s = k.sb("scw_s", [128, 2, 32, 3], F32); scb_s = k.sb("scb_s", [128, 2, 32], F32)
    k.ld(scw_s, scw_s[:], ssm_cw, ssm_cw.t[:, :, :, :]); k.ld(scb_s, scb_s[:], ssm_cb, ssm_cb.t[:, :, :])
    dt_sb = k.view(xbuf[0], xbuf[0][:].rearrange("p c n -> p (c n)")[:, 0:NCH * 64].rearrange("p (c n) -> p c n", n=64), group='ssdx')
    la_sb = k.view(xbuf[1], xbuf[1][:].rearrange("p c n -> p (c n)")[:, 0:NCH * 64].rearrange("p (c n) -> p c n", n=64), group='ssdx')
    lab_sb = k.sb("lab_sb", [128, 1, 64], BF16)
    bc64 = k.sb("bc64", [128, 64], F32); nega = k.sb("nega", [128, 64], F32); dsk = k.sb("dsk", [128, 32], F32)
    nwb = k.sb("nwb", [128, 2048], BF16)
    W0, W1 = wbuf[0], wbuf[1]
    Ebuf = k.view(W0, W0[:, 0:4096], group='ssd'); DTb = k.view(W0, W0[:, 4096:8192], group='ssd')
    xv = k.view(W0, W0[:, 8192:10240], group='ssd'); xw = k.view(W0, W0[:, 10240:12288], group='ssd')
    Sbf = k.view(W1, W1[:, 0:2048], group='ssd'); STb = k.view(W1, W1[:, 2048:3072], group='ssd')
    yo = [k.view(W1, W1[:, 3072 + i * 2048:3072 + (i + 1) * 2048], group='ssd') for i in range(2)]
    xs_tm = [k.view(inb[i], inb[i][:, 0:2048], group='ssd') for i in range(2)]
    B_tm = [k.view(inb[i], inb[i][:, 2048:3072], group='ssd') for i in range(2)]
    BC_fm = [k.view(inb[i], inb[i][:, 3072:5120].rearrange("p (g t) -> p g t", t=128), group='ssd') for i in range(2)]
    yfb = [k.view(inb[i], inb[i][:, 5120:7168], group='ssd') for i in range(2)]
    zsb = [k.view(inb[i], inb[i][:, 7168:9216], group='ssd') for i in range(2)]
    ecs = k.sb("ecs", [128, 64], F32)
    Sst = k.sb("Sst", [128, 2048], F32)
    yc = k.sb("yc", [128, 2048], F32)
    ytmp = tmpf
    ss8 = k.sb("ss8", [128, 16], F32)

    def chunk_order(L, d):
        n = L // 128
        return list(range(n)) if d == 0 else list(range(n - 1, -1, -1))

    def ssd(li, j):
        norm_stage(li, 0)
        W = ssm_in_w
        gemm(H, False, 8, W, W.t[j, :, 0:2048], 2048, 'tm', epi_store_tm(S3, 0, AF.Silu))
        gemm(H, False, 8, W, W.t[j, :, 2048:6144], 4096, 'fm', epi_store_fm(S1, 0))
        if DBG_STOP == 'gemm':
            return
        conv_stage(S1, S2, 32, lambda blk, tap: scw_s[:, j, blk, tap - 3:tap - 2], lambda blk: scb_s[:, j, blk:blk + 1], False, [scw_s, scb_s])
        if DBG_STOP == 'conv':
            return
        k.ld(bc64, bc64[:], ssm_dtb, ssm_dtb.t[j].partition_broadcast(128))
        k.ld(nega, nega[:], ssm_alog, ssm_alog.t[j].partition_broadcast(128))
        k.A(lambda e: e.activation(out=nega[:], in_=nega[:], func=AF.Exp), r=[nega], w=[nega])
        k.V(lambda e: e.tensor_scalar(out=nega[:], in0=nega[:], scalar1=-1.0, scalar2=None, op0=ALU.mult), r=[nega], w=[nega])
        k.ld(dsk, dsk[:], ssm_d, ssm_d.t[j].partition_broadcast(128))
        k.ld(nwb, nwb[:], ssm_nw, ssm_nw.t[j].partition_broadcast(128), q='pool')

        def epi_dt(ps, c0, t, sub, last):
            ch = t * 4 + sub
            k.V(lambda e, ps=ps, ch=ch: e.tensor_tensor(out=dt_sb[:, ch, :], in0=ps[:, 0:64], in1=bc64[:], op=ALU.add), r=[ps, bc64], w=[dt_sb])
            k.A(lambda e, ch=ch: e.activation(out=dt_sb[:, ch, :], in_=dt_sb[:, ch, :], func=AF.Exp), r=[dt_sb], w=[dt_sb])
            k.A(lambda e, ch=ch: e.activation(out=dt_sb[:, ch, :], in_=dt_sb[:, ch, :], func=AF.Ln, bias=1.0, scale=1.0), r=[dt_sb], w=[dt_sb])
            k.V(lambda e, ch=ch: e.tensor_tensor(out=la_sb[:, ch, :], in0=dt_sb[:, ch, :], in1=nega[:], op=ALU.mult), r=[dt_sb, nega], w=[la_sb])
        k.enter('ssdx')
        gemm(H, False, 8, W, W.t[j, :, 6144:6208], 64, 'tm', epi_dt)
        if DBG_STOP == 'dt':
            return
        k.enter('ssd')
        pidx = 0
        for (o, L, kd) in (SEQS[:1] if DBG_STOP in ('scan1', 'scan1b') else SEQS):
            for d in ((0,) if DBG_STOP == 'scan1' else (0, 1)):
                if kd == 'p':
                    k.V(lambda e: e.memset(Sst[:], 0.0), w=[Sst])
                    k.V(lambda e: e.memset(Sbf[:], 0.0), w=[Sbf])
                else:
                    k.ld(Sst, Sst[:], st_ssm, st_ssm.t[j, d])
                    k.A(lambda e: e.activation(out=Sbf[:], in_=Sst[:], func=AF.Copy), r=[Sst], w=[Sbf])
                ie = 127 if d == 0 else 0
                order = chunk_order(L, d)

                def ssd_loads(n_, c):
                    t0 = o + c * 128
                    tsl = slice(t0, t0 + 128)
                    xs = xs_tm[n_ % 2]; Bt = B_tm[n_ % 2]; BC = BC_fm[n_ % 2]
                    for q4 in range(4):
                        k.ld(xs, xs[:, q4 * 512:(q4 + 1) * 512], S2, S2.t[q4 * 512:(q4 + 1) * 512, tsl], tr=True)
                    for q4 in range(2):
                        k.ld(Bt, Bt[:, q4 * 512:(q4 + 1) * 512], S2, S2.t[2048 + q4 * 512:2048 + (q4 + 1) * 512, tsl], tr=True)
                    k.ld(BC, BC[:], S2, S2.t[2048:4096, tsl].rearrange("(g n) t -> n g t", n=128))
                    if d == 1:
                        k.ld(yfb[n_ % 2], yfb[n_ % 2][:], S4, S4.t[tsl, :])
                        k.ld(zsb[n_ % 2], zsb[n_ % 2][:], S3, S3.t[tsl, :])
                ssd_loads(0, order[0])
                for n_, c in enumerate(order):
                    if n_ + 1 < len(order):
                        ssd_loads(n_ + 1, order[n_ + 1])
                    t0 = o + c * 128; ch = t0 // 128
                    tsl = slice(t0, t0 + 128)
                    xs = xs_tm[n_ % 2]; Bt = B_tm[n_ % 2]; BC = BC_fm[n_ % 2]
                    la = la_sb[:, ch, d * 32:(d + 1) * 32]; lab = lab_sb[:, 0, 0:32]; dtc = dt_sb[:, ch, d * 32:(d + 1) * 32]
                    k.V(lambda e: e.tensor_copy(out=lab_sb[:, 0, 0:32], in_=la), r=[la_sb], w=[lab_sb])
                    E3 = Ebuf[:].rearrange("p (h i) -> p h i", i=128)
                    D3 = DTb[:].rearrange("p (h i) -> p h i", i=128)
                    k.G(lambda e, la=la: e.tensor_tensor(out=E3, in0=bc(la, [128, 32, 128], 2), in1=bc(cmb[:, U_[d], :], [128, 32, 128], 1), op=ALU.mult), r=[la_sb, cmb], w=[Ebuf])
                    for q8 in range(8):
                        ps = k.psum()
                        k.M(lambda e, ps=ps, q8=q8: e.matmul(ps[:], cmb[:, V_[d], :], Ebuf[:, q8 * 512:(q8 + 1) * 512], start=True, stop=True), r=[cmb, Ebuf], w=[ps])
                        k.A(lambda e, ps=ps, q8=q8: e.activation(out=DTb[:, q8 * 512:(q8 + 1) * 512], in_=ps[:], func=AF.Exp), r=[ps], w=[DTb])
                    for half in range(2):
                        ps = k.psum()
                        for g4 in range(4):
                            g = half * 4 + g4
                            k.M(lambda e, ps=ps, g=g, g4=g4, BC=BC: e.matmul(ps[:, g4 * 128:(g4 + 1) * 128], BC[:, g, :], BC[:, 8 + g, :], start=True, stop=True), r=[BC], w=[ps])
                        k.V(lambda e, ps=ps, half=half: e.tensor_tensor(out=STb[:, half * 512:(half + 1) * 512].rearrange("p (g i) -> p g i", i=128), in0=ps[:].rearrange("p (g i) -> p g i", i=128),
                                                                        in1=bc(cmb[:, U_[d], :], [128, 4, 128], 1), op=ALU.mult), r=[ps, cmb], w=[STb])
                    k.V(lambda e: e.tensor_copy(out=wdec[:], in_=DTb[:].rearrange("p (h i) -> p h i", i=128)[:, :, ie]), r=[DTb], w=[wdec])
                    k.V(lambda e: e.tensor_tensor(out=DTb[:].rearrange("p (g r i) -> p g r i", r=4, i=128), in0=DTb[:].rearrange("p (g r i) -> p g r i", r=4, i=128),
                                                   in1=bc(STb[:].rearrange("p (g i) -> p g i", i=128), [128, 8, 4, 128], 2), op=ALU.mult), r=[DTb, STb], w=[DTb])
                    k.G(lambda e, xs=xs, dtc=dtc: e.tensor_tensor(out=xv[:].rearrange("p (h q) -> p h q", q=64), in0=xs[:].rearrange("p (h q) -> p h q", q=64),
                                                                   in1=bc(dtc, [128, 32, 64], 2), op=ALU.mult), r=[xs, dt_sb], w=[xv])
                    k.G(lambda e: e.tensor_tensor(out=xw[:].rearrange("p (h q) -> p h q", q=64), in0=xv[:].rearrange("p (h q) -> p h q", q=64),
                                                   in1=bc(wdec[:], [128, 32, 64], 2), op=ALU.mult), r=[xv, wdec], w=[xw])
                    ps = k.psum()
                    k.M(lambda e, ps=ps, lab=lab: e.matmul(ps[:, 0:32], cmb[:, U_[d], :], lab, start=True, stop=True), r=[cmb, lab_sb], w=[ps])
                    k.M(lambda e, ps=ps, lab=lab: e.matmul(ps[:, 32:64], cmb[:, ONES, :], lab, start=True, stop=True), r=[cmb, lab_sb], w=[ps])
                    k.A(lambda e, ps=ps: e.activation(out=ecs[:], in_=ps[:, 0:64], func=AF.Exp), r=[ps], w=[ecs])
                    for pr in range(4):
                        psA = k.psum(); psB = k.psum()
                        for gg in range(2):
                            g = pr * 2 + gg
                            for r_ in range(4):
                                h = g * 4 + r_
                                k.M(lambda e, psA=psA, h=h, gg=gg, r_=r_: e.matmul(psA[:, gg * 256 + r_ * 64:gg * 256 + (r_ + 1) * 64], DTb[:, h * 128:(h + 1) * 128], xv[:, h * 64:(h + 1) * 64],
                                                                                       start=True, stop=True), r=[DTb, xv], w=[psA])
                            k.M(lambda e, psB=psB, g=g, gg=gg, BC=BC: e.matmul(psB[:, gg * 256:(gg + 1) * 256], BC[:, 8 + g, :], Sbf[:, g * 256:(g + 1) * 256], start=True, stop=True), r=[BC, Sbf], w=[psB])
                        k.V(lambda e, psB=psB, pr=pr: e.tensor_tensor(out=ytmp[:].rearrange("p (h q) -> p h q", q=64), in0=psB[:].rearrange("p (h q) -> p h q", q=64),
                                                                       in1=bc(ecs[:, pr * 8:(pr + 1) * 8], [128, 8, 64], 2), op=ALU.mult), r=[psB, ecs], w=[ytmp])
                        k.V(lambda e, psA=psA, pr=pr: e.tensor_tensor(out=yc[:, pr * 512:(pr + 1) * 512], in0=psA[:], in1=ytmp[:], op=ALU.add), r=[psA, ytmp], w=[yc])
                    k.G(lambda e: e.tensor_tensor(out=Sst[:].rearrange("p (h q) -> p h q", q=64), in0=Sst[:].rearrange("p (h q) -> p h q", q=64),
                                                   in1=bc(ecs[:, 32:64], [128, 32, 64], 2), op=ALU.mult), r=[Sst, ecs], w=[Sst])
                    for pr in range(4):
                        ps = k.psum()
                        for gg in range(2):
                            g = pr * 2 + gg
                            k.M(lambda e, ps=ps, g=g, gg=gg, Bt=Bt: e.matmul(ps[:, gg * 256:(gg + 1) * 256], Bt[:, g * 128:(g + 1) * 128], xw[:, g * 256:(g + 1) * 256], start=True, stop=True), r=[Bt, xw], w=[ps])
                        k.V(lambda e, ps=ps, pr=pr: e.tensor_tensor(out=Sst[:, pr * 512:(pr + 1) * 512], in0=ps[:], in1=Sst[:, pr * 512:(pr + 1) * 512], op=ALU.add), r=[ps, Sst], w=[Sst])
                    k.A(lambda e: e.activation(out=Sbf[:], in_=Sst[:], func=AF.Copy), r=[Sst], w=[Sbf])
                    if d == 0:
                        y_ = yo[n_ % 2]
                        k.A(lambda e, y_=y_: e.activation(out=y_[:], in_=yc[:], func=AF.Copy), r=[yc], w=[y_])
                        k.stq(S4, S4.t[tsl, :], y_, y_[:])
                    else:
                        yf_ = yfb[n_ % 2]; zs_ = zsb[n_ % 2]; y_ = yo[n_ % 2]
                        k.V(lambda e, yf_=yf_: e.tensor_tensor(out=yc[:], in0=yc[:], in1=yf_[:], op=ALU.add), r=[yc, yf_], w=[yc])
                        k.G(lambda e, xs=xs: e.tensor_tensor(out=xv[:].rearrange("p (h q) -> p h q", q=64), in0=xs[:].rearrange("p (h q) -> p h q", q=64),
                                                              in1=bc(dsk[:], [128, 32, 64], 2), op=ALU.mult), r=[xs, dsk], w=[xv])
                        k.V(lambda e: e.tensor_tensor(out=yc[:], in0=yc[:], in1=xv[:], op=ALU.add), r=[yc, xv], w=[yc])
                        k.V(lambda e, zs_=zs_: e.tensor_tensor(out=yc[:], in0=yc[:], in1=zs_[:], op=ALU.mult), r=[yc, zs_], w=[yc])
                        k.G(lambda e: e.tensor_tensor(out=Ebuf[:, 0:2048], in0=yc[:], in1=yc[:], op=ALU.mult), r=[yc], w=[Ebuf])
                        k.V(lambda e: e.tensor_reduce(out=ss8[:, 0:8], in_=Ebuf[:, 0:2048].rearrange("p (g q) -> p g q", q=256), axis=AX.X, op=ALU.add), r=[Ebuf], w=[ss8])
                        k.A(lambda e: e.activation(out=ss8[:, 0:8], in_=ss8[:, 0:8], func=AF.Sqrt, bias=EPS, scale=1.0 / 256), r=[ss8], w=[ss8])
                        k.V(lambda e: e.reciprocal(out=ss8[:, 0:8], in_=ss8[:, 0:8]), r=[ss8], w=[ss8])
                        k.V(lambda e: e.tensor_tensor(out=yc[:].rearrange("p (g q) -> p g q", q=256), in0=yc[:].rearrange("p (g q) -> p g q", q=256),
                                                       in1=bc(ss8[:, 0:8], [128, 8, 256], 2), op=ALU.mult), r=[yc, ss8], w=[yc])
                        k.V(lambda e, y_=y_: e.tensor_tensor(out=y_[:], in0=yc[:], in1=nwb[:], op=ALU.mult), r=[yc, nwb], w=[y_])
                        k.stq(S4, S4.t[tsl, :], y_, y_[:])
                if kd == 'p':
                    k.stq(ns_ssm, ns_ssm.t[pidx, j, d], Sst, Sst[:])
            if kd == 'p':
                pidx += 1
        k.leave('ssd'); k.leave('ssdx')
        if DBG_STOP in ('scan1', 'scan1b', 'scan'):
            return
        gemm(S4, True, 16, ssm_out_w, ssm_out_w.t[j, :, :], 1024, 'fm', wsplit=True, **resid_hooks(li, 2))

    HAS_GLA = any(kd_ == 'gla' for _, kd_, _ in LAYERS)
    lrT = k.sb("lrT", [64, 512], BF16)
    gnw_s = k.sb("gnw_s", [128, 1, 2], F32)
    k.ld(gnw_s, gnw_s[:], gla_nw, gla_nw.t[:, :, :])
    w2s = k.sb("w2s", [64, 1024 if HAS_GLA else 8], BF16)
    gbb = k.sb("gbb", [128, 1024 if HAS_GLA else 8], BF16)
    gq_fm = [k.view(inb[i], inb[i][:, 0:512].rearrange("p (h t) -> p h t", t=128), group='gla') for i in range(2)]
    gk_fm = [k.view(inb[i], inb[i][:, 512:1024].rearrange("p (h t) -> p h t", t=128), group='gla') for i in range(2)]
    gk_tm = [k.view(inb[i], inb[i][:, 1024:1536], group='gla') for i in range(2)]
    gv_tm = [k.view(inb[i], inb[i][:, 1536:2560], group='gla') for i in range(2)]
    gsp_tm = [k.view(inb[i], inb[i][:, 2560:3072], group='gla') for i in range(2)]
    gof = [k.view(inb[i], inb[i][:, 3072:4096].rearrange("p (b t) -> p b t", t=128), group='gla') for i in range(2)]
    grs = [k.view(inb[i], inb[i][:, 4096:5120].rearrange("p (b t) -> p b t", t=128), group='gla') for i in range(2)]
    gqg = k.view(W0, W0[:, 0:512].rearrange("p (h t) -> p h t", t=128), group='gla')
    gkg = k.view(W0, W0[:, 512:1024].rearrange("p (h t) -> p h t", t=128), group='gla')
    gAT = k.view(W0, W0[:, 1024:1536].rearrange("p (h t) -> p h t", t=128), group='gla')
    gkw = k.view(W0, W0[:, 1536:2048], group='gla')
    gsq = k.view(W0, W0[:, 2048:3072].rearrange("p (b t) -> p b t", t=128), group='gla')
    gon = [k.view(W1, W1[:, 3072 + i * 1024:3072 + (i + 1) * 1024].rearrange("p (b t) -> p b t", t=128), group='gla') for i in range(2)]
    gsps = [k.sb("gsps%d" % i, [128, 1024 if HAS_GLA else 8], BF16) for i in range(2)]
    Sg = k.view(Sst, Sst[:, 0:1024].rearrange("p (h e) -> p h e", e=256), group='gla')
    Sgb = k.view(W1, W1[:, 0:1024].rearrange("p (h e) -> p h e", e=256), group='gla')
    gob = k.view(yc, yc[:, 0:1024].rearrange("p (b t) -> p b t", t=128), group='gla')

    def gla(li, j):
        norm_stage(li, 0)
        W = gla_in_w
        gemm(H, False, 8, W, W.t[j, :, 0:512], 512, 'fm', epi_store_fm(S1, 0, AF.Copy, 128.0 ** -0.5))
        gemm(H, False, 8, W, W.t[j, :, 512:1024], 512, 'fm', epi_store_fm(S1, 512))
        gemm(H, False, 8, W, W.t[j, :, 1024:2048], 1024, 'tm', epi_store_tm(S3, 0))
        gemm(H, False, 8, W, W.t[j, :, 2048:3072], 1024, 'fm', epi_store_fm(S6, 0, AF.Silu))

        k.ld(w2s, w2s[:], gla_w2, gla_w2.t[j], q='pool')
        k.ld(gbb, gbb[:], gla_gb, gla_gb.t[j].partition_broadcast(128), q='pool')

        def epi_lr(ps, cb, t, sub, last):
            k.V(lambda e: e.tensor_copy(out=lrT[:], in_=ps[0:64, :]), r=[ps], w=[lrT])
            for sbk in range(4):
                ch = t * 4 + sbk
                tsl = slice(ch * 128, (ch + 1) * 128)
                sp_ = gsps[ch % 2]
                for hf in range(2):
                    ps2 = k.psum()
                    k.M(lambda e: e.matmul(ps2[:], lrT[:, sbk * 128:(sbk + 1) * 128], w2s[:, hf * 512:(hf + 1) * 512], start=True, stop=True), r=[lrT, w2s], w=[ps2])
                    k.V(lambda e: e.tensor_tensor(out=rsb[:], in0=ps2[:], in1=gbb[:, hf * 512:(hf + 1) * 512], op=ALU.add), r=[ps2, gbb], w=[rsb])
                    k.A(lambda e: e.activation(out=rsb[:], in_=rsb[:], func=AF.Exp, scale=-1.0), r=[rsb], w=[rsb])
                    k.A(lambda e: e.activation(out=sp_[:, hf * 512:(hf + 1) * 512], in_=rsb[:], func=AF.Ln, bias=1.0, scale=1.0), r=[rsb], w=[sp_])
                k.stq(S7, S7.t[tsl, :], sp_, sp_[:])
        gemm(H, False, 8, gla_w1, gla_w1.t[j, :, :], 128, 'fm', epi_lr)
        k.enter('gla')
        pidx = 0
        for (o, L, kd) in SEQS:
            for d in (0, 1):
                if kd == 'p':
                    k.V(lambda e: e.memset(Sg[:], 0.0), w=[Sg])
                    k.V(lambda e: e.memset(Sgb[:], 0.0), w=[Sgb])
                else:
                    k.ld(Sg, Sg[:], st_gla, st_gla.t[j, d].rearrange("h d e -> d h e"))
                    k.A(lambda e: e.activation(out=Sgb[:], in_=Sg[:], func=AF.Copy), r=[Sg], w=[Sgb])
                ie = 127 if d == 0 else 0
                order = chunk_order(L, d)

                def gla_loads(n_, c):
                    t0 = o + c * 128
                    tsl = slice(t0, t0 + 128)
                    q_ = gq_fm[n_ % 2]; k_ = gk_fm[n_ % 2]; kt = gk_tm[n_ % 2]; v_ = gv_tm[n_ % 2]; sp_ = gsp_tm[n_ % 2]
                    k.ld(q_, q_[:], S1, S1.t[0:512, tsl].rearrange("(h d) t -> d h t", d=128))
                    k.ld(k_, k_[:], S1, S1.t[512:1024, tsl].rearrange("(h d) t -> d h t", d=128))
                    k.ld(kt, kt[:], S1, S1.t[512:1024, tsl], tr=True)
                    k.ld(v_, v_[:], S3, S3.t[tsl, 0:1024])
                    k.ld(sp_, sp_[:], S7, S7.t[tsl, d * 512:(d + 1) * 512])
                    if d == 1:
                        k.ld(gof[n_ % 2], gof[n_ % 2][:], S5, S5.t[0:1024, tsl].rearrange("(b p) t -> p b t", p=128))
                        k.ld(grs[n_ % 2], grs[n_ % 2][:], S6, S6.t[0:1024, tsl].rearrange("(b p) t -> p b t", p=128))
                gla_loads(0, order[0])
                for n_, c in enumerate(order):
                    if n_ + 1 < len(order):
                        gla_loads(n_ + 1, order[n_ + 1])
                    t0 = o + c * 128
                    tsl = slice(t0, t0 + 128)
                    q_ = gq_fm[n_ % 2]; k_ = gk_fm[n_ % 2]; kt = gk_tm[n_ % 2]; v_ = gv_tm[n_ % 2]; sp_ = gsp_tm[n_ % 2]
                    ps = k.psum()
                    k.M(lambda e: e.matmul(ps[:], cmb[:, VG_[d], :], sp_[:], start=True, stop=True), r=[cmb, sp_], w=[ps])
                    k.A(lambda e: e.activation(out=rsb[:], in_=ps[:], func=AF.Exp), r=[ps], w=[rsb])
                    k.V(lambda e: e.tensor_tensor(out=gkw[:], in0=kt[:], in1=rsb[:], op=ALU.mult), r=[kt, rsb], w=[gkw])
                    psc = k.psum()
                    for h in range(4):
                        k.M(lambda e: e.matmul(psc[:, h * 128:(h + 1) * 128], sp_[:, h * 128:(h + 1) * 128], cmb[:, UG_[d], :], start=True, stop=True), r=[sp_, cmb], w=[psc])
                    k.A(lambda e: e.activation(out=tmpf[:], in_=psc[:], func=AF.Exp), r=[psc], w=[tmpf])
                    k.V(lambda e: e.tensor_tensor(out=gqg[:], in0=q_[:], in1=tmpf[:].rearrange("p (h t) -> p h t", t=128), op=ALU.mult), r=[q_, tmpf], w=[gqg])
                    k.A(lambda e: e.activation(out=tmpf[:], in_=psc[:], func=AF.Exp, scale=-1.0), r=[psc], w=[tmpf])
                    k.V(lambda e: e.tensor_tensor(out=gkg[:], in0=k_[:], in1=tmpf[:].rearrange("p (h t) -> p h t", t=128), op=ALU.mult), r=[k_, tmpf], w=[gkg])
                    k.A(lambda e: e.activation(out=ecs[:, 0:4], in_=psc[:].rearrange("p (h t) -> p h t", t=128)[:, :, ie], func=AF.Exp), r=[psc], w=[ecs])
                    pss = k.psum()
                    for h in range(4):
                        k.M(lambda e: e.matmul(pss[:, h * 128:(h + 1) * 128], gkg[:, h, :], gqg[:, h, :], start=True, stop=True), r=[gkg, gqg], w=[pss])
                    k.V(lambda e: e.tensor_tensor(out=gAT[:], in0=pss[:].rearrange("p (h t) -> p h t", t=128), in1=bc(cmb[:, U_[d], :], [128, 4, 128], 1), op=ALU.mult), r=[pss, cmb], w=[gAT])
                    pso = [k.psum(), k.psum()]
                    for h in range(4):
                        for eb in range(2):
                            blk = h * 2 + eb
                            po = pso[blk // 4]
                            k.M(lambda e: e.matmul(po[:, (blk % 4) * 128:(blk % 4 + 1) * 128], v_[:, h * 256 + eb * 128:h * 256 + (eb + 1) * 128], gAT[:, h, :], start=True, stop=False), r=[v_, gAT], w=[po])
                            k.M(lambda e: e.matmul(po[:, (blk % 4) * 128:(blk % 4 + 1) * 128], Sgb[:, h, eb * 128:(eb + 1) * 128], gqg[:, h, :], start=False, stop=True), r=[Sgb, gqg], w=[po])
                    for h in range(4):
                        pst = k.psum()
                        k.M(lambda e: e.matmul(pst[:, 0:256], gkw[:, h * 128:(h + 1) * 128], v_[:, h * 256:(h + 1) * 256], start=True, stop=True), r=[gkw, v_], w=[pst])
                        k.V(lambda e: e.scalar_tensor_tensor(out=Sg[:, h, :], in0=Sg[:, h, :], scalar=ecs[:, h:h + 1], in1=pst[:, 0:256], op0=ALU.mult, op1=ALU.add), r=[Sg, ecs, pst], w=[Sg])
                    on_ = gon[n_ % 2]
                    if d == 0:
                        for hf in range(2):
                            k.A(lambda e: e.activation(out=on_[:, hf * 4:(hf + 1) * 4, :], in_=pso[hf][:].rearrange("p (b t) -> p b t", t=128), func=AF.Copy), r=[pso[hf]], w=[on_])
                        k.stq(S5, S5.t[0:1024, tsl].rearrange("(b p) t -> p b t", p=128), on_, on_[:])
                    else:
                        of_ = gof[n_ % 2]; rs_ = grs[n_ % 2]
                        for hf in range(2):
                            k.V(lambda e: e.tensor_tensor(out=gob[:, hf * 4:(hf + 1) * 4, :], in0=pso[hf][:].rearrange("p (b t) -> p b t", t=128), in1=of_[:, hf * 4:(hf + 1) * 4, :], op=ALU.add), r=[pso[hf], of_], w=[gob])
                        k.G(lambda e: e.tensor_tensor(out=gsq[:], in0=gob[:], in1=gob[:], op=ALU.mult), r=[gob], w=[gsq])
                        psn = k.psum()
                        for h in range(4):
                            for eb in range(2):
                                k.M(lambda e: e.matmul(psn[:, h * 128:(h + 1) * 128], cmb[:, ONES, :], gsq[:, h * 2 + eb, :], start=(eb == 0), stop=(eb == 1)), r=[cmb, gsq], w=[psn])
                        k.A(lambda e: e.activation(out=rsb[:], in_=psn[:], func=AF.Sqrt, bias=EPS, scale=1.0 / 256), r=[psn], w=[rsb])
                        k.V(lambda e: e.reciprocal(out=rsb[:], in_=rsb[:]), r=[rsb], w=[rsb])
                        for eb in range(2):
                            gv_ = gob[:].rearrange("p (h b) t -> p h b t", b=2)[:, :, eb, :]
                            k.V(lambda e: e.scalar_tensor_tensor(out=gv_, in0=gv_, scalar=gnw_s[:, j, eb:eb + 1], in1=rsb[:].rearrange("p (h t) -> p h t", t=128), op0=ALU.mult, op1=ALU.mult), r=[gob, gnw_s, rsb], w=[gob])
                        k.V(lambda e: e.tensor_tensor(out=on_[:], in0=gob[:], in1=rs_[:], op=ALU.mult), r=[gob, rs_], w=[on_])
                        k.stq(S5, S5.t[0:1024, tsl].rearrange("(b p) t -> p b t", p=128), on_, on_[:])
                    k.A(lambda e: e.activation(out=Sgb[:], in_=Sg[:], func=AF.Copy), r=[Sg], w=[Sgb])
                if kd == 'p':
                    k.stq(ns_gla, ns_gla.t[pidx, j, d].rearrange("h d e -> d h e"), Sg, Sg[:])
            if kd == 'p':
                pidx += 1
        k.leave('gla')
        gemm(S5, False, 8, gla_out_w, gla_out_w.t[j, :, :], 1024, 'fm', **resid_hooks(li, 2))

    gcw_s = k.sb("gcw_s", [128, 1, 32, 3], F32)
    k.ld(gcw_s, gcw_s[:], gdn_cw, gdn_cw.t[:, :, :, :])
    dnw_s = k.sb("dnw_s", [128, 1, 2], F32)
    k.ld(dnw_s, dnw_s[:], gdn_nw, gdn_nw.t[:, :, :])
    xb0f = xbuf[0][:].rearrange("p c n -> p (c n)")
    d_beta = k.view(xbuf[0], xb0f[:, 0:NCH * 16].rearrange("p (c n) -> p c n", n=16), group='gdnx')
    d_nbeta = k.view(xbuf[0], xb0f[:, NCH * 16:2 * NCH * 16].rearrange("p (c n) -> p c n", n=16), group='gdnx')
    d_lg = k.view(xbuf[0], xb0f[:, 2 * NCH * 16:3 * NCH * 16].rearrange("p (c n) -> p c n", n=16), group='gdnx')
    d_lgb = k.sb("d_lgb", [128, NCH, 16], BF16)
    dbc16 = k.sb("dbc16", [128, 16], F32); dnega = k.sb("dnega", [128, 16], F32)
    h3 = lambda ap: ap.rearrange("p (h t) -> p h t", t=128)
    dq_fm = [k.view(inb[i], h3(inb[i][:, 0:1024]), group='gdn') for i in range(2)]
    dk_fm = [k.view(inb[i], h3(inb[i][:, 1024:2048]), group='gdn') for i in range(2)]
    dk_tm = [k.view(inb[i], h3(inb[i][:, 2048:3072]), group='gdn') for i in range(2)]
    dv_tm = [k.view(inb[i], inb[i][:, 3072:5120].rearrange("p (h e) -> p h e", e=256), group='gdn') for i in range(2)]
    dof = [k.view(inb[i], h3(inb[i][:, 5120:7168]), group='gdn') for i in range(2)]
    dzs = [k.view(inb[i], h3(inb[i][:, 7168:9216]), group='gdn') for i in range(2)]
    dET = k.view(W0, h3(W0[:, 0:1024]), group='gdn'); dEN = k.view(W0, h3(W0[:, 1024:2048]), group='gdn')
    dLT = k.view(W0, h3(W0[:, 2048:3072]), group='gdn'); dLTm = k.view(W0, h3(W0[:, 3072:4096]), group='gdn'); dLms = k.view(W0, h3(W0[:, 4096:5120]), group='gdn')
    dB = [k.view(W0, h3(W0[:, 5120 + i * 1024:6144 + i * 1024]), group='gdn') for i in range(2)]
    dA = [k.view(W0, h3(W0[:, 7168 + i * 1024:8192 + i * 1024]), group='gdn') for i in range(2)]
    dRTb = k.view(W0, h3(W0[:, 9216:10240]), group='gdn'); daT = k.view(W0, h3(W0[:, 10240:11264]), group='gdn'); dqd = k.view(W0, h3(W0[:, 11264:12288]), group='gdn')
    dSb = k.view(W1, W1[:, 0:2048].rearrange("p (h e) -> p h e", e=256), group='gdn')
    dvb = k.view(W1, W1[:, 2048:4096].rearrange("p (h e) -> p h e", e=256), group='gdn')
    dsq = k.view(W1, h3(W1[:, 2048:4096]), group='gdn', share=dvb)
    dkbe = k.view(W1, h3(W1[:, 4096:5120]), group='gdn'); dnwk = k.view(W1, h3(W1[:, 5120:6144]), group='gdn')
    dvn = k.view(W1, W1[:, 6144:8192].rearrange("p (h e) -> p h e", e=256), group='gdn')
    dkd = k.view(W1, h3(W1[:, 8192:9216]), group='gdn')
    don = k.view(W1, h3(W1[:, 9216:11264]), group='gdn')
    dY = k.view(W1, h3(W1[:, 11264:12288]), group='gdn'); dY2 = k.view(W1, h3(W1[:, 5120:6144]), group='gdn', share=dnwk)
    lmask = k.sb("lmask_s", [128, 14, 128], BF16)
    k.ld(lmask, lmask[:], lmk, lmk.t[:, :, :], q='pool')
    dS = k.view(Sst, Sst[:].rearrange("p (h e) -> p h e", e=256), group='gdn')
    dRTf = k.view(yc, h3(yc[:, 0:1024]), group='gdn')
    dob = k.view(yc, h3(yc[:]), group='gdn', share=dRTf)

    def gdn(li, j):
        norm_stage(li, 0)
        W = gdn_in_w
        gemm(H, False, 8, W, W.t[j, :, 0:4096], 4096, 'fm', epi_store_fm(S1, 0))
        conv_stage(S1, S2, 32, lambda blk, tap: gcw_s[:, j, blk, tap - 3:tap - 2], None, False, [gcw_s], l2blocks=tuple(range(16)), qscale_blocks=tuple(range(8)))
        gemm(H, False, 8, W, W.t[j, :, 4096:6144], 2048, 'fm', epi_store_fm(S6, 0, AF.Silu))
        k.ld(dbc16, dbc16[:], gdn_dtb, gdn_dtb.t[j].partition_broadcast(128))
        k.ld(dnega, dnega[:], gdn_alog, gdn_alog.t[j].partition_broadcast(128))
        k.A(lambda e: e.activation(out=dnega[:], in_=dnega[:], func=AF.Exp), r=[dnega], w=[dnega])
        k.V(lambda e: e.tensor_scalar(out=dnega[:], in0=dnega[:], scalar1=-1.0, scalar2=None, op0=ALU.mult), r=[dnega], w=[dnega])

        def epi_ab(ps, c0, t, sub, last):
            ch = t * 4 + sub
            k.A(lambda e: e.activation(out=d_beta[:, ch, :], in_=ps[:, 16:32], func=AF.Sigmoid), r=[ps], w=[d_beta])
            k.V(lambda e: e.tensor_scalar(out=d_nbeta[:, ch, :], in0=d_beta[:, ch, :], scalar1=-1.0, scalar2=None, op0=ALU.mult), r=[d_beta], w=[d_nbeta])
            k.V(lambda e: e.tensor_tensor(out=d_lg[:, ch, :], in0=ps[:, 0:16], in1=dbc16[:], op=ALU.add), r=[ps, dbc16], w=[d_lg])
            k.A(lambda e: e.activation(out=d_lg[:, ch, :], in_=d_lg[:, ch, :], func=AF.Exp), r=[d_lg], w=[d_lg])
            k.A(lambda e: e.activation(out=d_lg[:, ch, :], in_=d_lg[:, ch, :], func=AF.Ln, bias=1.0, scale=1.0), r=[d_lg], w=[d_lg])
            k.V(lambda e: e.tensor_tensor(out=d_lg[:, ch, :], in0=d_lg[:, ch, :], in1=dnega[:], op=ALU.mult), r=[d_lg, dnega], w=[d_lg])
            k.V(lambda e: e.tensor_copy(out=d_lgb[:, ch, :], in_=d_lg[:, ch, :]), r=[d_lg], w=[d_lgb])
        k.enter('gdnx')
        gemm(H, False, 8, W, W.t[j, :, 6144:6176], 32, 'tm', epi_ab)
        k.enter('gdn')
        pidx = 0
        for (o, L, kd) in SEQS:
            for d in (0, 1):
                if kd == 'p':
                    k.V(lambda e: e.memset(dS[:], 0.0), w=[dS])
                    k.V(lambda e: e.memset(dSb[:], 0.0), w=[dSb])
                else:
                    k.ld(dS, dS[:], st_gdn, st_gdn.t[j, d].rearrange("h d e -> d h e"))
                    k.A(lambda e: e.activation(out=dSb[:], in_=dS[:], func=AF.Copy), r=[dS], w=[dSb])
                ie = 127 if d == 0 else 0
                hs = slice(d * 8, (d + 1) * 8)
                order = chunk_order(L, d)

                def gdn_loads(n_, c):
                    t0 = o + c * 128
                    tsl = slice(t0, t0 + 128)
                    q_ = dq_fm[n_ % 2]; k_ = dk_fm[n_ % 2]; kt = dk_tm[n_ % 2]; v_ = dv_tm[n_ % 2]
                    k.ld(q_, q_[:], S2, S2.t[0:1024, tsl].rearrange("(h d) t -> d h t", d=128))
                    k.ld(k_, k_[:], S2, S2.t[1024:2048, tsl].rearrange("(h d) t -> d h t", d=128))
                    for q2 in range(2):
                        k.ld(kt, kt[:, q2 * 4:(q2 + 1) * 4, :].rearrange("p h t -> p (h t)"), S2, S2.t[1024 + q2 * 512:1024 + (q2 + 1) * 512, tsl], tr=True)
                    for q4 in range(4):
                        k.ld(v_, v_[:, q4 * 2:(q4 + 1) * 2, :].rearrange("p h e -> p (h e)"), S2, S2.t[2048 + q4 * 512:2048 + (q4 + 1) * 512, tsl], tr=True)
                    if d == 1:
                        k.ld(dof[n_ % 2], dof[n_ % 2][:], S5, S5.t[0:2048, tsl].rearrange("(b p) t -> p b t", p=128))
                        k.ld(dzs[n_ % 2], dzs[n_ % 2][:], S6, S6.t[0:2048, tsl].rearrange("(b p) t -> p b t", p=128))
                gdn_loads(0, order[0])
                for n_, c in enumerate(order):
                    if n_ + 1 < len(order):
                        gdn_loads(n_ + 1, order[n_ + 1])
                    t0 = o + c * 128; ch = t0 // 128
                    tsl = slice(t0, t0 + 128)
                    q_ = dq_fm[n_ % 2]; k_ = dk_fm[n_ % 2]; kt = dk_tm[n_ % 2]; v_ = dv_tm[n_ % 2]
                    lg = d_lg[:, ch, hs]; lgb = d_lgb[:, ch, hs]; beta = d_beta[:, ch, hs]; nbeta = d_nbeta[:, ch, hs]
                    k.G(lambda e: e.tensor_tensor(out=dET[:], in0=bc(lg, [128, 8, 128], 2), in1=bc(cmb[:, U_[d], :], [128, 8, 128], 1), op=ALU.mult), r=[d_lg, cmb], w=[dET])
                    k.G(lambda e: e.tensor_tensor(out=dEN[:], in0=bc(lg, [128, 8, 128], 2), in1=bc(cmb[:, V_[d], :], [128, 8, 128], 1), op=ALU.mult), r=[d_lg, cmb], w=[dEN])
                    for hf in range(2):
                        ps = k.psum()
                        k.M(lambda e: e.matmul(ps[:], cmb[:, V_[d], :], dET[:, hf * 4:(hf + 1) * 4, :].rearrange("p h t -> p (h t)"), start=True, stop=True), r=[cmb, dET], w=[ps])
                        k.A(lambda e: e.activation(out=dLT[:, hf * 4:(hf + 1) * 4, :], in_=h3(ps[:]), func=AF.Exp), r=[ps], w=[dLT])
                        ps2 = k.psum()
                        k.M(lambda e: e.matmul(ps2[:], cmb[:, U_[d], :], dEN[:, hf * 4:(hf + 1) * 4, :].rearrange("p h t -> p (h t)"), start=True, stop=True), r=[cmb, dEN], w=[ps2])
                        k.A(lambda e: e.activation(out=dLms[:, hf * 4:(hf + 1) * 4, :], in_=h3(ps2[:]), func=AF.Exp), r=[ps2], w=[dLms])
                        ps3 = k.psum()
                        k.M(lambda e: e.matmul(ps3[:], cmb[:, ONES, :], dET[:, hf * 4:(hf + 1) * 4, :].rearrange("p h t -> p (h t)"), start=True, stop=True), r=[cmb, dET], w=[ps3])
                        k.A(lambda e: e.activation(out=tmpf[:], in_=ps3[:], func=AF.Exp), r=[ps3], w=[tmpf])
                        k.V(lambda e: e.tensor_tensor(out=dqd[:, hf * 4:(hf + 1) * 4, :], in0=q_[:, hf * 4:(hf + 1) * 4, :], in1=h3(tmpf[:]), op=ALU.mult), r=[q_, tmpf], w=[dqd])
                    k.V(lambda e: e.tensor_copy(out=ecs[:, 24:32], in_=dLT[:, :, ie]), r=[dLT], w=[ecs])
                    k.V(lambda e: e.tensor_tensor(out=dLTm[:], in0=dLT[:], in1=bc(cmb[:, U_[d], :], [128, 8, 128], 1), op=ALU.mult), r=[dLT, cmb], w=[dLTm])
                    k.V(lambda e: e.tensor_tensor(out=dLms[:], in0=dLms[:], in1=bc(cmb[:, V_[d], :], [128, 8, 128], 1), op=ALU.mult), r=[dLms, cmb], w=[dLms])
                    ps = k.psum()
                    k.M(lambda e: e.matmul(ps[:, 0:8], cmb[:, U_[d], :], lgb, start=True, stop=True), r=[cmb, d_lgb], w=[ps])
                    k.M(lambda e: e.matmul(ps[:, 8:16], cmb[:, ONES, :], lgb, start=True, stop=True), r=[cmb, d_lgb], w=[ps])
                    k.A(lambda e: e.activation(out=ecs[:, 0:16], in_=ps[:, 0:16], func=AF.Exp), r=[ps], w=[ecs])
                    k.V(lambda e: e.tensor_tensor(out=ecs[:, 16:24], in0=ecs[:, 0:8], in1=beta, op=ALU.mult), r=[ecs, d_beta], w=[ecs])
                    mN, mT, OffN, OffT = dB[0], dA[0], dB[1], dA[1]
                    dD = dLms
                    for hf in range(2):
                        ps = k.psum()
                        for h4 in range(4):
                            h = hf * 4 + h4
                            k.M(lambda e: e.matmul(ps[:, h4 * 128:(h4 + 1) * 128], k_[:, h, :], k_[:, h, :], start=True, stop=True), r=[k_], w=[ps])
                        for h4 in range(4):
                            h = hf * 4 + h4
                            k.V(lambda e: e.scalar_tensor_tensor(out=mN[:, h, :], in0=ps[:, h4 * 128:(h4 + 1) * 128], scalar=d_beta[:, ch, d * 8 + h:d * 8 + h + 1], in1=dLms[:, h, :], op0=ALU.mult, op1=ALU.mult),
                                r=[ps, d_beta, dLms], w=[mN])
                    for hf in range(2):
                        ps = k.psum()
                        for h4 in range(4):
                            h = hf * 4 + h4
                            k.M(lambda e: e.matmul(ps[:, h4 * 128:(h4 + 1) * 128], mN[:, h, :], cmb[:, ID, :], start=True, stop=True), r=[mN, cmb], w=[ps])
                        k.A(lambda e: e.activation(out=mT[:, hf * 4:(hf + 1) * 4, :], in_=h3(ps[:]), func=AF.Copy), r=[ps], w=[mT])
                    MN = lambda lv: lmask[:, (lv if d == 0 else 7 + lv), :]
                    MT_ = lambda lv: lmask[:, (7 + lv if d == 0 else lv), :]
                    k.V(lambda e: e.tensor_tensor(out=OffN[:], in0=mN[:], in1=bc(MN(0), [128, 8, 128], 1), op=ALU.mult), r=[mN, lmask], w=[OffN])
                    k.V(lambda e: e.tensor_tensor(out=dD[:], in0=bc(cmb[:, ID, :], [128, 8, 128], 1), in1=OffN[:], op=ALU.subtract), r=[cmb, OffN], w=[dD])
                    k.V(lambda e: e.tensor_tensor(out=OffT[:], in0=mT[:], in1=bc(MT_(0), [128, 8, 128], 1), op=ALU.mult), r=[mT, lmask], w=[OffT])
                    k.V(lambda e: e.tensor_tensor(out=dRTb[:], in0=bc(cmb[:, ID, :], [128, 8, 128], 1), in1=OffT[:], op=ALU.subtract), r=[cmb, OffT], w=[dRTb])
                    for lv in range(1, 7):
                        k.V(lambda e: e.tensor_tensor(out=OffN[:], in0=mN[:], in1=bc(MN(lv), [128, 8, 128], 1), op=ALU.mult), r=[mN, lmask], w=[OffN])
                        k.G(lambda e: e.tensor_tensor(out=OffT[:], in0=mT[:], in1=bc(MT_(lv), [128, 8, 128], 1), op=ALU.mult), r=[mT, lmask], w=[OffT])
                        for hf in range(2):
                            ps = k.psum()
                            for h4 in range(4):
                                h = hf * 4 + h4
                                k.M(lambda e: e.matmul(ps[:, h4 * 128:(h4 + 1) * 128], OffT[:, h, :], dD[:, h, :], start=True, stop=True), r=[OffT, dD], w=[ps])
                            k.A(lambda e: e.activation(out=dY[:, hf * 4:(hf + 1) * 4, :], in_=h3(ps[:]), func=AF.Copy), r=[ps], w=[dY])
                            ps2 = k.psum()
                            for h4 in range(4):
                                h = hf * 4 + h4
                                k.M(lambda e: e.matmul(ps2[:, h4 * 128:(h4 + 1) * 128], OffN[:, h, :], dRTb[:, h, :], start=True, stop=True), r=[OffN, dRTb], w=[ps2])
                            k.A(lambda e: e.activation(out=dY2[:, hf * 4:(hf + 1) * 4, :], in_=h3(ps2[:]), func=AF.Copy), r=[ps2], w=[dY2])
                        pX = [k.psum(), k.psum()]; pX2 = [k.psum(), k.psum()]
                        for hf in range(2):
                            for h4 in range(4):
                                h = hf * 4 + h4
                                k.M(lambda e: e.matmul(pX[hf][:, h4 * 128:(h4 + 1) * 128], dRTb[:, h, :], dY[:, h, :], start=True, stop=True), r=[dRTb, dY], w=[pX[hf]])
                            for h4 in range(4):
                                h = hf * 4 + h4
                                k.M(lambda e: e.matmul(pX2[hf][:, h4 * 128:(h4 + 1) * 128], dD[:, h, :], dY2[:, h, :], start=True, stop=True), r=[dD, dY2], w=[pX2[hf]])
                        for hf in range(2):
                            k.V(lambda e: e.tensor_tensor(out=dD[:, hf * 4:(hf + 1) * 4, :], in0=dD[:, hf * 4:(hf + 1) * 4, :], in1=h3(pX[hf][:]), op=ALU.subtract), r=[dD, pX[hf]], w=[dD])
                            k.V(lambda e: e.tensor_tensor(out=dRTb[:, hf * 4:(hf + 1) * 4, :], in0=dRTb[:, hf * 4:(hf + 1) * 4, :], in1=h3(pX2[hf][:]), op=ALU.subtract), r=[dRTb, pX2[hf]], w=[dRTb])
                    k.G(lambda e: e.tensor_tensor(out=dvb[:], in0=v_[:], in1=bc(beta, [128, 8, 256], 2), op=ALU.mult), r=[v_, d_beta], w=[dvb])
                    k.G(lambda e: e.tensor_tensor(out=dkbe[:], in0=kt[:], in1=bc(ecs[:, 16:24], [128, 8, 128], 2), op=ALU.mult), r=[kt, ecs], w=[dkbe])
                    k.G(lambda e: e.tensor_tensor(out=dkd[:], in0=kt[:], in1=bc(ecs[:, 24:32], [128, 8, 128], 2), op=ALU.mult), r=[kt, ecs], w=[dkd])
                    for hf in range(2):
                        ps = k.psum()
                        for h4 in range(4):
                            h = hf * 4 + h4
                            k.M(lambda e: e.matmul(ps[:, h4 * 128:(h4 + 1) * 128], dkbe[:, h, :], dRTb[:, h, :], start=True, stop=True), r=[dkbe, dRTb], w=[ps])
                        k.A(lambda e: e.activation(out=dnwk[:, hf * 4:(hf + 1) * 4, :], in_=h3(ps[:]), func=AF.Copy, scale=-1.0), r=[ps], w=[dnwk])
                    for hp in range(4):
                        ps = k.psum()
                        for h2 in range(2):
                            h = hp * 2 + h2
                            k.M(lambda e: e.matmul(ps[:, h2 * 256:(h2 + 1) * 256], dRTb[:, h, :], dvb[:, h, :], start=True, stop=False), r=[dRTb, dvb], w=[ps])
                            k.M(lambda e: e.matmul(ps[:, h2 * 256:(h2 + 1) * 256], dnwk[:, h, :], dSb[:, h, :], start=False, stop=True), r=[dnwk, dSb], w=[ps])
                        k.A(lambda e: e.activation(out=dvn[:, hp * 2:(hp + 1) * 2, :], in_=ps[:].rearrange("p (h e) -> p h e", e=256), func=AF.Copy), r=[ps], w=[dvn])
                    for hf in range(2):
                        ps = k.psum()
                        for h4 in range(4):
                            h = hf * 4 + h4
                            k.M(lambda e: e.matmul(ps[:, h4 * 128:(h4 + 1) * 128], k_[:, h, :], q_[:, h, :], start=True, stop=True), r=[k_, q_], w=[ps])
                        k.V(lambda e: e.tensor_tensor(out=daT[:, hf * 4:(hf + 1) * 4, :], in0=h3(ps[:]), in1=dLTm[:, hf * 4:(hf + 1) * 4, :], op=ALU.mult), r=[ps, dLTm], w=[daT])
                    pso = [k.psum() for _ in range(4)]
                    for h in range(8):
                        for eb in range(2):
                            blk = h * 2 + eb
                            po = pso[blk // 4]; cs_ = slice((blk % 4) * 128, (blk % 4 + 1) * 128)
                            k.M(lambda e: e.matmul(po[:, cs_], dSb[:, h, eb * 128:(eb + 1) * 128], dqd[:, h, :], start=True, stop=False), r=[dSb, dqd], w=[po])
                            k.M(lambda e: e.matmul(po[:, cs_], dvn[:, h, eb * 128:(eb + 1) * 128], daT[:, h, :], start=False, stop=True), r=[dvn, daT], w=[po])
                    if d == 0:
                        for q4 in range(4):
                            k.A(lambda e: e.activation(out=don[:, q4 * 4:(q4 + 1) * 4, :], in_=h3(pso[q4][:]), func=AF.Copy), r=[pso[q4]], w=[don])
                        k.stq(S5, S5.t[0:2048, tsl].rearrange("(b p) t -> p b t", p=128), don, don[:])
                    else:
                        of_ = dof[n_ % 2]; zs_ = dzs[n_ % 2]
                        for q4 in range(4):
                            k.V(lambda e: e.tensor_tensor(out=dob[:, q4 * 4:(q4 + 1) * 4, :], in0=h3(pso[q4][:]), in1=of_[:, q4 * 4:(q4 + 1) * 4, :], op=ALU.add), r=[pso[q4], of_], w=[dob])
                    for h in range(8):
                        pst = k.psum()
                        k.M(lambda e: e.matmul(pst[:, 0:256], dkd[:, h, :], dvn[:, h, :], start=True, stop=True), r=[dkd, dvn], w=[pst])
                        k.V(lambda e: e.scalar_tensor_tensor(out=dS[:, h, :], in0=dS[:, h, :], scalar=ecs[:, 8 + h:9 + h], in1=pst[:, 0:256], op0=ALU.mult, op1=ALU.add), r=[dS, ecs, pst], w=[dS])
                    k.A(lambda e: e.activation(out=dSb[:], in_=dS[:], func=AF.Copy), r=[dS], w=[dSb])
                    if d == 1:
                        k.G(lambda e: e.tensor_tensor(out=dsq[:], in0=dob[:], in1=dob[:], op=ALU.mult), r=[dob], w=[dsq])
                        for hf in range(2):
                            psn = k.psum()
                            for h4 in range(4):
                                h = hf * 4 + h4
                                for eb in range(2):
                                    k.M(lambda e: e.matmul(psn[:, h4 * 128:(h4 + 1) * 128], cmb[:, ONES, :], dsq[:, h * 2 + eb, :], start=(eb == 0), stop=(eb == 1)), r=[cmb, dsq], w=[psn])
                            k.A(lambda e: e.activation(out=rsb[:], in_=psn[:], func=AF.Sqrt, bias=EPS, scale=1.0 / 256), r=[psn], w=[rsb])
                            k.V(lambda e: e.reciprocal(out=rsb[:], in_=rsb[:]), r=[rsb], w=[rsb])
                            for eb in range(2):
                                gv_ = dob[:, hf * 8:(hf + 1) * 8, :].rearrange("p (h b) t -> p h b t", b=2)[:, :, eb, :]
                                k.V(lambda e: e.scalar_tensor_tensor(out=gv_, in0=gv_, scalar=dnw_s[:, j, eb:eb + 1], in1=h3(rsb[:]), op0=ALU.mult, op1=ALU.mult), r=[dob, dnw_s, rsb], w=[dob])
                        k.V(lambda e: e.tensor_tensor(out=don[:], in0=dob[:], in1=zs_[:], op=ALU.mult), r=[dob, zs_], w=[don])
                        k.stq(S5, S5.t[0:2048, tsl].rearrange("(b p) t -> p b t", p=128), don, don[:])
                if kd == 'p':
                    k.stq(ns_gdn, ns_gdn.t[pidx, j, d].rearrange("h d e -> d h e"), dS, dS[:])
            if kd == 'p':
                pidx += 1
        k.leave('gdn'); k.leave('gdnx')
        gemm(S5, False, 16, gdn_out_w, gdn_out_w.t[j, :, :], 1024, 'fm', wsplit=True, **resid_hooks(li, 2))

    wdec = k.sb("wdec", [128, 32], F32)

    for (li, kind, j) in LAYERS:
        if kind == 'ssd':
            ssd(li, j)
        elif kind == 'gla':
            gla(li, j)
        elif kind == 'gdn':
            gdn(li, j)
        elif kind == 'none':
            pass
        else:
            raise NotImplementedError(kind)
        if kind != 'none' or True:
            ffn(li)
    norm_stage(0, 0, final=True)
    outs = [yT, X, ns_ssm, ns_gla, ns_gdn, S1, S2, S3, S4, S5, H]
    k.P.wait_all('sp', [o.b for o in outs])
    k.P.run()
    st.close()
    return nc, k.P.ninst


def _consts():
    i = np.arange(128)
    kk, ii = np.meshgrid(i, i, indexing='ij')
    cm = np.zeros((128, 10, 128), np.float32)
    cm[:, 0] = (kk <= ii); cm[:, 1] = (kk >= ii); cm[:, 2] = (kk > ii); cm[:, 3] = (kk < ii)
    cm[:, 4] = (kk == ii); cm[:, 5] = 1.0
    cm[:, 6] = (kk <= ii) / -16.0; cm[:, 7] = (kk >= ii) / -16.0; cm[:, 8] = (kk > ii) / -16.0; cm[:, 9] = (kk < ii) / -16.0
    return cm


def _fm(v, nblk):
    v = np.asarray(v, np.float32)
    lead = v.shape[:-1]
    v = v.reshape(*lead, nblk, 128)
    return np.ascontiguousarray(np.moveaxis(v, -1, 0))


def prep_common(w):
    d = {}
    d['cmask'] = _consts()
    ii, jj = np.meshgrid(np.arange(128), np.arange(128), indexing='ij')
    lm = np.zeros((128, 14, 128), np.float32)
    for lv in range(7):
        b = 1 << lv
        m = ((ii // (2 * b)) == (jj // (2 * b))) & ((ii % (2 * b)) >= b) & ((jj % (2 * b)) < b)
        lm[:, lv] = m; lm[:, 7 + lv] = m.T
    d['lmask'] = lm
    d['nmw'] = _fm(w['norm_mix_w'], 8); d['nfw'] = _fm(w['norm_ffn_w'], 8); d['fnw'] = _fm(w['final_norm_w'], 8)
    d['ada_w'] = w['ada_w']; d['ada_b'] = _fm(w['ada_b'], 48)
    d['ffn_in_w'] = w['ffn_in_w']; d['ffn_out_w'] = w['ffn_out_w']
    d['ffn_cw'] = _fm(np.moveaxis(np.asarray(w['ffn_conv_w']).reshape(4, 9, 2816), 1, 2).reshape(4, 2816 * 9).reshape(4, 22, 128, 9).transpose(0, 1, 3, 2).reshape(4, 22, 9, 128), 1)[..., 0] if False else None
    cw = np.asarray(w['ffn_conv_w'], np.float32).reshape(4, 9, 22, 128)
    d['ffn_cw'] = np.ascontiguousarray(cw.transpose(3, 0, 2, 1))
    d['ffn_cb'] = _fm(w['ffn_conv_b'], 22)
    d['ssm_in_w'] = w['ssm_in_w']
    cw = np.asarray(w['ssm_conv_w'], np.float32).reshape(2, 3, 32, 128)
    d['ssm_cw'] = np.ascontiguousarray(cw.transpose(3, 0, 2, 1))
    d['ssm_cb'] = _fm(w['ssm_conv_b'], 32)
    d['ssm_dtb'] = np.asarray(w['ssm_dt_bias'], np.float32).reshape(2, 1, 64)
    d['ssm_alog'] = np.asarray(w['ssm_a_log'], np.float32).reshape(2, 1, 64)
    d['ssm_d'] = np.asarray(w['ssm_d'], np.float32).reshape(2, 1, 32)
    d['ssm_nw'] = np.asarray(w['ssm_norm_w'], np.float32).reshape(2, 1, 2048)
    d['ssm_out_w'] = w['ssm_out_w']
    d['gla_in_w'] = w['gla_in_w']
    w1 = np.zeros((1, 1024, 128), np.float32)
    g1 = np.asarray(w['gla_gate_w1'], np.float32)
    w1[:, :, 0:16] = g1[:, 0]; w1[:, :, 32:48] = g1[:, 1]
    d['gla_w1'] = w1
    w2 = np.zeros((1, 64, 2, 512), np.float32)
    g2 = np.asarray(w['gla_gate_w2'], np.float32)
    w2[:, 0:16, 0] = g2[:, 0]; w2[:, 32:48, 1] = g2[:, 1]
    d['gla_w2'] = w2.reshape(1, 64, 1024)
    d['gla_gb'] = np.asarray(w['gla_gate_b'], np.float32).reshape(1, 1, 1024)
    d['gla_nw'] = _fm(w['gla_norm_w'], 2); d['gla_out_w'] = w['gla_out_w']
    d['gdn_in_w'] = w['gdn_in_w']
    cw = np.asarray(w['gdn_conv_w'], np.float32).reshape(1, 3, 32, 128)
    d['gdn_cw'] = np.ascontiguousarray(cw.transpose(3, 0, 2, 1))
    d['gdn_dtb'] = np.asarray(w['gdn_dt_bias'], np.float32).reshape(1, 1, 16)
    d['gdn_alog'] = np.asarray(w['gdn_a_log'], np.float32).reshape(1, 1, 16)
    d['gdn_nw'] = _fm(w['gdn_norm_w'], 2); d['gdn_out_w'] = w['gdn_out_w']
    return {k_: np.ascontiguousarray(np.asarray(v, np.float32)) for k_, v in d.items()}


FULL_SEQS = [(0, 256, 'p'), (256, 256, 'p'), (512, 4096, 's')]
FULL_LAYERS = [(0, 'ssd', 0), (1, 'gla', 0), (2, 'gdn', 0), (3, 'ssd', 1)]


def kernel(**inp):
    w = {k_: np.asarray(v) for k_, v in inp.items()}
    common = prep_common(w)
    nc, ninst = build_program(FULL_SEQS, FULL_LAYERS)
    xp = np.asarray(w['x_prompt'], np.float32); xs = np.asarray(w['x_sample'], np.float32)
    in_maps = []
    for core in range(8):
        b = core % 2
        xcat = np.concatenate([xp[2 * core], xp[2 * core + 1], xs[b]], axis=0)
        m = dict(common)
        m['xT'] = np.ascontiguousarray(xcat.T)
        cond = np.stack([np.asarray(w['c_ctx'], np.float32), np.asarray(w['c'], np.float32)[b]], axis=1)
        m['condT'] = np.ascontiguousarray(cond.reshape(8, 128, 2).transpose(1, 0, 2))
        ss = np.asarray(w['state_ssm'], np.float32)[b]
        m['st_ssm'] = np.ascontiguousarray(ss.transpose(0, 1, 3, 2, 4).reshape(2, 2, 128, 2048))
        m['st_gla'] = np.ascontiguousarray(np.asarray(w['state_gla'], np.float32)[b])
        m['st_gdn'] = np.ascontiguousarray(np.asarray(w['state_gdn'], np.float32)[b])
        in_maps.append(m)
    res = run_bass_kernel_spmd(nc, in_maps, core_ids=list(range(8)))
    y_prompt = np.zeros((16, 256, 1024), np.float32); y_sample = np.zeros((2, 4096, 1024), np.float32)
    n_ssm = np.zeros((16, 2, 2, 32, 128, 64), np.float32); n_gla = np.zeros((16, 1, 2, 4, 128, 256), np.float32)
    n_gdn = np.zeros((16, 1, 2, 8, 128, 256), np.float32)
    for core in range(8):
        r = res.results[core]
        y = np.asarray(r['yT']).T
        y_prompt[2 * core] = y[0:256]; y_prompt[2 * core + 1] = y[256:512]
        if core < 2:
            y_sample[core] = y[512:]
        s_ = np.asarray(r['ns_ssm']).reshape(2, 2, 2, 128, 32, 64).transpose(0, 1, 2, 4, 3, 5)
        n_ssm[2 * core:2 * core + 2] = s_
        n_gla[2 * core:2 * core + 2] = np.asarray(r['ns_gla'])
        n_gdn[2 * core:2 * core + 2] = np.asarray(r['ns_gdn'])
    return (y_prompt, y_sample, n_ssm, n_gla, n_gdn)
```

```python
import contextlib
import numpy as np
import concourse.bass as bass
import concourse.mybir as mybir
from concourse.bass_utils import run_bass_kernel_spmd

F32 = mybir.dt.float32
BF16 = mybir.dt.bfloat16
AF = mybir.ActivationFunctionType
ALU = mybir.AluOpType
AX = mybir.AxisListType
ENGS = ['pe', 'act', 'dve', 'pool', 'sp']
EPS = 1e-6
DBG_STOP = None


class Buf:
    __slots__ = ('name', 'lw', 'rd', 'acc')

    def __init__(self, name, acc=False):
        self.name = name
        self.lw = {}
        self.rd = {}
        self.acc = acc


class _Rec:
    def __init__(self):
        self.call = None

    def __getattr__(self, name):
        def f(*a, **kw):
            self.call = (name, a, kw)
            return self
        return f


class Prog:
    def __init__(self, nc):
        self.nc = nc
        self.q = {e: [] for e in ENGS}
        self.cnt = {e: 0 for e in ENGS}
        self.known = {e: {} for e in ENGS}
        self.dma_cnt = {}
        self.ninst = 0

    def _need(self, eng, key, val, src, waits):
        if src == eng and eng == 'pe':
            return
        if self.known[eng].get(key, 0) >= val:
            return
        if waits.get(key, 0) < val:
            waits[key] = val

    def emit(self, eng, fn, reads=(), writes=(), dma_key=None):
        rec = _Rec()
        fn(rec)
        _c = rec.call
        fn = lambda e, _c=_c: getattr(e, _c[0])(*_c[1], **_c[2])
        waits = {}
        for b in reads:
            for k, (v, s) in b.lw.items():
                self._need(eng, k, v, s, waits)
        for b in writes:
            if not b.acc:
                for k, (v, s) in b.lw.items():
                    self._need(eng, k, v, s, waits)
            for k, (v, s) in b.rd.items():
                self._need(eng, k, v, s, waits)
        for k, v in waits.items():
            self.known[eng][k] = v
        if dma_key is None:
            self.cnt[eng] += 1
            key, val, src, inc = eng, self.cnt[eng], eng, 1
        else:
            self.dma_cnt[dma_key] = self.dma_cnt.get(dma_key, 0) + 16
            key, val, src, inc = dma_key, self.dma_cnt[dma_key], None, 16
        for b in reads:
            if b.rd.get(key, (0, None))[0] < val:
                b.rd[key] = (val, src)
        for b in writes:
            if b.acc:
                b.lw[key] = (val, src)
            else:
                b.lw = {key: (val, src)}
            b.rd = {}
        self.q[eng].append((list(waits.items()), fn, (key, inc)))
        self.ninst += 1
        return (key, val, src)

    def wait_all(self, eng, bufs):
        waits = {}
        for b in bufs:
            for k, (v, s) in b.lw.items():
                self._need(eng, k, v, s, waits)
        self.q[eng].append((list(waits.items()), None, None))

    def run(self):
        nc = self.nc
        keys = list(ENGS) + sorted(self.dma_cnt.keys())
        with contextlib.ExitStack() as st:
            sems = {k: st.enter_context(nc.semaphore("s%d" % i)) for i, k in enumerate(keys)}
            block = st.enter_context(nc.Block())

            def mk(engname):
                def body(e):
                    for waits, fn, inc in self.q[engname]:
                        for k, v in waits:
                            e.wait_ge(sems[k], v)
                        if fn is not None:
                            fn(e).then_inc(sems[inc[0]], inc[1])
                return body
            block.tensor(mk('pe'))
            block.scalar(mk('act'))
            block.vector(mk('dve'))
            block.gpsimd(mk('pool'))
            block.sync(mk('sp'))


class Tl:
    def __init__(self, t, name, acc=False):
        self.t = t
        self.b = Buf(name, acc)
        self.name = name

    def __getitem__(self, k):
        return self.t[k]


class KB:
    def __init__(self, nc, st):
        self.nc = nc
        self.st = st
        self.P = Prog(nc)
        self.ps = [Tl(st.enter_context(nc.psum_tensor("ps%d" % i, [128, 512], F32)), "ps%d" % i) for i in range(8)]
        self.psi = 0
        self.nkey = 0
        self.keymap = {}
        self.groups = {}
        self.nview = 0
        self.cast_tok = Buf('castdma', acc=True)
        self.tr_tok = Buf('trdma', acc=True)

    def sb(self, name, shape, dt):
        return Tl(self.st.enter_context(self.nc.sbuf_tensor(name, shape, dt)), name)

    def dram(self, name, shape, dt, kind=None):
        if kind is None:
            t = self.nc.dram_tensor(name, shape, dt)
        else:
            t = self.nc.dram_tensor(name, shape, dt, kind=kind)
        return Tl(t.ap(), name, acc=True)

    def view(self, parent, ap, group=None, share=None):
        v = Tl(ap, parent.name)
        if group is None:
            v.b = parent.b
        else:
            self.nview += 1
            v.name = "%s#%d" % (parent.name, self.nview)
            if share is not None:
                v.b = share.b
                v.name = share.name
            self.groups.setdefault(group, []).append((parent, v))
        return v

    @staticmethod
    def _union(dicts):
        out = {}
        for d_ in dicts:
            for k_, (v_, s_) in d_.items():
                if out.get(k_, (0, None))[0] < v_:
                    out[k_] = (v_, s_)
        return out

    def enter(self, group):
        for parent, v in self.groups.get(group, []):
            v.b.lw = self._union([parent.b.lw, parent.b.rd])
            v.b.rd = {}

    def leave(self, group):
        parents = {}
        for parent, v in self.groups.get(group, []):
            parents.setdefault(id(parent), (parent, []))[1].append(v)
        for parent, vs in parents.values():
            parent.b.lw = self._union([parent.b.lw, parent.b.rd] + [v.b.lw for v in vs] + [v.b.rd for v in vs])
            parent.b.rd = {}

    def psum(self):
        p = self.ps[self.psi % 8]
        self.psi += 1
        return p

    def _b(self, l):
        return [x.b for x in l]

    def M(self, fn, r=(), w=()):
        return self.P.emit('pe', fn, self._b(r), self._b(w))

    def A(self, fn, r=(), w=()):
        return self.P.emit('act', fn, self._b(r), self._b(w))

    def V(self, fn, r=(), w=()):
        return self.P.emit('dve', fn, self._b(r), self._b(w))

    def G(self, fn, r=(), w=()):
        return self.P.emit('pool', fn, self._b(r), self._b(w))

    def _key(self, s):
        if s not in self.keymap:
            self.keymap[s] = "d%04d" % len(self.keymap)
        return self.keymap[s]

    def ld(self, dst, dst_ap, src, src_ap, q='sp', tr=False):
        key = self._key('ld:' + dst.name)
        if tr:
            fn = lambda e: e.dma_start_transpose(out=dst_ap, in_=src_ap)
        else:
            fn = lambda e: e.dma_start(out=dst_ap, in_=src_ap)
        if tr:
            return self.P.emit(q, fn, [src.b, self.cast_tok], [dst.b, self.tr_tok], dma_key=key)
        if q == 'pool':
            return self.P.emit(q, fn, [src.b, self.tr_tok], [dst.b, self.cast_tok], dma_key=key)
        return self.P.emit(q, fn, [src.b], [dst.b], dma_key=key)

    def stq(self, dst, dst_ap, src, src_ap, q='sp'):
        key = self._key('st:' + src.name)
        return self.P.emit(q, lambda e: e.dma_start(out=dst_ap, in_=src_ap), [src.b], [dst.b], dma_key=key)


def bc(ap, shape, axis):
    return ap.unsqueeze(axis).broadcast_to(shape)


def build_program(SEQS, LAYERS, dbg=()):
    T = sum(L for _, L, _ in SEQS)
    NT = T // 512
    assert T % 512 == 0
    NCH = T // 128
    NP = sum(1 for s in SEQS if s[2] == 'p')
    nc = bass.Bass("TRN2", target_bir_lowering=False)
    st = contextlib.ExitStack()
    k = KB(nc, st)
    IN = lambda name, shape, dt=F32: k.dram(name, shape, dt, kind="ExternalInput")
    OUT = lambda name, shape, dt=F32: k.dram(name, shape, dt, kind="ExternalOutput")
    dbgset = set(dbg)
    SCR = lambda name, shape, dt=BF16: k.dram(name, shape, dt, kind=("ExternalOutput" if name in dbgset else None))

    xT = IN("xT", [1024, T])
    condT = IN("condT", [128, 8, 2])
    cm = IN("cmask", [128, 10, 128])
    nmw = IN("nmw", [128, 4, 8]); nfw = IN("nfw", [128, 4, 8]); fnw = IN("fnw", [128, 8])
    ada_w = IN("ada_w", [4, 1024, 6144]); ada_b = IN("ada_b", [128, 4, 48])
    ffn_in_w = IN("ffn_in_w", [4, 1024, 5632]); ffn_cw = IN("ffn_cw", [128, 4, 22, 9]); ffn_cb = IN("ffn_cb", [128, 4, 22])
    ffn_out_w = IN("ffn_out_w", [4, 2816, 1024])
    ssm_in_w = IN("ssm_in_w", [2, 1024, 6208]); ssm_cw = IN("ssm_cw", [128, 2, 32, 3]); ssm_cb = IN("ssm_cb", [128, 2, 32])
    ssm_dtb = IN("ssm_dtb", [2, 1, 64]); ssm_alog = IN("ssm_alog", [2, 1, 64]); ssm_d = IN("ssm_d", [2, 1, 32])
    ssm_nw = IN("ssm_nw", [2, 1, 2048]); ssm_out_w = IN("ssm_out_w", [2, 2048, 1024])
    gla_in_w = IN("gla_in_w", [1, 1024, 3072]); gla_w1 = IN("gla_w1", [1, 1024, 128]); gla_w2 = IN("gla_w2", [1, 64, 1024]); gla_gb = IN("gla_gb", [1, 1, 1024])
    gla_nw = IN("gla_nw", [128, 1, 2]); gla_out_w = IN("gla_out_w", [1, 1024, 1024])
    gdn_in_w = IN("gdn_in_w", [1, 1024, 6176]); gdn_cw = IN("gdn_cw", [128, 1, 32, 3])
    gdn_dtb = IN("gdn_dtb", [1, 1, 16]); gdn_alog = IN("gdn_alog", [1, 1, 16]); gdn_nw = IN("gdn_nw", [128, 1, 2])
    gdn_out_w = IN("gdn_out_w", [1, 2048, 1024])
    lmk = IN("lmask", [128, 14, 128])
    st_ssm = IN("st_ssm", [2, 2, 128, 2048]); st_gla = IN("st_gla", [1, 2, 4, 128, 256]); st_gdn = IN("st_gdn", [1, 2, 8, 128, 256])
    yT = OUT("yT", [1024, T])
    ns_ssm = OUT("ns_ssm", [max(NP, 1), 2, 2, 128, 2048]); ns_gla = OUT("ns_gla", [max(NP, 1), 1, 2, 4, 128, 256])
    ns_gdn = OUT("ns_gdn", [max(NP, 1), 1, 2, 8, 128, 256])
    X = k.dram("X", [1024, T], F32, kind=("ExternalOutput" if "X" in dbgset else None))
    H = SCR("H", [1024, T])
    S1 = SCR("S1", [4096, T])
    S2 = SCR("S2", [4096, T])
    S3 = SCR("S3", [T, 2048])
    S4 = SCR("S4", [T, 2048])
    S5 = SCR("S5", [2816, T])
    S6 = SCR("S6", [2048, T])
    S7 = SCR("S7", [T, 1024])

    cmb = k.sb("cmb", [128, 10, 128], BF16)
    k.ld(cmb, cmb[:], cm, cm.t[:, :, :], q='pool')
    Uf, Ub, Vf, Vb, ID, ONES, UGf, UGb, VGf, VGb = range(10)
    UG_ = {0: UGf, 1: UGb}; VG_ = {0: VGf, 1: VGb}
    U_ = {0: Uf, 1: Ub}; V_ = {0: Vf, 1: Vb}
    nmw_s = k.sb("nmw_s", [128, 4, 8], F32); nfw_s = k.sb("nfw_s", [128, 4, 8], F32); fnw_s = k.sb("fnw_s", [128, 8], F32)
    k.ld(nmw_s, nmw_s[:], nmw, nmw.t[:, :, :]); k.ld(nfw_s, nfw_s[:], nfw, nfw.t[:, :, :]); k.ld(fnw_s, fnw_s[:], fnw, fnw.t[:, :])
    adab_s = k.sb("adab_s", [128, 4, 48], F32)
    k.ld(adab_s, adab_s[:], ada_b, ada_b.t[:, :, :])
    mod = k.sb("mod", [128, 4, 48, 2], F32)
    gm = k.sb("gm", [128, 4, 2, 8, 2], F32)

    xbuf = [k.sb("xb%d" % i, [128, 8, 512], F32) for i in range(2)]
    for t in range(NT):
        xb = xbuf[t % 2]
        k.ld(xb, xb[:], xT, xT.t[:, t * 512:(t + 1) * 512].rearrange("(c p) t -> p c t", p=128))
        k.stq(X, X.t[:, t * 512:(t + 1) * 512].rearrange("(c p) t -> p c t", p=128), xb, xb[:])
    cnd = k.sb("cnd", [128, 8, 2], F32); cndb = k.sb("cndb", [128, 8, 2], BF16)
    k.ld(cnd, cnd[:], condT, condT.t[:, :, :])
    k.A(lambda e: e.activation(out=cndb[:], in_=cnd[:], func=AF.Silu), r=[cnd], w=[cndb])
    wbuf = [k.sb("wb%d" % i, [128, 12288], BF16) for i in range(2)]
    wi = [0]

    def load_w(src, src_ap3, kc, ncols, after=()):
        wb = wbuf[wi[0] % 2]; wi[0] += 1
        view = wb[:, 0:kc * ncols].rearrange("p (c n) -> p c n", c=kc)
        key = k._key('ld:' + wb.name)
        for c in range(kc):
            k.P.emit('pool', lambda e: e.dma_start(out=view[:, c, :], in_=src_ap3[c * 128:(c + 1) * 128, :]), [src.b, k.tr_tok] + [a.b for a in after], [wb.b, k.cast_tok], dma_key=key)
        return wb, view

    for (li, kind, j) in LAYERS:
        for qd in range(6):
            wb, wv = load_w(ada_w, ada_w.t[li, :, qd * 1024:(qd + 1) * 1024], 8, 1024)
            ps = k.psum()
            for cb in range(8):
                for c in range(8):
                    k.M(lambda e, c=c, cb=cb, wv=wv, ps=ps: e.matmul(ps[:, cb * 2:cb * 2 + 2], wv[:, c, cb * 128:(cb + 1) * 128], cndb[:, c, :],
                                                                       start=(c == 0), stop=(c == 7)), r=[wb, cndb], w=[ps])
            k.V(lambda e, ps=ps, qd=qd, li=li: e.tensor_tensor(out=mod[:, li, qd * 8:(qd + 1) * 8, :], in0=ps[:, 0:16].rearrange("p (a b) -> p a b", b=2),
                                                                in1=bc(adab_s[:, li, qd * 8:(qd + 1) * 8], [128, 8, 2], 2), op=ALU.add), r=[ps, adab_s], w=[mod])
        for sub, nw in ((0, nmw_s), (1, nfw_s)):
            sc = 1 + 3 * sub
            k.V(lambda e, sub=sub, li=li, sc=sc: e.tensor_scalar(out=gm[:, li, sub], in0=mod[:, li, sc * 8:(sc + 1) * 8, :], scalar1=1.0, scalar2=None, op0=ALU.add), r=[mod], w=[gm])
            k.V(lambda e, sub=sub, li=li, nw=nw: e.tensor_tensor(out=gm[:, li, sub], in0=gm[:, li, sub], in1=bc(nw[:, li, :], [128, 8, 2], 2), op=ALU.mult), r=[gm, nw], w=[gm])

    def ci_of_tile(t):
        off = t * 512
        for (o, L, kd) in SEQS:
            if o <= off < o + L:
                return 0 if kd == 'p' else 1
        raise ValueError

    sqb = k.sb("sqb", [128, 8, 512], BF16)
    rsb = k.sb("rsb", [128, 512], F32)
    inb = [k.sb("inb%d" % i, [128, 22 * 512], BF16) for i in range(2)]
    hb = [k.view(inb[i], inb[i][:, 0:4096].rearrange("p (c n) -> p c n", c=8)) for i in range(2)]
    tmpf = k.sb("tmpf", [128, 512], F32)

    def norm_stage(li, sub, final=False):
        def nload(t):
            k.ld(xbuf[t % 2], xbuf[t % 2][:], X, X.t[:, t * 512:(t + 1) * 512].rearrange("(c p) t -> p c t", p=128))
        nload(0)
        for t in range(NT):
            ci = ci_of_tile(t)
            xb = xbuf[t % 2]
            sl = slice(t * 512, (t + 1) * 512)
            if t + 1 < NT:
                nload(t + 1)
            k.A(lambda e, xb=xb: e.activation(out=sqb[:], in_=xb[:], func=AF.Square), r=[xb], w=[sqb])
            ps = k.psum()
            for c in range(8):
                k.M(lambda e, c=c, ps=ps: e.matmul(ps[:], cmb[:, ONES, :], sqb[:, c, :], start=(c == 0), stop=(c == 7)), r=[cmb, sqb], w=[ps])
            k.A(lambda e, ps=ps: e.activation(out=rsb[:], in_=ps[:], func=AF.Sqrt, bias=EPS, scale=1.0 / 1024), r=[ps], w=[rsb])
            k.V(lambda e: e.reciprocal(out=rsb[:], in_=rsb[:]), r=[rsb], w=[rsb])
            if final:
                for c in range(8):
                    k.V(lambda e, c=c, xb=xb: e.scalar_tensor_tensor(out=xb[:, c, :], in0=xb[:, c, :], scalar=fnw_s[:, c:c + 1], in1=rsb[:], op0=ALU.mult, op1=ALU.mult),
                        r=[xb, fnw_s, rsb], w=[xb])
                k.stq(yT, yT.t[:, sl].rearrange("(c p) t -> p c t", p=128), xb, xb[:])
            else:
                h = hb[t % 2]
                shq = 0 if sub == 0 else 3
                for c in range(8):
                    k.V(lambda e, c=c, xb=xb: e.scalar_tensor_tensor(out=tmpf[:], in0=xb[:, c, :], scalar=gm[:, li, sub, c, ci:ci + 1], in1=rsb[:], op0=ALU.mult, op1=ALU.mult),
                        r=[xb, gm, rsb], w=[tmpf])
                    k.A(lambda e, c=c, h=h: e.activation(out=h[:, c, :], in_=tmpf[:], func=AF.Identity, bias=mod[:, li, shq * 8 + c, ci:ci + 1], scale=1.0),
                        r=[tmpf, mod], w=[h])
                k.stq(H, H.t[:, sl].rearrange("(c p) t -> p c t", p=128), h, h[:])

    ini = [0]

    pref = {}

    def prefetch_w(tag, W, Wap, kc, ncols, gw=None):
        if gw is None:
            gw = max(128, (12288 // kc) // 128 * 128)
            gw = min(gw, 2048)
        gn_ = min(gw, ncols)
        pref[tag] = load_w(W, Wap[:, 0:gn_], kc, gn_)

    def gemm(src, src_tm, kc, W, Wap, ncols, out_mode, epi, gw=None, pre=None, post=None, wsplit=False, tag=None):
        if wsplit:
            gw = ncols
        if gw is None:
            gw = max(128, (12288 // kc) // 128 * 128)
            gw = min(gw, 2048)
        seq = [(g0, min(gw, ncols - g0), t) for g0 in range(0, ncols, gw) for t in range(NT)]
        loaded = {}

        def emit_load(i, wb_dep):
            g0, gn, t = seq[i]
            ib = inb[ini[0] % 2]; ini[0] += 1
            iv = ib[:, 0:kc * 512].rearrange("p (c n) -> p c n", c=kc)
            sl = slice(t * 512, (t + 1) * 512)
            if src_tm:
                key = k._key('ld:' + ib.name)
                for c in range(kc):
                    k.P.emit('sp', lambda e: e.dma_start_transpose(out=iv[:, c, :], in_=src.t[sl, c * 128:(c + 1) * 128]), [src.b, k.cast_tok] + ([wb_dep.b] if wb_dep is not None else []), [ib.b, k.tr_tok], dma_key=key)
            else:
                k.ld(ib, iv[:], src, src.t[0:kc * 128, sl].rearrange("(c p) t -> p c t", p=128))
            loaded[i] = (ib, iv)
        wb = wv = None
        emit_load(0, None)
        groups_ = [(g0, min(gw, ncols - g0)) for g0 in range(0, ncols, gw)]
        wq = {}

        def wload(gi):
            g0_, gn_ = groups_[gi]
            if wsplit:
                h_ = kc // 2
                a_ = load_w(W, Wap[0:h_ * 128, g0_:g0_ + gn_], h_, gn_, after=(inb if src_tm else ()))
                b_ = load_w(W, Wap[h_ * 128:kc * 128, g0_:g0_ + gn_], h_, gn_, after=(inb if src_tm else ()))
                wq[gi] = (a_, b_)
            elif gi == 0 and tag is not None and tag in pref:
                wq[gi] = pref.pop(tag)
            else:
                wq[gi] = load_w(W, Wap[:, g0_:g0_ + gn_], kc, gn_, after=(inb if src_tm else ()))
        wload(0)
        for i, (g0, gn, t) in enumerate(seq):
            if t == 0:
                gi = g0 // gw
                if gi not in wq:
                    wload(gi)
                if wsplit:
                    (wbA, wvA), (wbB, wvB) = wq.pop(gi)
                    wb = wbB
                else:
                    wb, wv = wq.pop(gi)
                if gi + 1 < len(groups_) and not src_tm:
                    wload(gi + 1)
            elif t == 1 and src_tm and (g0 // gw) + 1 < len(groups_) and False:
                pass
            if t == 0 and src_tm and (g0 // gw) + 1 < len(groups_):
                pass
            if i + 1 < len(seq):
                emit_load(i + 1, wb)
            ib, iv = loaded.pop(i)
            ncb = gn // 128
            if pre is not None:
                pre(t, g0 // 128, g0 // 128 + ncb)
            if out_mode == 'fm':
                for cb in range(ncb):
                    ps = k.psum()
                    for c in range(kc):
                        if wsplit:
                            wb_, wv_, c_ = (wbA, wvA, c) if c < kc // 2 else (wbB, wvB, c - kc // 2)
                        else:
                            wb_, wv_, c_ = wb, wv, c
                        k.M(lambda e: e.matmul(ps[:], wv_[:, c_, cb * 128:(cb + 1) * 128], iv[:, c, :], start=(c == 0), stop=(c == kc - 1)), r=[wb_, ib], w=[ps])
                    epi(ps, (g0 // 128) + cb, t, None, cb == ncb - 1)
            else:
                for sbk in range(4):
                    for p0 in range(0, gn, 512):
                        pw = min(512, gn - p0)
                        ps = k.psum()
                        for c in range(kc):
                            k.M(lambda e: e.matmul(ps[:, 0:pw], iv[:, c, sbk * 128:(sbk + 1) * 128], wv[:, c, p0:p0 + pw], start=(c == 0), stop=(c == kc - 1)), r=[wb, ib], w=[ps])
                        epi(ps, g0 + p0, t, sbk, False)
            if post is not None:
                post(t, g0 // 128, g0 // 128 + ncb)

    stg = [k.sb("stg%d" % i, [128, 512], BF16) for i in range(3)]
    sti = [0]

    stgb = [k.sb("stgb%d" % i, [128, 4, 512], BF16) for i in range(2)]
    stgi = [0]

    def epi_store_fm(dst, row0, func=None, scale=1.0):
        state = {'n': 0, 'cb0': None, 'buf': None}

        def epi(ps, cb, t, sub, last):
            if state['n'] == 0:
                state['buf'] = stgb[stgi[0] % 2]; stgi[0] += 1
                state['cb0'] = cb
            s_ = state['buf']; n = state['n']
            if func is None:
                if cb % 2 == 0:
                    k.V(lambda e: e.tensor_copy(out=s_[:, n, :], in_=ps[:]), r=[ps], w=[s_])
                else:
                    k.A(lambda e: e.activation(out=s_[:, n, :], in_=ps[:], func=AF.Copy), r=[ps], w=[s_])
            else:
                k.A(lambda e: e.activation(out=s_[:, n, :], in_=ps[:], func=func, scale=scale), r=[ps], w=[s_])
            state['n'] += 1
            if state['n'] == 4 or last:
                n_ = state['n']; c0 = state['cb0']
                k.stq(dst, dst.t[row0 + c0 * 128:row0 + (c0 + n_) * 128, t * 512:(t + 1) * 512].rearrange("(c p) t -> p c t", p=128), s_, s_[:, 0:n_, :])
                state['n'] = 0
        return epi

    def epi_store_tm(dst, col0, func=None):
        def epi(ps, c0, t, sub, last):
            s = stg[sti[0] % 3]; sti[0] += 1
            if func is None:
                k.V(lambda e, s=s, ps=ps: e.tensor_copy(out=s[:], in_=ps[:]), r=[ps], w=[s])
            else:
                k.A(lambda e, s=s, ps=ps: e.activation(out=s[:], in_=ps[:], func=func), r=[ps], w=[s])
            r0 = t * 512 + sub * 128
            k.stq(dst, dst.t[r0:r0 + 128, col0 + c0:col0 + c0 + 512], s, s[:])
        return epi

    def resid_hooks(li, gq):
        def pre(t, c_lo, c_hi):
            xb = xbuf[t % 2]
            k.ld(xb, xb[:, c_lo:c_hi, :], X, X.t[c_lo * 128:c_hi * 128, t * 512:(t + 1) * 512].rearrange("(c p) t -> p c t", p=128))

        def epi(ps, cb, t, sub, last):
            ci = ci_of_tile(t)
            xb = xbuf[t % 2]
            k.V(lambda e: e.scalar_tensor_tensor(out=xb[:, cb, :], in0=ps[:], scalar=mod[:, li, gq * 8 + cb, ci:ci + 1], in1=xb[:, cb, :], op0=ALU.mult, op1=ALU.add),
                r=[ps, mod, xb], w=[xb])

        def post(t, c_lo, c_hi):
            xb = xbuf[t % 2]
            k.stq(X, X.t[c_lo * 128:c_hi * 128, t * 512:(t + 1) * 512].rearrange("(c p) t -> p c t", p=128), xb, xb[:, c_lo:c_hi, :])
        return dict(epi=epi, pre=pre, post=post)

    cvin = [k.sb("cvin%d" % i, [128, 10, 66], BF16) for i in range(2)]
    dgb = [k.sb("dgb%d" % i, [128, 9, 128], BF16) for i in range(2)]
    cvout = [k.sb("cvout%d" % i, [128, 512], BF16) for i in range(2)]
    cvi = [0]

    def seq_of(tok):
        for (o, L, kd) in SEQS:
            if o <= tok < o + L:
                return (o, L, kd)
        raise ValueError

    def conv_stage(src, dst, nblk, wfn, bfn, grid_for_sample, wdeps, l2blocks=(), qscale_blocks=()):
        items = [(blk, t) for blk in range(nblk) for t in range(NT)]
        info = {}

        def conv_load(i):
            blk, t = items[i]
            rs_ = slice(blk * 128, (blk + 1) * 128)
            t0 = t * 512
            (o, L, kd) = seq_of(t0)
            ci_ = cvin[cvi[0] % 2]; co = cvout[cvi[0] % 2]; cvi[0] += 1
            grid = grid_for_sample and kd == 's'
            k.G(lambda e: e.memset(ci_[:], 0.0), w=[ci_])
            pieces = []
            if grid:
                ROWS_ = L // 64
                r0 = (t0 - o) // 64
                lo = max(0, r0 - 1); hi = min(ROWS_, r0 + 9)
                k.ld(ci_, ci_[:, lo - (r0 - 1):hi - (r0 - 1), 1:65], src, src.t[rs_, o + lo * 64:o + hi * 64].rearrange("p (r c) -> p r c", c=64))
            else:
                flat = ci_[:].rearrange("p r c -> p (r c)")
                pos = t0
                base = 0
                while pos < t0 + 512:
                    (o2, L2, _) = seq_of(pos)
                    pe_ = min(t0 + 512, o2 + L2)
                    pa, pb = pos, pe_
                    lo = max(o2, pa - 1); hi = min(o2 + L2, pb + 1)
                    k.ld(ci_, flat[:, base + (lo - (pa - 1)):base + (hi - (pa - 1))], src, src.t[rs_, lo:hi])
                    pieces.append((pa, pb, base)); base += (pb - pa) + 2
                    pos = pe_
            info[i] = (ci_, co, grid, pieces)

        conv_load(0)
        for i, (blk, t) in enumerate(items):
            rs_ = slice(blk * 128, (blk + 1) * 128)
            t0 = t * 512
            if t == 0:
                dg = dgb[blk % 2]
                for tap in (range(9) if grid_for_sample else range(3, 6)):
                    k.V(lambda e: e.tensor_scalar(out=dg[:, tap, :], in0=cmb[:, ID, :], scalar1=wfn(blk, tap), scalar2=None, op0=ALU.mult), r=[cmb] + wdeps, w=[dg])
            if i + 1 < len(items):
                conv_load(i + 1)
            ci_, co, grid, pieces = info.pop(i)
            ps = k.psum()
            if grid:
                pv = ps[:].rearrange("p (r c) -> p r c", c=64)
                ti = 0
                for dy in range(3):
                    for dx in range(3):
                        k.M(lambda e: e.matmul(pv, dg[:, dy * 3 + dx, :], ci_[:, dy:dy + 8, dx:dx + 64], start=(ti == 0), stop=(ti == 8)), r=[dg, ci_], w=[ps])
                        ti += 1
            else:
                flat = ci_[:].rearrange("p r c -> p (r c)")
                for (pa, pb, base) in pieces:
                    n_ = pb - pa
                    for dx in range(3):
                        k.M(lambda e: e.matmul(ps[:, pa - t0:pb - t0], dg[:, 3 + dx, :], flat[:, base + dx:base + dx + n_], start=(dx == 0), stop=(dx == 2)), r=[dg, ci_], w=[ps])
            bap = bfn(blk) if bfn is not None else None
            if bap is not None:
                k.A(lambda e: e.activation(out=co[:], in_=ps[:], func=AF.Silu, bias=bap, scale=1.0), r=[ps] + wdeps, w=[co])
            else:
                k.A(lambda e: e.activation(out=co[:], in_=ps[:], func=AF.Silu), r=[ps], w=[co])
            if blk in l2blocks:
                k.V(lambda e: e.tensor_tensor(out=sqb[:, 0, :], in0=co[:], in1=co[:], op=ALU.mult), r=[co], w=[sqb])
                ps2 = k.psum()
                k.M(lambda e: e.matmul(ps2[:], cmb[:, ONES, :], sqb[:, 0, :], start=True, stop=True), r=[cmb, sqb], w=[ps2])
                k.A(lambda e: e.activation(out=rsb[:], in_=ps2[:], func=AF.Sqrt, bias=1e-6, scale=1.0), r=[ps2], w=[rsb])
                k.V(lambda e: e.reciprocal(out=rsb[:], in_=rsb[:]), r=[rsb], w=[rsb])
                qs = (128.0 ** -0.5) if blk in qscale_blocks else 1.0
                k.V(lambda e: e.scalar_tensor_tensor(out=co[:], in0=co[:], scalar=qs, in1=rsb[:], op0=ALU.mult, op1=ALU.mult), r=[co, rsb], w=[co])
            k.stq(dst, dst.t[rs_, t0:t0 + 512], co, co[:])

    fcw_s = k.sb("fcw_s", [128, 4, 22, 9], F32); fcb_s = k.sb("fcb_s", [128, 4, 22], F32)
    k.ld(fcw_s, fcw_s[:], ffn_cw, ffn_cw.t[:, :, :, :]); k.ld(fcb_s, fcb_s[:], ffn_cb, ffn_cb.t[:, :, :])
    avb = [k.view(wbuf[i], wbuf[i][:, 8192:12288].rearrange("p (c n) -> p c n", c=8), group='ffnv') for i in range(2)]
    aai = [0]

    def ffn(li):
        prefetch_w('fa', ffn_in_w, ffn_in_w.t[li, :, 0:2816], 8, 2816)
        norm_stage(li, 1)
        gemm(H, False, 8, ffn_in_w, ffn_in_w.t[li, :, 0:2816], 2816, 'fm', epi_store_fm(S1, 0), tag='fa')
        prefetch_w('fv', ffn_in_w, ffn_in_w.t[li, :, 2816:5632], 8, 2816, gw=1024)
        conv_stage(S1, S2, 22, lambda blk, tap: fcw_s[:, li, blk, tap:tap + 1], lambda blk: fcb_s[:, li, blk:blk + 1], True, [fcw_s, fcb_s])

        vstate = {}

        def pre_v(t, c_lo, c_hi):
            a = avb[aai[0] % 2]; aai[0] += 1
            k.ld(a, a[:, 0:c_hi - c_lo, :], S2, S2.t[c_lo * 128:c_hi * 128, t * 512:(t + 1) * 512].rearrange("(c p) t -> p c t", p=128))
            vstate['a'] = a; vstate['lo'] = c_lo

        def epi_v(ps, cb, t, sub, last):
            a = vstate['a']; n = cb - vstate['lo']
            k.V(lambda e: e.tensor_tensor(out=a[:, n, :], in0=ps[:], in1=a[:, n, :], op=ALU.mult), r=[ps, a], w=[a])

        def post_v(t, c_lo, c_hi):
            a = vstate['a']
            k.stq(S5, S5.t[c_lo * 128:c_hi * 128, t * 512:(t + 1) * 512].rearrange("(c p) t -> p c t", p=128), a, a[:, 0:c_hi - c_lo, :])
        k.enter('ffnv')
        gemm(H, False, 8, ffn_in_w, ffn_in_w.t[li, :, 2816:5632], 2816, 'fm', epi_v, gw=1024, pre=pre_v, post=post_v, tag='fv')
        k.leave('ffnv')
        gemm(S5, False, 22, ffn_out_w, ffn_out_w.t[li, :, :], 1024, 'fm', gw=512, **resid_hooks(li, 5))

    scw_s = k.sb("scw_s", [128, 2, 32, 3], F32); scb_s = k.sb("scb_s", [128, 2, 32], F32)
    k.ld(scw_s, scw_s[:], ssm_cw, ssm_cw.t[:, :, :, :]); k.ld(scb_s, scb_s[:], ssm_cb, ssm_cb.t[:, :, :])
    dt_sb = k.view(xbuf[0], xbuf[0][:].rearrange("p c n -> p (c n)")[:, 0:NCH * 64].rearrange("p (c n) -> p c n", n=64), group='ssdx')
    la_sb = k.view(xbuf[1], xbuf[1][:].rearrange("p c n -> p (c n)")[:, 0:NCH * 64].rearrange("p (c n) -> p c n", n=64), group='ssdx')
    lab_sb = k.sb("lab_sb", [128, 1, 64], BF16)
    bc64 = k.sb("bc64", [128, 64], F32); nega = k.sb("nega", [128, 64], F32); dsk = k.sb("dsk", [128, 32], F32)
    nwb = k.sb("nwb", [128, 2048], BF16)
    W0, W1 = wbuf[0], wbuf[1]
    Ebuf = k.view(W0, W0[:, 0:4096], group='ssd'); DTb = k.view(W0, W0[:, 4096:8192], group='ssd')
    xv = k.view(W0, W0[:, 8192:10240], group='ssd'); xw = k.view(W0, W0[:, 10240:12288], group='ssd')
    Sbf = k.view(W1, W1[:, 0:2048], group='ssd'); STb = k.view(W1, W1[:, 2048:3072], group='ssd')
    yo = [k.view(W1, W1[:, 3072 + i * 2048:3072 + (i + 1) * 2048], group='ssd') for i in range(2)]
    xs_tm = [k.view(inb[i], inb[i][:, 0:2048], group='ssd') for i in range(2)]
    B_tm = [k.view(inb[i], inb[i][:, 2048:3072], group='ssd') for i in range(2)]
    BC_fm = [k.view(inb[i], inb[i][:, 3072:5120].rearrange("p (g t) -> p g t", t=128), group='ssd') for i in range(2)]
    yfb = [k.view(inb[i], inb[i][:, 5120:7168], group='ssd') for i in range(2)]
    zsb = [k.view(inb[i], inb[i][:, 7168:9216], group='ssd') for i in range(2)]
    ecs = k.sb("ecs", [128, 64], F32)
    Sst = k.sb("Sst", [128, 2048], F32)
    yc = k.sb("yc", [128, 2048], F32)
    ytmp = tmpf
    ss8 = k.sb("ss8", [128, 16], F32)

    def chunk_order(L, d):
        n = L // 128
        return list(range(n)) if d == 0 else list(range(n - 1, -1, -1))

    def ssd(li, j):
        W = ssm_in_w
        prefetch_w('sz', W, W.t[j, :, 0:2048], 8, 2048)
        norm_stage(li, 0)
        gemm(H, False, 8, W, W.t[j, :, 0:2048], 2048, 'tm', epi_store_tm(S3, 0, AF.Silu), tag='sz')
        prefetch_w('sx', W, W.t[j, :, 2048:6144], 8, 4096)
        gemm(H, False, 8, W, W.t[j, :, 2048:6144], 4096, 'fm', epi_store_fm(S1, 0), tag='sx')
        prefetch_w('sd', W, W.t[j, :, 6144:6208], 8, 64)
        if DBG_STOP == 'gemm':
            return
        conv_stage(S1, S2, 32, lambda blk, tap: scw_s[:, j, blk, tap - 3:tap - 2], lambda blk: scb_s[:, j, blk:blk + 1], False, [scw_s, scb_s])
        if DBG_STOP == 'conv':
            return
        k.ld(bc64, bc64[:], ssm_dtb, ssm_dtb.t[j].partition_broadcast(128))
        k.ld(nega, nega[:], ssm_alog, ssm_alog.t[j].partition_broadcast(128))
        k.A(lambda e: e.activation(out=nega[:], in_=nega[:], func=AF.Exp), r=[nega], w=[nega])
        k.V(lambda e: e.tensor_scalar(out=nega[:], in0=nega[:], scalar1=-1.0, scalar2=None, op0=ALU.mult), r=[nega], w=[nega])
        k.ld(dsk, dsk[:], ssm_d, ssm_d.t[j].partition_broadcast(128))
        k.ld(nwb, nwb[:], ssm_nw, ssm_nw.t[j].partition_broadcast(128), q='pool')

        def epi_dt(ps, c0, t, sub, last):
            ch = t * 4 + sub
            k.V(lambda e, ps=ps, ch=ch: e.tensor_tensor(out=dt_sb[:, ch, :], in0=ps[:, 0:64], in1=bc64[:], op=ALU.add), r=[ps, bc64], w=[dt_sb])
            k.A(lambda e, ch=ch: e.activation(out=dt_sb[:, ch, :], in_=dt_sb[:, ch, :], func=AF.Exp), r=[dt_sb], w=[dt_sb])
            k.A(lambda e, ch=ch: e.activation(out=dt_sb[:, ch, :], in_=dt_sb[:, ch, :], func=AF.Ln, bias=1.0, scale=1.0), r=[dt_sb], w=[dt_sb])
            k.V(lambda e, ch=ch: e.tensor_tensor(out=la_sb[:, ch, :], in0=dt_sb[:, ch, :], in1=nega[:], op=ALU.mult), r=[dt_sb, nega], w=[la_sb])
        k.enter('ssdx')
        gemm(H, False, 8, W, W.t[j, :, 6144:6208], 64, 'tm', epi_dt, tag='sd')
        if DBG_STOP == 'dt':
            return
        k.enter('ssd')
        pidx = 0
        for (o, L, kd) in (SEQS[:1] if DBG_STOP in ('scan1', 'scan1b') else SEQS):
            for d in ((0,) if DBG_STOP == 'scan1' else (0, 1)):
                if kd == 'p':
                    k.V(lambda e: e.memset(Sst[:], 0.0), w=[Sst])
                    k.V(lambda e: e.memset(Sbf[:], 0.0), w=[Sbf])
                else:
                    k.ld(Sst, Sst[:], st_ssm, st_ssm.t[j, d])
                    k.A(lambda e: e.activation(out=Sbf[:], in_=Sst[:], func=AF.Copy), r=[Sst], w=[Sbf])
                ie = 127 if d == 0 else 0
                order = chunk_order(L, d)

                def ssd_loads(n_, c):
                    t0 = o + c * 128
                    tsl = slice(t0, t0 + 128)
                    xs = xs_tm[n_ % 2]; Bt = B_tm[n_ % 2]; BC = BC_fm[n_ % 2]
                    for q4 in range(4):
                        k.ld(xs, xs[:, q4 * 512:(q4 + 1) * 512], S2, S2.t[q4 * 512:(q4 + 1) * 512, tsl], tr=True)
                    for q4 in range(2):
                        k.ld(Bt, Bt[:, q4 * 512:(q4 + 1) * 512], S2, S2.t[2048 + q4 * 512:2048 + (q4 + 1) * 512, tsl], tr=True)
                    k.ld(BC, BC[:], S2, S2.t[2048:4096, tsl].rearrange("(g n) t -> n g t", n=128))
                    if d == 1:
                        k.ld(yfb[n_ % 2], yfb[n_ % 2][:], S4, S4.t[tsl, :])
                        k.ld(zsb[n_ % 2], zsb[n_ % 2][:], S3, S3.t[tsl, :])
                ssd_loads(0, order[0])
                for n_, c in enumerate(order):
                    if n_ + 1 < len(order):
                        ssd_loads(n_ + 1, order[n_ + 1])
                    t0 = o + c * 128; ch = t0 // 128
                    tsl = slice(t0, t0 + 128)
                    xs = xs_tm[n_ % 2]; Bt = B_tm[n_ % 2]; BC = BC_fm[n_ % 2]
                    la = la_sb[:, ch, d * 32:(d + 1) * 32]; lab = lab_sb[:, 0, 0:32]; dtc = dt_sb[:, ch, d * 32:(d + 1) * 32]
                    k.V(lambda e: e.tensor_copy(out=lab_sb[:, 0, 0:32], in_=la), r=[la_sb], w=[lab_sb])
                    E3 = Ebuf[:].rearrange("p (h i) -> p h i", i=128)
                    D3 = DTb[:].rearrange("p (h i) -> p h i", i=128)
                    k.G(lambda e, la=la: e.tensor_tensor(out=E3, in0=bc(la, [128, 32, 128], 2), in1=bc(cmb[:, U_[d], :], [128, 32, 128], 1), op=ALU.mult), r=[la_sb, cmb], w=[Ebuf])
                    for q8 in range(8):
                        ps = k.psum()
                        k.M(lambda e, ps=ps, q8=q8: e.matmul(ps[:], cmb[:, V_[d], :], Ebuf[:, q8 * 512:(q8 + 1) * 512], start=True, stop=True), r=[cmb, Ebuf], w=[ps])
                        k.A(lambda e, ps=ps, q8=q8: e.activation(out=DTb[:, q8 * 512:(q8 + 1) * 512], in_=ps[:], func=AF.Exp), r=[ps], w=[DTb])
                    for half in range(2):
                        ps = k.psum()
                        for g4 in range(4):
                            g = half * 4 + g4
                            k.M(lambda e, ps=ps, g=g, g4=g4, BC=BC: e.matmul(ps[:, g4 * 128:(g4 + 1) * 128], BC[:, g, :], BC[:, 8 + g, :], start=True, stop=True), r=[BC], w=[ps])
                        k.V(lambda e, ps=ps, half=half: e.tensor_tensor(out=STb[:, half * 512:(half + 1) * 512].rearrange("p (g i) -> p g i", i=128), in0=ps[:].rearrange("p (g i) -> p g i", i=128),
                                                                        in1=bc(cmb[:, U_[d], :], [128, 4, 128], 1), op=ALU.mult), r=[ps, cmb], w=[STb])
                    k.V(lambda e: e.tensor_copy(out=wdec[:], in_=DTb[:].rearrange("p (h i) -> p h i", i=128)[:, :, ie]), r=[DTb], w=[wdec])
                    k.V(lambda e: e.tensor_tensor(out=DTb[:].rearrange("p (g r i) -> p g r i", r=4, i=128), in0=DTb[:].rearrange("p (g r i) -> p g r i", r=4, i=128),
                                                   in1=bc(STb[:].rearrange("p (g i) -> p g i", i=128), [128, 8, 4, 128], 2), op=ALU.mult), r=[DTb, STb], w=[DTb])
                    k.G(lambda e, xs=xs, dtc=dtc: e.tensor_tensor(out=xv[:].rearrange("p (h q) -> p h q", q=64), in0=xs[:].rearrange("p (h q) -> p h q", q=64),
                                                                   in1=bc(dtc, [128, 32, 64], 2), op=ALU.mult), r=[xs, dt_sb], w=[xv])
                    k.G(lambda e: e.tensor_tensor(out=xw[:].rearrange("p (h q) -> p h q", q=64), in0=xv[:].rearrange("p (h q) -> p h q", q=64),
                                                   in1=bc(wdec[:], [128, 32, 64], 2), op=ALU.mult), r=[xv, wdec], w=[xw])
                    ps = k.psum()
                    k.M(lambda e, ps=ps, lab=lab: e.matmul(ps[:, 0:32], cmb[:, U_[d], :], lab, start=True, stop=True), r=[cmb, lab_sb], w=[ps])
                    k.M(lambda e, ps=ps, lab=lab: e.matmul(ps[:, 32:64], cmb[:, ONES, :], lab, start=True, stop=True), r=[cmb, lab_sb], w=[ps])
                    k.A(lambda e, ps=ps: e.activation(out=ecs[:], in_=ps[:, 0:64], func=AF.Exp), r=[ps], w=[ecs])
                    for pr in range(4):
                        psA = k.psum(); psB = k.psum()
                        for gg in range(2):
                            g = pr * 2 + gg
                            for r_ in range(4):
                                h = g * 4 + r_
                                k.M(lambda e, psA=psA, h=h, gg=gg, r_=r_: e.matmul(psA[:, gg * 256 + r_ * 64:gg * 256 + (r_ + 1) * 64], DTb[:, h * 128:(h + 1) * 128], xv[:, h * 64:(h + 1) * 64],
                                                                                       start=True, stop=True), r=[DTb, xv], w=[psA])
                            k.M(lambda e, psB=psB, g=g, gg=gg, BC=BC: e.matmul(psB[:, gg * 256:(gg + 1) * 256], BC[:, 8 + g, :], Sbf[:, g * 256:(g + 1) * 256], start=True, stop=True), r=[BC, Sbf], w=[psB])
                        k.V(lambda e, psB=psB, pr=pr: e.tensor_tensor(out=ytmp[:].rearrange("p (h q) -> p h q", q=64), in0=psB[:].rearrange("p (h q) -> p h q", q=64),
                                                                       in1=bc(ecs[:, pr * 8:(pr + 1) * 8], [128, 8, 64], 2), op=ALU.mult), r=[psB, ecs], w=[ytmp])
                        k.V(lambda e, psA=psA, pr=pr: e.tensor_tensor(out=yc[:, pr * 512:(pr + 1) * 512], in0=psA[:], in1=ytmp[:], op=ALU.add), r=[psA, ytmp], w=[yc])
                    k.G(lambda e: e.tensor_tensor(out=Sst[:].rearrange("p (h q) -> p h q", q=64), in0=Sst[:].rearrange("p (h q) -> p h q", q=64),
                                                   in1=bc(ecs[:, 32:64], [128, 32, 64], 2), op=ALU.mult), r=[Sst, ecs], w=[Sst])
                    for pr in range(4):
                        ps = k.psum()
                        for gg in range(2):
                            g = pr * 2 + gg
                            k.M(lambda e, ps=ps, g=g, gg=gg, Bt=Bt: e.matmul(ps[:, gg * 256:(gg + 1) * 256], Bt[:, g * 128:(g + 1) * 128], xw[:, g * 256:(g + 1) * 256], start=True, stop=True), r=[Bt, xw], w=[ps])
                        k.V(lambda e, ps=ps, pr=pr: e.tensor_tensor(out=Sst[:, pr * 512:(pr + 1) * 512], in0=ps[:], in1=Sst[:, pr * 512:(pr + 1) * 512], op=ALU.add), r=[ps, Sst], w=[Sst])
                    k.A(lambda e: e.activation(out=Sbf[:], in_=Sst[:], func=AF.Copy), r=[Sst], w=[Sbf])
                    if d == 0:
                        y_ = yo[n_ % 2]
                        k.A(lambda e, y_=y_: e.activation(out=y_[:], in_=yc[:], func=AF.Copy), r=[yc], w=[y_])
                        k.stq(S4, S4.t[tsl, :], y_, y_[:])
                    else:
                        yf_ = yfb[n_ % 2]; zs_ = zsb[n_ % 2]; y_ = yo[n_ % 2]
                        k.V(lambda e, yf_=yf_: e.tensor_tensor(out=yc[:], in0=yc[:], in1=yf_[:], op=ALU.add), r=[yc, yf_], w=[yc])
                        k.G(lambda e, xs=xs: e.tensor_tensor(out=xv[:].rearrange("p (h q) -> p h q", q=64), in0=xs[:].rearrange("p (h q) -> p h q", q=64),
                                                              in1=bc(dsk[:], [128, 32, 64], 2), op=ALU.mult), r=[xs, dsk], w=[xv])
                        k.V(lambda e: e.tensor_tensor(out=yc[:], in0=yc[:], in1=xv[:], op=ALU.add), r=[yc, xv], w=[yc])
                        k.V(lambda e, zs_=zs_: e.tensor_tensor(out=yc[:], in0=yc[:], in1=zs_[:], op=ALU.mult), r=[yc, zs_], w=[yc])
                        k.G(lambda e: e.tensor_tensor(out=Ebuf[:, 0:2048], in0=yc[:], in1=yc[:], op=ALU.mult), r=[yc], w=[Ebuf])
                        k.V(lambda e: e.tensor_reduce(out=ss8[:, 0:8], in_=Ebuf[:, 0:2048].rearrange("p (g q) -> p g q", q=256), axis=AX.X, op=ALU.add), r=[Ebuf], w=[ss8])
                        k.A(lambda e: e.activation(out=ss8[:, 0:8], in_=ss8[:, 0:8], func=AF.Sqrt, bias=EPS, scale=1.0 / 256), r=[ss8], w=[ss8])
                        k.V(lambda e: e.reciprocal(out=ss8[:, 0:8], in_=ss8[:, 0:8]), r=[ss8], w=[ss8])
                        k.V(lambda e: e.tensor_tensor(out=yc[:].rearrange("p (g q) -> p g q", q=256), in0=yc[:].rearrange("p (g q) -> p g q", q=256),
                                                       in1=bc(ss8[:, 0:8], [128, 8, 256], 2), op=ALU.mult), r=[yc, ss8], w=[yc])
                        k.V(lambda e, y_=y_: e.tensor_tensor(out=y_[:], in0=yc[:], in1=nwb[:], op=ALU.mult), r=[yc, nwb], w=[y_])
                        k.stq(S4, S4.t[tsl, :], y_, y_[:])
                if kd == 'p':
                    k.stq(ns_ssm, ns_ssm.t[pidx, j, d], Sst, Sst[:])
            if kd == 'p':
                pidx += 1
        k.leave('ssd'); k.leave('ssdx')
        if DBG_STOP in ('scan1', 'scan1b', 'scan'):
            return
        gemm(S4, True, 16, ssm_out_w, ssm_out_w.t[j, :, :], 1024, 'fm', wsplit=True, **resid_hooks(li, 2))

    HAS_GLA = any(kd_ == 'gla' for _, kd_, _ in LAYERS)
    lrT = k.sb("lrT", [64, 512], BF16)
    gnw_s = k.sb("gnw_s", [128, 1, 2], F32)
    k.ld(gnw_s, gnw_s[:], gla_nw, gla_nw.t[:, :, :])
    w2s = k.sb("w2s", [64, 1024 if HAS_GLA else 8], BF16)
    gbb = k.sb("gbb", [128, 1024 if HAS_GLA else 8], BF16)
    gq_fm = [k.view(inb[i], inb[i][:, 0:512].rearrange("p (h t) -> p h t", t=128), group='gla') for i in range(2)]
    gk_fm = [k.view(inb[i], inb[i][:, 512:1024].rearrange("p (h t) -> p h t", t=128), group='gla') for i in range(2)]
    gk_tm = [k.view(inb[i], inb[i][:, 1024:1536], group='gla') for i in range(2)]
    gv_tm = [k.view(inb[i], inb[i][:, 1536:2560], group='gla') for i in range(2)]
    gsp_tm = [k.view(inb[i], inb[i][:, 2560:3072], group='gla') for i in range(2)]
    gof = [k.view(inb[i], inb[i][:, 3072:4096].rearrange("p (b t) -> p b t", t=128), group='gla') for i in range(2)]
    grs = [k.view(inb[i], inb[i][:, 4096:5120].rearrange("p (b t) -> p b t", t=128), group='gla') for i in range(2)]
    gqg = k.view(W0, W0[:, 0:512].rearrange("p (h t) -> p h t", t=128), group='gla')
    gkg = k.view(W0, W0[:, 512:1024].rearrange("p (h t) -> p h t", t=128), group='gla')
    gAT = k.view(W0, W0[:, 1024:1536].rearrange("p (h t) -> p h t", t=128), group='gla')
    gkw = k.view(W0, W0[:, 1536:2048], group='gla')
    gsq = k.view(W0, W0[:, 2048:3072].rearrange("p (b t) -> p b t", t=128), group='gla')
    gon = [k.view(W1, W1[:, 3072 + i * 1024:3072 + (i + 1) * 1024].rearrange("p (b t) -> p b t", t=128), group='gla') for i in range(2)]
    gsps = [k.sb("gsps%d" % i, [128, 1024 if HAS_GLA else 8], BF16) for i in range(2)]
    Sg = k.view(Sst, Sst[:, 0:1024].rearrange("p (h e) -> p h e", e=256), group='gla')
    Sgb = k.view(W1, W1[:, 0:1024].rearrange("p (h e) -> p h e", e=256), group='gla')
    gob = k.view(yc, yc[:, 0:1024].rearrange("p (b t) -> p b t", t=128), group='gla')

    def gla(li, j):
        W = gla_in_w
        prefetch_w('gq', W, W.t[j, :, 0:512], 8, 512)
        norm_stage(li, 0)
        gemm(H, False, 8, W, W.t[j, :, 0:512], 512, 'fm', epi_store_fm(S1, 0, AF.Copy, 128.0 ** -0.5), tag='gq')
        prefetch_w('gk', W, W.t[j, :, 512:1024], 8, 512)
        gemm(H, False, 8, W, W.t[j, :, 512:1024], 512, 'fm', epi_store_fm(S1, 512), tag='gk')
        prefetch_w('gv', W, W.t[j, :, 1024:2048], 8, 1024)
        gemm(H, False, 8, W, W.t[j, :, 1024:2048], 1024, 'tm', epi_store_tm(S3, 0), tag='gv')
        prefetch_w('gr', W, W.t[j, :, 2048:3072], 8, 1024)
        gemm(H, False, 8, W, W.t[j, :, 2048:3072], 1024, 'fm', epi_store_fm(S6, 0, AF.Silu), tag='gr')

        k.ld(w2s, w2s[:], gla_w2, gla_w2.t[j], q='pool')
        k.ld(gbb, gbb[:], gla_gb, gla_gb.t[j].partition_broadcast(128), q='pool')

        def epi_lr(ps, cb, t, sub, last):
            k.V(lambda e: e.tensor_copy(out=lrT[:], in_=ps[0:64, :]), r=[ps], w=[lrT])
            for sbk in range(4):
                ch = t * 4 + sbk
                tsl = slice(ch * 128, (ch + 1) * 128)
                sp_ = gsps[ch % 2]
                for hf in range(2):
                    ps2 = k.psum()
                    k.M(lambda e: e.matmul(ps2[:], lrT[:, sbk * 128:(sbk + 1) * 128], w2s[:, hf * 512:(hf + 1) * 512], start=True, stop=True), r=[lrT, w2s], w=[ps2])
                    k.V(lambda e: e.tensor_tensor(out=rsb[:], in0=ps2[:], in1=gbb[:, hf * 512:(hf + 1) * 512], op=ALU.add), r=[ps2, gbb], w=[rsb])
                    k.A(lambda e: e.activation(out=rsb[:], in_=rsb[:], func=AF.Exp, scale=-1.0), r=[rsb], w=[rsb])
                    k.A(lambda e: e.activation(out=sp_[:, hf * 512:(hf + 1) * 512], in_=rsb[:], func=AF.Ln, bias=1.0, scale=1.0), r=[rsb], w=[sp_])
                k.stq(S7, S7.t[tsl, :], sp_, sp_[:])
        gemm(H, False, 8, gla_w1, gla_w1.t[j, :, :], 128, 'fm', epi_lr)
        k.enter('gla')
        pidx = 0
        for (o, L, kd) in SEQS:
            for d in (0, 1):
                if kd == 'p':
                    k.V(lambda e: e.memset(Sg[:], 0.0), w=[Sg])
                    k.V(lambda e: e.memset(Sgb[:], 0.0), w=[Sgb])
                else:
                    k.ld(Sg, Sg[:], st_gla, st_gla.t[j, d].rearrange("h d e -> d h e"))
                    k.A(lambda e: e.activation(out=Sgb[:], in_=Sg[:], func=AF.Copy), r=[Sg], w=[Sgb])
                ie = 127 if d == 0 else 0
                order = chunk_order(L, d)

                def gla_loads(n_, c):
                    t0 = o + c * 128
                    tsl = slice(t0, t0 + 128)
                    q_ = gq_fm[n_ % 2]; k_ = gk_fm[n_ % 2]; kt = gk_tm[n_ % 2]; v_ = gv_tm[n_ % 2]; sp_ = gsp_tm[n_ % 2]
                    k.ld(q_, q_[:], S1, S1.t[0:512, tsl].rearrange("(h d) t -> d h t", d=128))
                    k.ld(k_, k_[:], S1, S1.t[512:1024, tsl].rearrange("(h d) t -> d h t", d=128))
                    k.ld(kt, kt[:], S1, S1.t[512:1024, tsl], tr=True)
                    k.ld(v_, v_[:], S3, S3.t[tsl, 0:1024])
                    k.ld(sp_, sp_[:], S7, S7.t[tsl, d * 512:(d + 1) * 512])
                    if d == 1:
                        k.ld(gof[n_ % 2], gof[n_ % 2][:], S5, S5.t[0:1024, tsl].rearrange("(b p) t -> p b t", p=128))
                        k.ld(grs[n_ % 2], grs[n_ % 2][:], S6, S6.t[0:1024, tsl].rearrange("(b p) t -> p b t", p=128))
                gla_loads(0, order[0])
                for n_, c in enumerate(order):
                    if n_ + 1 < len(order):
                        gla_loads(n_ + 1, order[n_ + 1])
                    t0 = o + c * 128
                    tsl = slice(t0, t0 + 128)
                    q_ = gq_fm[n_ % 2]; k_ = gk_fm[n_ % 2]; kt = gk_tm[n_ % 2]; v_ = gv_tm[n_ % 2]; sp_ = gsp_tm[n_ % 2]
                    ps = k.psum()
                    k.M(lambda e: e.matmul(ps[:], cmb[:, VG_[d], :], sp_[:], start=True, stop=True), r=[cmb, sp_], w=[ps])
                    k.A(lambda e: e.activation(out=rsb[:], in_=ps[:], func=AF.Exp), r=[ps], w=[rsb])
                    k.V(lambda e: e.tensor_tensor(out=gkw[:], in0=kt[:], in1=rsb[:], op=ALU.mult), r=[kt, rsb], w=[gkw])
                    psc = k.psum()
                    for h in range(4):
                        k.M(lambda e: e.matmul(psc[:, h * 128:(h + 1) * 128], sp_[:, h * 128:(h + 1) * 128], cmb[:, UG_[d], :], start=True, stop=True), r=[sp_, cmb], w=[psc])
                    k.A(lambda e: e.activation(out=tmpf[:], in_=psc[:], func=AF.Exp), r=[psc], w=[tmpf])
                    k.V(lambda e: e.tensor_tensor(out=gqg[:], in0=q_[:], in1=tmpf[:].rearrange("p (h t) -> p h t", t=128), op=ALU.mult), r=[q_, tmpf], w=[gqg])
                    k.A(lambda e: e.activation(out=tmpf[:], in_=psc[:], func=AF.Exp, scale=-1.0), r=[psc], w=[tmpf])
                    k.V(lambda e: e.tensor_tensor(out=gkg[:], in0=k_[:], in1=tmpf[:].rearrange("p (h t) -> p h t", t=128), op=ALU.mult), r=[k_, tmpf], w=[gkg])
                    k.A(lambda e: e.activation(out=ecs[:, 0:4], in_=psc[:].rearrange("p (h t) -> p h t", t=128)[:, :, ie], func=AF.Exp), r=[psc], w=[ecs])
                    pss = k.psum()
                    for h in range(4):
                        k.M(lambda e: e.matmul(pss[:, h * 128:(h + 1) * 128], gkg[:, h, :], gqg[:, h, :], start=True, stop=True), r=[gkg, gqg], w=[pss])
                    k.V(lambda e: e.tensor_tensor(out=gAT[:], in0=pss[:].rearrange("p (h t) -> p h t", t=128), in1=bc(cmb[:, U_[d], :], [128, 4, 128], 1), op=ALU.mult), r=[pss, cmb], w=[gAT])
                    pso = [k.psum(), k.psum()]
                    for h in range(4):
                        for eb in range(2):
                            blk = h * 2 + eb
                            po = pso[blk // 4]
                            k.M(lambda e: e.matmul(po[:, (blk % 4) * 128:(blk % 4 + 1) * 128], v_[:, h * 256 + eb * 128:h * 256 + (eb + 1) * 128], gAT[:, h, :], start=True, stop=False), r=[v_, gAT], w=[po])
                            k.M(lambda e: e.matmul(po[:, (blk % 4) * 128:(blk % 4 + 1) * 128], Sgb[:, h, eb * 128:(eb + 1) * 128], gqg[:, h, :], start=False, stop=True), r=[Sgb, gqg], w=[po])
                    for h in range(4):
                        pst = k.psum()
                        k.M(lambda e: e.matmul(pst[:, 0:256], gkw[:, h * 128:(h + 1) * 128], v_[:, h * 256:(h + 1) * 256], start=True, stop=True), r=[gkw, v_], w=[pst])
                        k.V(lambda e: e.scalar_tensor_tensor(out=Sg[:, h, :], in0=Sg[:, h, :], scalar=ecs[:, h:h + 1], in1=pst[:, 0:256], op0=ALU.mult, op1=ALU.add), r=[Sg, ecs, pst], w=[Sg])
                    on_ = gon[n_ % 2]
                    if d == 0:
                        for hf in range(2):
                            k.A(lambda e: e.activation(out=on_[:, hf * 4:(hf + 1) * 4, :], in_=pso[hf][:].rearrange("p (b t) -> p b t", t=128), func=AF.Copy), r=[pso[hf]], w=[on_])
                        k.stq(S5, S5.t[0:1024, tsl].rearrange("(b p) t -> p b t", p=128), on_, on_[:])
                    else:
                        of_ = gof[n_ % 2]; rs_ = grs[n_ % 2]
                        for hf in range(2):
                            k.V(lambda e: e.tensor_tensor(out=gob[:, hf * 4:(hf + 1) * 4, :], in0=pso[hf][:].rearrange("p (b t) -> p b t", t=128), in1=of_[:, hf * 4:(hf + 1) * 4, :], op=ALU.add), r=[pso[hf], of_], w=[gob])
                        k.G(lambda e: e.tensor_tensor(out=gsq[:], in0=gob[:], in1=gob[:], op=ALU.mult), r=[gob], w=[gsq])
                        psn = k.psum()
                        for h in range(4):
                            for eb in range(2):
                                k.M(lambda e: e.matmul(psn[:, h * 128:(h + 1) * 128], cmb[:, ONES, :], gsq[:, h * 2 + eb, :], start=(eb == 0), stop=(eb == 1)), r=[cmb, gsq], w=[psn])
                        k.A(lambda e: e.activation(out=rsb[:], in_=psn[:], func=AF.Sqrt, bias=EPS, scale=1.0 / 256), r=[psn], w=[rsb])
                        k.V(lambda e: e.reciprocal(out=rsb[:], in_=rsb[:]), r=[rsb], w=[rsb])
                        for eb in range(2):
                            gv_ = gob[:].rearrange("p (h b) t -> p h b t", b=2)[:, :, eb, :]
                            k.V(lambda e: e.scalar_tensor_tensor(out=gv_, in0=gv_, scalar=gnw_s[:, j, eb:eb + 1], in1=rsb[:].rearrange("p (h t) -> p h t", t=128), op0=ALU.mult, op1=ALU.mult), r=[gob, gnw_s, rsb], w=[gob])
                        k.V(lambda e: e.tensor_tensor(out=on_[:], in0=gob[:], in1=rs_[:], op=ALU.mult), r=[gob, rs_], w=[on_])
                        k.stq(S5, S5.t[0:1024, tsl].rearrange("(b p) t -> p b t", p=128), on_, on_[:])
                    k.A(lambda e: e.activation(out=Sgb[:], in_=Sg[:], func=AF.Copy), r=[Sg], w=[Sgb])
                if kd == 'p':
                    k.stq(ns_gla, ns_gla.t[pidx, j, d].rearrange("h d e -> d h e"), Sg, Sg[:])
            if kd == 'p':
                pidx += 1
        k.leave('gla')
        gemm(S5, False, 8, gla_out_w, gla_out_w.t[j, :, :], 1024, 'fm', **resid_hooks(li, 2))

    gcw_s = k.sb("gcw_s", [128, 1, 32, 3], F32)
    k.ld(gcw_s, gcw_s[:], gdn_cw, gdn_cw.t[:, :, :, :])
    dnw_s = k.sb("dnw_s", [128, 1, 2], F32)
    k.ld(dnw_s, dnw_s[:], gdn_nw, gdn_nw.t[:, :, :])
    xb0f = xbuf[0][:].rearrange("p c n -> p (c n)")
    d_beta = k.view(xbuf[0], xb0f[:, 0:NCH * 16].rearrange("p (c n) -> p c n", n=16), group='gdnx')
    d_nbeta = k.view(xbuf[0], xb0f[:, NCH * 16:2 * NCH * 16].rearrange("p (c n) -> p c n", n=16), group='gdnx')
    d_lg = k.view(xbuf[0], xb0f[:, 2 * NCH * 16:3 * NCH * 16].rearrange("p (c n) -> p c n", n=16), group='gdnx')
    d_lgb = k.sb("d_lgb", [128, NCH, 16], BF16)
    dbc16 = k.sb("dbc16", [128, 16], F32); dnega = k.sb("dnega", [128, 16], F32)
    h3 = lambda ap: ap.rearrange("p (h t) -> p h t", t=128)
    dq_fm = [k.view(inb[i], h3(inb[i][:, 0:1024]), group='gdn') for i in range(2)]
    dk_fm = [k.view(inb[i], h3(inb[i][:, 1024:2048]), group='gdn') for i in range(2)]
    dk_tm = [k.view(inb[i], h3(inb[i][:, 2048:3072]), group='gdn') for i in range(2)]
    dv_tm = [k.view(inb[i], inb[i][:, 3072:5120].rearrange("p (h e) -> p h e", e=256), group='gdn') for i in range(2)]
    dof = [k.view(inb[i], h3(inb[i][:, 5120:7168]), group='gdn') for i in range(2)]
    dzs = [k.view(inb[i], h3(inb[i][:, 7168:9216]), group='gdn') for i in range(2)]
    dET = k.view(W0, h3(W0[:, 0:1024]), group='gdn'); dEN = k.view(W0, h3(W0[:, 1024:2048]), group='gdn')
    dLT = k.view(W0, h3(W0[:, 2048:3072]), group='gdn'); dLTm = k.view(W0, h3(W0[:, 3072:4096]), group='gdn'); dLms = k.view(W0, h3(W0[:, 4096:5120]), group='gdn')
    dB = [k.view(W0, h3(W0[:, 5120 + i * 1024:6144 + i * 1024]), group='gdn') for i in range(2)]
    dA = [k.view(W0, h3(W0[:, 7168 + i * 1024:8192 + i * 1024]), group='gdn') for i in range(2)]
    dRTb = k.view(W0, h3(W0[:, 9216:10240]), group='gdn'); daT = k.view(W0, h3(W0[:, 10240:11264]), group='gdn'); dqd = k.view(W0, h3(W0[:, 11264:12288]), group='gdn')
    dSb = k.view(W1, W1[:, 0:2048].rearrange("p (h e) -> p h e", e=256), group='gdn')
    dvb = k.view(W1, W1[:, 2048:4096].rearrange("p (h e) -> p h e", e=256), group='gdn')
    dsq = k.view(W1, h3(W1[:, 2048:4096]), group='gdn', share=dvb)
    dkbe = k.view(W1, h3(W1[:, 4096:5120]), group='gdn'); dnwk = k.view(W1, h3(W1[:, 5120:6144]), group='gdn')
    dvn = k.view(W1, W1[:, 6144:8192].rearrange("p (h e) -> p h e", e=256), group='gdn')
    dkd = k.view(W1, h3(W1[:, 8192:9216]), group='gdn')
    don = k.view(W1, h3(W1[:, 9216:11264]), group='gdn')
    dY = k.view(W1, h3(W1[:, 11264:12288]), group='gdn'); dY2 = k.view(W1, h3(W1[:, 5120:6144]), group='gdn', share=dnwk)
    lmask = k.sb("lmask_s", [128, 14, 128], BF16)
    k.ld(lmask, lmask[:], lmk, lmk.t[:, :, :], q='pool')
    dS = k.view(Sst, Sst[:].rearrange("p (h e) -> p h e", e=256), group='gdn')
    dRTf = k.view(yc, h3(yc[:, 0:1024]), group='gdn')
    dob = k.view(yc, h3(yc[:]), group='gdn', share=dRTf)

    def gdn(li, j):
        W = gdn_in_w
        prefetch_w('dq', W, W.t[j, :, 0:4096], 8, 4096)
        norm_stage(li, 0)
        gemm(H, False, 8, W, W.t[j, :, 0:4096], 4096, 'fm', epi_store_fm(S1, 0), tag='dq')
        prefetch_w('dz', W, W.t[j, :, 4096:6144], 8, 2048)
        conv_stage(S1, S2, 32, lambda blk, tap: gcw_s[:, j, blk, tap - 3:tap - 2], None, False, [gcw_s], l2blocks=tuple(range(16)), qscale_blocks=tuple(range(8)))
        gemm(H, False, 8, W, W.t[j, :, 4096:6144], 2048, 'fm', epi_store_fm(S6, 0, AF.Silu), tag='dz')
        prefetch_w('da', W, W.t[j, :, 6144:6176], 8, 32)
        k.ld(dbc16, dbc16[:], gdn_dtb, gdn_dtb.t[j].partition_broadcast(128))
        k.ld(dnega, dnega[:], gdn_alog, gdn_alog.t[j].partition_broadcast(128))
        k.A(lambda e: e.activation(out=dnega[:], in_=dnega[:], func=AF.Exp), r=[dnega], w=[dnega])
        k.V(lambda e: e.tensor_scalar(out=dnega[:], in0=dnega[:], scalar1=-1.0, scalar2=None, op0=ALU.mult), r=[dnega], w=[dnega])

        def epi_ab(ps, c0, t, sub, last):
            ch = t * 4 + sub
            k.A(lambda e: e.activation(out=d_beta[:, ch, :], in_=ps[:, 16:32], func=AF.Sigmoid), r=[ps], w=[d_beta])
            k.V(lambda e: e.tensor_scalar(out=d_nbeta[:, ch, :], in0=d_beta[:, ch, :], scalar1=-1.0, scalar2=None, op0=ALU.mult), r=[d_beta], w=[d_nbeta])
            k.V(lambda e: e.tensor_tensor(out=d_lg[:, ch, :], in0=ps[:, 0:16], in1=dbc16[:], op=ALU.add), r=[ps, dbc16], w=[d_lg])
            k.A(lambda e: e.activation(out=d_lg[:, ch, :], in_=d_lg[:, ch, :], func=AF.Exp), r=[d_lg], w=[d_lg])
            k.A(lambda e: e.activation(out=d_lg[:, ch, :], in_=d_lg[:, ch, :], func=AF.Ln, bias=1.0, scale=1.0), r=[d_lg], w=[d_lg])
            k.V(lambda e: e.tensor_tensor(out=d_lg[:, ch, :], in0=d_lg[:, ch, :], in1=dnega[:], op=ALU.mult), r=[d_lg, dnega], w=[d_lg])
            k.V(lambda e: e.tensor_copy(out=d_lgb[:, ch, :], in_=d_lg[:, ch, :]), r=[d_lg], w=[d_lgb])
        k.enter('gdnx')
        gemm(H, False, 8, W, W.t[j, :, 6144:6176], 32, 'tm', epi_ab, tag='da')
        k.enter('gdn')
        pidx = 0
        for (o, L, kd) in SEQS:
            for d in (0, 1):
                if kd == 'p':
                    k.V(lambda e: e.memset(dS[:], 0.0), w=[dS])
                    k.V(lambda e: e.memset(dSb[:], 0.0), w=[dSb])
                else:
                    k.ld(dS, dS[:], st_gdn, st_gdn.t[j, d].rearrange("h d e -> d h e"))
                    k.A(lambda e: e.activation(out=dSb[:], in_=dS[:], func=AF.Copy), r=[dS], w=[dSb])
                ie = 127 if d == 0 else 0
                hs = slice(d * 8, (d + 1) * 8)
                order = chunk_order(L, d)

                def gdn_loads(n_, c):
                    t0 = o + c * 128
                    tsl = slice(t0, t0 + 128)
                    q_ = dq_fm[n_ % 2]; k_ = dk_fm[n_ % 2]; kt = dk_tm[n_ % 2]; v_ = dv_tm[n_ % 2]
                    k.ld(q_, q_[:], S2, S2.t[0:1024, tsl].rearrange("(h d) t -> d h t", d=128))
                    k.ld(k_, k_[:], S2, S2.t[1024:2048, tsl].rearrange("(h d) t -> d h t", d=128))
                    for q2 in range(2):
                        k.ld(kt, kt[:, q2 * 4:(q2 + 1) * 4, :].rearrange("p h t -> p (h t)"), S2, S2.t[1024 + q2 * 512:1024 + (q2 + 1) * 512, tsl], tr=True)
                    for q4 in range(4):
                        k.ld(v_, v_[:, q4 * 2:(q4 + 1) * 2, :].rearrange("p h e -> p (h e)"), S2, S2.t[2048 + q4 * 512:2048 + (q4 + 1) * 512, tsl], tr=True)
                    if d == 1:
                        k.ld(dof[n_ % 2], dof[n_ % 2][:], S5, S5.t[0:2048, tsl].rearrange("(b p) t -> p b t", p=128))
                        k.ld(dzs[n_ % 2], dzs[n_ % 2][:], S6, S6.t[0:2048, tsl].rearrange("(b p) t -> p b t", p=128))
                gdn_loads(0, order[0])
                for n_, c in enumerate(order):
                    if n_ + 1 < len(order):
                        gdn_loads(n_ + 1, order[n_ + 1])
                    t0 = o + c * 128; ch = t0 // 128
                    tsl = slice(t0, t0 + 128)
                    q_ = dq_fm[n_ % 2]; k_ = dk_fm[n_ % 2]; kt = dk_tm[n_ % 2]; v_ = dv_tm[n_ % 2]
                    lg = d_lg[:, ch, hs]; lgb = d_lgb[:, ch, hs]; beta = d_beta[:, ch, hs]; nbeta = d_nbeta[:, ch, hs]
                    k.G(lambda e: e.tensor_tensor(out=dET[:], in0=bc(lg, [128, 8, 128], 2), in1=bc(cmb[:, U_[d], :], [128, 8, 128], 1), op=ALU.mult), r=[d_lg, cmb], w=[dET])
                    k.G(lambda e: e.tensor_tensor(out=dEN[:], in0=bc(lg, [128, 8, 128], 2), in1=bc(cmb[:, V_[d], :], [128, 8, 128], 1), op=ALU.mult), r=[d_lg, cmb], w=[dEN])
                    for hf in range(2):
                        ps = k.psum()
                        k.M(lambda e: e.matmul(ps[:], cmb[:, V_[d], :], dET[:, hf * 4:(hf + 1) * 4, :].rearrange("p h t -> p (h t)"), start=True, stop=True), r=[cmb, dET], w=[ps])
                        k.A(lambda e: e.activation(out=dLT[:, hf * 4:(hf + 1) * 4, :], in_=h3(ps[:]), func=AF.Exp), r=[ps], w=[dLT])
                        ps2 = k.psum()
                        k.M(lambda e: e.matmul(ps2[:], cmb[:, U_[d], :], dEN[:, hf * 4:(hf + 1) * 4, :].rearrange("p h t -> p (h t)"), start=True, stop=True), r=[cmb, dEN], w=[ps2])
                        k.A(lambda e: e.activation(out=dLms[:, hf * 4:(hf + 1) * 4, :], in_=h3(ps2[:]), func=AF.Exp), r=[ps2], w=[dLms])
                        ps3 = k.psum()
                        k.M(lambda e: e.matmul(ps3[:], cmb[:, ONES, :], dET[:, hf * 4:(hf + 1) * 4, :].rearrange("p h t -> p (h t)"), start=True, stop=True), r=[cmb, dET], w=[ps3])
                        k.A(lambda e: e.activation(out=tmpf[:], in_=ps3[:], func=AF.Exp), r=[ps3], w=[tmpf])
                        k.V(lambda e: e.tensor_tensor(out=dqd[:, hf * 4:(hf + 1) * 4, :], in0=q_[:, hf * 4:(hf + 1) * 4, :], in1=h3(tmpf[:]), op=ALU.mult), r=[q_, tmpf], w=[dqd])
                    k.V(lambda e: e.tensor_copy(out=ecs[:, 24:32], in_=dLT[:, :, ie]), r=[dLT], w=[ecs])
                    k.V(lambda e: e.tensor_tensor(out=dLTm[:], in0=dLT[:], in1=bc(cmb[:, U_[d], :], [128, 8, 128], 1), op=ALU.mult), r=[dLT, cmb], w=[dLTm])
                    k.V(lambda e: e.tensor_tensor(out=dLms[:], in0=dLms[:], in1=bc(cmb[:, V_[d], :], [128, 8, 128], 1), op=ALU.mult), r=[dLms, cmb], w=[dLms])
                    ps = k.psum()
                    k.M(lambda e: e.matmul(ps[:, 0:8], cmb[:, U_[d], :], lgb, start=True, stop=True), r=[cmb, d_lgb], w=[ps])
                    k.M(lambda e: e.matmul(ps[:, 8:16], cmb[:, ONES, :], lgb, start=True, stop=True), r=[cmb, d_lgb], w=[ps])
                    k.A(lambda e: e.activation(out=ecs[:, 0:16], in_=ps[:, 0:16], func=AF.Exp), r=[ps], w=[ecs])
                    k.V(lambda e: e.tensor_tensor(out=ecs[:, 16:24], in0=ecs[:, 0:8], in1=beta, op=ALU.mult), r=[ecs, d_beta], w=[ecs])
                    mN, mT, OffN, OffT = dB[0], dA[0], dB[1], dA[1]
                    dD = dLms
                    for hf in range(2):
                        ps = k.psum()
                        for h4 in range(4):
                            h = hf * 4 + h4
                            k.M(lambda e: e.matmul(ps[:, h4 * 128:(h4 + 1) * 128], k_[:, h, :], k_[:, h, :], start=True, stop=True), r=[k_], w=[ps])
                        for h4 in range(4):
                            h = hf * 4 + h4
                            k.V(lambda e: e.scalar_tensor_tensor(out=mN[:, h, :], in0=ps[:, h4 * 128:(h4 + 1) * 128], scalar=d_beta[:, ch, d * 8 + h:d * 8 + h + 1], in1=dLms[:, h, :], op0=ALU.mult, op1=ALU.mult),
                                r=[ps, d_beta, dLms], w=[mN])
                    for hf in range(2):
                        ps = k.psum()
                        for h4 in range(4):
                            h = hf * 4 + h4
                            k.M(lambda e: e.matmul(ps[:, h4 * 128:(h4 + 1) * 128], mN[:, h, :], cmb[:, ID, :], start=True, stop=True), r=[mN, cmb], w=[ps])
                        k.A(lambda e: e.activation(out=mT[:, hf * 4:(hf + 1) * 4, :], in_=h3(ps[:]), func=AF.Copy), r=[ps], w=[mT])
                    MN = lambda lv: lmask[:, (lv if d == 0 else 7 + lv), :]
                    MT_ = lambda lv: lmask[:, (7 + lv if d == 0 else lv), :]
                    k.V(lambda e: e.tensor_tensor(out=OffN[:], in0=mN[:], in1=bc(MN(0), [128, 8, 128], 1), op=ALU.mult), r=[mN, lmask], w=[OffN])
                    k.V(lambda e: e.tensor_tensor(out=dD[:], in0=bc(cmb[:, ID, :], [128, 8, 128], 1), in1=OffN[:], op=ALU.subtract), r=[cmb, OffN], w=[dD])
                    k.V(lambda e: e.tensor_tensor(out=OffT[:], in0=mT[:], in1=bc(MT_(0), [128, 8, 128], 1), op=ALU.mult), r=[mT, lmask], w=[OffT])
                    k.V(lambda e: e.tensor_tensor(out=dRTb[:], in0=bc(cmb[:, ID, :], [128, 8, 128], 1), in1=OffT[:], op=ALU.subtract), r=[cmb, OffT], w=[dRTb])
                    for lv in range(1, 7):
                        k.V(lambda e: e.tensor_tensor(out=OffN[:], in0=mN[:], in1=bc(MN(lv), [128, 8, 128], 1), op=ALU.mult), r=[mN, lmask], w=[OffN])
                        k.G(lambda e: e.tensor_tensor(out=OffT[:], in0=mT[:], in1=bc(MT_(lv), [128, 8, 128], 1), op=ALU.mult), r=[mT, lmask], w=[OffT])
                        for hf in range(2):
                            ps = k.psum()
                            for h4 in range(4):
                                h = hf * 4 + h4
                                k.M(lambda e: e.matmul(ps[:, h4 * 128:(h4 + 1) * 128], OffT[:, h, :], dD[:, h, :], start=True, stop=True), r=[OffT, dD], w=[ps])
                            k.A(lambda e: e.activation(out=dY[:, hf * 4:(hf + 1) * 4, :], in_=h3(ps[:]), func=AF.Copy), r=[ps], w=[dY])
                            ps2 = k.psum()
                            for h4 in range(4):
                                h = hf * 4 + h4
                                k.M(lambda e: e.matmul(ps2[:, h4 * 128:(h4 + 1) * 128], OffN[:, h, :], dRTb[:, h, :], start=True, stop=True), r=[OffN, dRTb], w=[ps2])
                            k.A(lambda e: e.activation(out=dY2[:, hf * 4:(hf + 1) * 4, :], in_=h3(ps2[:]), func=AF.Copy), r=[ps2], w=[dY2])
                        pX = [k.psum(), k.psum()]; pX2 = [k.psum(), k.psum()]
                        for hf in range(2):
                            for h4 in range(4):
                                h = hf * 4 + h4
                                k.M(lambda e: e.matmul(pX[hf][:, h4 * 128:(h4 + 1) * 128], dRTb[:, h, :], dY[:, h, :], start=True, stop=True), r=[dRTb, dY], w=[pX[hf]])
                            for h4 in range(4):
                                h = hf * 4 + h4
                                k.M(lambda e: e.matmul(pX2[hf][:, h4 * 128:(h4 + 1) * 128], dD[:, h, :], dY2[:, h, :], start=True, stop=True), r=[dD, dY2], w=[pX2[hf]])
                        for hf in range(2):
                            k.V(lambda e: e.tensor_tensor(out=dD[:, hf * 4:(hf + 1) * 4, :], in0=dD[:, hf * 4:(hf + 1) * 4, :], in1=h3(pX[hf][:]), op=ALU.subtract), r=[dD, pX[hf]], w=[dD])
                            k.V(lambda e: e.tensor_tensor(out=dRTb[:, hf * 4:(hf + 1) * 4, :], in0=dRTb[:, hf * 4:(hf + 1) * 4, :], in1=h3(pX2[hf][:]), op=ALU.subtract), r=[dRTb, pX2[hf]], w=[dRTb])
                    k.G(lambda e: e.tensor_tensor(out=dvb[:], in0=v_[:], in1=bc(beta, [128, 8, 256], 2), op=ALU.mult), r=[v_, d_beta], w=[dvb])
                    k.G(lambda e: e.tensor_tensor(out=dkbe[:], in0=kt[:], in1=bc(ecs[:, 16:24], [128, 8, 128], 2), op=ALU.mult), r=[kt, ecs], w=[dkbe])
                    k.G(lambda e: e.tensor_tensor(out=dkd[:], in0=kt[:], in1=bc(ecs[:, 24:32], [128, 8, 128], 2), op=ALU.mult), r=[kt, ecs], w=[dkd])
                    for hf in range(2):
                        ps = k.psum()
                        for h4 in range(4):
                            h = hf * 4 + h4
                            k.M(lambda e: e.matmul(ps[:, h4 * 128:(h4 + 1) * 128], dkbe[:, h, :], dRTb[:, h, :], start=True, stop=True), r=[dkbe, dRTb], w=[ps])
                        k.A(lambda e: e.activation(out=dnwk[:, hf * 4:(hf + 1) * 4, :], in_=h3(ps[:]), func=AF.Copy, scale=-1.0), r=[ps], w=[dnwk])
                    for hp in range(4):
                        ps = k.psum()
                        for h2 in range(2):
                            h = hp * 2 + h2
                            k.M(lambda e: e.matmul(ps[:, h2 * 256:(h2 + 1) * 256], dRTb[:, h, :], dvb[:, h, :], start=True, stop=False), r=[dRTb, dvb], w=[ps])
                            k.M(lambda e: e.matmul(ps[:, h2 * 256:(h2 + 1) * 256], dnwk[:, h, :], dSb[:, h, :], start=False, stop=True), r=[dnwk, dSb], w=[ps])
                        k.A(lambda e: e.activation(out=dvn[:, hp * 2:(hp + 1) * 2, :], in_=ps[:].rearrange("p (h e) -> p h e", e=256), func=AF.Copy), r=[ps], w=[dvn])
                    for hf in range(2):
                        ps = k.psum()
                        for h4 in range(4):
                            h = hf * 4 + h4
                            k.M(lambda e: e.matmul(ps[:, h4 * 128:(h4 + 1) * 128], k_[:, h, :], q_[:, h, :], start=True, stop=True), r=[k_, q_], w=[ps])
                        k.V(lambda e: e.tensor_tensor(out=daT[:, hf * 4:(hf + 1) * 4, :], in0=h3(ps[:]), in1=dLTm[:, hf * 4:(hf + 1) * 4, :], op=ALU.mult), r=[ps, dLTm], w=[daT])
                    pso = [k.psum() for _ in range(4)]
                    for h in range(8):
                        for eb in range(2):
                            blk = h * 2 + eb
                            po = pso[blk // 4]; cs_ = slice((blk % 4) * 128, (blk % 4 + 1) * 128)
                            k.M(lambda e: e.matmul(po[:, cs_], dSb[:, h, eb * 128:(eb + 1) * 128], dqd[:, h, :], start=True, stop=False), r=[dSb, dqd], w=[po])
                            k.M(lambda e: e.matmul(po[:, cs_], dvn[:, h, eb * 128:(eb + 1) * 128], daT[:, h, :], start=False, stop=True), r=[dvn, daT], w=[po])
                    if d == 0:
                        for q4 in range(4):
                            k.A(lambda e: e.activation(out=don[:, q4 * 4:(q4 + 1) * 4, :], in_=h3(pso[q4][:]), func=AF.Copy), r=[pso[q4]], w=[don])
                        k.stq(S5, S5.t[0:2048, tsl].rearrange("(b p) t -> p b t", p=128), don, don[:])
                    else:
                        of_ = dof[n_ % 2]; zs_ = dzs[n_ % 2]
                        for q4 in range(4):
                            k.V(lambda e: e.tensor_tensor(out=dob[:, q4 * 4:(q4 + 1) * 4, :], in0=h3(pso[q4][:]), in1=of_[:, q4 * 4:(q4 + 1) * 4, :], op=ALU.add), r=[pso[q4], of_], w=[dob])
                    for h in range(8):
                        pst = k.psum()
                        k.M(lambda e: e.matmul(pst[:, 0:256], dkd[:, h, :], dvn[:, h, :], start=True, stop=True), r=[dkd, dvn], w=[pst])
                        k.V(lambda e: e.scalar_tensor_tensor(out=dS[:, h, :], in0=dS[:, h, :], scalar=ecs[:, 8 + h:9 + h], in1=pst[:, 0:256], op0=ALU.mult, op1=ALU.add), r=[dS, ecs, pst], w=[dS])
                    k.A(lambda e: e.activation(out=dSb[:], in_=dS[:], func=AF.Copy), r=[dS], w=[dSb])
                    if d == 1:
                        k.G(lambda e: e.tensor_tensor(out=dsq[:], in0=dob[:], in1=dob[:], op=ALU.mult), r=[dob], w=[dsq])
                        for hf in range(2):
                            psn = k.psum()
                            for h4 in range(4):
                                h = hf * 4 + h4
                                for eb in range(2):
                                    k.M(lambda e: e.matmul(psn[:, h4 * 128:(h4 + 1) * 128], cmb[:, ONES, :], dsq[:, h * 2 + eb, :], start=(eb == 0), stop=(eb == 1)), r=[cmb, dsq], w=[psn])
                            k.A(lambda e: e.activation(out=rsb[:], in_=psn[:], func=AF.Sqrt, bias=EPS, scale=1.0 / 256), r=[psn], w=[rsb])
                            k.V(lambda e: e.reciprocal(out=rsb[:], in_=rsb[:]), r=[rsb], w=[rsb])
                            for eb in range(2):
                                gv_ = dob[:, hf * 8:(hf + 1) * 8, :].rearrange("p (h b) t -> p h b t", b=2)[:, :, eb, :]
                                k.V(lambda e: e.scalar_tensor_tensor(out=gv_, in0=gv_, scalar=dnw_s[:, j, eb:eb + 1], in1=h3(rsb[:]), op0=ALU.mult, op1=ALU.mult), r=[dob, dnw_s, rsb], w=[dob])
                        k.V(lambda e: e.tensor_tensor(out=don[:], in0=dob[:], in1=zs_[:], op=ALU.mult), r=[dob, zs_], w=[don])
                        k.stq(S5, S5.t[0:2048, tsl].rearrange("(b p) t -> p b t", p=128), don, don[:])
                if kd == 'p':
                    k.stq(ns_gdn, ns_gdn.t[pidx, j, d].rearrange("h d e -> d h e"), dS, dS[:])
            if kd == 'p':
                pidx += 1
        k.leave('gdn'); k.leave('gdnx')
        gemm(S5, False, 16, gdn_out_w, gdn_out_w.t[j, :, :], 1024, 'fm', wsplit=True, **resid_hooks(li, 2))

    wdec = k.sb("wdec", [128, 32], F32)

    for (li, kind, j) in LAYERS:
        if kind == 'ssd':
            ssd(li, j)
        elif kind == 'gla':
            gla(li, j)
        elif kind == 'gdn':
            gdn(li, j)
        elif kind == 'none':
            pass
        else:
            raise NotImplementedError(kind)
        if kind != 'none' or True:
            ffn(li)
    norm_stage(0, 0, final=True)
    outs = [yT, X, ns_ssm, ns_gla, ns_gdn, S1, S2, S3, S4, S5, H]
    k.P.wait_all('sp', [o.b for o in outs])
    k.P.run()
    st.close()
    return nc, k.P.ninst


def _consts():
    i = np.arange(128)
    kk, ii = np.meshgrid(i, i, indexing='ij')
    cm = np.zeros((128, 10, 128), np.float32)
    cm[:, 0] = (kk <= ii); cm[:, 1] = (kk >= ii); cm[:, 2] = (kk > ii); cm[:, 3] = (kk < ii)
    cm[:, 4] = (kk == ii); cm[:, 5] = 1.0
    cm[:, 6] = (kk <= ii) / -16.0; cm[:, 7] = (kk >= ii) / -16.0; cm[:, 8] = (kk > ii) / -16.0; cm[:, 9] = (kk < ii) / -16.0
    return cm


def _fm(v, nblk):
    v = np.asarray(v, np.float32)
    lead = v.shape[:-1]
    v = v.reshape(*lead, nblk, 128)
    return np.ascontiguousarray(np.moveaxis(v, -1, 0))


def prep_common(w):
    d = {}
    d['cmask'] = _consts()
    ii, jj = np.meshgrid(np.arange(128), np.arange(128), indexing='ij')
    lm = np.zeros((128, 14, 128), np.float32)
    for lv in range(7):
        b = 1 << lv
        m = ((ii // (2 * b)) == (jj // (2 * b))) & ((ii % (2 * b)) >= b) & ((jj % (2 * b)) < b)
        lm[:, lv] = m; lm[:, 7 + lv] = m.T
    d['lmask'] = lm
    d['nmw'] = _fm(w['norm_mix_w'], 8); d['nfw'] = _fm(w['norm_ffn_w'], 8); d['fnw'] = _fm(w['final_norm_w'], 8)
    d['ada_w'] = w['ada_w']; d['ada_b'] = _fm(w['ada_b'], 48)
    d['ffn_in_w'] = w['ffn_in_w']; d['ffn_out_w'] = w['ffn_out_w']
    d['ffn_cw'] = _fm(np.moveaxis(np.asarray(w['ffn_conv_w']).reshape(4, 9, 2816), 1, 2).reshape(4, 2816 * 9).reshape(4, 22, 128, 9).transpose(0, 1, 3, 2).reshape(4, 22, 9, 128), 1)[..., 0] if False else None
    cw = np.asarray(w['ffn_conv_w'], np.float32).reshape(4, 9, 22, 128)
    d['ffn_cw'] = np.ascontiguousarray(cw.transpose(3, 0, 2, 1))
    d['ffn_cb'] = _fm(w['ffn_conv_b'], 22)
    d['ssm_in_w'] = w['ssm_in_w']
    cw = np.asarray(w['ssm_conv_w'], np.float32).reshape(2, 3, 32, 128)
    d['ssm_cw'] = np.ascontiguousarray(cw.transpose(3, 0, 2, 1))
    d['ssm_cb'] = _fm(w['ssm_conv_b'], 32)
    d['ssm_dtb'] = np.asarray(w['ssm_dt_bias'], np.float32).reshape(2, 1, 64)
    d['ssm_alog'] = np.asarray(w['ssm_a_log'], np.float32).reshape(2, 1, 64)
    d['ssm_d'] = np.asarray(w['ssm_d'], np.float32).reshape(2, 1, 32)
    d['ssm_nw'] = np.asarray(w['ssm_norm_w'], np.float32).reshape(2, 1, 2048)
    d['ssm_out_w'] = w['ssm_out_w']
    d['gla_in_w'] = w['gla_in_w']
    w1 = np.zeros((1, 1024, 128), np.float32)
    g1 = np.asarray(w['gla_gate_w1'], np.float32)
    w1[:, :, 0:16] = g1[:, 0]; w1[:, :, 32:48] = g1[:, 1]
    d['gla_w1'] = w1
    w2 = np.zeros((1, 64, 2, 512), np.float32)
    g2 = np.asarray(w['gla_gate_w2'], np.float32)
    w2[:, 0:16, 0] = g2[:, 0]; w2[:, 32:48, 1] = g2[:, 1]
    d['gla_w2'] = w2.reshape(1, 64, 1024)
    d['gla_gb'] = np.asarray(w['gla_gate_b'], np.float32).reshape(1, 1, 1024)
    d['gla_nw'] = _fm(w['gla_norm_w'], 2); d['gla_out_w'] = w['gla_out_w']
    d['gdn_in_w'] = w['gdn_in_w']
    cw = np.asarray(w['gdn_conv_w'], np.float32).reshape(1, 3, 32, 128)
    d['gdn_cw'] = np.ascontiguousarray(cw.transpose(3, 0, 2, 1))
    d['gdn_dtb'] = np.asarray(w['gdn_dt_bias'], np.float32).reshape(1, 1, 16)
    d['gdn_alog'] = np.asarray(w['gdn_a_log'], np.float32).reshape(1, 1, 16)
    d['gdn_nw'] = _fm(w['gdn_norm_w'], 2); d['gdn_out_w'] = w['gdn_out_w']
    return {k_: np.ascontiguousarray(np.asarray(v, np.float32)) for k_, v in d.items()}


FULL_SEQS = [(0, 256, 'p'), (256, 256, 'p'), (512, 4096, 's')]
FULL_LAYERS = [(0, 'ssd', 0), (1, 'gla', 0), (2, 'gdn', 0), (3, 'ssd', 1)]


def kernel(**inp):
    w = {k_: np.asarray(v) for k_, v in inp.items()}
    common = prep_common(w)
    nc, ninst = build_program(FULL_SEQS, FULL_LAYERS)
    xp = np.asarray(w['x_prompt'], np.float32); xs = np.asarray(w['x_sample'], np.float32)
    in_maps = []
    for core in range(8):
        b = core % 2
        xcat = np.concatenate([xp[2 * core], xp[2 * core + 1], xs[b]], axis=0)
        m = dict(common)
        m['xT'] = np.ascontiguousarray(xcat.T)
        cond = np.stack([np.asarray(w['c_ctx'], np.float32), np.asarray(w['c'], np.float32)[b]], axis=1)
        m['condT'] = np.ascontiguousarray(cond.reshape(8, 128, 2).transpose(1, 0, 2))
        ss = np.asarray(w['state_ssm'], np.float32)[b]
        m['st_ssm'] = np.ascontiguousarray(ss.transpose(0, 1, 3, 2, 4).reshape(2, 2, 128, 2048))
        m['st_gla'] = np.ascontiguousarray(np.asarray(w['state_gla'], np.float32)[b])
        m['st_gdn'] = np.ascontiguousarray(np.asarray(w['state_gdn'], np.float32)[b])
        in_maps.append(m)
    res = run_bass_kernel_spmd(nc, in_maps, core_ids=list(range(8)))
    y_prompt = np.zeros((16, 256, 1024), np.float32); y_sample = np.zeros((2, 4096, 1024), np.float32)
    n_ssm = np.zeros((16, 2, 2, 32, 128, 64), np.float32); n_gla = np.zeros((16, 1, 2, 4, 128, 256), np.float32)
    n_gdn = np.zeros((16, 1, 2, 8, 128, 256), np.float32)
    for core in range(8):
        r = res.results[core]
        y = np.asarray(r['yT']).T
        y_prompt[2 * core] = y[0:256]; y_prompt[2 * core + 1] = y[256:512]
        if core < 2:
            y_sample[core] = y[512:]
        s_ = np.asarray(r['ns_ssm']).reshape(2, 2, 2, 128, 32, 64).transpose(0, 1, 2, 4, 3, 5)
        n_ssm[2 * core:2 * core + 2] = s_
        n_gla[2 * core:2 * core + 2] = np.asarray(r['ns_gla'])
        n_gdn[2 * core:2 * core + 2] = np.asarray(r['ns_gdn'])
    return (y_prompt, y_sample, n_ssm, n_gla, n_gdn)
```

```python
import contextlib
import numpy as np
import concourse.bass as bass
import concourse.mybir as mybir
from concourse.bass_utils import run_bass_kernel_spmd

F32 = mybir.dt.float32
BF16 = mybir.dt.bfloat16
AF = mybir.ActivationFunctionType
ALU = mybir.AluOpType
AX = mybir.AxisListType
ENGS = ['pe', 'act', 'dve', 'pool', 'sp']
EPS = 1e-6
DBG_STOP = None


class Buf:
    __slots__ = ('name', 'lw', 'rd', 'acc')

    def __init__(self, name, acc=False):
        self.name = name
        self.lw = {}
        self.rd = {}
        self.acc = acc


class _Rec:
    def __init__(self):
        self.call = None

    def __getattr__(self, name):
        def f(*a, **kw):
            self.call = (name, a, kw)
            return self
        return f


class Prog:
    def __init__(self, nc):
        self.nc = nc
        self.q = {e: [] for e in ENGS}
        self.cnt = {e: 0 for e in ENGS}
        self.known = {e: {} for e in ENGS}
        self.dma_cnt = {}
        self.ninst = 0

    def _need(self, eng, key, val, src, waits):
        if src == eng and eng == 'pe':
            return
        if self.known[eng].get(key, 0) >= val:
            return
        if waits.get(key, 0) < val:
            waits[key] = val

    def emit(self, eng, fn, reads=(), writes=(), dma_key=None):
        rec = _Rec()
        fn(rec)
        _c = rec.call
        fn = lambda e, _c=_c: getattr(e, _c[0])(*_c[1], **_c[2])
        waits = {}
        for b in reads:
            for k, (v, s) in b.lw.items():
                self._need(eng, k, v, s, waits)
        for b in writes:
            if not b.acc:
                for k, (v, s) in b.lw.items():
                    self._need(eng, k, v, s, waits)
            for k, (v, s) in b.rd.items():
                self._need(eng, k, v, s, waits)
        for k, v in waits.items():
            self.known[eng][k] = v
        if dma_key is None:
            self.cnt[eng] += 1
            key, val, src, inc = eng, self.cnt[eng], eng, 1
        else:
            self.dma_cnt[dma_key] = self.dma_cnt.get(dma_key, 0) + 16
            key, val, src, inc = dma_key, self.dma_cnt[dma_key], None, 16
        for b in reads:
            if b.rd.get(key, (0, None))[0] < val:
                b.rd[key] = (val, src)
        for b in writes:
            if b.acc:
                b.lw[key] = (val, src)
            else:
                b.lw = {key: (val, src)}
            b.rd = {}
        self.q[eng].append((list(waits.items()), fn, (key, inc)))
        self.ninst += 1
        return (key, val, src)

    def wait_all(self, eng, bufs):
        waits = {}
        for b in bufs:
            for k, (v, s) in b.lw.items():
                self._need(eng, k, v, s, waits)
        self.q[eng].append((list(waits.items()), None, None))

    def run(self):
        nc = self.nc
        keys = list(ENGS) + sorted(self.dma_cnt.keys())
        with contextlib.ExitStack() as st:
            sems = {k: st.enter_context(nc.semaphore("s%d" % i)) for i, k in enumerate(keys)}
            block = st.enter_context(nc.Block())

            def mk(engname):
                def body(e):
                    for waits, fn, inc in self.q[engname]:
                        for k, v in waits:
                            e.wait_ge(sems[k], v)
                        if fn is not None:
                            fn(e).then_inc(sems[inc[0]], inc[1])
                return body
            block.tensor(mk('pe'))
            block.scalar(mk('act'))
            block.vector(mk('dve'))
            block.gpsimd(mk('pool'))
            block.sync(mk('sp'))


class Tl:
    def __init__(self, t, name, acc=False):
        self.t = t
        self.b = Buf(name, acc)
        self.name = name

    def __getitem__(self, k):
        return self.t[k]


class KB:
    def __init__(self, nc, st):
        self.nc = nc
        self.st = st
        self.P = Prog(nc)
        self.ps = [Tl(st.enter_context(nc.psum_tensor("ps%d" % i, [128, 512], F32)), "ps%d" % i) for i in range(8)]
        self.psi = 0
        self.nkey = 0
        self.keymap = {}
        self.groups = {}
        self.nview = 0
        self.cast_tok = Buf('castdma', acc=True)
        self.tr_tok = Buf('trdma', acc=True)

    def sb(self, name, shape, dt):
        return Tl(self.st.enter_context(self.nc.sbuf_tensor(name, shape, dt)), name)

    def dram(self, name, shape, dt, kind=None):
        if kind is None:
            t = self.nc.dram_tensor(name, shape, dt)
        else:
            t = self.nc.dram_tensor(name, shape, dt, kind=kind)
        return Tl(t.ap(), name, acc=True)

    def view(self, parent, ap, group=None, share=None):
        v = Tl(ap, parent.name)
        if group is None:
            v.b = parent.b
        else:
            self.nview += 1
            v.name = "%s#%d" % (parent.name, self.nview)
            if share is not None:
                v.b = share.b
                v.name = share.name
            self.groups.setdefault(group, []).append((parent, v))
        return v

    @staticmethod
    def _union(dicts):
        out = {}
        for d_ in dicts:
            for k_, (v_, s_) in d_.items():
                if out.get(k_, (0, None))[0] < v_:
                    out[k_] = (v_, s_)
        return out

    def enter(self, group):
        for parent, v in self.groups.get(group, []):
            v.b.lw = self._union([parent.b.lw, parent.b.rd])
            v.b.rd = {}

    def leave(self, group):
        parents = {}
        for parent, v in self.groups.get(group, []):
            parents.setdefault(id(parent), (parent, []))[1].append(v)
        for parent, vs in parents.values():
            parent.b.lw = self._union([parent.b.lw, parent.b.rd] + [v.b.lw for v in vs] + [v.b.rd for v in vs])
            parent.b.rd = {}

    def psum(self):
        p = self.ps[self.psi % 8]
        self.psi += 1
        return p

    def _b(self, l):
        return [x.b for x in l]

    def M(self, fn, r=(), w=()):
        return self.P.emit('pe', fn, self._b(r), self._b(w))

    def A(self, fn, r=(), w=()):
        return self.P.emit('act', fn, self._b(r), self._b(w))

    def V(self, fn, r=(), w=()):
        return self.P.emit('dve', fn, self._b(r), self._b(w))

    def G(self, fn, r=(), w=()):
        return self.P.emit('pool', fn, self._b(r), self._b(w))

    def _key(self, s):
        if s not in self.keymap:
            self.keymap[s] = "d%04d" % len(self.keymap)
        return self.keymap[s]

    def ld(self, dst, dst_ap, src, src_ap, q='sp', tr=False):
        key = self._key('ld:' + dst.name)
        if tr:
            fn = lambda e: e.dma_start_transpose(out=dst_ap, in_=src_ap)
        else:
            fn = lambda e: e.dma_start(out=dst_ap, in_=src_ap)
        if tr:
            return self.P.emit(q, fn, [src.b, self.cast_tok], [dst.b, self.tr_tok], dma_key=key)
        if q == 'pool':
            return self.P.emit(q, fn, [src.b, self.tr_tok], [dst.b, self.cast_tok], dma_key=key)
        return self.P.emit(q, fn, [src.b], [dst.b], dma_key=key)

    def stq(self, dst, dst_ap, src, src_ap, q='sp'):
        key = self._key('st:' + src.name)
        return self.P.emit(q, lambda e: e.dma_start(out=dst_ap, in_=src_ap), [src.b], [dst.b], dma_key=key)


def bc(ap, shape, axis):
    return ap.unsqueeze(axis).broadcast_to(shape)


def build_program(SEQS, LAYERS, dbg=()):
    T = sum(L for _, L, _ in SEQS)
    NT = T // 512
    assert T % 512 == 0
    NCH = T // 128
    NP = sum(1 for s in SEQS if s[2] == 'p')
    nc = bass.Bass("TRN2", target_bir_lowering=False)
    st = contextlib.ExitStack()
    k = KB(nc, st)
    IN = lambda name, shape, dt=F32: k.dram(name, shape, dt, kind="ExternalInput")
    OUT = lambda name, shape, dt=F32: k.dram(name, shape, dt, kind="ExternalOutput")
    dbgset = set(dbg)
    SCR = lambda name, shape, dt=BF16: k.dram(name, shape, dt, kind=("ExternalOutput" if name in dbgset else None))

    xT = IN("xT", [1024, T])
    condT = IN("condT", [128, 8, 2])
    cm = IN("cmask", [128, 10, 128])
    nmw = IN("nmw", [128, 4, 8]); nfw = IN("nfw", [128, 4, 8]); fnw = IN("fnw", [128, 8])
    ada_w = IN("ada_w", [4, 1024, 6144]); ada_b = IN("ada_b", [128, 4, 48])
    ffn_in_w = IN("ffn_in_w", [4, 1024, 5632]); ffn_cw = IN("ffn_cw", [128, 4, 22, 9]); ffn_cb = IN("ffn_cb", [128, 4, 22])
    ffn_out_w = IN("ffn_out_w", [4, 2816, 1024])
    ssm_in_w = IN("ssm_in_w", [2, 1024, 6208]); ssm_cw = IN("ssm_cw", [128, 2, 32, 3]); ssm_cb = IN("ssm_cb", [128, 2, 32])
    ssm_dtb = IN("ssm_dtb", [2, 1, 64]); ssm_alog = IN("ssm_alog", [2, 1, 64]); ssm_d = IN("ssm_d", [2, 1, 32])
    ssm_nw = IN("ssm_nw", [2, 1, 2048]); ssm_out_w = IN("ssm_out_w", [2, 2048, 1024])
    gla_in_w = IN("gla_in_w", [1, 1024, 3072]); gla_w1 = IN("gla_w1", [1, 1024, 128]); gla_w2 = IN("gla_w2", [1, 64, 1024]); gla_gb = IN("gla_gb", [1, 1, 1024])
    gla_nw = IN("gla_nw", [128, 1, 2]); gla_out_w = IN("gla_out_w", [1, 1024, 1024])
    gdn_in_w = IN("gdn_in_w", [1, 1024, 6176]); gdn_cw = IN("gdn_cw", [128, 1, 32, 3])
    gdn_dtb = IN("gdn_dtb", [1, 1, 16]); gdn_alog = IN("gdn_alog", [1, 1, 16]); gdn_nw = IN("gdn_nw", [128, 1, 2])
    gdn_out_w = IN("gdn_out_w", [1, 2048, 1024])
    lmk = IN("lmask", [128, 14, 128])
    st_ssm = IN("st_ssm", [2, 2, 128, 2048]); st_gla = IN("st_gla", [1, 2, 4, 128, 256]); st_gdn = IN("st_gdn", [1, 2, 8, 128, 256])
    yT = OUT("yT", [1024, T])
    ns_ssm = OUT("ns_ssm", [max(NP, 1), 2, 2, 128, 2048]); ns_gla = OUT("ns_gla", [max(NP, 1), 1, 2, 4, 128, 256])
    ns_gdn = OUT("ns_gdn", [max(NP, 1), 1, 2, 8, 128, 256])
    X = k.dram("X", [1024, T], F32, kind=("ExternalOutput" if "X" in dbgset else None))
    H = SCR("H", [1024, T])
    S1 = SCR("S1", [4096, T])
    S2 = SCR("S2", [4096, T])
    S3 = SCR("S3", [T, 2048])
    S4 = SCR("S4", [T, 2048])
    S5 = SCR("S5", [2816, T])
    S6 = SCR("S6", [2048, T])
    S7 = SCR("S7", [T, 1024])

    cmb = k.sb("cmb", [128, 10, 128], BF16)
    k.ld(cmb, cmb[:], cm, cm.t[:, :, :], q='pool')
    Uf, Ub, Vf, Vb, ID, ONES, UGf, UGb, VGf, VGb = range(10)
    UG_ = {0: UGf, 1: UGb}; VG_ = {0: VGf, 1: VGb}
    U_ = {0: Uf, 1: Ub}; V_ = {0: Vf, 1: Vb}
    nmw_s = k.sb("nmw_s", [128, 4, 8], F32); nfw_s = k.sb("nfw_s", [128, 4, 8], F32); fnw_s = k.sb("fnw_s", [128, 8], F32)
    k.ld(nmw_s, nmw_s[:], nmw, nmw.t[:, :, :]); k.ld(nfw_s, nfw_s[:], nfw, nfw.t[:, :, :]); k.ld(fnw_s, fnw_s[:], fnw, fnw.t[:, :])
    adab_s = k.sb("adab_s", [128, 4, 48], F32)
    k.ld(adab_s, adab_s[:], ada_b, ada_b.t[:, :, :])
    mod = k.sb("mod", [128, 4, 48, 2], F32)
    gm = k.sb("gm", [128, 4, 2, 8, 2], F32)

    xbuf = [k.sb("xb%d" % i, [128, 8, 512], F32) for i in range(2)]
    for t in range(NT):
        xb = xbuf[t % 2]
        k.ld(xb, xb[:], xT, xT.t[:, t * 512:(t + 1) * 512].rearrange("(c p) t -> p c t", p=128))
        k.stq(X, X.t[:, t * 512:(t + 1) * 512].rearrange("(c p) t -> p c t", p=128), xb, xb[:])
    cnd = k.sb("cnd", [128, 8, 2], F32); cndb = k.sb("cndb", [128, 8, 2], BF16)
    k.ld(cnd, cnd[:], condT, condT.t[:, :, :])
    k.A(lambda e: e.activation(out=cndb[:], in_=cnd[:], func=AF.Silu), r=[cnd], w=[cndb])
    wbuf = [k.sb("wb%d" % i, [128, 12288], BF16) for i in range(2)]
    wi = [0]

    def load_w(src, src_ap3, kc, ncols, after=()):
        wb = wbuf[wi[0] % 2]; wi[0] += 1
        view = wb[:, 0:kc * ncols].rearrange("p (c n) -> p c n", c=kc)
        key = k._key('ld:' + wb.name)
        for c in range(kc):
            k.P.emit('pool', lambda e: e.dma_start(out=view[:, c, :], in_=src_ap3[c * 128:(c + 1) * 128, :]), [src.b, k.tr_tok] + [a.b for a in after], [wb.b, k.cast_tok], dma_key=key)
        return wb, view

    for (li, kind, j) in LAYERS:
        for qd in range(6):
            wb, wv = load_w(ada_w, ada_w.t[li, :, qd * 1024:(qd + 1) * 1024], 8, 1024)
            ps = k.psum()
            for cb in range(8):
                for c in range(8):
                    k.M(lambda e, c=c, cb=cb, wv=wv, ps=ps: e.matmul(ps[:, cb * 2:cb * 2 + 2], wv[:, c, cb * 128:(cb + 1) * 128], cndb[:, c, :],
                                                                       start=(c == 0), stop=(c == 7)), r=[wb, cndb], w=[ps])
            k.V(lambda e, ps=ps, qd=qd, li=li: e.tensor_tensor(out=mod[:, li, qd * 8:(qd + 1) * 8, :], in0=ps[:, 0:16].rearrange("p (a b) -> p a b", b=2),
                                                                in1=bc(adab_s[:, li, qd * 8:(qd + 1) * 8], [128, 8, 2], 2), op=ALU.add), r=[ps, adab_s], w=[mod])
        for sub, nw in ((0, nmw_s), (1, nfw_s)):
            sc = 1 + 3 * sub
            k.V(lambda e, sub=sub, li=li, sc=sc: e.tensor_scalar(out=gm[:, li, sub], in0=mod[:, li, sc * 8:(sc + 1) * 8, :], scalar1=1.0, scalar2=None, op0=ALU.add), r=[mod], w=[gm])
            k.V(lambda e, sub=sub, li=li, nw=nw: e.tensor_tensor(out=gm[:, li, sub], in0=gm[:, li, sub], in1=bc(nw[:, li, :], [128, 8, 2], 2), op=ALU.mult), r=[gm, nw], w=[gm])

    def ci_of_tile(t):
        off = t * 512
        for (o, L, kd) in SEQS:
            if o <= off < o + L:
                return 0 if kd == 'p' else 1
        raise ValueError

    sqb = k.sb("sqb", [128, 8, 512], BF16)
    rsb = k.sb("rsb", [128, 512], F32)
    inb = [k.sb("inb%d" % i, [128, 22 * 512], BF16) for i in range(2)]
    hb = [k.view(inb[i], inb[i][:, 0:4096].rearrange("p (c n) -> p c n", c=8)) for i in range(2)]
    tmpf = k.sb("tmpf", [128, 512], F32)
    tmpf2 = k.sb("tmpf2", [128, 512], F32)

    def norm_stage(li, sub, final=False):
        def nload(t):
            k.ld(xbuf[t % 2], xbuf[t % 2][:], X, X.t[:, t * 512:(t + 1) * 512].rearrange("(c p) t -> p c t", p=128))
        nload(0)
        for t in range(NT):
            ci = ci_of_tile(t)
            xb = xbuf[t % 2]
            sl = slice(t * 512, (t + 1) * 512)
            if t + 1 < NT:
                nload(t + 1)
            k.A(lambda e, xb=xb: e.activation(out=sqb[:], in_=xb[:], func=AF.Square), r=[xb], w=[sqb])
            ps = k.psum()
            for c in range(8):
                k.M(lambda e, c=c, ps=ps: e.matmul(ps[:], cmb[:, ONES, :], sqb[:, c, :], start=(c == 0), stop=(c == 7)), r=[cmb, sqb], w=[ps])
            k.A(lambda e, ps=ps: e.activation(out=rsb[:], in_=ps[:], func=AF.Sqrt, bias=EPS, scale=1.0 / 1024), r=[ps], w=[rsb])
            k.V(lambda e: e.reciprocal(out=rsb[:], in_=rsb[:]), r=[rsb], w=[rsb])
            if final:
                for c in range(8):
                    k.V(lambda e, c=c, xb=xb: e.scalar_tensor_tensor(out=xb[:, c, :], in0=xb[:, c, :], scalar=fnw_s[:, c:c + 1], in1=rsb[:], op0=ALU.mult, op1=ALU.mult),
                        r=[xb, fnw_s, rsb], w=[xb])
                k.stq(yT, yT.t[:, sl].rearrange("(c p) t -> p c t", p=128), xb, xb[:])
            else:
                h = hb[t % 2]
                shq = 0 if sub == 0 else 3
                for c in range(8):
                    tq = tmpf if c % 2 == 0 else tmpf2
                    k.V(lambda e, c=c, xb=xb: e.scalar_tensor_tensor(out=tq[:], in0=xb[:, c, :], scalar=gm[:, li, sub, c, ci:ci + 1], in1=rsb[:], op0=ALU.mult, op1=ALU.mult),
                        r=[xb, gm, rsb], w=[tq])
                    k.A(lambda e, c=c, h=h: e.activation(out=h[:, c, :], in_=tq[:], func=AF.Identity, bias=mod[:, li, shq * 8 + c, ci:ci + 1], scale=1.0),
                        r=[tq, mod], w=[h])
                k.stq(H, H.t[:, sl].rearrange("(c p) t -> p c t", p=128), h, h[:])

    ini = [0]

    pref = {}

    def prefetch_w(tag, W, Wap, kc, ncols, gw=None):
        if gw is None:
            gw = max(128, (12288 // kc) // 128 * 128)
            gw = min(gw, 2048)
        gn_ = min(gw, ncols)
        pref[tag] = load_w(W, Wap[:, 0:gn_], kc, gn_)

    def gemm(src, src_tm, kc, W, Wap, ncols, out_mode, epi, gw=None, pre=None, post=None, wsplit=False, tag=None):
        if wsplit:
            gw = ncols
        if gw is None:
            gw = max(128, (12288 // kc) // 128 * 128)
            gw = min(gw, 2048)
        seq = [(g0, min(gw, ncols - g0), t) for g0 in range(0, ncols, gw) for t in range(NT)]
        loaded = {}

        def emit_load(i, wb_dep):
            g0, gn, t = seq[i]
            ib = inb[ini[0] % 2]; ini[0] += 1
            iv = ib[:, 0:kc * 512].rearrange("p (c n) -> p c n", c=kc)
            sl = slice(t * 512, (t + 1) * 512)
            if src_tm:
                key = k._key('ld:' + ib.name)
                for c in range(kc):
                    k.P.emit('sp', lambda e: e.dma_start_transpose(out=iv[:, c, :], in_=src.t[sl, c * 128:(c + 1) * 128]), [src.b, k.cast_tok] + ([wb_dep.b] if wb_dep is not None else []), [ib.b, k.tr_tok], dma_key=key)
            else:
                k.ld(ib, iv[:], src, src.t[0:kc * 128, sl].rearrange("(c p) t -> p c t", p=128))
            loaded[i] = (ib, iv)
        wb = wv = None
        emit_load(0, None)
        groups_ = [(g0, min(gw, ncols - g0)) for g0 in range(0, ncols, gw)]
        wq = {}

        def wload(gi):
            g0_, gn_ = groups_[gi]
            if wsplit:
                h_ = kc // 2
                a_ = load_w(W, Wap[0:h_ * 128, g0_:g0_ + gn_], h_, gn_, after=(inb if src_tm else ()))
                b_ = load_w(W, Wap[h_ * 128:kc * 128, g0_:g0_ + gn_], h_, gn_, after=(inb if src_tm else ()))
                wq[gi] = (a_, b_)
            elif gi == 0 and tag is not None and tag in pref:
                wq[gi] = pref.pop(tag)
            else:
                wq[gi] = load_w(W, Wap[:, g0_:g0_ + gn_], kc, gn_, after=(inb if src_tm else ()))
        wload(0)
        for i, (g0, gn, t) in enumerate(seq):
            if t == 0:
                gi = g0 // gw
                if gi not in wq:
                    wload(gi)
                if wsplit:
                    (wbA, wvA), (wbB, wvB) = wq.pop(gi)
                    wb = wbB
                else:
                    wb, wv = wq.pop(gi)
                if gi + 1 < len(groups_) and not src_tm:
                    wload(gi + 1)
            elif t == 1 and src_tm and (g0 // gw) + 1 < len(groups_) and False:
                pass
            if t == 0 and src_tm and (g0 // gw) + 1 < len(groups_):
                pass
            if i + 1 < len(seq):
                emit_load(i + 1, wb)
            ib, iv = loaded.pop(i)
            ncb = gn // 128
            if pre is not None:
                pre(t, g0 // 128, g0 // 128 + ncb)
            if out_mode == 'fm':
                for cb in range(ncb):
                    ps = k.psum()
                    for c in range(kc):
                        if wsplit:
                            wb_, wv_, c_ = (wbA, wvA, c) if c < kc // 2 else (wbB, wvB, c - kc // 2)
                        else:
                            wb_, wv_, c_ = wb, wv, c
                        k.M(lambda e: e.matmul(ps[:], wv_[:, c_, cb * 128:(cb + 1) * 128], iv[:, c, :], start=(c == 0), stop=(c == kc - 1)), r=[wb_, ib], w=[ps])
                    epi(ps, (g0 // 128) + cb, t, None, cb == ncb - 1)
            else:
                for sbk in range(4):
                    for p0 in range(0, gn, 512):
                        pw = min(512, gn - p0)
                        ps = k.psum()
                        for c in range(kc):
                            k.M(lambda e: e.matmul(ps[:, 0:pw], iv[:, c, sbk * 128:(sbk + 1) * 128], wv[:, c, p0:p0 + pw], start=(c == 0), stop=(c == kc - 1)), r=[wb, ib], w=[ps])
                        epi(ps, g0 + p0, t, sbk, False)
            if post is not None:
                post(t, g0 // 128, g0 // 128 + ncb)

    stg = [k.sb("stg%d" % i, [128, 512], BF16) for i in range(3)]
    sti = [0]

    stgb = [k.sb("stgb%d" % i, [128, 4, 512], BF16) for i in range(2)]
    stgi = [0]

    def epi_store_fm(dst, row0, func=None, scale=1.0):
        state = {'n': 0, 'cb0': None, 'buf': None}

        def epi(ps, cb, t, sub, last):
            if state['n'] == 0:
                state['buf'] = stgb[stgi[0] % 2]; stgi[0] += 1
                state['cb0'] = cb
            s_ = state['buf']; n = state['n']
            if func is None:
                if cb % 2 == 0:
                    k.V(lambda e: e.tensor_copy(out=s_[:, n, :], in_=ps[:]), r=[ps], w=[s_])
                else:
                    k.A(lambda e: e.activation(out=s_[:, n, :], in_=ps[:], func=AF.Copy), r=[ps], w=[s_])
            else:
                k.A(lambda e: e.activation(out=s_[:, n, :], in_=ps[:], func=func, scale=scale), r=[ps], w=[s_])
            state['n'] += 1
            if state['n'] == 4 or last:
                n_ = state['n']; c0 = state['cb0']
                k.stq(dst, dst.t[row0 + c0 * 128:row0 + (c0 + n_) * 128, t * 512:(t + 1) * 512].rearrange("(c p) t -> p c t", p=128), s_, s_[:, 0:n_, :])
                state['n'] = 0
        return epi

    def epi_store_tm(dst, col0, func=None):
        def epi(ps, c0, t, sub, last):
            s = stg[sti[0] % 3]; sti[0] += 1
            if func is None:
                k.V(lambda e, s=s, ps=ps: e.tensor_copy(out=s[:], in_=ps[:]), r=[ps], w=[s])
            else:
                k.A(lambda e, s=s, ps=ps: e.activation(out=s[:], in_=ps[:], func=func), r=[ps], w=[s])
            r0 = t * 512 + sub * 128
            k.stq(dst, dst.t[r0:r0 + 128, col0 + c0:col0 + c0 + 512], s, s[:])
        return epi

    def resid_hooks(li, gq):
        def pre(t, c_lo, c_hi):
            xb = xbuf[t % 2]
            k.ld(xb, xb[:, c_lo:c_hi, :], X, X.t[c_lo * 128:c_hi * 128, t * 512:(t + 1) * 512].rearrange("(c p) t -> p c t", p=128))

        def epi(ps, cb, t, sub, last):
            ci = ci_of_tile(t)
            xb = xbuf[t % 2]
            k.V(lambda e: e.scalar_tensor_tensor(out=xb[:, cb, :], in0=ps[:], scalar=mod[:, li, gq * 8 + cb, ci:ci + 1], in1=xb[:, cb, :], op0=ALU.mult, op1=ALU.add),
                r=[ps, mod, xb], w=[xb])

        def post(t, c_lo, c_hi):
            xb = xbuf[t % 2]
            k.stq(X, X.t[c_lo * 128:c_hi * 128, t * 512:(t + 1) * 512].rearrange("(c p) t -> p c t", p=128), xb, xb[:, c_lo:c_hi, :])
        return dict(epi=epi, pre=pre, post=post)

    cvin = [k.sb("cvin%d" % i, [128, 10, 66], BF16) for i in range(2)]
    dgb = [k.sb("dgb%d" % i, [128, 9, 128], BF16) for i in range(2)]
    cvout = [k.sb("cvout%d" % i, [128, 512], BF16) for i in range(2)]
    cvi = [0]

    def seq_of(tok):
        for (o, L, kd) in SEQS:
            if o <= tok < o + L:
                return (o, L, kd)
        raise ValueError

    def conv_stage(src, dst, nblk, wfn, bfn, grid_for_sample, wdeps, l2blocks=(), qscale_blocks=()):
        items = [(blk, t) for blk in range(nblk) for t in range(NT)]
        info = {}

        def conv_load(i):
            blk, t = items[i]
            rs_ = slice(blk * 128, (blk + 1) * 128)
            t0 = t * 512
            (o, L, kd) = seq_of(t0)
            ci_ = cvin[cvi[0] % 2]; co = cvout[cvi[0] % 2]; cvi[0] += 1
            grid = grid_for_sample and kd == 's'
            k.G(lambda e: e.memset(ci_[:], 0.0), w=[ci_])
            pieces = []
            if grid:
                ROWS_ = L // 64
                r0 = (t0 - o) // 64
                lo = max(0, r0 - 1); hi = min(ROWS_, r0 + 9)
                k.ld(ci_, ci_[:, lo - (r0 - 1):hi - (r0 - 1), 1:65], src, src.t[rs_, o + lo * 64:o + hi * 64].rearrange("p (r c) -> p r c", c=64))
            else:
                flat = ci_[:].rearrange("p r c -> p (r c)")
                pos = t0
                base = 0
                while pos < t0 + 512:
                    (o2, L2, _) = seq_of(pos)
                    pe_ = min(t0 + 512, o2 + L2)
                    pa, pb = pos, pe_
                    lo = max(o2, pa - 1); hi = min(o2 + L2, pb + 1)
                    k.ld(ci_, flat[:, base + (lo - (pa - 1)):base + (hi - (pa - 1))], src, src.t[rs_, lo:hi])
                    pieces.append((pa, pb, base)); base += (pb - pa) + 2
                    pos = pe_
            info[i] = (ci_, co, grid, pieces)

        conv_load(0)
        for i, (blk, t) in enumerate(items):
            rs_ = slice(blk * 128, (blk + 1) * 128)
            t0 = t * 512
            if t == 0:
                dg = dgb[blk % 2]
                for tap in (range(9) if grid_for_sample else range(3, 6)):
                    k.V(lambda e: e.tensor_scalar(out=dg[:, tap, :], in0=cmb[:, ID, :], scalar1=wfn(blk, tap), scalar2=None, op0=ALU.mult), r=[cmb] + wdeps, w=[dg])
            if i + 1 < len(items):
                conv_load(i + 1)
            ci_, co, grid, pieces = info.pop(i)
            ps = k.psum()
            if grid:
                pv = ps[:].rearrange("p (r c) -> p r c", c=64)
                ti = 0
                for dy in range(3):
                    for dx in range(3):
                        k.M(lambda e: e.matmul(pv, dg[:, dy * 3 + dx, :], ci_[:, dy:dy + 8, dx:dx + 64], start=(ti == 0), stop=(ti == 8)), r=[dg, ci_], w=[ps])
                        ti += 1
            else:
                flat = ci_[:].rearrange("p r c -> p (r c)")
                for (pa, pb, base) in pieces:
                    n_ = pb - pa
                    for dx in range(3):
                        k.M(lambda e: e.matmul(ps[:, pa - t0:pb - t0], dg[:, 3 + dx, :], flat[:, base + dx:base + dx + n_], start=(dx == 0), stop=(dx == 2)), r=[dg, ci_], w=[ps])
            bap = bfn(blk) if bfn is not None else None
            if bap is not None:
                k.A(lambda e: e.activation(out=co[:], in_=ps[:], func=AF.Silu, bias=bap, scale=1.0), r=[ps] + wdeps, w=[co])
            else:
                k.A(lambda e: e.activation(out=co[:], in_=ps[:], func=AF.Silu), r=[ps], w=[co])
            if blk in l2blocks:
                k.V(lambda e: e.tensor_tensor(out=sqb[:, 0, :], in0=co[:], in1=co[:], op=ALU.mult), r=[co], w=[sqb])
                ps2 = k.psum()
                k.M(lambda e: e.matmul(ps2[:], cmb[:, ONES, :], sqb[:, 0, :], start=True, stop=True), r=[cmb, sqb], w=[ps2])
                k.A(lambda e: e.activation(out=rsb[:], in_=ps2[:], func=AF.Sqrt, bias=1e-6, scale=1.0), r=[ps2], w=[rsb])
                k.V(lambda e: e.reciprocal(out=rsb[:], in_=rsb[:]), r=[rsb], w=[rsb])
                qs = (128.0 ** -0.5) if blk in qscale_blocks else 1.0
                k.V(lambda e: e.scalar_tensor_tensor(out=co[:], in0=co[:], scalar=qs, in1=rsb[:], op0=ALU.mult, op1=ALU.mult), r=[co, rsb], w=[co])
            k.stq(dst, dst.t[rs_, t0:t0 + 512], co, co[:])

    fcw_s = k.sb("fcw_s", [128, 4, 22, 9], F32); fcb_s = k.sb("fcb_s", [128, 4, 22], F32)
    k.ld(fcw_s, fcw_s[:], ffn_cw, ffn_cw.t[:, :, :, :]); k.ld(fcb_s, fcb_s[:], ffn_cb, ffn_cb.t[:, :, :])
    avb = [k.view(wbuf[i], wbuf[i][:, 8192:12288].rearrange("p (c n) -> p c n", c=8), group='ffnv') for i in range(2)]
    aai = [0]

    def ffn(li):
        prefetch_w('fa', ffn_in_w, ffn_in_w.t[li, :, 0:2816], 8, 2816)
        norm_stage(li, 1)
        gemm(H, False, 8, ffn_in_w, ffn_in_w.t[li, :, 0:2816], 2816, 'fm', epi_store_fm(S1, 0), tag='fa')
        prefetch_w('fv', ffn_in_w, ffn_in_w.t[li, :, 2816:5632], 8, 2816, gw=1024)
        conv_stage(S1, S2, 22, lambda blk, tap: fcw_s[:, li, blk, tap:tap + 1], lambda blk: fcb_s[:, li, blk:blk + 1], True, [fcw_s, fcb_s])

        vstate = {}

        def pre_v(t, c_lo, c_hi):
            a = avb[aai[0] % 2]; aai[0] += 1
            k.ld(a, a[:, 0:c_hi - c_lo, :], S2, S2.t[c_lo * 128:c_hi * 128, t * 512:(t + 1) * 512].rearrange("(c p) t -> p c t", p=128))
            vstate['a'] = a; vstate['lo'] = c_lo

        def epi_v(ps, cb, t, sub, last):
            a = vstate['a']; n = cb - vstate['lo']
            k.V(lambda e: e.tensor_tensor(out=a[:, n, :], in0=ps[:], in1=a[:, n, :], op=ALU.mult), r=[ps, a], w=[a])

        def post_v(t, c_lo, c_hi):
            a = vstate['a']
            k.stq(S5, S5.t[c_lo * 128:c_hi * 128, t * 512:(t + 1) * 512].rearrange("(c p) t -> p c t", p=128), a, a[:, 0:c_hi - c_lo, :])
        k.enter('ffnv')
        gemm(H, False, 8, ffn_in_w, ffn_in_w.t[li, :, 2816:5632], 2816, 'fm', epi_v, gw=1024, pre=pre_v, post=post_v, tag='fv')
        k.leave('ffnv')
        gemm(S5, False, 22, ffn_out_w, ffn_out_w.t[li, :, :], 1024, 'fm', gw=512, **resid_hooks(li, 5))

    scw_s = k.sb("scw_s", [128, 2, 32, 3], F32); scb_s = k.sb("scb_s", [128, 2, 32], F32)
    k.ld(scw_s, scw_s[:], ssm_cw, ssm_cw.t[:, :, :, :]); k.ld(scb_s, scb_s[:], ssm_cb, ssm_cb.t[:, :, :])
    dt_sb = k.view(xbuf[0], xbuf[0][:].rearrange("p c n -> p (c n)")[:, 0:NCH * 64].rearrange("p (c n) -> p c n", n=64), group='ssdx')
    la_sb = k.view(xbuf[1], xbuf[1][:].rearrange("p c n -> p (c n)")[:, 0:NCH * 64].rearrange("p (c n) -> p c n", n=64), group='ssdx')
    lab_sb = k.sb("lab_sb", [128, 1, 64], BF16)
    bc64 = k.sb("bc64", [128, 64], F32); nega = k.sb("nega", [128, 64], F32); dsk = k.sb("dsk", [128, 32], F32)
    nwb = k.sb("nwb", [128, 2048], BF16)
    W0, W1 = wbuf[0], wbuf[1]
    Ebuf = k.view(W0, W0[:, 0:4096], group='ssd'); DTb = k.view(W0, W0[:, 4096:8192], group='ssd')
    xv = k.view(W0, W0[:, 8192:10240], group='ssd'); xw = k.view(W0, W0[:, 10240:12288], group='ssd')
    Sbf = k.view(W1, W1[:, 0:2048], group='ssd'); STb = k.view(W1, W1[:, 2048:3072], group='ssd')
    yo = [k.view(W1, W1[:, 3072 + i * 2048:3072 + (i + 1) * 2048], group='ssd') for i in range(2)]
    xs_tm = [k.view(inb[i], inb[i][:, 0:2048], group='ssd') for i in range(2)]
    B_tm = [k.view(inb[i], inb[i][:, 2048:3072], group='ssd') for i in range(2)]
    BC_fm = [k.view(inb[i], inb[i][:, 3072:5120].rearrange("p (g t) -> p g t", t=128), group='ssd') for i in range(2)]
    yfb = [k.view(inb[i], inb[i][:, 5120:7168], group='ssd') for i in range(2)]
    zsb = [k.view(inb[i], inb[i][:, 7168:9216], group='ssd') for i in range(2)]
    ecs = k.sb("ecs", [128, 64], F32)
    Sst = k.sb("Sst", [128, 2048], F32)
    yc = k.sb("yc", [128, 2048], F32)
    ytmp = tmpf
    ss8 = k.sb("ss8", [128, 16], F32)

    def chunk_order(L, d):
        n = L // 128
        return list(range(n)) if d == 0 else list(range(n - 1, -1, -1))

    def ssd(li, j):
        W = ssm_in_w
        prefetch_w('sz', W, W.t[j, :, 0:2048], 8, 2048)
        norm_stage(li, 0)
        gemm(H, False, 8, W, W.t[j, :, 0:2048], 2048, 'tm', epi_store_tm(S3, 0, AF.Silu), tag='sz')
        prefetch_w('sx', W, W.t[j, :, 2048:6144], 8, 4096)
        gemm(H, False, 8, W, W.t[j, :, 2048:6144], 4096, 'fm', epi_store_fm(S1, 0), tag='sx')
        prefetch_w('sd', W, W.t[j, :, 6144:6208], 8, 64)
        if DBG_STOP == 'gemm':
            return
        conv_stage(S1, S2, 32, lambda blk, tap: scw_s[:, j, blk, tap - 3:tap - 2], lambda blk: scb_s[:, j, blk:blk + 1], False, [scw_s, scb_s])
        if DBG_STOP == 'conv':
            return
        k.ld(bc64, bc64[:], ssm_dtb, ssm_dtb.t[j].partition_broadcast(128))
        k.ld(nega, nega[:], ssm_alog, ssm_alog.t[j].partition_broadcast(128))
        k.A(lambda e: e.activation(out=nega[:], in_=nega[:], func=AF.Exp), r=[nega], w=[nega])
        k.V(lambda e: e.tensor_scalar(out=nega[:], in0=nega[:], scalar1=-1.0, scalar2=None, op0=ALU.mult), r=[nega], w=[nega])
        k.ld(dsk, dsk[:], ssm_d, ssm_d.t[j].partition_broadcast(128))
        k.ld(nwb, nwb[:], ssm_nw, ssm_nw.t[j].partition_broadcast(128), q='pool')

        def epi_dt(ps, c0, t, sub, last):
            ch = t * 4 + sub
            k.V(lambda e, ps=ps, ch=ch: e.tensor_tensor(out=dt_sb[:, ch, :], in0=ps[:, 0:64], in1=bc64[:], op=ALU.add), r=[ps, bc64], w=[dt_sb])
            k.A(lambda e, ch=ch: e.activation(out=dt_sb[:, ch, :], in_=dt_sb[:, ch, :], func=AF.Exp), r=[dt_sb], w=[dt_sb])
            k.A(lambda e, ch=ch: e.activation(out=dt_sb[:, ch, :], in_=dt_sb[:, ch, :], func=AF.Ln, bias=1.0, scale=1.0), r=[dt_sb], w=[dt_sb])
            k.V(lambda e, ch=ch: e.tensor_tensor(out=la_sb[:, ch, :], in0=dt_sb[:, ch, :], in1=nega[:], op=ALU.mult), r=[dt_sb, nega], w=[la_sb])
        k.enter('ssdx')
        gemm(H, False, 8, W, W.t[j, :, 6144:6208], 64, 'tm', epi_dt, tag='sd')
        if DBG_STOP == 'dt':
            return
        k.enter('ssd')
        pidx = 0
        for (o, L, kd) in (SEQS[:1] if DBG_STOP in ('scan1', 'scan1b') else SEQS):
            for d in ((0,) if DBG_STOP == 'scan1' else (0, 1)):
                if kd == 'p':
                    k.V(lambda e: e.memset(Sst[:], 0.0), w=[Sst])
                    k.V(lambda e: e.memset(Sbf[:], 0.0), w=[Sbf])
                else:
                    k.ld(Sst, Sst[:], st_ssm, st_ssm.t[j, d])
                    k.A(lambda e: e.activation(out=Sbf[:], in_=Sst[:], func=AF.Copy), r=[Sst], w=[Sbf])
                ie = 127 if d == 0 else 0
                order = chunk_order(L, d)

                def ssd_loads(n_, c):
                    t0 = o + c * 128
                    tsl = slice(t0, t0 + 128)
                    xs = xs_tm[n_ % 2]; Bt = B_tm[n_ % 2]; BC = BC_fm[n_ % 2]
                    for q4 in range(4):
                        k.ld(xs, xs[:, q4 * 512:(q4 + 1) * 512], S2, S2.t[q4 * 512:(q4 + 1) * 512, tsl], tr=True)
                    for q4 in range(2):
                        k.ld(Bt, Bt[:, q4 * 512:(q4 + 1) * 512], S2, S2.t[2048 + q4 * 512:2048 + (q4 + 1) * 512, tsl], tr=True)
                    k.ld(BC, BC[:], S2, S2.t[2048:4096, tsl].rearrange("(g n) t -> n g t", n=128))
                    if d == 1:
                        k.ld(yfb[n_ % 2], yfb[n_ % 2][:], S4, S4.t[tsl, :])
                        k.ld(zsb[n_ % 2], zsb[n_ % 2][:], S3, S3.t[tsl, :])
                ssd_loads(0, order[0])
                for n_, c in enumerate(order):
                    if n_ + 1 < len(order):
                        ssd_loads(n_ + 1, order[n_ + 1])
                    t0 = o + c * 128; ch = t0 // 128
                    tsl = slice(t0, t0 + 128)
                    xs = xs_tm[n_ % 2]; Bt = B_tm[n_ % 2]; BC = BC_fm[n_ % 2]
                    la = la_sb[:, ch, d * 32:(d + 1) * 32]; lab = lab_sb[:, 0, 0:32]; dtc = dt_sb[:, ch, d * 32:(d + 1) * 32]
                    k.V(lambda e: e.tensor_copy(out=lab_sb[:, 0, 0:32], in_=la), r=[la_sb], w=[lab_sb])
                    E3 = Ebuf[:].rearrange("p (h i) -> p h i", i=128)
                    D3 = DTb[:].rearrange("p (h i) -> p h i", i=128)
                    k.G(lambda e, la=la: e.tensor_tensor(out=E3, in0=bc(la, [128, 32, 128], 2), in1=bc(cmb[:, U_[d], :], [128, 32, 128], 1), op=ALU.mult), r=[la_sb, cmb], w=[Ebuf])
                    for q8 in range(8):
                        ps = k.psum()
                        k.M(lambda e, ps=ps, q8=q8: e.matmul(ps[:], cmb[:, V_[d], :], Ebuf[:, q8 * 512:(q8 + 1) * 512], start=True, stop=True), r=[cmb, Ebuf], w=[ps])
                        k.A(lambda e, ps=ps, q8=q8: e.activation(out=DTb[:, q8 * 512:(q8 + 1) * 512], in_=ps[:], func=AF.Exp), r=[ps], w=[DTb])
                    for half in range(2):
                        ps = k.psum()
                        for g4 in range(4):
                            g = half * 4 + g4
                            k.M(lambda e, ps=ps, g=g, g4=g4, BC=BC: e.matmul(ps[:, g4 * 128:(g4 + 1) * 128], BC[:, g, :], BC[:, 8 + g, :], start=True, stop=True), r=[BC], w=[ps])
                        k.V(lambda e, ps=ps, half=half: e.tensor_tensor(out=STb[:, half * 512:(half + 1) * 512].rearrange("p (g i) -> p g i", i=128), in0=ps[:].rearrange("p (g i) -> p g i", i=128),
                                                                        in1=bc(cmb[:, U_[d], :], [128, 4, 128], 1), op=ALU.mult), r=[ps, cmb], w=[STb])
                    k.V(lambda e: e.tensor_copy(out=wdec[:], in_=DTb[:].rearrange("p (h i) -> p h i", i=128)[:, :, ie]), r=[DTb], w=[wdec])
                    k.V(lambda e: e.tensor_tensor(out=DTb[:].rearrange("p (g r i) -> p g r i", r=4, i=128), in0=DTb[:].rearrange("p (g r i) -> p g r i", r=4, i=128),
                                                   in1=bc(STb[:].rearrange("p (g i) -> p g i", i=128), [128, 8, 4, 128], 2), op=ALU.mult), r=[DTb, STb], w=[DTb])
                    k.G(lambda e, xs=xs, dtc=dtc: e.tensor_tensor(out=xv[:].rearrange("p (h q) -> p h q", q=64), in0=xs[:].rearrange("p (h q) -> p h q", q=64),
                                                                   in1=bc(dtc, [128, 32, 64], 2), op=ALU.mult), r=[xs, dt_sb], w=[xv])
                    k.G(lambda e: e.tensor_tensor(out=xw[:].rearrange("p (h q) -> p h q", q=64), in0=xv[:].rearrange("p (h q) -> p h q", q=64),
                                                   in1=bc(wdec[:], [128, 32, 64], 2), op=ALU.mult), r=[xv, wdec], w=[xw])
                    ps = k.psum()
                    k.M(lambda e, ps=ps, lab=lab: e.matmul(ps[:, 0:32], cmb[:, U_[d], :], lab, start=True, stop=True), r=[cmb, lab_sb], w=[ps])
                    k.M(lambda e, ps=ps, lab=lab: e.matmul(ps[:, 32:64], cmb[:, ONES, :], lab, start=True, stop=True), r=[cmb, lab_sb], w=[ps])
                    k.A(lambda e, ps=ps: e.activation(out=ecs[:], in_=ps[:, 0:64], func=AF.Exp), r=[ps], w=[ecs])
                    for pr in range(4):
                        psA = k.psum(); psB = k.psum()
                        for gg in range(2):
                            g = pr * 2 + gg
                            for r_ in range(4):
                                h = g * 4 + r_
                                k.M(lambda e, psA=psA, h=h, gg=gg, r_=r_: e.matmul(psA[:, gg * 256 + r_ * 64:gg * 256 + (r_ + 1) * 64], DTb[:, h * 128:(h + 1) * 128], xv[:, h * 64:(h + 1) * 64],
                                                                                       start=True, stop=True), r=[DTb, xv], w=[psA])
                            k.M(lambda e, psB=psB, g=g, gg=gg, BC=BC: e.matmul(psB[:, gg * 256:(gg + 1) * 256], BC[:, 8 + g, :], Sbf[:, g * 256:(g + 1) * 256], start=True, stop=True), r=[BC, Sbf], w=[psB])
                        k.V(lambda e, psB=psB, pr=pr: e.tensor_tensor(out=ytmp[:].rearrange("p (h q) -> p h q", q=64), in0=psB[:].rearrange("p (h q) -> p h q", q=64),
                                                                       in1=bc(ecs[:, pr * 8:(pr + 1) * 8], [128, 8, 64], 2), op=ALU.mult), r=[psB, ecs], w=[ytmp])
                        k.V(lambda e, psA=psA, pr=pr: e.tensor_tensor(out=yc[:, pr * 512:(pr + 1) * 512], in0=psA[:], in1=ytmp[:], op=ALU.add), r=[psA, ytmp], w=[yc])
                    k.G(lambda e: e.tensor_tensor(out=Sst[:].rearrange("p (h q) -> p h q", q=64), in0=Sst[:].rearrange("p (h q) -> p h q", q=64),
                                                   in1=bc(ecs[:, 32:64], [128, 32, 64], 2), op=ALU.mult), r=[Sst, ecs], w=[Sst])
                    for pr in range(4):
                        ps = k.psum()
                        for gg in range(2):
                            g = pr * 2 + gg
                            k.M(lambda e, ps=ps, g=g, gg=gg, Bt=Bt: e.matmul(ps[:, gg * 256:(gg + 1) * 256], Bt[:, g * 128:(g + 1) * 128], xw[:, g * 256:(g + 1) * 256], start=True, stop=True), r=[Bt, xw], w=[ps])
                        k.V(lambda e, ps=ps, pr=pr: e.tensor_tensor(out=Sst[:, pr * 512:(pr + 1) * 512], in0=ps[:], in1=Sst[:, pr * 512:(pr + 1) * 512], op=ALU.add), r=[ps, Sst], w=[Sst])
                    k.A(lambda e: e.activation(out=Sbf[:], in_=Sst[:], func=AF.Copy), r=[Sst], w=[Sbf])
                    if d == 0:
                        y_ = yo[n_ % 2]
                        k.A(lambda e, y_=y_: e.activation(out=y_[:], in_=yc[:], func=AF.Copy), r=[yc], w=[y_])
                        k.stq(S4, S4.t[tsl, :], y_, y_[:])
                    else:
                        yf_ = yfb[n_ % 2]; zs_ = zsb[n_ % 2]; y_ = yo[n_ % 2]
                        k.V(lambda e, yf_=yf_: e.tensor_tensor(out=yc[:], in0=yc[:], in1=yf_[:], op=ALU.add), r=[yc, yf_], w=[yc])
                        k.G(lambda e, xs=xs: e.tensor_tensor(out=xv[:].rearrange("p (h q) -> p h q", q=64), in0=xs[:].rearrange("p (h q) -> p h q", q=64),
                                                              in1=bc(dsk[:], [128, 32, 64], 2), op=ALU.mult), r=[xs, dsk], w=[xv])
                        k.V(lambda e: e.tensor_tensor(out=yc[:], in0=yc[:], in1=xv[:], op=ALU.add), r=[yc, xv], w=[yc])
                        k.V(lambda e, zs_=zs_: e.tensor_tensor(out=yc[:], in0=yc[:], in1=zs_[:], op=ALU.mult), r=[yc, zs_], w=[yc])
                        k.G(lambda e: e.tensor_tensor(out=Ebuf[:, 0:2048], in0=yc[:], in1=yc[:], op=ALU.mult), r=[yc], w=[Ebuf])
                        k.V(lambda e: e.tensor_reduce(out=ss8[:, 0:8], in_=Ebuf[:, 0:2048].rearrange("p (g q) -> p g q", q=256), axis=AX.X, op=ALU.add), r=[Ebuf], w=[ss8])
                        k.A(lambda e: e.activation(out=ss8[:, 0:8], in_=ss8[:, 0:8], func=AF.Sqrt, bias=EPS, scale=1.0 / 256), r=[ss8], w=[ss8])
                        k.V(lambda e: e.reciprocal(out=ss8[:, 0:8], in_=ss8[:, 0:8]), r=[ss8], w=[ss8])
                        k.V(lambda e: e.tensor_tensor(out=yc[:].rearrange("p (g q) -> p g q", q=256), in0=yc[:].rearrange("p (g q) -> p g q", q=256),
                                                       in1=bc(ss8[:, 0:8], [128, 8, 256], 2), op=ALU.mult), r=[yc, ss8], w=[yc])
                        k.V(lambda e, y_=y_: e.tensor_tensor(out=y_[:], in0=yc[:], in1=nwb[:], op=ALU.mult), r=[yc, nwb], w=[y_])
                        k.stq(S4, S4.t[tsl, :], y_, y_[:])
                if kd == 'p':
                    k.stq(ns_ssm, ns_ssm.t[pidx, j, d], Sst, Sst[:])
            if kd == 'p':
                pidx += 1
        k.leave('ssd'); k.leave('ssdx')
        if DBG_STOP in ('scan1', 'scan1b', 'scan'):
            return
        gemm(S4, True, 16, ssm_out_w, ssm_out_w.t[j, :, :], 1024, 'fm', wsplit=True, **resid_hooks(li, 2))

    HAS_GLA = any(kd_ == 'gla' for _, kd_, _ in LAYERS)
    lrT = k.sb("lrT", [64, 512], BF16)
    gnw_s = k.sb("gnw_s", [128, 1, 2], F32)
    k.ld(gnw_s, gnw_s[:], gla_nw, gla_nw.t[:, :, :])
    w2s = k.sb("w2s", [64, 1024 if HAS_GLA else 8], BF16)
    gbb = k.sb("gbb", [128, 1024 if HAS_GLA else 8], BF16)
    gq_fm = [k.view(inb[i], inb[i][:, 0:512].rearrange("p (h t) -> p h t", t=128), group='gla') for i in range(2)]
    gk_fm = [k.view(inb[i], inb[i][:, 512:1024].rearrange("p (h t) -> p h t", t=128), group='gla') for i in range(2)]
    gk_tm = [k.view(inb[i], inb[i][:, 1024:1536], group='gla') for i in range(2)]
    gv_tm = [k.view(inb[i], inb[i][:, 1536:2560], group='gla') for i in range(2)]
    gsp_tm = [k.view(inb[i], inb[i][:, 2560:3072], group='gla') for i in range(2)]
    gof = [k.view(inb[i], inb[i][:, 3072:4096].rearrange("p (b t) -> p b t", t=128), group='gla') for i in range(2)]
    grs = [k.view(inb[i], inb[i][:, 4096:5120].rearrange("p (b t) -> p b t", t=128), group='gla') for i in range(2)]
    gqg = k.view(W0, W0[:, 0:512].rearrange("p (h t) -> p h t", t=128), group='gla')
    gkg = k.view(W0, W0[:, 512:1024].rearrange("p (h t) -> p h t", t=128), group='gla')
    gAT = k.view(W0, W0[:, 1024:1536].rearrange("p (h t) -> p h t", t=128), group='gla')
    gkw = k.view(W0, W0[:, 1536:2048], group='gla')
    gsq = k.view(W0, W0[:, 2048:3072].rearrange("p (b t) -> p b t", t=128), group='gla')
    gon = [k.view(W1, W1[:, 3072 + i * 1024:3072 + (i + 1) * 1024].rearrange("p (b t) -> p b t", t=128), group='gla') for i in range(2)]
    gsps = [k.sb("gsps%d" % i, [128, 1024 if HAS_GLA else 8], BF16) for i in range(2)]
    Sg = k.view(Sst, Sst[:, 0:1024].rearrange("p (h e) -> p h e", e=256), group='gla')
    Sgb = k.view(W1, W1[:, 0:1024].rearrange("p (h e) -> p h e", e=256), group='gla')
    gob = k.view(yc, yc[:, 0:1024].rearrange("p (b t) -> p b t", t=128), group='gla')

    def gla(li, j):
        W = gla_in_w
        prefetch_w('gq', W, W.t[j, :, 0:512], 8, 512)
        norm_stage(li, 0)
        gemm(H, False, 8, W, W.t[j, :, 0:512], 512, 'fm', epi_store_fm(S1, 0, AF.Copy, 128.0 ** -0.5), tag='gq')
        prefetch_w('gk', W, W.t[j, :, 512:1024], 8, 512)
        gemm(H, False, 8, W, W.t[j, :, 512:1024], 512, 'fm', epi_store_fm(S1, 512), tag='gk')
        prefetch_w('gv', W, W.t[j, :, 1024:2048], 8, 1024)
        gemm(H, False, 8, W, W.t[j, :, 1024:2048], 1024, 'tm', epi_store_tm(S3, 0), tag='gv')
        prefetch_w('gr', W, W.t[j, :, 2048:3072], 8, 1024)
        gemm(H, False, 8, W, W.t[j, :, 2048:3072], 1024, 'fm', epi_store_fm(S6, 0, AF.Silu), tag='gr')

        k.ld(w2s, w2s[:], gla_w2, gla_w2.t[j], q='pool')
        k.ld(gbb, gbb[:], gla_gb, gla_gb.t[j].partition_broadcast(128), q='pool')

        def epi_lr(ps, cb, t, sub, last):
            k.V(lambda e: e.tensor_copy(out=lrT[:], in_=ps[0:64, :]), r=[ps], w=[lrT])
            for sbk in range(4):
                ch = t * 4 + sbk
                tsl = slice(ch * 128, (ch + 1) * 128)
                sp_ = gsps[ch % 2]
                for hf in range(2):
                    ps2 = k.psum()
                    k.M(lambda e: e.matmul(ps2[:], lrT[:, sbk * 128:(sbk + 1) * 128], w2s[:, hf * 512:(hf + 1) * 512], start=True, stop=True), r=[lrT, w2s], w=[ps2])
                    k.V(lambda e: e.tensor_tensor(out=rsb[:], in0=ps2[:], in1=gbb[:, hf * 512:(hf + 1) * 512], op=ALU.add), r=[ps2, gbb], w=[rsb])
                    k.A(lambda e: e.activation(out=rsb[:], in_=rsb[:], func=AF.Exp, scale=-1.0), r=[rsb], w=[rsb])
                    k.A(lambda e: e.activation(out=sp_[:, hf * 512:(hf + 1) * 512], in_=rsb[:], func=AF.Ln, bias=1.0, scale=1.0), r=[rsb], w=[sp_])
                k.stq(S7, S7.t[tsl, :], sp_, sp_[:])
        gemm(H, False, 8, gla_w1, gla_w1.t[j, :, :], 128, 'fm', epi_lr)
        k.enter('gla')
        pidx = 0
        for (o, L, kd) in SEQS:
            for d in (0, 1):
                if kd == 'p':
                    k.V(lambda e: e.memset(Sg[:], 0.0), w=[Sg])
                    k.V(lambda e: e.memset(Sgb[:], 0.0), w=[Sgb])
                else:
                    k.ld(Sg, Sg[:], st_gla, st_gla.t[j, d].rearrange("h d e -> d h e"))
                    k.A(lambda e: e.activation(out=Sgb[:], in_=Sg[:], func=AF.Copy), r=[Sg], w=[Sgb])
                ie = 127 if d == 0 else 0
                order = chunk_order(L, d)

                def gla_loads(n_, c):
                    t0 = o + c * 128
                    tsl = slice(t0, t0 + 128)
                    q_ = gq_fm[n_ % 2]; k_ = gk_fm[n_ % 2]; kt = gk_tm[n_ % 2]; v_ = gv_tm[n_ % 2]; sp_ = gsp_tm[n_ % 2]
                    k.ld(q_, q_[:], S1, S1.t[0:512, tsl].rearrange("(h d) t -> d h t", d=128))
                    k.ld(k_, k_[:], S1, S1.t[512:1024, tsl].rearrange("(h d) t -> d h t", d=128))
                    k.ld(kt, kt[:], S1, S1.t[512:1024, tsl], tr=True)
                    k.ld(v_, v_[:], S3, S3.t[tsl, 0:1024])
                    k.ld(sp_, sp_[:], S7, S7.t[tsl, d * 512:(d + 1) * 512])
                    if d == 1:
                        k.ld(gof[n_ % 2], gof[n_ % 2][:], S5, S5.t[0:1024, tsl].rearrange("(b p) t -> p b t", p=128))
                        k.ld(grs[n_ % 2], grs[n_ % 2][:], S6, S6.t[0:1024, tsl].rearrange("(b p) t -> p b t", p=128))
                gla_loads(0, order[0])
                for n_, c in enumerate(order):
                    if n_ + 1 < len(order):
                        gla_loads(n_ + 1, order[n_ + 1])
                    t0 = o + c * 128
                    tsl = slice(t0, t0 + 128)
                    q_ = gq_fm[n_ % 2]; k_ = gk_fm[n_ % 2]; kt = gk_tm[n_ % 2]; v_ = gv_tm[n_ % 2]; sp_ = gsp_tm[n_ % 2]
                    ps = k.psum()
                    k.M(lambda e: e.matmul(ps[:], cmb[:, VG_[d], :], sp_[:], start=True, stop=True), r=[cmb, sp_], w=[ps])
                    k.A(lambda e: e.activation(out=rsb[:], in_=ps[:], func=AF.Exp), r=[ps], w=[rsb])
                    k.V(lambda e: e.tensor_tensor(out=gkw[:], in0=kt[:], in1=rsb[:], op=ALU.mult), r=[kt, rsb], w=[gkw])
                    psc = k.psum()
                    for h in range(4):
                        k.M(lambda e: e.matmul(psc[:, h * 128:(h + 1) * 128], sp_[:, h * 128:(h + 1) * 128], cmb[:, UG_[d], :], start=True, stop=True), r=[sp_, cmb], w=[psc])
                    k.A(lambda e: e.activation(out=tmpf[:], in_=psc[:], func=AF.Exp), r=[psc], w=[tmpf])
                    k.V(lambda e: e.tensor_tensor(out=gqg[:], in0=q_[:], in1=tmpf[:].rearrange("p (h t) -> p h t", t=128), op=ALU.mult), r=[q_, tmpf], w=[gqg])
                    k.A(lambda e: e.activation(out=tmpf[:], in_=psc[:], func=AF.Exp, scale=-1.0), r=[psc], w=[tmpf])
                    k.V(lambda e: e.tensor_tensor(out=gkg[:], in0=k_[:], in1=tmpf[:].rearrange("p (h t) -> p h t", t=128), op=ALU.mult), r=[k_, tmpf], w=[gkg])
                    k.A(lambda e: e.activation(out=ecs[:, 0:4], in_=psc[:].rearrange("p (h t) -> p h t", t=128)[:, :, ie], func=AF.Exp), r=[psc], w=[ecs])
                    pss = k.psum()
                    for h in range(4):
                        k.M(lambda e: e.matmul(pss[:, h * 128:(h + 1) * 128], gkg[:, h, :], gqg[:, h, :], start=True, stop=True), r=[gkg, gqg], w=[pss])
                    k.V(lambda e: e.tensor_tensor(out=gAT[:], in0=pss[:].rearrange("p (h t) -> p h t", t=128), in1=bc(cmb[:, U_[d], :], [128, 4, 128], 1), op=ALU.mult), r=[pss, cmb], w=[gAT])
                    pso = [k.psum(), k.psum()]
                    for h in range(4):
                        for eb in range(2):
                            blk = h * 2 + eb
                            po = pso[blk // 4]
                            k.M(lambda e: e.matmul(po[:, (blk % 4) * 128:(blk % 4 + 1) * 128], v_[:, h * 256 + eb * 128:h * 256 + (eb + 1) * 128], gAT[:, h, :], start=True, stop=False), r=[v_, gAT], w=[po])
                            k.M(lambda e: e.matmul(po[:, (blk % 4) * 128:(blk % 4 + 1) * 128], Sgb[:, h, eb * 128:(eb + 1) * 128], gqg[:, h, :], start=False, stop=True), r=[Sgb, gqg], w=[po])
                    for h in range(4):
                        pst = k.psum()
                        k.M(lambda e: e.matmul(pst[:, 0:256], gkw[:, h * 128:(h + 1) * 128], v_[:, h * 256:(h + 1) * 256], start=True, stop=True), r=[gkw, v_], w=[pst])
                        k.V(lambda e: e.scalar_tensor_tensor(out=Sg[:, h, :], in0=Sg[:, h, :], scalar=ecs[:, h:h + 1], in1=pst[:, 0:256], op0=ALU.mult, op1=ALU.add), r=[Sg, ecs, pst], w=[Sg])
                    on_ = gon[n_ % 2]
                    if d == 0:
                        for hf in range(2):
                            k.A(lambda e: e.activation(out=on_[:, hf * 4:(hf + 1) * 4, :], in_=pso[hf][:].rearrange("p (b t) -> p b t", t=128), func=AF.Copy), r=[pso[hf]], w=[on_])
                        k.stq(S5, S5.t[0:1024, tsl].rearrange("(b p) t -> p b t", p=128), on_, on_[:])
                    else:
                        of_ = gof[n_ % 2]; rs_ = grs[n_ % 2]
                        for hf in range(2):
                            k.V(lambda e: e.tensor_tensor(out=gob[:, hf * 4:(hf + 1) * 4, :], in0=pso[hf][:].rearrange("p (b t) -> p b t", t=128), in1=of_[:, hf * 4:(hf + 1) * 4, :], op=ALU.add), r=[pso[hf], of_], w=[gob])
                        k.G(lambda e: e.tensor_tensor(out=gsq[:], in0=gob[:], in1=gob[:], op=ALU.mult), r=[gob], w=[gsq])
                        psn = k.psum()
                        for h in range(4):
                            for eb in range(2):
                                k.M(lambda e: e.matmul(psn[:, h * 128:(h + 1) * 128], cmb[:, ONES, :], gsq[:, h * 2 + eb, :], start=(eb == 0), stop=(eb == 1)), r=[cmb, gsq], w=[psn])
                        k.A(lambda e: e.activation(out=rsb[:], in_=psn[:], func=AF.Sqrt, bias=EPS, scale=1.0 / 256), r=[psn], w=[rsb])
                        k.V(lambda e: e.reciprocal(out=rsb[:], in_=rsb[:]), r=[rsb], w=[rsb])
                        for eb in range(2):
                            gv_ = gob[:].rearrange("p (h b) t -> p h b t", b=2)[:, :, eb, :]
                            k.V(lambda e: e.scalar_tensor_tensor(out=gv_, in0=gv_, scalar=gnw_s[:, j, eb:eb + 1], in1=rsb[:].rearrange("p (h t) -> p h t", t=128), op0=ALU.mult, op1=ALU.mult), r=[gob, gnw_s, rsb], w=[gob])
                        k.V(lambda e: e.tensor_tensor(out=on_[:], in0=gob[:], in1=rs_[:], op=ALU.mult), r=[gob, rs_], w=[on_])
                        k.stq(S5, S5.t[0:1024, tsl].rearrange("(b p) t -> p b t", p=128), on_, on_[:])
                    k.A(lambda e: e.activation(out=Sgb[:], in_=Sg[:], func=AF.Copy), r=[Sg], w=[Sgb])
                if kd == 'p':
                    k.stq(ns_gla, ns_gla.t[pidx, j, d].rearrange("h d e -> d h e"), Sg, Sg[:])
            if kd == 'p':
                pidx += 1
        k.leave('gla')
        gemm(S5, False, 8, gla_out_w, gla_out_w.t[j, :, :], 1024, 'fm', **resid_hooks(li, 2))

    gcw_s = k.sb("gcw_s", [128, 1, 32, 3], F32)
    k.ld(gcw_s, gcw_s[:], gdn_cw, gdn_cw.t[:, :, :, :])
    dnw_s = k.sb("dnw_s", [128, 1, 2], F32)
    k.ld(dnw_s, dnw_s[:], gdn_nw, gdn_nw.t[:, :, :])
    xb0f = xbuf[0][:].rearrange("p c n -> p (c n)")
    d_beta = k.view(xbuf[0], xb0f[:, 0:NCH * 16].rearrange("p (c n) -> p c n", n=16), group='gdnx')
    d_nbeta = k.view(xbuf[0], xb0f[:, NCH * 16:2 * NCH * 16].rearrange("p (c n) -> p c n", n=16), group='gdnx')
    d_lg = k.view(xbuf[0], xb0f[:, 2 * NCH * 16:3 * NCH * 16].rearrange("p (c n) -> p c n", n=16), group='gdnx')
    d_lgb = k.sb("d_lgb", [128, NCH, 16], BF16)
    dbc16 = k.sb("dbc16", [128, 16], F32); dnega = k.sb("dnega", [128, 16], F32)
    h3 = lambda ap: ap.rearrange("p (h t) -> p h t", t=128)
    dq_fm = [k.view(inb[i], h3(inb[i][:, 0:1024]), group='gdn') for i in range(2)]
    dk_fm = [k.view(inb[i], h3(inb[i][:, 1024:2048]), group='gdn') for i in range(2)]
    dk_tm = [k.view(inb[i], h3(inb[i][:, 2048:3072]), group='gdn') for i in range(2)]
    dv_tm = [k.view(inb[i], inb[i][:, 3072:5120].rearrange("p (h e) -> p h e", e=256), group='gdn') for i in range(2)]
    dof = [k.view(inb[i], h3(inb[i][:, 5120:7168]), group='gdn') for i in range(2)]
    dzs = [k.view(inb[i], h3(inb[i][:, 7168:9216]), group='gdn') for i in range(2)]
    dET = k.view(W0, h3(W0[:, 0:1024]), group='gdn'); dEN = k.view(W0, h3(W0[:, 1024:2048]), group='gdn')
    dLT = k.view(W0, h3(W0[:, 2048:3072]), group='gdn'); dLTm = k.view(W0, h3(W0[:, 3072:4096]), group='gdn'); dLms = k.view(W0, h3(W0[:, 4096:5120]), group='gdn')
    dB = [k.view(W0, h3(W0[:, 5120 + i * 1024:6144 + i * 1024]), group='gdn') for i in range(2)]
    dA = [k.view(W0, h3(W0[:, 7168 + i * 1024:8192 + i * 1024]), group='gdn') for i in range(2)]
    dRTb = k.view(W0, h3(W0[:, 9216:10240]), group='gdn'); daT = k.view(W0, h3(W0[:, 10240:11264]), group='gdn'); dqd = k.view(W0, h3(W0[:, 11264:12288]), group='gdn')
    dSb = k.view(W1, W1[:, 0:2048].rearrange("p (h e) -> p h e", e=256), group='gdn')
    dvb = k.view(W1, W1[:, 2048:4096].rearrange("p (h e) -> p h e", e=256), group='gdn')
    dsq = k.view(W1, h3(W1[:, 2048:4096]), group='gdn', share=dvb)
    dkbe = k.view(W1, h3(W1[:, 4096:5120]), group='gdn'); dnwk = k.view(W1, h3(W1[:, 5120:6144]), group='gdn')
    dvn = k.view(W1, W1[:, 6144:8192].rearrange("p (h e) -> p h e", e=256), group='gdn')
    dkd = k.view(W1, h3(W1[:, 8192:9216]), group='gdn')
    don = k.view(W1, h3(W1[:, 9216:11264]), group='gdn')
    dY = k.view(W1, h3(W1[:, 11264:12288]), group='gdn'); dY2 = k.view(W1, h3(W1[:, 5120:6144]), group='gdn', share=dnwk)
    lmask = k.sb("lmask_s", [128, 14, 128], BF16)
    k.ld(lmask, lmask[:], lmk, lmk.t[:, :, :], q='pool')
    dS = k.view(Sst, Sst[:].rearrange("p (h e) -> p h e", e=256), group='gdn')
    dRTf = k.view(yc, h3(yc[:, 0:1024]), group='gdn')
    dob = k.view(yc, h3(yc[:]), group='gdn', share=dRTf)

    def gdn(li, j):
        W = gdn_in_w
        prefetch_w('dq', W, W.t[j, :, 0:4096], 8, 4096)
        norm_stage(li, 0)
        gemm(H, False, 8, W, W.t[j, :, 0:4096], 4096, 'fm', epi_store_fm(S1, 0), tag='dq')
        prefetch_w('dz', W, W.t[j, :, 4096:6144], 8, 2048)
        conv_stage(S1, S2, 32, lambda blk, tap: gcw_s[:, j, blk, tap - 3:tap - 2], None, False, [gcw_s], l2blocks=tuple(range(16)), qscale_blocks=tuple(range(8)))
        gemm(H, False, 8, W, W.t[j, :, 4096:6144], 2048, 'fm', epi_store_fm(S6, 0, AF.Silu), tag='dz')
        prefetch_w('da', W, W.t[j, :, 6144:6176], 8, 32)
        k.ld(dbc16, dbc16[:], gdn_dtb, gdn_dtb.t[j].partition_broadcast(128))
        k.ld(dnega, dnega[:], gdn_alog, gdn_alog.t[j].partition_broadcast(128))
        k.A(lambda e: e.activation(out=dnega[:], in_=dnega[:], func=AF.Exp), r=[dnega], w=[dnega])
        k.V(lambda e: e.tensor_scalar(out=dnega[:], in0=dnega[:], scalar1=-1.0, scalar2=None, op0=ALU.mult), r=[dnega], w=[dnega])

        def epi_ab(ps, c0, t, sub, last):
            ch = t * 4 + sub
            k.A(lambda e: e.activation(out=d_beta[:, ch, :], in_=ps[:, 16:32], func=AF.Sigmoid), r=[ps], w=[d_beta])
            k.V(lambda e: e.tensor_scalar(out=d_nbeta[:, ch, :], in0=d_beta[:, ch, :], scalar1=-1.0, scalar2=None, op0=ALU.mult), r=[d_beta], w=[d_nbeta])
            k.V(lambda e: e.tensor_tensor(out=d_lg[:, ch, :], in0=ps[:, 0:16], in1=dbc16[:], op=ALU.add), r=[ps, dbc16], w=[d_lg])
            k.A(lambda e: e.activation(out=d_lg[:, ch, :], in_=d_lg[:, ch, :], func=AF.Exp), r=[d_lg], w=[d_lg])
            k.A(lambda e: e.activation(out=d_lg[:, ch, :], in_=d_lg[:, ch, :], func=AF.Ln, bias=1.0, scale=1.0), r=[d_lg], w=[d_lg])
            k.V(lambda e: e.tensor_tensor(out=d_lg[:, ch, :], in0=d_lg[:, ch, :], in1=dnega[:], op=ALU.mult), r=[d_lg, dnega], w=[d_lg])
            k.V(lambda e: e.tensor_copy(out=d_lgb[:, ch, :], in_=d_lg[:, ch, :]), r=[d_lg], w=[d_lgb])
        k.enter('gdnx')
        gemm(H, False, 8, W, W.t[j, :, 6144:6176], 32, 'tm', epi_ab, tag='da')
        k.enter('gdn')
        pidx = 0
        for (o, L, kd) in SEQS:
            for d in (0, 1):
                if kd == 'p':
                    k.V(lambda e: e.memset(dS[:], 0.0), w=[dS])
                    k.V(lambda e: e.memset(dSb[:], 0.0), w=[dSb])
                else:
                    k.ld(dS, dS[:], st_gdn, st_gdn.t[j, d].rearrange("h d e -> d h e"))
                    k.A(lambda e: e.activation(out=dSb[:], in_=dS[:], func=AF.Copy), r=[dS], w=[dSb])
                ie = 127 if d == 0 else 0
                hs = slice(d * 8, (d + 1) * 8)
                order = chunk_order(L, d)

                def gdn_loads(n_, c):
                    t0 = o + c * 128
                    tsl = slice(t0, t0 + 128)
                    q_ = dq_fm[n_ % 2]; k_ = dk_fm[n_ % 2]; kt = dk_tm[n_ % 2]; v_ = dv_tm[n_ % 2]
                    k.ld(q_, q_[:], S2, S2.t[0:1024, tsl].rearrange("(h d) t -> d h t", d=128))
                    k.ld(k_, k_[:], S2, S2.t[1024:2048, tsl].rearrange("(h d) t -> d h t", d=128))
                    for q2 in range(2):
                        k.ld(kt, kt[:, q2 * 4:(q2 + 1) * 4, :].rearrange("p h t -> p (h t)"), S2, S2.t[1024 + q2 * 512:1024 + (q2 + 1) * 512, tsl], tr=True)
                    for q4 in range(4):
                        k.ld(v_, v_[:, q4 * 2:(q4 + 1) * 2, :].rearrange("p h e -> p (h e)"), S2, S2.t[2048 + q4 * 512:2048 + (q4 + 1) * 512, tsl], tr=True)
                    if d == 1:
                        k.ld(dof[n_ % 2], dof[n_ % 2][:], S5, S5.t[0:2048, tsl].rearrange("(b p) t -> p b t", p=128))
                        k.ld(dzs[n_ % 2], dzs[n_ % 2][:], S6, S6.t[0:2048, tsl].rearrange("(b p) t -> p b t", p=128))
                gdn_loads(0, order[0])
                for n_, c in enumerate(order):
                    if n_ + 1 < len(order):
                        gdn_loads(n_ + 1, order[n_ + 1])
                    t0 = o + c * 128; ch = t0 // 128
                    tsl = slice(t0, t0 + 128)
                    q_ = dq_fm[n_ % 2]; k_ = dk_fm[n_ % 2]; kt = dk_tm[n_ % 2]; v_ = dv_tm[n_ % 2]
                    lg = d_lg[:, ch, hs]; lgb = d_lgb[:, ch, hs]; beta = d_beta[:, ch, hs]; nbeta = d_nbeta[:, ch, hs]
                    k.G(lambda e: e.tensor_tensor(out=dET[:], in0=bc(lg, [128, 8, 128], 2), in1=bc(cmb[:, U_[d], :], [128, 8, 128], 1), op=ALU.mult), r=[d_lg, cmb], w=[dET])
                    k.G(lambda e: e.tensor_tensor(out=dEN[:], in0=bc(lg, [128, 8, 128], 2), in1=bc(cmb[:, V_[d], :], [128, 8, 128], 1), op=ALU.mult), r=[d_lg, cmb], w=[dEN])
                    for hf in range(2):
                        ps = k.psum()
                        k.M(lambda e: e.matmul(ps[:], cmb[:, V_[d], :], dET[:, hf * 4:(hf + 1) * 4, :].rearrange("p h t -> p (h t)"), start=True, stop=True), r=[cmb, dET], w=[ps])
                        k.A(lambda e: e.activation(out=dLT[:, hf * 4:(hf + 1) * 4, :], in_=h3(ps[:]), func=AF.Exp), r=[ps], w=[dLT])
                        ps2 = k.psum()
                        k.M(lambda e: e.matmul(ps2[:], cmb[:, U_[d], :], dEN[:, hf * 4:(hf + 1) * 4, :].rearrange("p h t -> p (h t)"), start=True, stop=True), r=[cmb, dEN], w=[ps2])
                        k.A(lambda e: e.activation(out=dLms[:, hf * 4:(hf + 1) * 4, :], in_=h3(ps2[:]), func=AF.Exp), r=[ps2], w=[dLms])
                        ps3 = k.psum()
                        k.M(lambda e: e.matmul(ps3[:], cmb[:, ONES, :], dET[:, hf * 4:(hf + 1) * 4, :].rearrange("p h t -> p (h t)"), start=True, stop=True), r=[cmb, dET], w=[ps3])
                        k.A(lambda e: e.activation(out=tmpf[:], in_=ps3[:], func=AF.Exp), r=[ps3], w=[tmpf])
                        k.V(lambda e: e.tensor_tensor(out=dqd[:, hf * 4:(hf + 1) * 4, :], in0=q_[:, hf * 4:(hf + 1) * 4, :], in1=h3(tmpf[:]), op=ALU.mult), r=[q_, tmpf], w=[dqd])
                    k.V(lambda e: e.tensor_copy(out=ecs[:, 24:32], in_=dLT[:, :, ie]), r=[dLT], w=[ecs])
                    k.V(lambda e: e.tensor_tensor(out=dLTm[:], in0=dLT[:], in1=bc(cmb[:, U_[d], :], [128, 8, 128], 1), op=ALU.mult), r=[dLT, cmb], w=[dLTm])
                    k.V(lambda e: e.tensor_tensor(out=dLms[:], in0=dLms[:], in1=bc(cmb[:, V_[d], :], [128, 8, 128], 1), op=ALU.mult), r=[dLms, cmb], w=[dLms])
                    ps = k.psum()
                    k.M(lambda e: e.matmul(ps[:, 0:8], cmb[:, U_[d], :], lgb, start=True, stop=True), r=[cmb, d_lgb], w=[ps])
                    k.M(lambda e: e.matmul(ps[:, 8:16], cmb[:, ONES, :], lgb, start=True, stop=True), r=[cmb, d_lgb], w=[ps])
                    k.A(lambda e: e.activation(out=ecs[:, 0:16], in_=ps[:, 0:16], func=AF.Exp), r=[ps], w=[ecs])
                    k.V(lambda e: e.tensor_tensor(out=ecs[:, 16:24], in0=ecs[:, 0:8], in1=beta, op=ALU.mult), r=[ecs, d_beta], w=[ecs])
                    mN, mT, OffN, OffT = dB[0], dA[0], dB[1], dA[1]
                    dD = dLms
                    for hf in range(2):
                        ps = k.psum()
                        for h4 in range(4):
                            h = hf * 4 + h4
                            k.M(lambda e: e.matmul(ps[:, h4 * 128:(h4 + 1) * 128], k_[:, h, :], k_[:, h, :], start=True, stop=True), r=[k_], w=[ps])
                        for h4 in range(4):
                            h = hf * 4 + h4
                            k.V(lambda e: e.scalar_tensor_tensor(out=mN[:, h, :], in0=ps[:, h4 * 128:(h4 + 1) * 128], scalar=d_beta[:, ch, d * 8 + h:d * 8 + h + 1], in1=dLms[:, h, :], op0=ALU.mult, op1=ALU.mult),
                                r=[ps, d_beta, dLms], w=[mN])
                    for hf in range(2):
                        ps = k.psum()
                        for h4 in range(4):
                            h = hf * 4 + h4
                            k.M(lambda e: e.matmul(ps[:, h4 * 128:(h4 + 1) * 128], mN[:, h, :], cmb[:, ID, :], start=True, stop=True), r=[mN, cmb], w=[ps])
                        k.A(lambda e: e.activation(out=mT[:, hf * 4:(hf + 1) * 4, :], in_=h3(ps[:]), func=AF.Copy), r=[ps], w=[mT])
                    MN = lambda lv: lmask[:, (lv if d == 0 else 7 + lv), :]
                    MT_ = lambda lv: lmask[:, (7 + lv if d == 0 else lv), :]
                    k.V(lambda e: e.tensor_tensor(out=OffN[:], in0=mN[:], in1=bc(MN(0), [128, 8, 128], 1), op=ALU.mult), r=[mN, lmask], w=[OffN])
                    k.V(lambda e: e.tensor_tensor(out=dD[:], in0=bc(cmb[:, ID, :], [128, 8, 128], 1), in1=OffN[:], op=ALU.subtract), r=[cmb, OffN], w=[dD])
                    k.V(lambda e: e.tensor_tensor(out=OffT[:], in0=mT[:], in1=bc(MT_(0), [128, 8, 128], 1), op=ALU.mult), r=[mT, lmask], w=[OffT])
                    k.V(lambda e: e.tensor_tensor(out=dRTb[:], in0=bc(cmb[:, ID, :], [128, 8, 128], 1), in1=OffT[:], op=ALU.subtract), r=[cmb, OffT], w=[dRTb])
                    for lv in range(1, 7):
                        k.V(lambda e: e.tensor_tensor(out=OffN[:], in0=mN[:], in1=bc(MN(lv), [128, 8, 128], 1), op=ALU.mult), r=[mN, lmask], w=[OffN])
                        k.G(lambda e: e.tensor_tensor(out=OffT[:], in0=mT[:], in1=bc(MT_(lv), [128, 8, 128], 1), op=ALU.mult), r=[mT, lmask], w=[OffT])
                        for hf in range(2):
                            ps = k.psum()
                            for h4 in range(4):
                                h = hf * 4 + h4
                                k.M(lambda e: e.matmul(ps[:, h4 * 128:(h4 + 1) * 128], OffT[:, h, :], dD[:, h, :], start=True, stop=True), r=[OffT, dD], w=[ps])
                            k.A(lambda e: e.activation(out=dY[:, hf * 4:(hf + 1) * 4, :], in_=h3(ps[:]), func=AF.Copy), r=[ps], w=[dY])
                            ps2 = k.psum()
                            for h4 in range(4):
                                h = hf * 4 + h4
                                k.M(lambda e: e.matmul(ps2[:, h4 * 128:(h4 + 1) * 128], OffN[:, h, :], dRTb[:, h, :], start=True, stop=True), r=[OffN, dRTb], w=[ps2])
                            k.A(lambda e: e.activation(out=dY2[:, hf * 4:(hf + 1) * 4, :], in_=h3(ps2[:]), func=AF.Copy), r=[ps2], w=[dY2])
                        pX = [k.psum(), k.psum()]; pX2 = [k.psum(), k.psum()]
                        for hf in range(2):
                            for h4 in range(4):
                                h = hf * 4 + h4
                                k.M(lambda e: e.matmul(pX[hf][:, h4 * 128:(h4 + 1) * 128], dRTb[:, h, :], dY[:, h, :], start=True, stop=True), r=[dRTb, dY], w=[pX[hf]])
                            for h4 in range(4):
                                h = hf * 4 + h4
                                k.M(lambda e: e.matmul(pX2[hf][:, h4 * 128:(h4 + 1) * 128], dD[:, h, :], dY2[:, h, :], start=True, stop=True), r=[dD, dY2], w=[pX2[hf]])
                        for hf in range(2):
                            k.V(lambda e: e.tensor_tensor(out=dD[:, hf * 4:(hf + 1) * 4, :], in0=dD[:, hf * 4:(hf + 1) * 4, :], in1=h3(pX[hf][:]), op=ALU.subtract), r=[dD, pX[hf]], w=[dD])
                            k.V(lambda e: e.tensor_tensor(out=dRTb[:, hf * 4:(hf + 1) * 4, :], in0=dRTb[:, hf * 4:(hf + 1) * 4, :], in1=h3(pX2[hf][:]), op=ALU.subtract), r=[dRTb, pX2[hf]], w=[dRTb])
                    k.G(lambda e: e.tensor_tensor(out=dvb[:], in0=v_[:], in1=bc(beta, [128, 8, 256], 2), op=ALU.mult), r=[v_, d_beta], w=[dvb])
                    k.G(lambda e: e.tensor_tensor(out=dkbe[:], in0=kt[:], in1=bc(ecs[:, 16:24], [128, 8, 128], 2), op=ALU.mult), r=[kt, ecs], w=[dkbe])
                    k.G(lambda e: e.tensor_tensor(out=dkd[:], in0=kt[:], in1=bc(ecs[:, 24:32], [128, 8, 128], 2), op=ALU.mult), r=[kt, ecs], w=[dkd])
                    for hf in range(2):
                        ps = k.psum()
                        for h4 in range(4):
                            h = hf * 4 + h4
                            k.M(lambda e: e.matmul(ps[:, h4 * 128:(h4 + 1) * 128], dkbe[:, h, :], dRTb[:, h, :], start=True, stop=True), r=[dkbe, dRTb], w=[ps])
                        k.A(lambda e: e.activation(out=dnwk[:, hf * 4:(hf + 1) * 4, :], in_=h3(ps[:]), func=AF.Copy, scale=-1.0), r=[ps], w=[dnwk])
                    for hp in range(4):
                        ps = k.psum()
                        for h2 in range(2):
                            h = hp * 2 + h2
                            k.M(lambda e: e.matmul(ps[:, h2 * 256:(h2 + 1) * 256], dRTb[:, h, :], dvb[:, h, :], start=True, stop=False), r=[dRTb, dvb], w=[ps])
                            k.M(lambda e: e.matmul(ps[:, h2 * 256:(h2 + 1) * 256], dnwk[:, h, :], dSb[:, h, :], start=False, stop=True), r=[dnwk, dSb], w=[ps])
                        k.A(lambda e: e.activation(out=dvn[:, hp * 2:(hp + 1) * 2, :], in_=ps[:].rearrange("p (h e) -> p h e", e=256), func=AF.Copy), r=[ps], w=[dvn])
                    for hf in range(2):
                        ps = k.psum()
                        for h4 in range(4):
                            h = hf * 4 + h4
                            k.M(lambda e: e.matmul(ps[:, h4 * 128:(h4 + 1) * 128], k_[:, h, :], q_[:, h, :], start=True, stop=True), r=[k_, q_], w=[ps])
                        k.V(lambda e: e.tensor_tensor(out=daT[:, hf * 4:(hf + 1) * 4, :], in0=h3(ps[:]), in1=dLTm[:, hf * 4:(hf + 1) * 4, :], op=ALU.mult), r=[ps, dLTm], w=[daT])
                    pso = [k.psum() for _ in range(4)]
                    for h in range(8):
                        for eb in range(2):
                            blk = h * 2 + eb
                            po = pso[blk // 4]; cs_ = slice((blk % 4) * 128, (blk % 4 + 1) * 128)
                            k.M(lambda e: e.matmul(po[:, cs_], dSb[:, h, eb * 128:(eb + 1) * 128], dqd[:, h, :], start=True, stop=False), r=[dSb, dqd], w=[po])
                            k.M(lambda e: e.matmul(po[:, cs_], dvn[:, h, eb * 128:(eb + 1) * 128], daT[:, h, :], start=False, stop=True), r=[dvn, daT], w=[po])
                    if d == 0:
                        for q4 in range(4):
                            k.A(lambda e: e.activation(out=don[:, q4 * 4:(q4 + 1) * 4, :], in_=h3(pso[q4][:]), func=AF.Copy), r=[pso[q4]], w=[don])
                        k.stq(S5, S5.t[0:2048, tsl].rearrange("(b p) t -> p b t", p=128), don, don[:])
                    else:
                        of_ = dof[n_ % 2]; zs_ = dzs[n_ % 2]
                        for q4 in range(4):
                            k.V(lambda e: e.tensor_tensor(out=dob[:, q4 * 4:(q4 + 1) * 4, :], in0=h3(pso[q4][:]), in1=of_[:, q4 * 4:(q4 + 1) * 4, :], op=ALU.add), r=[pso[q4], of_], w=[dob])
                    for h in range(8):
                        pst = k.psum()
                        k.M(lambda e: e.matmul(pst[:, 0:256], dkd[:, h, :], dvn[:, h, :], start=True, stop=True), r=[dkd, dvn], w=[pst])
                        k.V(lambda e: e.scalar_tensor_tensor(out=dS[:, h, :], in0=dS[:, h, :], scalar=ecs[:, 8 + h:9 + h], in1=pst[:, 0:256], op0=ALU.mult, op1=ALU.add), r=[dS, ecs, pst], w=[dS])
                    k.A(lambda e: e.activation(out=dSb[:], in_=dS[:], func=AF.Copy), r=[dS], w=[dSb])
                    if d == 1:
                        k.G(lambda e: e.tensor_tensor(out=dsq[:], in0=dob[:], in1=dob[:], op=ALU.mult), r=[dob], w=[dsq])
                        for hf in range(2):
                            psn = k.psum()
                            for h4 in range(4):
                                h = hf * 4 + h4
                                for eb in range(2):
                                    k.M(lambda e: e.matmul(psn[:, h4 * 128:(h4 + 1) * 128], cmb[:, ONES, :], dsq[:, h * 2 + eb, :], start=(eb == 0), stop=(eb == 1)), r=[cmb, dsq], w=[psn])
                            k.A(lambda e: e.activation(out=rsb[:], in_=psn[:], func=AF.Sqrt, bias=EPS, scale=1.0 / 256), r=[psn], w=[rsb])
                            k.V(lambda e: e.reciprocal(out=rsb[:], in_=rsb[:]), r=[rsb], w=[rsb])
                            for eb in range(2):
                                gv_ = dob[:, hf * 8:(hf + 1) * 8, :].rearrange("p (h b) t -> p h b t", b=2)[:, :, eb, :]
                                k.V(lambda e: e.scalar_tensor_tensor(out=gv_, in0=gv_, scalar=dnw_s[:, j, eb:eb + 1], in1=h3(rsb[:]), op0=ALU.mult, op1=ALU.mult), r=[dob, dnw_s, rsb], w=[dob])
                        k.V(lambda e: e.tensor_tensor(out=don[:], in0=dob[:], in1=zs_[:], op=ALU.mult), r=[dob, zs_], w=[don])
                        k.stq(S5, S5.t[0:2048, tsl].rearrange("(b p) t -> p b t", p=128), don, don[:])
                if kd == 'p':
                    k.stq(ns_gdn, ns_gdn.t[pidx, j, d].rearrange("h d e -> d h e"), dS, dS[:])
            if kd == 'p':
                pidx += 1
        k.leave('gdn'); k.leave('gdnx')
        gemm(S5, False, 16, gdn_out_w, gdn_out_w.t[j, :, :], 1024, 'fm', wsplit=True, **resid_hooks(li, 2))

    wdec = k.sb("wdec", [128, 32], F32)

    for (li, kind, j) in LAYERS:
        if kind == 'ssd':
            ssd(li, j)
        elif kind == 'gla':
            gla(li, j)
        elif kind == 'gdn':
            gdn(li, j)
        elif kind == 'none':
            pass
        else:
            raise NotImplementedError(kind)
        if kind != 'none' or True:
            ffn(li)
    norm_stage(0, 0, final=True)
    outs = [yT, X, ns_ssm, ns_gla, ns_gdn, S1, S2, S3, S4, S5, H]
    k.P.wait_all('sp', [o.b for o in outs])
    k.P.run()
    st.close()
    return nc, k.P.ninst


def _consts():
    i = np.arange(128)
    kk, ii = np.meshgrid(i, i, indexing='ij')
    cm = np.zeros((128, 10, 128), np.float32)
    cm[:, 0] = (kk <= ii); cm[:, 1] = (kk >= ii); cm[:, 2] = (kk > ii); cm[:, 3] = (kk < ii)
    cm[:, 4] = (kk == ii); cm[:, 5] = 1.0
    cm[:, 6] = (kk <= ii) / -16.0; cm[:, 7] = (kk >= ii) / -16.0; cm[:, 8] = (kk > ii) / -16.0; cm[:, 9] = (kk < ii) / -16.0
    return cm


def _fm(v, nblk):
    v = np.asarray(v, np.float32)
    lead = v.shape[:-1]
    v = v.reshape(*lead, nblk, 128)
    return np.ascontiguousarray(np.moveaxis(v, -1, 0))


def prep_common(w):
    d = {}
    d['cmask'] = _consts()
    ii, jj = np.meshgrid(np.arange(128), np.arange(128), indexing='ij')
    lm = np.zeros((128, 14, 128), np.float32)
    for lv in range(7):
        b = 1 << lv
        m = ((ii // (2 * b)) == (jj // (2 * b))) & ((ii % (2 * b)) >= b) & ((jj % (2 * b)) < b)
        lm[:, lv] = m; lm[:, 7 + lv] = m.T
    d['lmask'] = lm
    d['nmw'] = _fm(w['norm_mix_w'], 8); d['nfw'] = _fm(w['norm_ffn_w'], 8); d['fnw'] = _fm(w['final_norm_w'], 8)
    d['ada_w'] = w['ada_w']; d['ada_b'] = _fm(w['ada_b'], 48)
    d['ffn_in_w'] = w['ffn_in_w']; d['ffn_out_w'] = w['ffn_out_w']
    d['ffn_cw'] = _fm(np.moveaxis(np.asarray(w['ffn_conv_w']).reshape(4, 9, 2816), 1, 2).reshape(4, 2816 * 9).reshape(4, 22, 128, 9).transpose(0, 1, 3, 2).reshape(4, 22, 9, 128), 1)[..., 0] if False else None
    cw = np.asarray(w['ffn_conv_w'], np.float32).reshape(4, 9, 22, 128)
    d['ffn_cw'] = np.ascontiguousarray(cw.transpose(3, 0, 2, 1))
    d['ffn_cb'] = _fm(w['ffn_conv_b'], 22)
    d['ssm_in_w'] = w['ssm_in_w']
    cw = np.asarray(w['ssm_conv_w'], np.float32).reshape(2, 3, 32, 128)
    d['ssm_cw'] = np.ascontiguousarray(cw.transpose(3, 0, 2, 1))
    d['ssm_cb'] = _fm(w['ssm_conv_b'], 32)
    d['ssm_dtb'] = np.asarray(w['ssm_dt_bias'], np.float32).reshape(2, 1, 64)
    d['ssm_alog'] = np.asarray(w['ssm_a_log'], np.float32).reshape(2, 1, 64)
    d['ssm_d'] = np.asarray(w['ssm_d'], np.float32).reshape(2, 1, 32)
    d['ssm_nw'] = np.asarray(w['ssm_norm_w'], np.float32).reshape(2, 1, 2048)
    d['ssm_out_w'] = w['ssm_out_w']
    d['gla_in_w'] = w['gla_in_w']
    w1 = np.zeros((1, 1024, 128), np.float32)
    g1 = np.asarray(w['gla_gate_w1'], np.float32)
    w1[:, :, 0:16] = g1[:, 0]; w1[:, :, 32:48] = g1[:, 1]
    d['gla_w1'] = w1
    w2 = np.zeros((1, 64, 2, 512), np.float32)
    g2 = np.asarray(w['gla_gate_w2'], np.float32)
    w2[:, 0:16, 0] = g2[:, 0]; w2[:, 32:48, 1] = g2[:, 1]
    d['gla_w2'] = w2.reshape(1, 64, 1024)
    d['gla_gb'] = np.asarray(w['gla_gate_b'], np.float32).reshape(1, 1, 1024)
    d['gla_nw'] = _fm(w['gla_norm_w'], 2); d['gla_out_w'] = w['gla_out_w']
    d['gdn_in_w'] = w['gdn_in_w']
    cw = np.asarray(w['gdn_conv_w'], np.float32).reshape(1, 3, 32, 128)
    d['gdn_cw'] = np.ascontiguousarray(cw.transpose(3, 0, 2, 1))
    d['gdn_dtb'] = np.asarray(w['gdn_dt_bias'], np.float32).reshape(1, 1, 16)
    d['gdn_alog'] = np.asarray(w['gdn_a_log'], np.float32).reshape(1, 1, 16)
    d['gdn_nw'] = _fm(w['gdn_norm_w'], 2); d['gdn_out_w'] = w['gdn_out_w']
    return {k_: np.ascontiguousarray(np.asarray(v, np.float32)) for k_, v in d.items()}


FULL_SEQS = [(0, 256, 'p'), (256, 256, 'p'), (512, 4096, 's')]
FULL_LAYERS = [(0, 'ssd', 0), (1, 'gla', 0), (2, 'gdn', 0), (3, 'ssd', 1)]


def kernel(**inp):
    w = {k_: np.asarray(v) for k_, v in inp.items()}
    common = prep_common(w)
    nc, ninst = build_program(FULL_SEQS, FULL_LAYERS)
    xp = np.asarray(w['x_prompt'], np.float32); xs = np.asarray(w['x_sample'], np.float32)
    in_maps = []
    for core in range(8):
        b = core % 2
        xcat = np.concatenate([xp[2 * core], xp[2 * core + 1], xs[b]], axis=0)
        m = dict(common)
        m['xT'] = np.ascontiguousarray(xcat.T)
        cond = np.stack([np.asarray(w['c_ctx'], np.float32), np.asarray(w['c'], np.float32)[b]], axis=1)
        m['condT'] = np.ascontiguousarray(cond.reshape(8, 128, 2).transpose(1, 0, 2))
        ss = np.asarray(w['state_ssm'], np.float32)[b]
        m['st_ssm'] = np.ascontiguousarray(ss.transpose(0, 1, 3, 2, 4).reshape(2, 2, 128, 2048))
        m['st_gla'] = np.ascontiguousarray(np.asarray(w['state_gla'], np.float32)[b])
        m['st_gdn'] = np.ascontiguousarray(np.asarray(w['state_gdn'], np.float32)[b])
        in_maps.append(m)
    res = run_bass_kernel_spmd(nc, in_maps, core_ids=list(range(8)))
    y_prompt = np.zeros((16, 256, 1024), np.float32); y_sample = np.zeros((2, 4096, 1024), np.float32)
    n_ssm = np.zeros((16, 2, 2, 32, 128, 64), np.float32); n_gla = np.zeros((16, 1, 2, 4, 128, 256), np.float32)
    n_gdn = np.zeros((16, 1, 2, 8, 128, 256), np.float32)
    for core in range(8):
        r = res.results[core]
        y = np.asarray(r['yT']).T
        y_prompt[2 * core] = y[0:256]; y_prompt[2 * core + 1] = y[256:512]
        if core < 2:
            y_sample[core] = y[512:]
        s_ = np.asarray(r['ns_ssm']).reshape(2, 2, 2, 128, 32, 64).transpose(0, 1, 2, 4, 3, 5)
        n_ssm[2 * core:2 * core + 2] = s_
        n_gla[2 * core:2 * core + 2] = np.asarray(r['ns_gla'])
        n_gdn[2 * core:2 * core + 2] = np.asarray(r['ns_gdn'])
    return (y_prompt, y_sample, n_ssm, n_gla, n_gdn)
```
